# Optimizing a Trainium2 kernel written in Bass

```python
import jax, jax.numpy as jnp
from jax import lax
import numpy as np

D_MODEL = 1024
BATCH = 16
SEQ = 2048
DEPTH = 1

MLA_HEADS = 8
MLA_NOPE_DIM = 64
MLA_ROPE_DIM = 32
MLA_V_DIM = 64
MLA_Q_RANK = 384
MLA_KV_RANK = 256
MLA_QK_DIM = MLA_NOPE_DIM + MLA_ROPE_DIM
ROPE_THETA = 10000.0
Q_BLOCK = 128
MLSTM_HEADS = 4
MLSTM_HEAD_DIM = 128
MLSTM_WIDTH = MLSTM_HEADS * MLSTM_HEAD_DIM
CONV_WIDTH = 4
CHUNK = 64
D_FF = 2816
N_BRANCHES = 2
N_MOD = 9
EPS = 1e-6
IN_SPLITS = (MLA_Q_RANK, MLA_KV_RANK + MLA_ROPE_DIM, MLSTM_WIDTH, MLSTM_WIDTH, MLSTM_WIDTH,
             MLSTM_HEADS, MLSTM_HEADS, N_BRANCHES * D_MODEL)
D_IN = sum(IN_SPLITS)

kernel_name = "hybrid_mla_mlstm_macaron_adaln"


def rms_norm(x, g):
    xf = x.astype(jnp.float32)
    y = xf * lax.rsqrt(jnp.mean(xf * xf, axis=-1, keepdims=True) + EPS)
    return (y * g.astype(jnp.float32)).astype(x.dtype)


def modulate(x, shift, scale):
    return x * (1.0 + scale[:, None, :]) + shift[:, None, :]


def swiglu(x, w_gate, w_up, w_down):
    return (jax.nn.silu(x @ w_gate) * (x @ w_up)) @ w_down


def rope_angles(positions, dim):
    inv_freq = ROPE_THETA ** (-jnp.arange(0, dim, 2, dtype=jnp.float32) / dim)
    ang = positions.astype(jnp.float32)[..., None] * inv_freq
    return jnp.cos(ang), jnp.sin(ang)


def apply_rope(x, cos, sin):
    xf = x.astype(jnp.float32)
    x1, x2 = jnp.split(xf, 2, axis=-1)
    return jnp.concatenate([x1 * cos - x2 * sin, x2 * cos + x1 * sin], axis=-1).astype(x.dtype)


def mla_attention(q_lat, kv_lat, positions, q_a_norm, w_q_b, kv_a_norm, w_kv_b):
    B, S, _ = q_lat.shape
    H = MLA_HEADS
    q = (rms_norm(q_lat, q_a_norm) @ w_q_b).reshape(B, S, H, MLA_QK_DIM)
    q_nope, q_pe = q[..., :MLA_NOPE_DIM], q[..., MLA_NOPE_DIM:]
    c_kv, k_pe = kv_lat[..., :MLA_KV_RANK], kv_lat[..., MLA_KV_RANK:]
    kv = (rms_norm(c_kv, kv_a_norm) @ w_kv_b).reshape(B, S, H, MLA_NOPE_DIM + MLA_V_DIM)
    k_nope, v = kv[..., :MLA_NOPE_DIM], kv[..., MLA_NOPE_DIM:]
    cos, sin = rope_angles(positions, MLA_ROPE_DIM)
    q_pe = apply_rope(q_pe, cos[:, :, None, :], sin[:, :, None, :])
    k_pe = apply_rope(k_pe, cos, sin)
    k = jnp.concatenate([k_nope, jnp.broadcast_to(k_pe[:, :, None, :], (B, S, H, MLA_ROPE_DIM))], axis=-1)
    q = jnp.concatenate([q_nope, q_pe], axis=-1) * (MLA_QK_DIM ** -0.5)
    nb = S // Q_BLOCK
    qb = q.reshape(B, nb, Q_BLOCK, H, MLA_QK_DIM).transpose(1, 0, 3, 2, 4)
    k_pos = jnp.arange(S)

    def block(args):
        q_blk, blk = args
        s = jnp.einsum('bhqd,bkhd->bhqk', q_blk, k).astype(jnp.float32)
        q_pos = blk * Q_BLOCK + jnp.arange(Q_BLOCK)
        mask = k_pos[None, :] <= q_pos[:, None]
        s = jnp.where(mask, s, jnp.finfo(jnp.float32).min)
        p = jax.nn.softmax(s, axis=-1).astype(v.dtype)
        return jnp.einsum('bhqk,bkhd->bqhd', p, v)

    o = lax.map(block, (qb, jnp.arange(nb)))
    return o.transpose(1, 0, 2, 3, 4).reshape(B, S, H * MLA_V_DIM)


def causal_depthwise_conv(x, w, b):
    y = lax.conv_general_dilated(x, w.astype(x.dtype)[:, None, :], window_strides=(1,),
                                 padding=[(CONV_WIDTH - 1, 0)], dimension_numbers=('NWC', 'WIO', 'NWC'),
                                 feature_group_count=x.shape[-1])
    return y + b


def mlstm(x_m, v_in, o_pre, i_pre, f_pre, conv_w, conv_b, w_q_m, w_k_m, b_i, b_f, head_norm):
    B, S, _ = x_m.shape
    H, dh, L = MLSTM_HEADS, MLSTM_HEAD_DIM, CHUNK
    nc = S // L
    xc = jax.nn.silu(causal_depthwise_conv(x_m, conv_w, conv_b)).reshape(B, S, H, dh)
    q = jnp.einsum('bshd,hde->bhse', xc, w_q_m).astype(jnp.float32)
    k = jnp.einsum('bshd,hde->bhse', xc, w_k_m).astype(jnp.float32) * (dh ** -0.5)
    v = v_in.reshape(B, S, H, dh).transpose(0, 2, 1, 3).astype(jnp.float32)
    log_i = (i_pre + b_i).astype(jnp.float32).transpose(0, 2, 1)
    log_f = jax.nn.log_sigmoid((f_pre + b_f).astype(jnp.float32)).transpose(0, 2, 1)
    q = q.reshape(B, H, nc, L, dh)
    k = k.reshape(B, H, nc, L, dh)
    v = v.reshape(B, H, nc, L, dh)
    log_i = log_i.reshape(B, H, nc, L)
    bcum = jnp.cumsum(log_f.reshape(B, H, nc, L), axis=-1)
    g = bcum[..., -1]
    causal = jnp.tril(jnp.ones((L, L), dtype=bool))
    d_log = bcum[..., :, None] - bcum[..., None, :] + log_i[..., None, :]
    d_log = jnp.where(causal, d_log, -jnp.inf)
    w_state = g[..., None] - bcum + log_i
    m_loc = jnp.max(w_state, axis=-1)
    e_state = jnp.exp(w_state - m_loc[..., None])
    c_loc = jnp.einsum('bhcl,bhcld,bhcle->bhcde', e_state, v, k)
    n_loc = jnp.einsum('bhcl,bhcle->bhce', e_state, k)

    def step(carry, inp):
        C, n, m = carry
        c_l, n_l, m_l, g_c = inp
        m_new = jnp.maximum(g_c + m, m_l)
        a = jnp.exp(g_c + m - m_new)
        bb = jnp.exp(m_l - m_new)
        C_new = a[..., None, None] * C + bb[..., None, None] * c_l
        n_new = a[..., None] * n + bb[..., None] * n_l
        return (C_new, n_new, m_new), (C, n, m)

    init = (jnp.zeros((B, H, dh, dh), jnp.float32), jnp.zeros((B, H, dh), jnp.float32),
            jnp.zeros((B, H), jnp.float32))
    xs = (c_loc.transpose(2, 0, 1, 3, 4), n_loc.transpose(2, 0, 1, 3), m_loc.transpose(2, 0, 1), g.transpose(2, 0, 1))
    _, (C_prev, n_prev, m_prev) = lax.scan(step, init, xs)
    C_prev = C_prev.transpose(1, 2, 0, 3, 4)
    n_prev = n_prev.transpose(1, 2, 0, 3)
    m_prev = m_prev.transpose(1, 2, 0)
    inter_log = bcum + m_prev[..., None]
    m_t = jnp.maximum(inter_log, jnp.max(d_log, axis=-1))
    inter_w = jnp.exp(inter_log - m_t)
    s_mat = jnp.exp(d_log - m_t[..., None]) * jnp.einsum('bhcte,bhcse->bhcts', q, k)
    num = inter_w[..., None] * jnp.einsum('bhcde,bhcte->bhctd', C_prev, q) + jnp.einsum('bhcts,bhcsd->bhctd', s_mat, v)
    den = inter_w * jnp.einsum('bhce,bhcte->bhct', n_prev, q) + jnp.sum(s_mat, axis=-1)
    h = num / jnp.maximum(jnp.abs(den), jnp.exp(-m_t))[..., None]
    h = h.reshape(B, H, S, dh).transpose(0, 2, 1, 3)
    h = jax.nn.sigmoid(o_pre.astype(jnp.float32)).reshape(B, S, H, dh) * h
    h = h * lax.rsqrt(jnp.mean(h * h, axis=-1, keepdims=True) + EPS) * head_norm.astype(jnp.float32)
    return h.reshape(B, S, H * dh).astype(x_m.dtype)


def setup_inputs(seed: int = 0) -> dict:
    key = jax.random.key(seed)
    ks = iter(jax.random.split(key, 48))

    def normal(shape, s):
        return jax.random.normal(next(ks), shape, jnp.float32) * s

    def dense(shape, extra=1.0):
        return normal(shape, extra * shape[-2] ** -0.5)

    def gain(shape):
        return 1.0 + normal(shape, 0.05)

    Dp = DEPTH
    x = normal((BATCH, SEQ, D_MODEL), 1.0)
    c = normal((BATCH, D_MODEL), 1.0)
    offs = jax.random.randint(next(ks), (BATCH, 1), 0, 4096, dtype=jnp.int32)
    positions = (jnp.arange(SEQ, dtype=jnp.int32)[None, :] + offs).astype(jnp.int32)
    b_f = jnp.linspace(3.0, 6.0, MLSTM_HEADS, dtype=jnp.float32)[None, :] + normal((Dp, MLSTM_HEADS), 0.1)
    return {
        "x": x, "c": c, "positions": positions,
        "w_ada": dense((Dp, D_MODEL, N_MOD * D_MODEL), 0.5),
        "b_ada": normal((Dp, N_MOD * D_MODEL), 0.02),
        "norm_ff1": gain((Dp, D_MODEL)),
        "ff1_w_gate": dense((Dp, D_MODEL, D_FF)),
        "ff1_w_up": dense((Dp, D_MODEL, D_FF)),
        "ff1_w_down": dense((Dp, D_FF, D_MODEL)),
        "norm_mix": gain((Dp, D_MODEL)),
        "w_in": dense((Dp, D_MODEL, D_IN)),
        "q_a_norm": gain((Dp, MLA_Q_RANK)),
        "w_q_b": dense((Dp, MLA_Q_RANK, MLA_HEADS * MLA_QK_DIM)),
        "kv_a_norm": gain((Dp, MLA_KV_RANK)),
        "w_kv_b": dense((Dp, MLA_KV_RANK, MLA_HEADS * (MLA_NOPE_DIM + MLA_V_DIM))),
        "conv_w": dense((Dp, CONV_WIDTH, MLSTM_WIDTH)),
        "conv_b": normal((Dp, MLSTM_WIDTH), 0.02),
        "w_q_m": dense((Dp, MLSTM_HEADS, MLSTM_HEAD_DIM, MLSTM_HEAD_DIM)),
        "w_k_m": dense((Dp, MLSTM_HEADS, MLSTM_HEAD_DIM, MLSTM_HEAD_DIM)),
        "b_i": normal((Dp, MLSTM_HEADS), 0.1),
        "b_f": b_f,
        "mlstm_norm": gain((Dp, MLSTM_HEADS, MLSTM_HEAD_DIM)),
        "w_mla_out": dense((Dp, MLA_HEADS * MLA_V_DIM, D_MODEL)),
        "w_mlstm_out": dense((Dp, MLSTM_WIDTH, D_MODEL)),
        "w_o": dense((Dp, D_MODEL, D_MODEL)),
        "norm_ff2": gain((Dp, D_MODEL)),
        "ff2_w_gate": dense((Dp, D_MODEL, D_FF)),
        "ff2_w_up": dense((Dp, D_MODEL, D_FF)),
        "ff2_w_down": dense((Dp, D_FF, D_MODEL)),
        "norm_final": gain((D_MODEL,)),
    }


def reference(x, c, positions, w_ada, b_ada, norm_ff1, ff1_w_gate, ff1_w_up, ff1_w_down,
              norm_mix, w_in, q_a_norm, w_q_b, kv_a_norm, w_kv_b, conv_w, conv_b, w_q_m, w_k_m,
              b_i, b_f, mlstm_norm, w_mla_out, w_mlstm_out, w_o, norm_ff2, ff2_w_gate, ff2_w_up,
              ff2_w_down, norm_final):
    B, S, D = x.shape
    split_idx = np.cumsum(IN_SPLITS)[:-1].tolist()
    silu_c = jax.nn.silu(c)
    h = x
    for l in range(DEPTH):
        mod = silu_c @ w_ada[l] + b_ada[l]
        sh1, sc1, gt1, sh2, sc2, gt2, sh3, sc3, gt3 = jnp.split(mod, N_MOD, axis=-1)
        u = modulate(rms_norm(h, norm_ff1[l]), sh1, sc1)
        h = h + 0.5 * gt1[:, None, :] * swiglu(u, ff1_w_gate[l], ff1_w_up[l], ff1_w_down[l])
        u = modulate(rms_norm(h, norm_mix[l]), sh2, sc2)
        proj = u @ w_in[l]
        q_lat, kv_lat, x_m, v_m, o_m, i_m, f_m, gate_pre = jnp.split(proj, split_idx, axis=-1)
        y_a = mla_attention(q_lat, kv_lat, positions, q_a_norm[l], w_q_b[l], kv_a_norm[l], w_kv_b[l]) @ w_mla_out[l]
        y_b = mlstm(x_m, v_m, o_m, i_m, f_m, conv_w[l], conv_b[l], w_q_m[l], w_k_m[l], b_i[l], b_f[l],
                    mlstm_norm[l]) @ w_mlstm_out[l]
        gates = jax.nn.sigmoid(gate_pre).reshape(B, S, N_BRANCHES, D)
        y = gates[:, :, 0, :] * y_a + gates[:, :, 1, :] * y_b
        h = h + gt2[:, None, :] * (y @ w_o[l])
        u = modulate(rms_norm(h, norm_ff2[l]), sh3, sc3)
        h = h + 0.5 * gt3[:, None, :] * swiglu(u, ff2_w_gate[l], ff2_w_up[l], ff2_w_down[l])
    return rms_norm(h, norm_final)
```

```python
import contextlib
import numpy as np
import concourse.bass as bass
import concourse.mybir as mybir
from concourse.bass_utils import run_bass_kernel_spmd

F32 = mybir.dt.float32
BF16 = mybir.dt.bfloat16
I32 = mybir.dt.int32
ACT = mybir.ActivationFunctionType
ALU = mybir.AluOpType

P = 128
D = 1024
KC = 8
DFF = 2816
FC = 22
TG = 512
NTT = 4
S = 2048
NG = S // TG
NSEQ = 2
NH = 8
DIN = 4264
EPS = 1e-6
SEM_CAP = 24000
C1 = 6.28125
C2 = float(2 * np.pi - 6.28125)


class Eng:
    def __init__(self, fw, name, h, nsem):
        self.name = name
        self.h = h
        self.sems = [fw.es.enter_context(fw.nc.semaphore(f"s_{name}{i}")) for i in range(nsem)]
        self.n = 0
        self.seen = {}
        self.seen_d = {}

    def sem_val(self, seq):
        i = (seq - 1) // SEM_CAP
        return self.sems[i], (seq - 1) % SEM_CAP + 1


class Buf:
    def __init__(self, ap, name=""):
        self.ap = ap
        self.name = name
        self.w = None
        self.r = []
        self.dsem = None
        self.dval = 0

    def __getitem__(self, k):
        return self.ap[k]


class FW:
    def __init__(self, nc):
        self.nc = nc
        self.es = contextlib.ExitStack()
        self.es.__enter__()
        self.pe = Eng(self, "pe", nc.tensor, 3)
        self.act = Eng(self, "act", nc.scalar, 3)
        self.dve = Eng(self, "dve", nc.vector, 4)
        self.pool = Eng(self, "pool", nc.gpsimd, 1)
        self.sp = Eng(self, "sp", nc.sync, 1)
        self.engs = [self.pe, self.act, self.dve, self.pool, self.sp]

    def close(self):
        self.es.__exit__(None, None, None)

    def buf(self, name, shape, dt, es=None):
        self.uid = getattr(self, "uid", 0) + 1
        name = f"{name}_{self.uid}"
        t = (es or self.es).enter_context(self.nc.sbuf_tensor(name, list(shape), dt))
        return Buf(t[tuple(slice(None) for _ in shape)], name)

    def bufs(self, name, shape, dt, n, es=None):
        self.uid = getattr(self, "uid", 0) + 1
        name = f"{name}_{self.uid}_"
        t = (es or self.es).enter_context(self.nc.sbuf_tensor(name, [shape[0], n] + list(shape[1:]), dt))
        full = t[tuple(slice(None) for _ in range(len(shape) + 1))]
        out = []
        for i in range(n):
            idx = (slice(None), i) + tuple(slice(None) for _ in shape[1:])
            out.append(Buf(t[idx], f"{name}{i}"))
        return out, full

    def _wait(self, E, deps):
        for d in deps:
            if d[0] == 'e':
                _, F, seq = d
                if F is E and E is self.pe:
                    continue
                if E.seen.get(F.name, 0) >= seq:
                    continue
                s, v = F.sem_val(seq)
                E.h.wait_ge(s, v)
                E.seen[F.name] = seq
            else:
                _, sem, val, key = d
                if E.seen_d.get(key, 0) >= val:
                    continue
                E.h.wait_ge(sem, val)
                E.seen_d[key] = val

    @staticmethod
    def _deps(reads, writes):
        deps = []
        for b in reads:
            if b.w is not None:
                deps.append(b.w)
        for b in writes:
            if b.w is not None:
                deps.append(b.w)
            deps.extend(b.r)
        return deps

    @staticmethod
    def _mark(tag, reads, writes):
        for b in reads:
            if tag[0] == 'e':
                b.r = [t for t in b.r if not (t[0] == 'e' and t[1] is tag[1])]
            b.r.append(tag)
        for b in writes:
            b.w = tag
            b.r = []

    def op(self, E, fns, reads=(), writes=()):
        if callable(fns):
            fns = [fns]
        self._wait(E, self._deps(reads, writes))
        ins = None
        for f in fns:
            ins = f()
        seq = E.n + 1
        E.n = seq
        s, _ = E.sem_val(seq)
        ins.then_inc(s, 1)
        self._mark(('e', E, seq), reads, writes)

    def dma(self, Q, out, in_, reads=(), writes=(), key=None, **kw):
        if key.dsem is None:
            key.dsem = self.es.enter_context(self.nc.semaphore(f"d_{key.name}"))
        self._wait(Q, self._deps(reads, writes))
        key.dval += 16
        Q.h.dma_start(out=out, in_=in_, **kw).then_inc(key.dsem, 16)
        self._mark(('d', key.dsem, key.dval, id(key)), reads, writes)

    def barrier(self):
        ce = [self.pe, self.act, self.dve]
        for E in ce:
            self._wait(E, [('e', F, F.n) for F in ce if F is not E and F.n > 0])


def host_consts():
    c = np.zeros((P, 6 * P + 8), np.float32)
    idx = np.arange(P)
    c[:, 0:P] = np.eye(P, dtype=np.float32)
    s_, t_ = idx[:, None], idx[None, :]
    c[:, P:2 * P] = (t_ >= s_).astype(np.float32)
    c[:, 2 * P:3 * P] = ((t_ >= s_) & (s_ // 64 == t_ // 64)).astype(np.float32)
    c[:, 3 * P:4 * P] = (s_ // 64 == t_ // 64).astype(np.float32)
    c[:, 4 * P:5 * P] = (s_ < 64).astype(np.float32) * np.ones((1, P), np.float32)
    c[:, 5 * P:6 * P] = (s_ >= 64).astype(np.float32) * np.ones((1, P), np.float32)
    inv = (np.float32(10000.0) ** (-np.arange(0, 32, 2, dtype=np.float32) / np.float32(32))).astype(np.float32)
    o = 6 * P
    c[64:80, o] = -inv
    c[80:96, o] = inv
    c[64:80, o + 1] = inv
    c[80:96, o + 1] = inv
    c[:, o + 2] = 1024 * EPS
    c[:, o + 3] = 384 * EPS
    c[:, o + 4] = 256 * EPS
    c[:, o + 5] = 128 * EPS
    c[:, o + 6] = 1.0
    c[:, o + 7] = np.log(128.0 ** -0.5)
    return c


WNAMES = [("w_ada", [D, 9 * D]), ("b_ada", [9 * D]), ("norm_ff1", [D]), ("ff1_w_gate", [D, DFF]),
          ("ff1_w_up", [D, DFF]), ("ff1_w_down", [DFF, D]), ("norm_mix", [D]), ("w_in", [D, DIN]),
          ("q_a_norm", [384]), ("w_q_b", [384, 768]), ("w_q_bp", [384, 768]), ("kv_a_norm", [256]),
          ("w_kv_b", [256, 1024]), ("conv_w", [4, 512]), ("conv_b", [512]), ("w_q_m", [4, 128, 128]),
          ("w_k_m", [4, 128, 128]), ("b_i", [4]), ("b_f", [4]), ("mlstm_norm", [4, 128]),
          ("w_mla_out", [512, D]), ("w_mlstm_out", [512, D]), ("w_o", [D, D]), ("norm_ff2", [D]),
          ("ff2_w_gate", [D, DFF]), ("ff2_w_up", [D, DFF]), ("ff2_w_down", [DFF, D]), ("norm_final", [D])]


def build_program(do_mixer=True, do_ff2=True, dbg=None, ngroups=NSEQ * NG):
    nc = bass.Bass("TRN2", target_bir_lowering=False)
    x = nc.dram_tensor("x", [NSEQ * S, D], F32, kind="ExternalInput").ap()
    c_in = nc.dram_tensor("c", [NSEQ, D], F32, kind="ExternalInput").ap()
    pos = nc.dram_tensor("pos", [NSEQ, S], I32, kind="ExternalInput").ap()
    cst = nc.dram_tensor("cst", [P, 6 * P + 8], F32, kind="ExternalInput").ap()
    W = {n: nc.dram_tensor(n, s, F32, kind="ExternalInput").ap() for n, s in WNAMES}
    y = nc.dram_tensor("y", [NSEQ * S, D], F32, kind="ExternalOutput").ap()
    dbg_out = {}
    if dbg:
        for n, s in dbg.items():
            dbg_out[n] = nc.dram_tensor(n, s, F32, kind="ExternalOutput").ap()

    fw = FW(nc)
    pe, act, dve, pool, sp = fw.pe, fw.act, fw.dve, fw.pool, fw.sp
    V = nc.vector
    A = nc.scalar
    T = nc.tensor

    hb, h_full = fw.bufs("h", [P, TG], F32, KC)
    uT, _ = fw.bufs("uT", [P, TG], BF16, KC)
    NSLOT = 4
    SLOT = 4096
    ring = [fw.buf(f"ring{i}", [P, SLOT], BF16) for i in range(NSLOT)]
    ring_i = [0]
    io = [fw.buf(f"io{i}", [P, D], F32) for i in range(3)]
    io_i = [0]
    cb = fw.buf("cst", [P, 6 * P + 8], F32)
    ident = cb[:, 0:P]
    tri_f = cb[:, P:2 * P]
    Mbd = cb[:, 2 * P:3 * P]
    BDm = cb[:, 3 * P:4 * P]
    SEL = [cb[:, 4 * P:5 * P], cb[:, 5 * P:6 * P]]
    o_ = 6 * P
    invfS = cb[:, o_:o_ + 1]
    invfC = cb[:, o_ + 1:o_ + 2]
    epsc = {1024: cb[:, o_ + 2:o_ + 3], 384: cb[:, o_ + 3:o_ + 4], 256: cb[:, o_ + 4:o_ + 5], 128: cb[:, o_ + 5:o_ + 6]}
    onec = cb[:, o_ + 6:o_ + 7]
    lnsc = cb[:, o_ + 7:o_ + 8]
    ones_b = fw.buf("ones_b", [P, P], BF16)
    rstd = fw.buf("rstd", [P, TG], F32)
    tmpA = [fw.buf(f"tmpA{i}", [P, TG], F32) for i in range(2)]
    tmpA_i = [0]
    sqb = [fw.buf(f"sqb{i}", [P, TG], BF16) for i in range(2)]
    sq_i = [0]
    psb = [Buf(fw.es.enter_context(nc.psum_tensor(f"ps{i}", [P, TG], F32))[:, :], f"ps{i}") for i in range(8)]
    ps_i = [0]

    pinned = set()

    def nps(pin=False):
        while True:
            b = psb[ps_i[0] % 8]
            ps_i[0] += 1
            if id(b) not in pinned:
                break
        if pin:
            pinned.add(id(b))
        return b

    def unpin(b):
        pinned.discard(id(b))

    def nslot():
        b = ring[ring_i[0] % NSLOT]
        ring_i[0] += 1
        return b

    def nio():
        b = io[io_i[0] % 3]
        io_i[0] += 1
        return b

    def ntmp():
        b = tmpA[tmpA_i[0] % 2]
        tmpA_i[0] += 1
        return b

    def nsq():
        b = sqb[sq_i[0] % 2]
        sq_i[0] += 1
        return b

    def wload(src, a, b_=None):
        sl = nslot()
        n = a * (b_ or 1)
        if b_ is None:
            view = sl[:, 0:n]
        else:
            view = sl[:, 0:n].rearrange("p (a b) -> p a b", a=a)
        fw.dma(pool, view, src, writes=[sl], key=sl)
        return sl, view

    def mm(outb, out_ap, terms, reads, start=True, stop=True):
        n = len(terms)
        fns = []
        for i, (l, r) in enumerate(terms):
            fns.append(lambda l=l, r=r, i=i: T.matmul(out_ap, l, r, start=(start and i == 0), stop=(stop and i == n - 1), skip_group_check=True))
        fw.op(pe, fns, reads=reads, writes=[outb])

    def dump(name, ap, b):
        if dbg and name in dbg_out:
            fw.dma(sp, dbg_out[name], ap, reads=[b], key=b)

    fw.dma(sp, cb[:, :], cst[:, :], writes=[cb], key=cb)
    fw.op(dve, lambda: V.memset(ones_b[:, :], 1.0), writes=[ones_b])

    vst = fw.buf("vst", [P, 2, P], F32)
    fw.op(dve, lambda: V.memset(vst[:, :, :], 0.0), writes=[vst])
    rowsA = [("b_ada", 72), ("norm_ff1", 8), ("norm_mix", 8), ("norm_ff2", 8), ("norm_final", 8), ("q_a_norm", 3), ("kv_a_norm", 2)]
    r_ = 0
    for n_, k_ in rowsA:
        fw.dma(sp, vst[r_:r_ + k_, 0, :], W[n_].rearrange("(j p) -> j p", p=P), writes=[vst], key=vst)
        r_ += k_
    NA = r_
    fw.dma(sp, vst[0:16, 1, :], W["conv_w"].rearrange("j (hh p) -> (j hh) p", p=P), writes=[vst], key=vst)
    fw.dma(sp, vst[16:20, 1, :], W["conv_b"].rearrange("(hh p) -> hh p", p=P), writes=[vst], key=vst)
    fw.dma(sp, vst[20:24, 1, :], W["mlstm_norm"], writes=[vst], key=vst)
    fw.dma(sp, vst[24:40, 1, :], c_in.rearrange("b (kc p) -> (b kc) p", p=P), writes=[vst], key=vst)
    NB_ = 40
    vT = fw.buf("vT", [P, NA + NB_], F32)
    pvt = nps()
    fw.op(pe, lambda: T.transpose(pvt[:, 0:NA], vst[0:NA, 0, :], ident[0:NA, 0:NA]), reads=[vst, cb], writes=[pvt])
    fw.op(pe, lambda: T.transpose(pvt[:, NA:NA + NB_], vst[0:NB_, 1, :], ident[0:NB_, 0:NB_]), reads=[vst, cb], writes=[pvt])
    fw.op(act, lambda: A.copy(out=vT[:, :], in_=pvt[:, 0:NA + NB_]), reads=[pvt], writes=[vT])

    class SubBuf:
        def __init__(self, parent, ap):
            self.__dict__["parent"] = parent
            self.__dict__["ap"] = ap

        def __getattr__(self, k):
            return getattr(self.__dict__["parent"], k)

        def __setattr__(self, k, v):
            setattr(self.__dict__["parent"], k, v)

        def __getitem__(self, k):
            return self.__dict__["ap"][k]

    badaT = SubBuf(vT, vT[:, 0:72])
    nT = {k: SubBuf(vT, vT[:, 72 + 8 * i:80 + 8 * i]) for i, k in enumerate(["norm_ff1", "norm_mix", "norm_ff2", "norm_final"])}
    qnT = SubBuf(vT, vT[:, 104:107])
    kvnT = SubBuf(vT, vT[:, 107:109])
    convwT = SubBuf(vT, vT[:, NA:NA + 16].rearrange("p (j h) -> p j h", j=4))
    convbT = SubBuf(vT, vT[:, NA + 16:NA + 20])
    mnT = SubBuf(vT, vT[:, NA + 20:NA + 24])
    cT = SubBuf(vT, vT[:, NA + 24:NA + 40].rearrange("p (b k) -> p b k", b=NSEQ))
    bi_b = fw.buf("bi_b", [P, 4], F32)
    fw.dma(sp, bi_b[:, :], W["b_i"].rearrange("(o n) -> o n", o=1).partition_broadcast(P), writes=[bi_b], key=bi_b)
    bf_b = fw.buf("bf_b", [P, 4], F32)
    fw.dma(sp, bf_b[:, :], W["b_f"].rearrange("(o n) -> o n", o=1).partition_broadcast(P), writes=[bf_b], key=bf_b)
    for b_, sc in [(nT["norm_ff1"], 32.0), (nT["norm_mix"], 32.0), (nT["norm_ff2"], 32.0), (nT["norm_final"], 32.0),
                   (qnT, float(np.sqrt(384.0))), (kvnT, 16.0), (mnT, float(np.sqrt(128.0)))]:
        fw.op(dve, lambda b_=b_, sc=sc: V.tensor_scalar(out=b_.ap, in0=b_.ap, scalar1=sc, scalar2=None, op0=ALU.mult), reads=[b_], writes=[b_])
    kpeW = fw.buf("kpeW", [P, KC, 96], BF16)
    kpeWp = fw.buf("kpeWp", [P, KC, 96], BF16)
    wif = fw.buf("wif", [P, KC, 8], BF16)
    w_in_v = W["w_in"].rearrange("(kc p) n -> p kc n", p=P)
    if do_mixer:
        fw.op(dve, lambda: V.memset(kpeW[:, :, :], 0.0), writes=[kpeW])
        fw.op(dve, lambda: V.memset(kpeWp[:, :, :], 0.0), writes=[kpeWp])
        fw.dma(pool, kpeW[:, :, 64:96], w_in_v[:, :, 640:672], writes=[kpeW], key=kpeW)
        fw.dma(pool, kpeWp[:, :, 64:80], w_in_v[:, :, 656:672], writes=[kpeWp], key=kpeWp)
        fw.dma(pool, kpeWp[:, :, 80:96], w_in_v[:, :, 640:656], writes=[kpeWp], key=kpeWp)
        fw.dma(pool, wif[:, :, :], w_in_v[:, :, 2208:2216], writes=[wif], key=wif)

    def rstd_from(chunks, n):
        pss = nps()
        nch = len(chunks)
        for i, (b_, ap) in enumerate(chunks):
            sq = nsq()
            fw.op(act, lambda ap=ap, sq=sq: A.activation(out=sq[:, :], in_=ap, func=ACT.Square), reads=[b_], writes=[sq])
            mm(pss, pss[:, :], [(ones_b[:, :], sq[:, :])], [ones_b, sq], start=(i == 0), stop=(i == nch - 1))
        fw.op(act, lambda: A.activation(out=rstd[:, :], in_=pss[:, :], func=ACT.Sqrt, bias=epsc[n], scale=1.0), reads=[pss, cb], writes=[rstd])
        fw.op(dve, lambda: V.reciprocal(out=rstd[:, :], in_=rstd[:, :]), reads=[rstd], writes=[rstd])

    def norm_mod(k, b):
        rstd_from([(hb[kc], hb[kc][:, :]) for kc in range(KC)], 1024)
        for kc in range(KC):
            t = ntmp()
            fw.op(dve, lambda kc=kc, t=t: V.scalar_tensor_tensor(out=t[:, :], in0=hb[kc][:, :], scalar=Acol[:, k, b, kc:kc + 1], in1=rstd[:, :],
                                                                 op0=ALU.mult, op1=ALU.mult), reads=[hb[kc], Acol, rstd], writes=[t])
            fw.op(act, lambda kc=kc, t=t: A.activation(out=uT[kc][:, :], in_=t[:, :], func=ACT.Identity, bias=Bcol(k, b, kc), scale=1.0),
                  reads=[t, modT], writes=[uT[kc]])

    def ffn(k, b, wg, wu, wd, es):
        hT, _ = fw.bufs("hT", [P, TG], BF16, FC, es=es)
        sg = [fw.buf(f"sg{i}", [P, TG], F32, es=es) for i in range(2)]
        norm_mod(k, b)
        wg_v = wg.rearrange("(kc p) n -> p kc n", p=P)
        wu_v = wu.rearrange("(kc p) n -> p kc n", p=P)
        wd_v = wd.rearrange("(fc p) n -> p fc n", p=P)
        for fb in range(FC // 2):
            sg_, gv = wload(wg_v[:, :, fb * 256:(fb + 1) * 256], KC, 256)
            su_, uv = wload(wu_v[:, :, fb * 256:(fb + 1) * 256], KC, 256)
            for j in range(2):
                f = 2 * fb + j
                pg = nps()
                pu = nps()
                mm(pg, pg[:, :], [(gv[:, kc, j * 128:(j + 1) * 128], uT[kc][:, :]) for kc in range(KC)], [sg_] + uT)
                mm(pu, pu[:, :], [(uv[:, kc, j * 128:(j + 1) * 128], uT[kc][:, :]) for kc in range(KC)], [su_] + uT)
                s_ = sg[f % 2]
                fw.op(act, lambda pg=pg, s_=s_: A.activation(out=s_[:, :], in_=pg[:, :], func=ACT.Silu), reads=[pg], writes=[s_])
                fw.op(dve, lambda pu=pu, s_=s_, f=f: V.tensor_tensor(out=hT[f][:, :], in0=s_[:, :], in1=pu[:, :], op=ALU.mult), reads=[s_, pu], writes=[hT[f]])
        for half in range(2):
            accs = [nps(pin=True) for _ in range(4)]
            f0 = 0
            while f0 < FC:
                nf = min(4, FC - f0)
                sd_, dv = wload(wd_v[:, f0:f0 + nf, half * 512:(half + 1) * 512], nf, 512)
                for fi in range(nf):
                    f = f0 + fi
                    for i in range(4):
                        mm(accs[i], accs[i][:, :], [(dv[:, fi, i * 128:(i + 1) * 128], hT[f][:, :])], [sd_, hT[f]], start=(f == 0), stop=(f == FC - 1))
                f0 += nf
            for i in range(4):
                dc = half * 4 + i
                po = accs[i]
                fw.op(dve, lambda dc=dc, po=po: V.scalar_tensor_tensor(out=hb[dc][:, :], in0=po[:, :], scalar=Gcol[:, k, b, dc:dc + 1], in1=hb[dc][:, :],
                                                                       op0=ALU.mult, op1=ALU.add), reads=[po, Gcol, hb[dc]], writes=[hb[dc]])
                unpin(po)

    if do_mixer:
        kT = [[fw.buf(f"kT{h}_{g}", [P, TG], BF16) for g in range(NG)] for h in range(NH)]
        Vt = [fw.buf(f"V{t}", [P, NH * 65 + 63], BF16) for t in range(S // P)]
        for t in range(S // P):
            fw.op(dve, lambda t=t: V.memset(Vt[t][:, :], 1.0), writes=[Vt[t]])
        for h_ in range(NH):
            for g_ in range(NG):
                fw.op(dve, lambda h_=h_, g_=g_: V.memset(kT[h_][g_][:, :], 0.0), writes=[kT[h_][g_]])
        Sf = [fw.buf(f"Sf{hh}", [P, 256], F32) for hh in range(4)]
        Sbf = [[fw.buf(f"Sbf{hh}_{i}", [P, 256], BF16) for i in range(2)] for hh in range(4)]
        sbf_i = [0, 0, 0, 0]
        xtail = fw.buf("xtail", [P, 4, 3], F32)
        posb = fw.buf("posb", [96, TG], I32)
        Ct = fw.buf("Ct", [96, TG], F32)
        St = fw.buf("St", [96, TG], F32)
        tg = (fw.buf("ang", [96, TG], F32), fw.buf("ki", [96, TG], I32), fw.buf("kf", [96, TG], F32), fw.buf("mm_", [96, TG], F32))
        attnT = [fw.buf(f"attnT{h}", [64, TG], BF16) for h in range(NH)]
        hnT, hn_full = fw.bufs("hnT", [P, TG], BF16, 4)

    def rope_tables(b, g):
        c0 = g * TG
        fw.dma(sp, posb[:, :], pos[b:b + 1, c0:c0 + TG].partition_broadcast(96), writes=[posb], key=posb)
        rope_table(St, invfS, 0.0, None, tg)
        rope_table(Ct, invfC, float(np.pi / 2), None, tg)

    def rope_table(out_t, invcol, shift, es, tg):
        ang, ki, kf, m = tg
        fw.op(dve, lambda: V.tensor_copy(out=kf[:, :], in_=posb[:, :]), reads=[posb], writes=[kf])
        fw.op(dve, lambda: V.tensor_scalar(out=ang[:, :], in0=kf[:, :], scalar1=invcol[0:96, :], scalar2=shift, op0=ALU.mult, op1=ALU.add), reads=[kf, cb], writes=[ang])
        fw.op(dve, lambda: V.tensor_scalar(out=ki[:, :], in0=ang[:, :], scalar1=float(1.0 / (2 * np.pi)), scalar2=None, op0=ALU.mult), reads=[ang], writes=[ki])
        fw.op(dve, lambda: V.tensor_copy(out=kf[:, :], in_=ki[:, :]), reads=[ki], writes=[kf])
        fw.op(dve, lambda: V.scalar_tensor_tensor(out=ang[:, :], in0=kf[:, :], scalar=-C1, in1=ang[:, :], op0=ALU.mult, op1=ALU.add), reads=[kf, ang], writes=[ang])
        fw.op(dve, lambda: V.scalar_tensor_tensor(out=ang[:, :], in0=kf[:, :], scalar=-C2, in1=ang[:, :], op0=ALU.mult, op1=ALU.add), reads=[kf, ang], writes=[ang])
        fw.op(dve, lambda: V.tensor_scalar(out=m[:, :], in0=ang[:, :], scalar1=float(np.pi), scalar2=-float(2 * np.pi), op0=ALU.is_gt, op1=ALU.mult), reads=[ang], writes=[m])
        fw.op(dve, lambda: V.tensor_tensor(out=ang[:, :], in0=ang[:, :], in1=m[:, :], op=ALU.add), reads=[ang, m], writes=[ang])
        fw.op(dve, lambda: V.tensor_scalar(out=m[:, :], in0=ang[:, :], scalar1=-float(np.pi), scalar2=float(2 * np.pi), op0=ALU.is_lt, op1=ALU.mult), reads=[ang], writes=[m])
        fw.op(dve, lambda: V.tensor_tensor(out=ang[:, :], in0=ang[:, :], in1=m[:, :], op=ALU.add), reads=[ang, m], writes=[ang])
        fw.op(act, lambda: A.activation(out=out_t[:, :], in_=ang[:, :], func=ACT.Sin), reads=[ang], writes=[out_t])

    def mixer(b, g):
        c0 = g * TG
        norm_mod(1, b)
        with contextlib.ExitStack() as es:
            qln, _ = fw.bufs("qln", [P, TG], BF16, 3, es=es)
            ckn, _ = fw.bufs("ckn", [P, TG], BF16, 2, es=es)
            qT = [fw.buf(f"qT{h}", [P, TG], BF16, es=es) for h in range(NH)]
            for h in range(NH):
                fw.op(dve, lambda h=h: V.memset(qT[h][:, :], 0.0), writes=[qT[h]])
            pT = [fw.buf(f"pT{i}", [P, TG], BF16, es=es) for i in range(3)]
            osb = [fw.buf(f"osb{i}", [64, TG], F32, es=es) for i in range(2)]
            rden = [fw.buf(f"rden{i}", [65, TG], F32, es=es) for i in range(2)]
            rhi = [fw.buf(f"rhi{i}", [65, TG], BF16, es=es) for i in range(2)]
            rlo = [fw.buf(f"rlo{i}", [65, TG], BF16, es=es) for i in range(2)]
            t1b = fw.buf("t1b", [96, TG], F32, es=es)
            t2b = fw.buf("t2b", [96, TG], F32, es=es)


            def lat(col0, nch, outs, gcol, n):
                sl, wv = wload(w_in_v[:, :, col0:col0 + nch * P], KC, nch * P)
                pl = []
                for j in range(nch):
                    pj = nps()
                    mm(pj, pj[:, :], [(wv[:, kc, j * P:(j + 1) * P], uT[kc][:, :]) for kc in range(KC)], [sl] + uT)
                    pl.append(pj)
                rstd_from([(pj, pj[:, :]) for pj in pl], n)
                for j in range(nch):
                    fw.op(dve, lambda j=j: V.scalar_tensor_tensor(out=outs[j][:, :], in0=pl[j][:, :], scalar=gcol[:, j:j + 1], in1=rstd[:, :],
                                                                  op0=ALU.mult, op1=ALU.mult), reads=[pl[j], gcol, rstd], writes=[outs[j]])
            lat(0, 3, qln, qnT, 384)
            lat(384, 2, ckn, kvnT, 256)
            pk = nps()
            pkp = nps()
            mm(pk, pk[0:96, :], [(kpeW[:, kc, :], uT[kc][:, :]) for kc in range(KC)], [kpeW] + uT)
            mm(pkp, pkp[0:96, :], [(kpeWp[:, kc, :], uT[kc][:, :]) for kc in range(KC)], [kpeWp] + uT)
            fw.op(dve, lambda: V.tensor_tensor(out=t1b[64:96, :], in0=pk[64:96, :], in1=Ct[64:96, :], op=ALU.mult), reads=[pk, Ct], writes=[t1b])
            fw.op(dve, lambda: V.tensor_tensor(out=t2b[64:96, :], in0=pkp[64:96, :], in1=St[64:96, :], op=ALU.mult), reads=[pkp, St], writes=[t2b])
            for h in range(NH):
                fw.op(dve, lambda h=h: V.tensor_tensor(out=kT[h][g][64:96, :], in0=t1b[64:96, :], in1=t2b[64:96, :], op=ALU.add), reads=[t1b, t2b], writes=[kT[h][g]])
            slkv, wkv = wload(W["w_kv_b"].rearrange("(j p) n -> p j n", p=P), 2, 1024)
            for h in range(NH):
                pn = nps()
                mm(pn, pn[0:64, :], [(wkv[:, j, h * 128:h * 128 + 64], ckn[j][:, :]) for j in range(2)], [slkv] + ckn)
                fw.op(act, lambda h=h, pn=pn: A.copy(out=kT[h][g][0:64, :], in_=pn[0:64, :]), reads=[pn], writes=[kT[h][g]])
            for tt in range(NTT):
                pv = nps()
                mm(pv, pv[:, :],
                   [(ckn[j][:, tt * P:(tt + 1) * P], wkv[:, j, :].rearrange("p (h two d) -> p h two d", two=2, d=64)[:, :, 1, :]) for j in range(2)], [slkv] + ckn)
                vt = Vt[g * NTT + tt]
                fw.op(act, lambda pv=pv, vt=vt: A.copy(out=vt[:, 0:NH * 65].rearrange("p (h d) -> p h d", d=65)[:, :, 0:64], in_=pv[:, :].rearrange("p (h d) -> p h d", d=64)), reads=[pv], writes=[vt])
            slq, wq = wload(W["w_q_b"].rearrange("(j p) n -> p j n", p=P), 3, 768)
            slqp, wqp = wload(W["w_q_bp"].rearrange("(j p) n -> p j n", p=P), 3, 768)
            for h in range(NH):
                pq = nps()
                pqp = nps()
                mm(pq, pq[0:96, :], [(wq[:, j, h * 96:(h + 1) * 96], qln[j][:, :]) for j in range(3)], [slq] + qln)
                mm(pqp, pqp[0:96, :], [(wqp[:, j, h * 96:(h + 1) * 96], qln[j][:, :]) for j in range(3)], [slqp] + qln)
                fw.op(dve, lambda pq=pq: V.tensor_tensor(out=t1b[:, :], in0=pq[0:96, :], in1=Ct[:, :], op=ALU.mult), reads=[pq, Ct], writes=[t1b])
                fw.op(dve, lambda pqp=pqp: V.tensor_tensor(out=t2b[:, :], in0=pqp[0:96, :], in1=St[:, :], op=ALU.mult), reads=[pqp, St], writes=[t2b])
                fw.op(dve, lambda h=h: V.tensor_tensor(out=qT[h][0:96, :], in0=t1b[:, :], in1=t2b[:, :], op=ALU.add), reads=[t1b, t2b], writes=[qT[h]])
            sc = float(96.0 ** -0.5)
            nkt = 4 * g + 4
            items = [(h, kt) for h in range(NH) for kt in range(nkt)]
            pti = [0]

            def qk(h, kt):
                jd = kt - 4 * g
                q0 = 128 * jd if jd > 0 else 0
                n = TG - q0
                pss = nps()
                kb = kT[h][kt // 4]
                mm(pss, pss[:, 0:n], [(kb[:, (kt % 4) * P:(kt % 4 + 1) * P], qT[h][:, q0:TG])], [kb, qT[h]])
                return pss, q0, n, jd

            def epi1(h, po):
                ob = osb[h % 2]
                rd, rh, rl = rden[h % 2], rhi[h % 2], rlo[h % 2]
                fw.op(act, lambda: A.copy(out=ob[:, :], in_=po[0:64, :]), reads=[po], writes=[ob])
                fw.op(dve, lambda: V.reciprocal(out=rd[64:65, :], in_=po[64:65, :]), reads=[po], writes=[rd])
                unpin(po)
                fw.op(dve, lambda: V.tensor_copy(out=rh[64:65, :], in_=rd[64:65, :]), reads=[rd], writes=[rh])
                fw.op(dve, lambda: V.tensor_tensor(out=rl[64:65, :], in0=rd[64:65, :], in1=rh[64:65, :], op=ALU.subtract), reads=[rd, rh], writes=[rl])

            def epi2(h):
                ob = osb[h % 2]
                rh, rl = rhi[h % 2], rlo[h % 2]
                pb = nps()
                mm(pb, pb[0:64, :], [(ones_b[64:65, 0:64], rh[64:65, :]), (ones_b[64:65, 0:64], rl[64:65, :])], [ones_b, rh, rl])
                fw.op(dve, lambda: V.tensor_tensor(out=attnT[h][:, :], in0=ob[:, :], in1=pb[0:64, :], op=ALU.mult), reads=[ob, pb], writes=[attnT[h]])

            cur = qk(*items[0])
            po = None
            pending = None
            for idx, (h, kt) in enumerate(items):
                nxt = qk(*items[idx + 1]) if idx + 1 < len(items) else None
                if kt == 0:
                    po = nps(pin=True)
                pss, q0, n, jd = cur
                pt = pT[pti[0] % 3]
                pti[0] += 1
                fw.op(act, lambda pss=pss, pt=pt, n=n: A.activation(out=pt[:, 0:n], in_=pss[:, 0:n], func=ACT.Exp, scale=sc), reads=[pss], writes=[pt])
                if jd >= 0:
                    fw.op(dve, lambda pt=pt: V.tensor_tensor(out=pt[:, 0:P], in0=pt[:, 0:P], in1=tri_f, op=ALU.mult), reads=[pt, cb], writes=[pt])
                vt = Vt[kt]
                mm(po, po[:, q0:TG], [(vt[:, h * 65:h * 65 + P], pt[:, 0:n])], [vt, pt], start=(kt == 0), stop=(kt == nkt - 1))
                if pending is not None:
                    pending[1] -= 1
                    if pending[1] <= 0:
                        epi2(pending[0])
                        pending = None
                if kt == nkt - 1:
                    if pending is not None:
                        epi2(pending[0])
                    epi1(h, po)
                    pending = [h, 3]
                cur = nxt
            if pending is not None:
                epi2(pending[0])
            fw.barrier()
        with contextlib.ExitStack() as es:
            xbuf = [fw.buf(f"xbuf{i}", [P, TG + 3], F32, es=es) for i in range(2)]
            acc = [fw.buf(f"acc{i}", [P, TG], F32, es=es) for i in range(2)]
            xcT = [fw.buf(f"xcT{hh}", [P, TG], BF16, es=es) for hh in range(4)]
            og = fw.buf("og", [P, 4, TG], BF16, es=es)
            gsm = {n: fw.buf(f"g_{n}", [P, NTT, 4], F32, es=es) for n in ["logi", "logf", "a", "ea", "ks", "t"]}
            eg = fw.buf("eg", [P, NTT, 8], F32, es=es)
            rep2 = [fw.buf("rep", [P, 4, P], F32, es=es)] * 2
            Eb2 = [fw.buf("Eb", [P, 4, P], F32, es=es)] * 2
            qpT2 = [fw.buf(f"qpT{i}", [P, 4, P], BF16, es=es) for i in range(2)]
            kTm2 = [fw.buf("kTm", [P, 4, P], BF16, es=es)] * 2
            kpp2 = [fw.buf(f"kpp{i}", [P, 4, P], BF16, es=es) for i in range(2)]
            vm2 = [fw.buf(f"vm{i}", [P, TG], BF16, es=es) for i in range(2)]
            smT2 = [fw.buf(f"smT{i}", [P, 4, P], BF16, es=es) for i in range(2)]
            dd = fw.buf("dd", [P, 4, P], F32, es=es)
            hg = fw.buf("hg", [P, 4, P], F32, es=es)
            sq4 = fw.buf("sq4", [P, 4, P], BF16, es=es)
            rs4 = fw.buf("rs4", [P, 4, P], F32, es=es)

            pgi = nps()
            for tt in range(NTT):
                mm(pgi, pgi[:, tt * 8:(tt + 1) * 8], [(uT[kc][:, tt * P:(tt + 1) * P], wif[:, kc, :]) for kc in range(KC)], [wif] + uT)
            gview = pgi[:, 0:32].rearrange("p (t c) -> p t c", c=8)
            fw.op(dve, lambda: V.tensor_tensor(out=gsm["logi"][:, :, :], in0=gview[:, :, 0:4], in1=bi_b[:, :].unsqueeze(1).broadcast_to([P, NTT, 4]), op=ALU.add),
                  reads=[pgi, bi_b], writes=[gsm["logi"]])
            fw.op(dve, lambda: V.tensor_tensor(out=gsm["t"][:, :, :], in0=gview[:, :, 4:8], in1=bf_b[:, :].unsqueeze(1).broadcast_to([P, NTT, 4]), op=ALU.add),
                  reads=[pgi, bf_b], writes=[gsm["t"]])
            fw.op(act, lambda: A.activation(out=gsm["t"][:, :, :], in_=gsm["t"][:, :, :], func=ACT.Exp, scale=-1.0), reads=[gsm["t"]], writes=[gsm["t"]])
            fw.op(act, lambda: A.activation(out=gsm["t"][:, :, :], in_=gsm["t"][:, :, :], func=ACT.Ln, bias=onec, scale=1.0), reads=[gsm["t"], cb], writes=[gsm["t"]])
            fw.op(dve, lambda: V.tensor_scalar(out=gsm["logf"][:, :, :], in0=gsm["t"][:, :, :], scalar1=-1.0, scalar2=None, op0=ALU.mult), reads=[gsm["t"]], writes=[gsm["logf"]])
            slx, wx = wload(w_in_v[:, :, 672:1184], KC, 512)
            for hh in range(4):
                px = nps()
                mm(px, px[:, :], [(wx[:, kc, hh * P:(hh + 1) * P], uT[kc][:, :]) for kc in range(KC)], [slx] + uT)
                xb_ = xbuf[hh % 2]
                ac = acc[hh % 2]
                if g == 0:
                    fw.op(dve, lambda xb_=xb_: V.memset(xb_[:, 0:3], 0.0), writes=[xb_])
                else:
                    fw.op(dve, lambda xb_=xb_, hh=hh: V.tensor_copy(out=xb_[:, 0:3], in_=xtail[:, hh, :]), reads=[xtail], writes=[xb_])
                fw.op(act, lambda xb_=xb_, px=px: A.copy(out=xb_[:, 3:TG + 3], in_=px[:, :]), reads=[px], writes=[xb_])
                fw.op(dve, lambda xb_=xb_, hh=hh: V.tensor_copy(out=xtail[:, hh, :], in_=xb_[:, TG:TG + 3]), reads=[xb_], writes=[xtail])
                fw.op(act, lambda xb_=xb_, ac=ac, hh=hh: A.activation(out=ac[:, :], in_=xb_[:, 3:TG + 3], func=ACT.Identity, bias=convbT[:, hh:hh + 1], scale=convwT[:, 3, hh:hh + 1]),
                      reads=[xb_, convbT, convwT], writes=[ac])
                for j in range(3):
                    fw.op(dve, lambda xb_=xb_, ac=ac, hh=hh, j=j: V.scalar_tensor_tensor(out=ac[:, :], in0=xb_[:, j:j + TG], scalar=convwT[:, j, hh:hh + 1], in1=ac[:, :],
                                                                                         op0=ALU.mult, op1=ALU.add), reads=[xb_, convwT, ac], writes=[ac])
                fw.op(act, lambda ac=ac, hh=hh: A.activation(out=xcT[hh][:, :], in_=ac[:, :], func=ACT.Silu), reads=[ac], writes=[xcT[hh]])
            slo, wo_ = wload(w_in_v[:, :, 1696:2208], KC, 512)
            for hh in range(4):
                pg_ = nps()
                mm(pg_, pg_[:, :], [(wo_[:, kc, hh * P:(hh + 1) * P], uT[kc][:, :]) for kc in range(KC)], [slo] + uT)
                fw.op(act, lambda pg_=pg_, hh=hh: A.activation(out=og[:, hh, :], in_=pg_[:, :], func=ACT.Sigmoid), reads=[pg_], writes=[og])
            pgs = nps()
            for tt in range(NTT):
                for i_, lhs in enumerate([Mbd, BDm, SEL[0], SEL[1]]):
                    mm(pgs, pgs[:, tt * 16 + i_ * 4: tt * 16 + i_ * 4 + 4], [(lhs, gsm["logf"][:, tt, :])], [cb, gsm["logf"]])
            gs = pgs[:, 0:64].rearrange("p (t c) -> p t c", c=16)
            fw.op(dve, lambda: V.tensor_tensor(out=gsm["a"][:, :, :], in0=gsm["logi"][:, :, :], in1=gs[:, :, 0:4], op=ALU.subtract), reads=[gsm["logi"], pgs], writes=[gsm["a"]])
            fw.op(act, lambda: A.activation(out=gsm["ea"][:, :, :], in_=gsm["a"][:, :, :], func=ACT.Exp), reads=[gsm["a"]], writes=[gsm["ea"]])
            fw.op(dve, lambda: V.tensor_tensor(out=gsm["t"][:, :, :], in0=gsm["a"][:, :, :], in1=gs[:, :, 4:8], op=ALU.add), reads=[gsm["a"], pgs], writes=[gsm["t"]])
            fw.op(act, lambda: A.activation(out=gsm["ks"][:, :, :], in_=gsm["t"][:, :, :], func=ACT.Exp, bias=lnsc, scale=1.0), reads=[gsm["t"], cb], writes=[gsm["ks"]])
            fw.op(act, lambda: A.activation(out=eg[:, :, :], in_=gs[:, :, 8:16], func=ACT.Exp), reads=[pgs], writes=[eg])
            if dbg and "logf" in dbg_out and g == 0 and b == 0:
                dump("logf", gsm["logf"][:, :, :], gsm["logf"])
                dump("logi", gsm["logi"][:, :, :], gsm["logi"])
                dump("ga", gsm["a"][:, :, :], gsm["a"])
                dump("eg", eg[:, :, :], eg)
            slv, wv_ = wload(w_in_v[:, :, 1184:1696], KC, 512)
            slqm, wqm = wload(W["w_q_m"].rearrange("h d e -> d h e"), 4, P)
            slkm, wkm = wload(W["w_k_m"].rearrange("h d e -> d h e"), 4, P)
            if g == 0:
                for hh in range(4):
                    fw.op(dve, lambda hh=hh: V.memset(Sf[hh][:, :], 0.0), writes=[Sf[hh]])
                    fw.op(dve, lambda hh=hh: V.memset(Sbf[hh][sbf_i[hh] % 2][:, :], 0.0), writes=[Sbf[hh][sbf_i[hh] % 2]])
            def front(tt):
                i2 = tt % 2
                tc_ = slice(tt * P, (tt + 1) * P)
                vm, kpp, rep, Eb, qpT, kTm, smT = vm2[i2], kpp2[i2], rep2[i2], Eb2[i2], qpT2[i2], kTm2[i2], smT2[i2]
                pvm = nps()
                mm(pvm, pvm[:, :], [(uT[kc][:, tc_], wv_[:, kc, :]) for kc in range(KC)], [slv] + uT)
                fw.op(act, lambda: A.copy(out=vm[:, :], in_=pvm[:, :]), reads=[pvm], writes=[vm])
                pk2 = nps()
                for hh in range(4):
                    mm(pk2, pk2[:, hh * P:(hh + 1) * P], [(xcT[hh][:, tc_], wkm[:, hh, :])], [xcT[hh], slkm])
                fw.op(dve, lambda: V.tensor_tensor(out=kpp[:, :, :], in0=pk2[:, :].rearrange("p (h e) -> p h e", e=P),
                                                   in1=gsm["ks"][:, tt, :].unsqueeze(2).broadcast_to([P, 4, P]), op=ALU.mult), reads=[pk2, gsm["ks"]], writes=[kpp])
                pbb = nps()
                fw.op(dve, lambda: V.tensor_copy(out=rep[:, :, :], in_=gsm["logf"][:, tt, :].unsqueeze(2).broadcast_to([P, 4, P])), reads=[gsm["logf"]], writes=[rep])
                for hh in range(4):
                    mm(pbb, pbb[:, hh * P:(hh + 1) * P], [(rep[:, hh, :], Mbd)], [rep, cb])
                fw.op(act, lambda: A.activation(out=Eb[:, :, :], in_=pbb[:, :].rearrange("p (h e) -> p h e", e=P), func=ACT.Exp), reads=[pbb], writes=[Eb])
                pq_ = nps()
                pk_ = nps()
                for hh in range(4):
                    mm(pq_, pq_[:, hh * P:(hh + 1) * P], [(wqm[:, hh, :], xcT[hh][:, tc_])], [slqm, xcT[hh]])
                    mm(pk_, pk_[:, hh * P:(hh + 1) * P], [(wkm[:, hh, :], xcT[hh][:, tc_])], [slkm, xcT[hh]])
                fw.op(dve, lambda: V.tensor_tensor(out=qpT[:, :, :], in0=pq_[:, :].rearrange("p (h e) -> p h e", e=P), in1=Eb[:, :, :], op=ALU.mult), reads=[pq_, Eb], writes=[qpT])
                fw.op(act, lambda: A.activation(out=kTm[:, :, :], in_=pk_[:, :].rearrange("p (h e) -> p h e", e=P), func=ACT.Identity, scale=float(128.0 ** -0.5)), reads=[pk_], writes=[kTm])
                pS = nps()
                for hh in range(4):
                    mm(pS, pS[:, hh * P:(hh + 1) * P], [(kTm[:, hh, :], qpT[:, hh, :])], [kTm, qpT])
                for hh in range(4):
                    fw.op(dve, lambda hh=hh: V.scalar_tensor_tensor(out=smT[:, hh, :], in0=pS[:, hh * P:(hh + 1) * P], scalar=gsm["ea"][:, tt, hh:hh + 1], in1=Mbd,
                                                                    op0=ALU.mult, op1=ALU.mult), reads=[pS, gsm["ea"], cb], writes=[smT])

            def back(tt):
                i2 = tt % 2
                tc_ = slice(tt * P, (tt + 1) * P)
                vm, kpp, qpT, smT = vm2[i2], kpp2[i2], qpT2[i2], smT2[i2]
                pnum = nps(pin=True)
                pden = nps(pin=True)
                for hh in range(4):
                    hs = slice(hh * P, (hh + 1) * P)
                    mm(pnum, pnum[:, hs], [(vm[:, hs], smT[:, hh, :])], [vm, smT], start=(hh == 0), stop=False)
                    mm(pden, pden[:, hs], [(ones_b[:, :], smT[:, hh, :])], [ones_b, smT], start=(hh == 0), stop=False)
                for c in range(2):
                    cs = slice(c * 64, (c + 1) * 64)
                    for hh in range(4):
                        sb_ = Sbf[hh][sbf_i[hh] % 2]
                        mm(pnum, pnum[:, hh * P + c * 64: hh * P + (c + 1) * 64], [(sb_[:, 0:P], qpT[:, hh, cs])], [sb_, qpT], start=False, stop=True)
                        mm(pden, pden[:, hh * P + c * 64: hh * P + (c + 1) * 64], [(sb_[:, P:2 * P], qpT[:, hh, cs])], [sb_, qpT], start=False, stop=True)
                        pup = nps()
                        mm(pup, pup[:, 0:P], [(kpp[cs, hh, :], vm[cs, hh * P:(hh + 1) * P])], [kpp, vm])
                        mm(pup, pup[:, P:2 * P], [(kpp[cs, hh, :], ones_b[cs, :])], [kpp, ones_b])
                        fw.op(dve, lambda hh=hh, pup=pup, c=c: V.scalar_tensor_tensor(out=Sf[hh][:, :], in0=Sf[hh][:, :], scalar=eg[:, tt, c * 4 + hh:c * 4 + hh + 1], in1=pup[:, 0:256],
                                                                                    op0=ALU.mult, op1=ALU.add), reads=[Sf[hh], eg, pup], writes=[Sf[hh]])
                        sbf_i[hh] += 1
                        nb_ = Sbf[hh][sbf_i[hh] % 2]
                        fw.op(act, lambda hh=hh, nb_=nb_: A.copy(out=nb_[:, :], in_=Sf[hh][:, :]), reads=[Sf[hh]], writes=[nb_])
                fw.op(act, lambda: A.activation(out=dd[:, :, :], in_=pden[:, :].rearrange("p (h e) -> p h e", e=P), func=ACT.Abs), reads=[pden], writes=[dd])
                fw.op(dve, lambda: V.tensor_scalar(out=dd[:, :, :], in0=dd[:, :, :], scalar1=1.0, scalar2=None, op0=ALU.max), reads=[dd], writes=[dd])
                fw.op(dve, lambda: V.reciprocal(out=dd[:, :, :], in_=dd[:, :, :]), reads=[dd], writes=[dd])
                fw.op(dve, lambda: V.tensor_tensor(out=hg[:, :, :], in0=pnum[:, :].rearrange("p (h e) -> p h e", e=P), in1=dd[:, :, :], op=ALU.mult), reads=[pnum, dd], writes=[hg])
                unpin(pnum)
                unpin(pden)
                fw.op(dve, lambda: V.tensor_tensor(out=hg[:, :, :], in0=hg[:, :, :], in1=og[:, :, tc_], op=ALU.mult), reads=[hg, og], writes=[hg])
                fw.op(act, lambda: A.activation(out=sq4[:, :, :], in_=hg[:, :, :], func=ACT.Square), reads=[hg], writes=[sq4])
                pss4 = nps()
                mm(pss4, pss4[:, :], [(ones_b[:, :], sq4[:, :, :].rearrange("p h e -> p (h e)"))], [ones_b, sq4])
                fw.op(act, lambda: A.activation(out=rs4[:, :, :], in_=pss4[:, :].rearrange("p (h e) -> p h e", e=P), func=ACT.Sqrt, bias=epsc[128], scale=1.0), reads=[pss4, cb], writes=[rs4])
                fw.op(dve, lambda: V.reciprocal(out=rs4[:, :, :], in_=rs4[:, :, :]), reads=[rs4], writes=[rs4])
                for hh in range(4):
                    fw.op(dve, lambda hh=hh: V.scalar_tensor_tensor(out=hnT[hh][:, tc_], in0=hg[:, hh, :], scalar=mnT[:, hh:hh + 1], in1=rs4[:, hh, :],
                                                                    op0=ALU.mult, op1=ALU.mult), reads=[hg, mnT, rs4], writes=[hnT[hh]])

            front(0)
            for tt in range(NTT):
                if tt + 1 < NTT:
                    front(tt + 1)
                back(tt)
            fw.barrier()
        with contextlib.ExitStack() as es:
            yT, _ = fw.bufs("yT", [P, TG], BF16, KC, es=es)
            s0 = fw.buf("s0", [P, TG], F32, es=es)
            s1 = fw.buf("s1", [P, TG], F32, es=es)
            y1 = fw.buf("y1", [P, TG], F32, es=es)
            y2 = fw.buf("y2", [P, TG], F32, es=es)
            wmla_v = W["w_mla_out"].rearrange("(h d) n -> d h n", d=64)
            wmls_v = W["w_mlstm_out"].rearrange("(hh p) n -> p hh n", p=P)
            for dp in range(4):
                ds_ = slice(dp * 256, (dp + 1) * 256)
                sla = nslot()
                wa = sla[0:64, 0:2048].rearrange("p (a b) -> p a b", a=8)
                fw.dma(pool, wa, wmla_v[:, :, ds_], writes=[sla], key=sla)
                slb, wb = wload(wmls_v[:, :, ds_], 4, 256)
                slg0, wg0 = wload(w_in_v[:, :, 2216 + dp * 256:2216 + (dp + 1) * 256], KC, 256)
                slg1, wg1 = wload(w_in_v[:, :, 3240 + dp * 256:3240 + (dp + 1) * 256], KC, 256)
                for j in range(2):
                    dc = 2 * dp + j
                    js = slice(j * P, (j + 1) * P)
                    pa = nps()
                    pb_ = nps()
                    p0 = nps()
                    p1 = nps()
                    mm(pa, pa[:, :], [(wa[:, h, js], attnT[h][:, :]) for h in range(NH)], [sla] + attnT)
                    mm(pb_, pb_[:, :], [(wb[:, hh, js], hnT[hh][:, :]) for hh in range(4)], [slb] + hnT)
                    mm(p0, p0[:, :], [(wg0[:, kc, js], uT[kc][:, :]) for kc in range(KC)], [slg0] + uT)
                    mm(p1, p1[:, :], [(wg1[:, kc, js], uT[kc][:, :]) for kc in range(KC)], [slg1] + uT)
                    fw.op(act, lambda p0=p0: A.activation(out=s0[:, :], in_=p0[:, :], func=ACT.Sigmoid), reads=[p0], writes=[s0])
                    fw.op(act, lambda p1=p1: A.activation(out=s1[:, :], in_=p1[:, :], func=ACT.Sigmoid), reads=[p1], writes=[s1])
                    fw.op(dve, lambda pa=pa: V.tensor_tensor(out=y1[:, :], in0=s0[:, :], in1=pa[:, :], op=ALU.mult), reads=[s0, pa], writes=[y1])
                    fw.op(dve, lambda pb_=pb_: V.tensor_tensor(out=y2[:, :], in0=s1[:, :], in1=pb_[:, :], op=ALU.mult), reads=[s1, pb_], writes=[y2])
                    fw.op(dve, lambda dc=dc: V.tensor_tensor(out=yT[dc][:, :], in0=y1[:, :], in1=y2[:, :], op=ALU.add), reads=[y1, y2], writes=[yT[dc]])
            wo_v = W["w_o"].rearrange("(kc p) n -> p kc n", p=P)
            for dp in range(4):
                slo_, wov = wload(wo_v[:, :, dp * 256:(dp + 1) * 256], KC, 256)
                for j in range(2):
                    dc = 2 * dp + j
                    po = nps()
                    mm(po, po[:, :], [(wov[:, kc, j * P:(j + 1) * P], yT[kc][:, :]) for kc in range(KC)], [slo_] + yT)
                    fw.op(dve, lambda dc=dc, po=po: V.scalar_tensor_tensor(out=hb[dc][:, :], in0=po[:, :], scalar=Gcol[:, 1, b, dc:dc + 1], in1=hb[dc][:, :],
                                                                           op0=ALU.mult, op1=ALU.add), reads=[po, Gcol, hb[dc]], writes=[hb[dc]])
            fw.barrier()

    def load_group(gi):
        b, g = gi // NG, gi % NG
        r0 = b * S + g * TG
        for tt in range(NTT):
            xb_ = nio()
            fw.dma(sp, xb_[:, :], x[r0 + tt * P:r0 + (tt + 1) * P, :], writes=[xb_], key=xb_)
            for half in range(2):
                pt_ = nps()
                for i in range(4):
                    kc = half * 4 + i
                    fw.op(pe, lambda pt_=pt_, xb_=xb_, kc=kc, i=i: T.transpose(pt_[:, i * P:(i + 1) * P], xb_[:, kc * P:(kc + 1) * P], ident), reads=[xb_, cb], writes=[pt_])
                if half == 0:
                    fw.op(act, lambda pt_=pt_, half=half, tt=tt: A.copy(out=h_full[:, half * 4:(half + 1) * 4, tt * P:(tt + 1) * P], in_=pt_[:, :].rearrange("p (a t) -> p a t", a=4)),
                          reads=[pt_], writes=hb[half * 4:(half + 1) * 4])
                else:
                    fw.op(dve, lambda pt_=pt_, half=half, tt=tt: V.tensor_copy(out=h_full[:, half * 4:(half + 1) * 4, tt * P:(tt + 1) * P], in_=pt_[:, :].rearrange("p (a t) -> p a t", a=4)),
                          reads=[pt_], writes=hb[half * 4:(half + 1) * 4])
        if do_mixer:
            rope_tables(b, g)

    load_group(0)
    cTb = fw.buf("cTb", [P, NSEQ, KC], BF16)
    fw.op(act, lambda: A.activation(out=cTb[:, :, :], in_=cT[:, :, :], func=ACT.Silu), reads=[cT], writes=[cTb])
    modT = fw.buf("modT", [P, NSEQ, 72], F32)
    pm = nps()
    w_ada_v = W["w_ada"].rearrange("(kc p) n -> p kc n", p=P)
    for blk in range(36):
        sl, wv = wload(w_ada_v[:, :, blk * 256:(blk + 1) * 256], KC, 256)
        for j in range(2):
            jj = 2 * blk + j
            mm(pm, pm[:, jj * 2:(jj + 1) * 2], [(wv[:, kc, j * 128:(j + 1) * 128], cTb[:, :, kc]) for kc in range(KC)], [sl, cTb])
    for b in range(NSEQ):
        fw.op(dve, lambda b=b: V.tensor_tensor(out=modT[:, b, :], in0=pm[:, 0:144].rearrange("p (j b) -> p j b", b=2)[:, :, b],
                                               in1=badaT[:, :], op=ALU.add), reads=[pm, badaT], writes=[modT])
    Acol = fw.buf("Acol", [P, 3, NSEQ, KC], F32)
    Gcol = fw.buf("Gcol", [P, 3, NSEQ, KC], F32)
    gains = [nT["norm_ff1"], nT["norm_mix"], nT["norm_ff2"]]
    for k in range(3):
        for b in range(NSEQ):
            fw.op(dve, lambda k=k, b=b: V.scalar_tensor_tensor(out=Acol[:, k, b, :], in0=modT[:, b, (3 * k + 1) * 8:(3 * k + 2) * 8], scalar=1.0,
                                                               in1=gains[k][:, :], op0=ALU.add, op1=ALU.mult), reads=[modT, gains[k]], writes=[Acol])
            fw.op(dve, lambda k=k, b=b: V.tensor_scalar(out=Gcol[:, k, b, :], in0=modT[:, b, (3 * k + 2) * 8:(3 * k + 3) * 8],
                                                        scalar1=(1.0 if k == 1 else 0.5), scalar2=None, op0=ALU.mult), reads=[modT], writes=[Gcol])

    def Bcol(k, b, kc):
        return modT[:, b, 3 * k * 8 + kc:3 * k * 8 + kc + 1]


    out_bufs = []
    for gi in range(ngroups):
        b, g = gi // NG, gi % NG
        r0 = b * S + g * TG
        if gi > 0:
            load_group(gi)
        with contextlib.ExitStack() as es:
            ffn(0, b, W["ff1_w_gate"], W["ff1_w_up"], W["ff1_w_down"], es)
            fw.barrier()
        if gi == 0:
            dump("h1", h_full[:, :, :], hb[0])
        if do_mixer:
            mixer(b, g)
            if gi == 0:
                dump("h2", h_full[:, :, :], hb[0])
                if dbg and "att" in dbg_out:
                    for h_ in range(NH):
                        fw.dma(pool, dbg_out["att"][h_], attnT[h_][:, :], reads=[attnT[h_]], key=attnT[h_])
                    for h_ in range(4):
                        fw.dma(pool, dbg_out["ml"][h_], hnT[h_][:, :], reads=[hnT[h_]], key=hnT[h_])
        if do_ff2:
            with contextlib.ExitStack() as es:
                ffn(2, b, W["ff2_w_gate"], W["ff2_w_up"], W["ff2_w_down"], es)
                fw.barrier()
        rstd_from([(hb[kc], hb[kc][:, :]) for kc in range(KC)], 1024)
        with contextlib.ExitStack() as es:
            onb, on_full = fw.bufs("onb", [P, TG], F32, KC, es=es)
            for kc in range(KC):
                fw.op(dve, lambda kc=kc: V.scalar_tensor_tensor(out=onb[kc][:, :], in0=hb[kc][:, :], scalar=nT["norm_final"][:, kc:kc + 1], in1=rstd[:, :],
                                                                op0=ALU.mult, op1=ALU.mult), reads=[hb[kc], nT["norm_final"], rstd], writes=[onb[kc]])
            for tt in range(NTT):
                ob_ = nio()
                for half in range(2):
                    pt_ = nps()
                    for i in range(4):
                        kc = half * 4 + i
                        fw.op(pe, lambda pt_=pt_, kc=kc, i=i, tt=tt: T.transpose(pt_[:, i * P:(i + 1) * P], onb[kc][:, tt * P:(tt + 1) * P], ident), reads=[onb[kc], cb], writes=[pt_])
                    if half == 0:
                        fw.op(act, lambda pt_=pt_, ob_=ob_, half=half: A.copy(out=ob_[:, half * 512:(half + 1) * 512], in_=pt_[:, :]), reads=[pt_], writes=[ob_])
                    else:
                        fw.op(dve, lambda pt_=pt_, ob_=ob_, half=half: V.tensor_copy(out=ob_[:, half * 512:(half + 1) * 512], in_=pt_[:, :]), reads=[pt_], writes=[ob_])
                fw.dma(sp, y[r0 + tt * P:r0 + (tt + 1) * P, :], ob_[:, :], reads=[ob_], key=ob_)
                out_bufs.append(ob_)
            fw.barrier()
    deps = []
    for ob_ in io:
        if ob_.dsem is not None:
            deps.append(('d', ob_.dsem, ob_.dval, id(ob_)))
    fw._wait(sp, deps)
    if dbg:
        for bb in [hb[0]]:
            if bb.dsem is not None:
                fw._wait(sp, [('d', bb.dsem, bb.dval, id(bb))])
    fw.close()
    return nc


def prep_inputs(inputs):
    w = {}
    for n, s in WNAMES:
        if n == "w_q_bp":
            continue
        a = np.asarray(inputs[n], dtype=np.float32)
        if n != "norm_final":
            a = a[0]
        w[n] = np.ascontiguousarray(a)
    wq = w["w_q_b"]
    perm = np.arange(768)
    for h in range(NH):
        base = h * 96 + 64
        perm[base:base + 16] = np.arange(base + 16, base + 32)
        perm[base + 16:base + 32] = np.arange(base, base + 16)
    w["w_q_bp"] = np.ascontiguousarray(wq[:, perm])
    return w


_CACHE = {}


def kernel(**inputs):
    x = np.asarray(inputs["x"], dtype=np.float32)
    c = np.asarray(inputs["c"], dtype=np.float32)
    pos = np.asarray(inputs["positions"], dtype=np.int32)
    w = prep_inputs(inputs)
    cst = host_consts()
    if "nc" not in _CACHE:
        _CACHE["nc"] = build_program()
    nc = _CACHE["nc"]
    in_maps = []
    for i in range(8):
        m = {"x": np.ascontiguousarray(x[2 * i:2 * i + 2].reshape(NSEQ * S, D)),
             "c": np.ascontiguousarray(c[2 * i:2 * i + 2]),
             "pos": np.ascontiguousarray(pos[2 * i:2 * i + 2]),
             "cst": cst}
        m.update(w)
        in_maps.append(m)
    res = run_bass_kernel_spmd(nc, in_maps, core_ids=list(range(8)))
    out = np.concatenate([r["y"].reshape(NSEQ, S, D) for r in res.results], axis=0)
    return out.astype(np.float32)
```

```python
import contextlib
import numpy as np
import concourse.bass as bass
import concourse.mybir as mybir
from concourse.bass_utils import run_bass_kernel_spmd

F32 = mybir.dt.float32
BF16 = mybir.dt.bfloat16
I32 = mybir.dt.int32
ACT = mybir.ActivationFunctionType
ALU = mybir.AluOpType

P = 128
D = 1024
KC = 8
DFF = 2816
FC = 22
TG = 512
NTT = 4
S = 2048
NG = S // TG
NSEQ = 2
NH = 8
DIN = 4264
EPS = 1e-6
SEM_CAP = 24000
C1 = 6.28125
C2 = float(2 * np.pi - 6.28125)


class Eng:
    def __init__(self, fw, name, h, nsem):
        self.name = name
        self.h = h
        self.sems = [fw.es.enter_context(fw.nc.semaphore(f"s_{name}{i}")) for i in range(nsem)]
        self.n = 0
        self.seen = {}
        self.seen_d = {}

    def sem_val(self, seq):
        i = (seq - 1) // SEM_CAP
        return self.sems[i], (seq - 1) % SEM_CAP + 1


class Buf:
    def __init__(self, ap, name=""):
        self.ap = ap
        self.name = name
        self.w = None
        self.r = []
        self.dsem = None
        self.dval = 0

    def __getitem__(self, k):
        return self.ap[k]


class GroupBuf:
    def __init__(self, members, ap):
        self.members = members
        self.ap = ap
        self.name = members[0].name

    def __getitem__(self, k):
        return self.ap[k]


def _flat(bs):
    out = []
    for b in bs:
        out.extend(getattr(b, "members", None) or [b])
    return out


class FW:
    def __init__(self, nc):
        self.nc = nc
        self.es = contextlib.ExitStack()
        self.es.__enter__()
        self.pe = Eng(self, "pe", nc.tensor, 3)
        self.act = Eng(self, "act", nc.scalar, 3)
        self.dve = Eng(self, "dve", nc.vector, 4)
        self.pool = Eng(self, "pool", nc.gpsimd, 1)
        self.sp = Eng(self, "sp", nc.sync, 1)
        self.engs = [self.pe, self.act, self.dve, self.pool, self.sp]

    def close(self):
        self.es.__exit__(None, None, None)

    def buf(self, name, shape, dt, es=None):
        self.uid = getattr(self, "uid", 0) + 1
        name = f"{name}_{self.uid}"
        t = (es or self.es).enter_context(self.nc.sbuf_tensor(name, list(shape), dt))
        return Buf(t[tuple(slice(None) for _ in shape)], name)

    def bufs(self, name, shape, dt, n, es=None):
        self.uid = getattr(self, "uid", 0) + 1
        name = f"{name}_{self.uid}_"
        t = (es or self.es).enter_context(self.nc.sbuf_tensor(name, [shape[0], n] + list(shape[1:]), dt))
        full = t[tuple(slice(None) for _ in range(len(shape) + 1))]
        out = []
        for i in range(n):
            idx = (slice(None), i) + tuple(slice(None) for _ in shape[1:])
            out.append(Buf(t[idx], f"{name}{i}"))
        return out, full

    def _wait(self, E, deps):
        for d in deps:
            if d[0] == 'e':
                _, F, seq = d
                if F is E and E is self.pe:
                    continue
                if E.seen.get(F.name, 0) >= seq:
                    continue
                s, v = F.sem_val(seq)
                E.h.wait_ge(s, v)
                E.seen[F.name] = seq
            else:
                _, sem, val, key = d
                if E.seen_d.get(key, 0) >= val:
                    continue
                E.h.wait_ge(sem, val)
                E.seen_d[key] = val

    @staticmethod
    def _deps(reads, writes):
        reads, writes = _flat(reads), _flat(writes)
        deps = []
        for b in reads:
            if b.w is not None:
                deps.append(b.w)
        for b in writes:
            if b.w is not None:
                deps.append(b.w)
            deps.extend(b.r)
        return deps

    @staticmethod
    def _mark(tag, reads, writes):
        reads, writes = _flat(reads), _flat(writes)
        for b in reads:
            if tag[0] == 'e':
                b.r = [t for t in b.r if not (t[0] == 'e' and t[1] is tag[1])]
            b.r.append(tag)
        for b in writes:
            b.w = tag
            b.r = []

    def op(self, E, fns, reads=(), writes=()):
        if callable(fns):
            fns = [fns]
        self._wait(E, self._deps(reads, writes))
        ins = None
        for f in fns:
            ins = f()
        seq = E.n + 1
        E.n = seq
        s, _ = E.sem_val(seq)
        ins.then_inc(s, 1)
        self._mark(('e', E, seq), reads, writes)

    def dma(self, Q, out, in_, reads=(), writes=(), key=None, **kw):
        key = (getattr(key, "members", None) or [key])[0]
        if key.dsem is None:
            key.dsem = self.es.enter_context(self.nc.semaphore(f"d_{key.name}"))
        self._wait(Q, self._deps(reads, writes))
        key.dval += 16
        Q.h.dma_start(out=out, in_=in_, **kw).then_inc(key.dsem, 16)
        self._mark(('d', key.dsem, key.dval, id(key)), reads, writes)

    def barrier(self):
        ce = [self.pe, self.act, self.dve]
        for E in ce:
            self._wait(E, [('e', F, F.n) for F in ce if F is not E and F.n > 0])


def host_consts():
    c = np.zeros((P, 6 * P + 8), np.float32)
    idx = np.arange(P)
    c[:, 0:P] = np.eye(P, dtype=np.float32)
    s_, t_ = idx[:, None], idx[None, :]
    c[:, P:2 * P] = (t_ >= s_).astype(np.float32)
    c[:, 2 * P:3 * P] = ((t_ >= s_) & (s_ // 64 == t_ // 64)).astype(np.float32)
    c[:, 3 * P:4 * P] = (s_ // 64 == t_ // 64).astype(np.float32)
    c[:, 4 * P:5 * P] = (s_ < 64).astype(np.float32) * np.ones((1, P), np.float32)
    c[:, 5 * P:6 * P] = (s_ >= 64).astype(np.float32) * np.ones((1, P), np.float32)
    inv = (np.float32(10000.0) ** (-np.arange(0, 32, 2, dtype=np.float32) / np.float32(32))).astype(np.float32)
    o = 6 * P
    c[64:80, o] = -inv
    c[80:96, o] = inv
    c[64:80, o + 1] = inv
    c[80:96, o + 1] = inv
    c[:, o + 2] = 1024 * EPS
    c[:, o + 3] = 384 * EPS
    c[:, o + 4] = 256 * EPS
    c[:, o + 5] = 128 * EPS
    c[:, o + 6] = 1.0
    c[:, o + 7] = np.log(128.0 ** -0.5)
    return c


WNAMES = [("w_ada", [D, 9 * D]), ("b_ada", [9 * D]), ("norm_ff1", [D]), ("ff1_w_gate", [D, DFF]),
          ("ff1_w_up", [D, DFF]), ("ff1_w_down", [DFF, D]), ("norm_mix", [D]), ("w_in", [D, DIN]),
          ("q_a_norm", [384]), ("w_q_b", [384, 768]), ("w_q_bp", [384, 768]), ("kv_a_norm", [256]),
          ("w_kv_b", [256, 1024]), ("conv_w", [4, 512]), ("conv_b", [512]), ("w_q_m", [4, 128, 128]),
          ("w_k_m", [4, 128, 128]), ("b_i", [4]), ("b_f", [4]), ("mlstm_norm", [4, 128]),
          ("w_mla_out", [512, D]), ("w_mlstm_out", [512, D]), ("w_o", [D, D]), ("norm_ff2", [D]),
          ("ff2_w_gate", [D, DFF]), ("ff2_w_up", [D, DFF]), ("ff2_w_down", [DFF, D]), ("norm_final", [D])]


def build_program(do_mixer=True, do_ff2=True, dbg=None, ngroups=NSEQ * NG):
    nc = bass.Bass("TRN2", target_bir_lowering=False)
    x = nc.dram_tensor("x", [NSEQ * S, D], F32, kind="ExternalInput").ap()
    c_in = nc.dram_tensor("c", [NSEQ, D], F32, kind="ExternalInput").ap()
    pos = nc.dram_tensor("pos", [NSEQ, S], I32, kind="ExternalInput").ap()
    cst = nc.dram_tensor("cst", [P, 6 * P + 8], F32, kind="ExternalInput").ap()
    W = {n: nc.dram_tensor(n, s, F32, kind="ExternalInput").ap() for n, s in WNAMES}
    y = nc.dram_tensor("y", [NSEQ * S, D], F32, kind="ExternalOutput").ap()
    dbg_out = {}
    if dbg:
        for n, s in dbg.items():
            dbg_out[n] = nc.dram_tensor(n, s, F32, kind="ExternalOutput").ap()

    fw = FW(nc)
    pe, act, dve, pool, sp = fw.pe, fw.act, fw.dve, fw.pool, fw.sp
    V = nc.vector
    A = nc.scalar
    T = nc.tensor

    hb, h_full = fw.bufs("h", [P, TG], F32, KC)
    uT, _ = fw.bufs("uT", [P, TG], BF16, KC)
    NSLOT = 8
    SLOT = 2048
    ring_t = fw.es.enter_context(nc.sbuf_tensor("ring_t", [P, NSLOT * SLOT], BF16))
    ring = [Buf(ring_t[:, k * SLOT:(k + 1) * SLOT], f"ring{k}") for k in range(NSLOT)]
    ring_i = [0]
    io = [fw.buf(f"io{i}", [P, D], F32) for i in range(3)]
    io_i = [0]
    cb = fw.buf("cst", [P, 6 * P + 8], F32)
    ident = cb[:, 0:P]
    tri_f = cb[:, P:2 * P]
    Mbd = cb[:, 2 * P:3 * P]
    BDm = cb[:, 3 * P:4 * P]
    SEL = [cb[:, 4 * P:5 * P], cb[:, 5 * P:6 * P]]
    o_ = 6 * P
    invfS = cb[:, o_:o_ + 1]
    invfC = cb[:, o_ + 1:o_ + 2]
    epsc = {1024: cb[:, o_ + 2:o_ + 3], 384: cb[:, o_ + 3:o_ + 4], 256: cb[:, o_ + 4:o_ + 5], 128: cb[:, o_ + 5:o_ + 6]}
    onec = cb[:, o_ + 6:o_ + 7]
    lnsc = cb[:, o_ + 7:o_ + 8]
    ones_b = fw.buf("ones_b", [P, P], BF16)
    rstd = fw.buf("rstd", [P, TG], F32)
    tmpA = [fw.buf(f"tmpA{i}", [P, TG], F32) for i in range(2)]
    tmpA_i = [0]
    sqb = [fw.buf(f"sqb{i}", [P, TG], BF16) for i in range(2)]
    sq_i = [0]
    psb = [Buf(fw.es.enter_context(nc.psum_tensor(f"ps{i}", [P, TG], F32))[:, :], f"ps{i}") for i in range(8)]
    ps_i = [0]

    pinned = set()

    def nps(pin=False):
        while True:
            b = psb[ps_i[0] % 8]
            ps_i[0] += 1
            if id(b) not in pinned:
                break
        if pin:
            pinned.add(id(b))
        return b

    def unpin(b):
        pinned.discard(id(b))

    def nslot(n_elems=0):
        if n_elems > SLOT:
            if ring_i[0] % 2 == 1:
                ring_i[0] += 1
            k = ring_i[0] % NSLOT
            ring_i[0] += 2
            return GroupBuf([ring[k], ring[k + 1]], ring_t[:, k * SLOT:(k + 2) * SLOT])
        b = ring[ring_i[0] % NSLOT]
        ring_i[0] += 1
        return b

    def nio():
        b = io[io_i[0] % 3]
        io_i[0] += 1
        return b

    def ntmp():
        b = tmpA[tmpA_i[0] % 2]
        tmpA_i[0] += 1
        return b

    def nsq():
        b = sqb[sq_i[0] % 2]
        sq_i[0] += 1
        return b

    def wload(src, a, b_=None):
        n = a * (b_ or 1)
        sl = nslot(n)
        if b_ is None:
            view = sl[:, 0:n]
        else:
            view = sl[:, 0:n].rearrange("p (a b) -> p a b", a=a)
        fw.dma(pool, view, src, writes=[sl], key=sl)
        return sl, view

    def mm(outb, out_ap, terms, reads, start=True, stop=True):
        n = len(terms)
        fns = []
        for i, (l, r) in enumerate(terms):
            fns.append(lambda l=l, r=r, i=i: T.matmul(out_ap, l, r, start=(start and i == 0), stop=(stop and i == n - 1), skip_group_check=True))
        fw.op(pe, fns, reads=reads, writes=[outb])

    def dump(name, ap, b):
        if dbg and name in dbg_out:
            fw.dma(sp, dbg_out[name], ap, reads=[b], key=b)

    fw.dma(sp, cb[:, :], cst[:, :], writes=[cb], key=cb)
    fw.op(dve, lambda: V.memset(ones_b[:, :], 1.0), writes=[ones_b])

    vst = fw.buf("vst", [P, 2, P], F32)
    fw.op(dve, lambda: V.memset(vst[:, :, :], 0.0), writes=[vst])
    rowsA = [("b_ada", 72), ("norm_ff1", 8), ("norm_mix", 8), ("norm_ff2", 8), ("norm_final", 8), ("q_a_norm", 3), ("kv_a_norm", 2)]
    r_ = 0
    for n_, k_ in rowsA:
        fw.dma(sp, vst[r_:r_ + k_, 0, :], W[n_].rearrange("(j p) -> j p", p=P), writes=[vst], key=vst)
        r_ += k_
    NA = r_
    fw.dma(sp, vst[0:16, 1, :], W["conv_w"].rearrange("j (hh p) -> (j hh) p", p=P), writes=[vst], key=vst)
    fw.dma(sp, vst[16:20, 1, :], W["conv_b"].rearrange("(hh p) -> hh p", p=P), writes=[vst], key=vst)
    fw.dma(sp, vst[20:24, 1, :], W["mlstm_norm"], writes=[vst], key=vst)
    fw.dma(sp, vst[24:40, 1, :], c_in.rearrange("b (kc p) -> (b kc) p", p=P), writes=[vst], key=vst)
    NB_ = 40
    vT = fw.buf("vT", [P, NA + NB_], F32)
    pvt = nps()
    fw.op(pe, lambda: T.transpose(pvt[:, 0:NA], vst[0:NA, 0, :], ident[0:NA, 0:NA]), reads=[vst, cb], writes=[pvt])
    fw.op(pe, lambda: T.transpose(pvt[:, NA:NA + NB_], vst[0:NB_, 1, :], ident[0:NB_, 0:NB_]), reads=[vst, cb], writes=[pvt])
    fw.op(act, lambda: A.copy(out=vT[:, :], in_=pvt[:, 0:NA + NB_]), reads=[pvt], writes=[vT])

    class SubBuf:
        def __init__(self, parent, ap):
            self.__dict__["parent"] = parent
            self.__dict__["ap"] = ap

        def __getattr__(self, k):
            return getattr(self.__dict__["parent"], k)

        def __setattr__(self, k, v):
            setattr(self.__dict__["parent"], k, v)

        def __getitem__(self, k):
            return self.__dict__["ap"][k]

    badaT = SubBuf(vT, vT[:, 0:72])
    nT = {k: SubBuf(vT, vT[:, 72 + 8 * i:80 + 8 * i]) for i, k in enumerate(["norm_ff1", "norm_mix", "norm_ff2", "norm_final"])}
    qnT = SubBuf(vT, vT[:, 104:107])
    kvnT = SubBuf(vT, vT[:, 107:109])
    convwT = SubBuf(vT, vT[:, NA:NA + 16].rearrange("p (j h) -> p j h", j=4))
    convbT = SubBuf(vT, vT[:, NA + 16:NA + 20])
    mnT = SubBuf(vT, vT[:, NA + 20:NA + 24])
    cT = SubBuf(vT, vT[:, NA + 24:NA + 40].rearrange("p (b k) -> p b k", b=NSEQ))
    bi_b = fw.buf("bi_b", [P, 4], F32)
    fw.dma(sp, bi_b[:, :], W["b_i"].rearrange("(o n) -> o n", o=1).partition_broadcast(P), writes=[bi_b], key=bi_b)
    bf_b = fw.buf("bf_b", [P, 4], F32)
    fw.dma(sp, bf_b[:, :], W["b_f"].rearrange("(o n) -> o n", o=1).partition_broadcast(P), writes=[bf_b], key=bf_b)
    for b_, sc in [(nT["norm_ff1"], 32.0), (nT["norm_mix"], 32.0), (nT["norm_ff2"], 32.0), (nT["norm_final"], 32.0),
                   (qnT, float(np.sqrt(384.0))), (kvnT, 16.0), (mnT, float(np.sqrt(128.0)))]:
        fw.op(dve, lambda b_=b_, sc=sc: V.tensor_scalar(out=b_.ap, in0=b_.ap, scalar1=sc, scalar2=None, op0=ALU.mult), reads=[b_], writes=[b_])
    kpeW = fw.buf("kpeW", [P, KC, 96], BF16)
    kpeWp = fw.buf("kpeWp", [P, KC, 96], BF16)
    wif = fw.buf("wif", [P, KC, 8], BF16)
    w_in_v = W["w_in"].rearrange("(kc p) n -> p kc n", p=P)
    if do_mixer:
        fw.op(dve, lambda: V.memset(kpeW[:, :, :], 0.0), writes=[kpeW])
        fw.op(dve, lambda: V.memset(kpeWp[:, :, :], 0.0), writes=[kpeWp])
        fw.dma(pool, kpeW[:, :, 64:96], w_in_v[:, :, 640:672], writes=[kpeW], key=kpeW)
        fw.dma(pool, kpeWp[:, :, 64:80], w_in_v[:, :, 656:672], writes=[kpeWp], key=kpeWp)
        fw.dma(pool, kpeWp[:, :, 80:96], w_in_v[:, :, 640:656], writes=[kpeWp], key=kpeWp)
        fw.dma(pool, wif[:, :, :], w_in_v[:, :, 2208:2216], writes=[wif], key=wif)

    def rstd_from(chunks, n):
        pss = nps()
        nch = len(chunks)
        for i, (b_, ap) in enumerate(chunks):
            sq = nsq()
            fw.op(act, lambda ap=ap, sq=sq: A.activation(out=sq[:, :], in_=ap, func=ACT.Square), reads=[b_], writes=[sq])
            mm(pss, pss[:, :], [(ones_b[:, :], sq[:, :])], [ones_b, sq], start=(i == 0), stop=(i == nch - 1))
        fw.op(act, lambda: A.activation(out=rstd[:, :], in_=pss[:, :], func=ACT.Sqrt, bias=epsc[n], scale=1.0), reads=[pss, cb], writes=[rstd])
        fw.op(dve, lambda: V.reciprocal(out=rstd[:, :], in_=rstd[:, :]), reads=[rstd], writes=[rstd])

    def norm_mod(k, b):
        rstd_from([(hb[kc], hb[kc][:, :]) for kc in range(KC)], 1024)
        for kc in range(KC):
            t = ntmp()
            fw.op(dve, lambda kc=kc, t=t: V.scalar_tensor_tensor(out=t[:, :], in0=hb[kc][:, :], scalar=Acol[:, k, b, kc:kc + 1], in1=rstd[:, :],
                                                                 op0=ALU.mult, op1=ALU.mult), reads=[hb[kc], Acol, rstd], writes=[t])
            fw.op(act, lambda kc=kc, t=t: A.activation(out=uT[kc][:, :], in_=t[:, :], func=ACT.Identity, bias=Bcol(k, b, kc), scale=1.0),
                  reads=[t, modT], writes=[uT[kc]])

    def ffn(k, b, wg, wu, wd, es):
        hT, _ = fw.bufs("hT", [P, TG], BF16, FC, es=es)
        sg = [fw.buf(f"sg{i}", [P, TG], F32, es=es) for i in range(2)]
        norm_mod(k, b)
        wg_v = wg.rearrange("(kc p) n -> p kc n", p=P)
        wu_v = wu.rearrange("(kc p) n -> p kc n", p=P)
        wd_v = wd.rearrange("(fc p) n -> p fc n", p=P)
        for fb in range(FC // 2):
            sg_, gv = wload(wg_v[:, :, fb * 256:(fb + 1) * 256], KC, 256)
            su_, uv = wload(wu_v[:, :, fb * 256:(fb + 1) * 256], KC, 256)
            for j in range(2):
                f = 2 * fb + j
                pg = nps()
                pu = nps()
                mm(pg, pg[:, :], [(gv[:, kc, j * 128:(j + 1) * 128], uT[kc][:, :]) for kc in range(KC)], [sg_] + uT)
                mm(pu, pu[:, :], [(uv[:, kc, j * 128:(j + 1) * 128], uT[kc][:, :]) for kc in range(KC)], [su_] + uT)
                s_ = sg[f % 2]
                fw.op(act, lambda pg=pg, s_=s_: A.activation(out=s_[:, :], in_=pg[:, :], func=ACT.Silu), reads=[pg], writes=[s_])
                fw.op(dve, lambda pu=pu, s_=s_, f=f: V.tensor_tensor(out=hT[f][:, :], in0=s_[:, :], in1=pu[:, :], op=ALU.mult), reads=[s_, pu], writes=[hT[f]])
        for half in range(2):
            accs = [nps(pin=True) for _ in range(4)]
            f0 = 0
            while f0 < FC:
                nf = min(4, FC - f0)
                sd_, dv = wload(wd_v[:, f0:f0 + nf, half * 512:(half + 1) * 512], nf, 512)
                for fi in range(nf):
                    f = f0 + fi
                    for i in range(4):
                        mm(accs[i], accs[i][:, :], [(dv[:, fi, i * 128:(i + 1) * 128], hT[f][:, :])], [sd_, hT[f]], start=(f == 0), stop=(f == FC - 1))
                f0 += nf
            for i in range(4):
                dc = half * 4 + i
                po = accs[i]
                fw.op(dve, lambda dc=dc, po=po: V.scalar_tensor_tensor(out=hb[dc][:, :], in0=po[:, :], scalar=Gcol[:, k, b, dc:dc + 1], in1=hb[dc][:, :],
                                                                       op0=ALU.mult, op1=ALU.add), reads=[po, Gcol, hb[dc]], writes=[hb[dc]])
                unpin(po)

    if do_mixer:
        kT = [[fw.buf(f"kT{h}_{g}", [P, TG], BF16) for g in range(NG)] for h in range(NH)]
        Vt = [fw.buf(f"V{t}", [P, NH * 65 + 63], BF16) for t in range(S // P)]
        for t in range(S // P):
            fw.op(dve, lambda t=t: V.memset(Vt[t][:, :], 1.0), writes=[Vt[t]])
        for h_ in range(NH):
            for g_ in range(NG):
                fw.op(dve, lambda h_=h_, g_=g_: V.memset(kT[h_][g_][:, :], 0.0), writes=[kT[h_][g_]])
        Sf = [fw.buf(f"Sf{hh}", [P, 256], F32) for hh in range(4)]
        Sbf = [[fw.buf(f"Sbf{hh}_{i}", [P, 256], BF16) for i in range(2)] for hh in range(4)]
        sbf_i = [0, 0, 0, 0]
        xtail = fw.buf("xtail", [P, 4, 3], F32)
        posb = fw.buf("posb", [96, TG], I32)
        Ct = fw.buf("Ct", [96, TG], F32)
        St = fw.buf("St", [96, TG], F32)
        tg = (fw.buf("ang", [96, TG], F32), fw.buf("ki", [96, TG], I32), fw.buf("kf", [96, TG], F32), fw.buf("mm_", [96, TG], F32))
        attnT = [fw.buf(f"attnT{h}", [64, TG], BF16) for h in range(NH)]
        hnT, hn_full = fw.bufs("hnT", [P, TG], BF16, 4)

    def rope_tables(b, g):
        c0 = g * TG
        fw.dma(sp, posb[:, :], pos[b:b + 1, c0:c0 + TG].partition_broadcast(96), writes=[posb], key=posb)
        rope_table(St, invfS, 0.0, None, tg)
        rope_table(Ct, invfC, float(np.pi / 2), None, tg)

    def rope_table(out_t, invcol, shift, es, tg):
        ang, ki, kf, m = tg
        fw.op(dve, lambda: V.tensor_copy(out=kf[:, :], in_=posb[:, :]), reads=[posb], writes=[kf])
        fw.op(dve, lambda: V.tensor_scalar(out=ang[:, :], in0=kf[:, :], scalar1=invcol[0:96, :], scalar2=shift, op0=ALU.mult, op1=ALU.add), reads=[kf, cb], writes=[ang])
        fw.op(dve, lambda: V.tensor_scalar(out=ki[:, :], in0=ang[:, :], scalar1=float(1.0 / (2 * np.pi)), scalar2=None, op0=ALU.mult), reads=[ang], writes=[ki])
        fw.op(dve, lambda: V.tensor_copy(out=kf[:, :], in_=ki[:, :]), reads=[ki], writes=[kf])
        fw.op(dve, lambda: V.scalar_tensor_tensor(out=ang[:, :], in0=kf[:, :], scalar=-C1, in1=ang[:, :], op0=ALU.mult, op1=ALU.add), reads=[kf, ang], writes=[ang])
        fw.op(dve, lambda: V.scalar_tensor_tensor(out=ang[:, :], in0=kf[:, :], scalar=-C2, in1=ang[:, :], op0=ALU.mult, op1=ALU.add), reads=[kf, ang], writes=[ang])
        fw.op(dve, lambda: V.tensor_scalar(out=m[:, :], in0=ang[:, :], scalar1=float(np.pi), scalar2=-float(2 * np.pi), op0=ALU.is_gt, op1=ALU.mult), reads=[ang], writes=[m])
        fw.op(dve, lambda: V.tensor_tensor(out=ang[:, :], in0=ang[:, :], in1=m[:, :], op=ALU.add), reads=[ang, m], writes=[ang])
        fw.op(dve, lambda: V.tensor_scalar(out=m[:, :], in0=ang[:, :], scalar1=-float(np.pi), scalar2=float(2 * np.pi), op0=ALU.is_lt, op1=ALU.mult), reads=[ang], writes=[m])
        fw.op(dve, lambda: V.tensor_tensor(out=ang[:, :], in0=ang[:, :], in1=m[:, :], op=ALU.add), reads=[ang, m], writes=[ang])
        fw.op(act, lambda: A.activation(out=out_t[:, :], in_=ang[:, :], func=ACT.Sin), reads=[ang], writes=[out_t])

    def mixer(b, g):
        c0 = g * TG
        norm_mod(1, b)
        with contextlib.ExitStack() as es:
            qln, _ = fw.bufs("qln", [P, TG], BF16, 3, es=es)
            ckn, _ = fw.bufs("ckn", [P, TG], BF16, 2, es=es)
            qT = [fw.buf(f"qT{h}", [P, TG], BF16, es=es) for h in range(NH)]
            for h in range(NH):
                fw.op(dve, lambda h=h: V.memset(qT[h][:, :], 0.0), writes=[qT[h]])
            pT = [fw.buf(f"pT{i}", [P, TG], BF16, es=es) for i in range(3)]
            osb = [fw.buf(f"osb{i}", [64, TG], F32, es=es) for i in range(2)]
            rden = [fw.buf(f"rden{i}", [65, TG], F32, es=es) for i in range(2)]
            rhi = [fw.buf(f"rhi{i}", [65, TG], BF16, es=es) for i in range(2)]
            rlo = [fw.buf(f"rlo{i}", [65, TG], BF16, es=es) for i in range(2)]
            t1b = fw.buf("t1b", [96, TG], F32, es=es)
            t2b = fw.buf("t2b", [96, TG], F32, es=es)


            def lat(col0, nch, outs, gcol, n):
                sl, wv = wload(w_in_v[:, :, col0:col0 + nch * P], KC, nch * P)
                pl = []
                for j in range(nch):
                    pj = nps()
                    mm(pj, pj[:, :], [(wv[:, kc, j * P:(j + 1) * P], uT[kc][:, :]) for kc in range(KC)], [sl] + uT)
                    pl.append(pj)
                rstd_from([(pj, pj[:, :]) for pj in pl], n)
                for j in range(nch):
                    fw.op(dve, lambda j=j: V.scalar_tensor_tensor(out=outs[j][:, :], in0=pl[j][:, :], scalar=gcol[:, j:j + 1], in1=rstd[:, :],
                                                                  op0=ALU.mult, op1=ALU.mult), reads=[pl[j], gcol, rstd], writes=[outs[j]])
            lat(0, 3, qln, qnT, 384)
            lat(384, 2, ckn, kvnT, 256)
            pk = nps()
            pkp = nps()
            mm(pk, pk[0:96, :], [(kpeW[:, kc, :], uT[kc][:, :]) for kc in range(KC)], [kpeW] + uT)
            mm(pkp, pkp[0:96, :], [(kpeWp[:, kc, :], uT[kc][:, :]) for kc in range(KC)], [kpeWp] + uT)
            fw.op(dve, lambda: V.tensor_tensor(out=t1b[64:96, :], in0=pk[64:96, :], in1=Ct[64:96, :], op=ALU.mult), reads=[pk, Ct], writes=[t1b])
            fw.op(dve, lambda: V.tensor_tensor(out=t2b[64:96, :], in0=pkp[64:96, :], in1=St[64:96, :], op=ALU.mult), reads=[pkp, St], writes=[t2b])
            for h in range(NH):
                fw.op(dve, lambda h=h: V.tensor_tensor(out=kT[h][g][64:96, :], in0=t1b[64:96, :], in1=t2b[64:96, :], op=ALU.add), reads=[t1b, t2b], writes=[kT[h][g]])
            slkv, wkv = wload(W["w_kv_b"].rearrange("(j p) n -> p j n", p=P), 2, 1024)
            for h in range(NH):
                pn = nps()
                mm(pn, pn[0:64, :], [(wkv[:, j, h * 128:h * 128 + 64], ckn[j][:, :]) for j in range(2)], [slkv] + ckn)
                fw.op(act, lambda h=h, pn=pn: A.copy(out=kT[h][g][0:64, :], in_=pn[0:64, :]), reads=[pn], writes=[kT[h][g]])
            for tt in range(NTT):
                pv = nps()
                mm(pv, pv[:, :],
                   [(ckn[j][:, tt * P:(tt + 1) * P], wkv[:, j, :].rearrange("p (h two d) -> p h two d", two=2, d=64)[:, :, 1, :]) for j in range(2)], [slkv] + ckn)
                vt = Vt[g * NTT + tt]
                fw.op(act, lambda pv=pv, vt=vt: A.copy(out=vt[:, 0:NH * 65].rearrange("p (h d) -> p h d", d=65)[:, :, 0:64], in_=pv[:, :].rearrange("p (h d) -> p h d", d=64)), reads=[pv], writes=[vt])
            slq, wq = wload(W["w_q_b"].rearrange("(j p) n -> p j n", p=P), 3, 768)
            slqp, wqp = wload(W["w_q_bp"].rearrange("(j p) n -> p j n", p=P), 3, 768)
            for h in range(NH):
                pq = nps()
                pqp = nps()
                mm(pq, pq[0:96, :], [(wq[:, j, h * 96:(h + 1) * 96], qln[j][:, :]) for j in range(3)], [slq] + qln)
                mm(pqp, pqp[0:96, :], [(wqp[:, j, h * 96:(h + 1) * 96], qln[j][:, :]) for j in range(3)], [slqp] + qln)
                fw.op(dve, lambda pq=pq: V.tensor_tensor(out=t1b[:, :], in0=pq[0:96, :], in1=Ct[:, :], op=ALU.mult), reads=[pq, Ct], writes=[t1b])
                fw.op(dve, lambda pqp=pqp: V.tensor_tensor(out=t2b[:, :], in0=pqp[0:96, :], in1=St[:, :], op=ALU.mult), reads=[pqp, St], writes=[t2b])
                fw.op(dve, lambda h=h: V.tensor_tensor(out=qT[h][0:96, :], in0=t1b[:, :], in1=t2b[:, :], op=ALU.add), reads=[t1b, t2b], writes=[qT[h]])
            sc = float(96.0 ** -0.5)
            nkt = 4 * g + 4
            items = [(h, kt) for h in range(NH) for kt in range(nkt)]
            pti = [0]

            def qk(h, kt):
                jd = kt - 4 * g
                q0 = 128 * jd if jd > 0 else 0
                n = TG - q0
                pss = nps()
                kb = kT[h][kt // 4]
                mm(pss, pss[:, 0:n], [(kb[:, (kt % 4) * P:(kt % 4 + 1) * P], qT[h][:, q0:TG])], [kb, qT[h]])
                return pss, q0, n, jd

            def epi1(h, po):
                ob = osb[h % 2]
                rd, rh, rl = rden[h % 2], rhi[h % 2], rlo[h % 2]
                fw.op(act, lambda: A.copy(out=ob[:, :], in_=po[0:64, :]), reads=[po], writes=[ob])
                fw.op(dve, lambda: V.reciprocal(out=rd[64:65, :], in_=po[64:65, :]), reads=[po], writes=[rd])
                unpin(po)
                fw.op(dve, lambda: V.tensor_copy(out=rh[64:65, :], in_=rd[64:65, :]), reads=[rd], writes=[rh])
                fw.op(dve, lambda: V.tensor_tensor(out=rl[64:65, :], in0=rd[64:65, :], in1=rh[64:65, :], op=ALU.subtract), reads=[rd, rh], writes=[rl])

            def epi2(h):
                ob = osb[h % 2]
                rh, rl = rhi[h % 2], rlo[h % 2]
                pb = nps()
                mm(pb, pb[0:64, :], [(ones_b[64:65, 0:64], rh[64:65, :]), (ones_b[64:65, 0:64], rl[64:65, :])], [ones_b, rh, rl])
                fw.op(dve, lambda: V.tensor_tensor(out=attnT[h][:, :], in0=ob[:, :], in1=pb[0:64, :], op=ALU.mult), reads=[ob, pb], writes=[attnT[h]])

            AHEAD = 2
            inflight = [qk(*items[i]) for i in range(min(AHEAD, len(items)))]
            po = None
            pending = None
            for idx, (h, kt) in enumerate(items):
                if idx + AHEAD < len(items):
                    inflight.append(qk(*items[idx + AHEAD]))
                cur = inflight.pop(0)
                if kt == 0:
                    po = nps(pin=True)
                pss, q0, n, jd = cur
                pt = pT[pti[0] % 3]
                pti[0] += 1
                fw.op(act, lambda pss=pss, pt=pt, n=n: A.activation(out=pt[:, 0:n], in_=pss[:, 0:n], func=ACT.Exp, scale=sc), reads=[pss], writes=[pt])
                if jd >= 0:
                    fw.op(dve, lambda pt=pt: V.tensor_tensor(out=pt[:, 0:P], in0=pt[:, 0:P], in1=tri_f, op=ALU.mult), reads=[pt, cb], writes=[pt])
                vt = Vt[kt]
                mm(po, po[:, q0:TG], [(vt[:, h * 65:h * 65 + P], pt[:, 0:n])], [vt, pt], start=(kt == 0), stop=(kt == nkt - 1))
                if pending is not None:
                    pending[1] -= 1
                    if pending[1] <= 0:
                        epi2(pending[0])
                        pending = None
                if kt == nkt - 1:
                    if pending is not None:
                        epi2(pending[0])
                    epi1(h, po)
                    pending = [h, 3]
            if pending is not None:
                epi2(pending[0])
            fw.barrier()
        with contextlib.ExitStack() as es:
            xbuf = [fw.buf(f"xbuf{i}", [P, TG + 3], F32, es=es) for i in range(2)]
            acc = [fw.buf(f"acc{i}", [P, TG], F32, es=es) for i in range(2)]
            xcT = [fw.buf(f"xcT{hh}", [P, TG], BF16, es=es) for hh in range(4)]
            og = fw.buf("og", [P, 4, TG], BF16, es=es)
            gsm = {n: fw.buf(f"g_{n}", [P, NTT, 4], F32, es=es) for n in ["logi", "logf", "a", "ea", "ks", "t"]}
            eg = fw.buf("eg", [P, NTT, 8], F32, es=es)
            rep2 = [fw.buf("rep", [P, 4, P], F32, es=es)] * 2
            Eb2 = [fw.buf("Eb", [P, 4, P], F32, es=es)] * 2
            qpT2 = [fw.buf(f"qpT{i}", [P, 4, P], BF16, es=es) for i in range(2)]
            kTm2 = [fw.buf("kTm", [P, 4, P], BF16, es=es)] * 2
            kpp2 = [fw.buf(f"kpp{i}", [P, 4, P], BF16, es=es) for i in range(2)]
            vm2 = [fw.buf(f"vm{i}", [P, TG], BF16, es=es) for i in range(2)]
            smT2 = [fw.buf(f"smT{i}", [P, 4, P], BF16, es=es) for i in range(2)]
            dd = fw.buf("dd", [P, 4, P], F32, es=es)
            hg = fw.buf("hg", [P, 4, P], F32, es=es)
            sq4 = fw.buf("sq4", [P, 4, P], BF16, es=es)
            rs4 = fw.buf("rs4", [P, 4, P], F32, es=es)

            pgi = nps()
            for tt in range(NTT):
                mm(pgi, pgi[:, tt * 8:(tt + 1) * 8], [(uT[kc][:, tt * P:(tt + 1) * P], wif[:, kc, :]) for kc in range(KC)], [wif] + uT)
            gview = pgi[:, 0:32].rearrange("p (t c) -> p t c", c=8)
            fw.op(dve, lambda: V.tensor_tensor(out=gsm["logi"][:, :, :], in0=gview[:, :, 0:4], in1=bi_b[:, :].unsqueeze(1).broadcast_to([P, NTT, 4]), op=ALU.add),
                  reads=[pgi, bi_b], writes=[gsm["logi"]])
            fw.op(dve, lambda: V.tensor_tensor(out=gsm["t"][:, :, :], in0=gview[:, :, 4:8], in1=bf_b[:, :].unsqueeze(1).broadcast_to([P, NTT, 4]), op=ALU.add),
                  reads=[pgi, bf_b], writes=[gsm["t"]])
            fw.op(act, lambda: A.activation(out=gsm["t"][:, :, :], in_=gsm["t"][:, :, :], func=ACT.Exp, scale=-1.0), reads=[gsm["t"]], writes=[gsm["t"]])
            fw.op(act, lambda: A.activation(out=gsm["t"][:, :, :], in_=gsm["t"][:, :, :], func=ACT.Ln, bias=onec, scale=1.0), reads=[gsm["t"], cb], writes=[gsm["t"]])
            fw.op(dve, lambda: V.tensor_scalar(out=gsm["logf"][:, :, :], in0=gsm["t"][:, :, :], scalar1=-1.0, scalar2=None, op0=ALU.mult), reads=[gsm["t"]], writes=[gsm["logf"]])
            slx, wx = wload(w_in_v[:, :, 672:1184], KC, 512)
            for hh in range(4):
                px = nps()
                mm(px, px[:, :], [(wx[:, kc, hh * P:(hh + 1) * P], uT[kc][:, :]) for kc in range(KC)], [slx] + uT)
                xb_ = xbuf[hh % 2]
                ac = acc[hh % 2]
                if g == 0:
                    fw.op(dve, lambda xb_=xb_: V.memset(xb_[:, 0:3], 0.0), writes=[xb_])
                else:
                    fw.op(dve, lambda xb_=xb_, hh=hh: V.tensor_copy(out=xb_[:, 0:3], in_=xtail[:, hh, :]), reads=[xtail], writes=[xb_])
                fw.op(act, lambda xb_=xb_, px=px: A.copy(out=xb_[:, 3:TG + 3], in_=px[:, :]), reads=[px], writes=[xb_])
                fw.op(dve, lambda xb_=xb_, hh=hh: V.tensor_copy(out=xtail[:, hh, :], in_=xb_[:, TG:TG + 3]), reads=[xb_], writes=[xtail])
                fw.op(act, lambda xb_=xb_, ac=ac, hh=hh: A.activation(out=ac[:, :], in_=xb_[:, 3:TG + 3], func=ACT.Identity, bias=convbT[:, hh:hh + 1], scale=convwT[:, 3, hh:hh + 1]),
                      reads=[xb_, convbT, convwT], writes=[ac])
                for j in range(3):
                    fw.op(dve, lambda xb_=xb_, ac=ac, hh=hh, j=j: V.scalar_tensor_tensor(out=ac[:, :], in0=xb_[:, j:j + TG], scalar=convwT[:, j, hh:hh + 1], in1=ac[:, :],
                                                                                         op0=ALU.mult, op1=ALU.add), reads=[xb_, convwT, ac], writes=[ac])
                fw.op(act, lambda ac=ac, hh=hh: A.activation(out=xcT[hh][:, :], in_=ac[:, :], func=ACT.Silu), reads=[ac], writes=[xcT[hh]])
            slo, wo_ = wload(w_in_v[:, :, 1696:2208], KC, 512)
            for hh in range(4):
                pg_ = nps()
                mm(pg_, pg_[:, :], [(wo_[:, kc, hh * P:(hh + 1) * P], uT[kc][:, :]) for kc in range(KC)], [slo] + uT)
                fw.op(act, lambda pg_=pg_, hh=hh: A.activation(out=og[:, hh, :], in_=pg_[:, :], func=ACT.Sigmoid), reads=[pg_], writes=[og])
            pgs = nps()
            for tt in range(NTT):
                for i_, lhs in enumerate([Mbd, BDm, SEL[0], SEL[1]]):
                    mm(pgs, pgs[:, tt * 16 + i_ * 4: tt * 16 + i_ * 4 + 4], [(lhs, gsm["logf"][:, tt, :])], [cb, gsm["logf"]])
            gs = pgs[:, 0:64].rearrange("p (t c) -> p t c", c=16)
            fw.op(dve, lambda: V.tensor_tensor(out=gsm["a"][:, :, :], in0=gsm["logi"][:, :, :], in1=gs[:, :, 0:4], op=ALU.subtract), reads=[gsm["logi"], pgs], writes=[gsm["a"]])
            fw.op(act, lambda: A.activation(out=gsm["ea"][:, :, :], in_=gsm["a"][:, :, :], func=ACT.Exp), reads=[gsm["a"]], writes=[gsm["ea"]])
            fw.op(dve, lambda: V.tensor_tensor(out=gsm["t"][:, :, :], in0=gsm["a"][:, :, :], in1=gs[:, :, 4:8], op=ALU.add), reads=[gsm["a"], pgs], writes=[gsm["t"]])
            fw.op(act, lambda: A.activation(out=gsm["ks"][:, :, :], in_=gsm["t"][:, :, :], func=ACT.Exp, bias=lnsc, scale=1.0), reads=[gsm["t"], cb], writes=[gsm["ks"]])
            fw.op(act, lambda: A.activation(out=eg[:, :, :], in_=gs[:, :, 8:16], func=ACT.Exp), reads=[pgs], writes=[eg])
            if dbg and "logf" in dbg_out and g == 0 and b == 0:
                dump("logf", gsm["logf"][:, :, :], gsm["logf"])
                dump("logi", gsm["logi"][:, :, :], gsm["logi"])
                dump("ga", gsm["a"][:, :, :], gsm["a"])
                dump("eg", eg[:, :, :], eg)
            slv, wv_ = wload(w_in_v[:, :, 1184:1696], KC, 512)
            slqm, wqm = wload(W["w_q_m"].rearrange("h d e -> d h e"), 4, P)
            slkm, wkm = wload(W["w_k_m"].rearrange("h d e -> d h e"), 4, P)
            if g == 0:
                for hh in range(4):
                    fw.op(dve, lambda hh=hh: V.memset(Sf[hh][:, :], 0.0), writes=[Sf[hh]])
                    fw.op(dve, lambda hh=hh: V.memset(Sbf[hh][sbf_i[hh] % 2][:, :], 0.0), writes=[Sbf[hh][sbf_i[hh] % 2]])
            def front(tt):
                i2 = tt % 2
                tc_ = slice(tt * P, (tt + 1) * P)
                vm, kpp, rep, Eb, qpT, kTm, smT = vm2[i2], kpp2[i2], rep2[i2], Eb2[i2], qpT2[i2], kTm2[i2], smT2[i2]
                pvm = nps()
                mm(pvm, pvm[:, :], [(uT[kc][:, tc_], wv_[:, kc, :]) for kc in range(KC)], [slv] + uT)
                fw.op(act, lambda: A.copy(out=vm[:, :], in_=pvm[:, :]), reads=[pvm], writes=[vm])
                pk2 = nps()
                for hh in range(4):
                    mm(pk2, pk2[:, hh * P:(hh + 1) * P], [(xcT[hh][:, tc_], wkm[:, hh, :])], [xcT[hh], slkm])
                fw.op(dve, lambda: V.tensor_tensor(out=kpp[:, :, :], in0=pk2[:, :].rearrange("p (h e) -> p h e", e=P),
                                                   in1=gsm["ks"][:, tt, :].unsqueeze(2).broadcast_to([P, 4, P]), op=ALU.mult), reads=[pk2, gsm["ks"]], writes=[kpp])
                pbb = nps()
                fw.op(dve, lambda: V.tensor_copy(out=rep[:, :, :], in_=gsm["logf"][:, tt, :].unsqueeze(2).broadcast_to([P, 4, P])), reads=[gsm["logf"]], writes=[rep])
                for hh in range(4):
                    mm(pbb, pbb[:, hh * P:(hh + 1) * P], [(rep[:, hh, :], Mbd)], [rep, cb])
                fw.op(act, lambda: A.activation(out=Eb[:, :, :], in_=pbb[:, :].rearrange("p (h e) -> p h e", e=P), func=ACT.Exp), reads=[pbb], writes=[Eb])
                pq_ = nps()
                pk_ = nps()
                for hh in range(4):
                    mm(pq_, pq_[:, hh * P:(hh + 1) * P], [(wqm[:, hh, :], xcT[hh][:, tc_])], [slqm, xcT[hh]])
                    mm(pk_, pk_[:, hh * P:(hh + 1) * P], [(wkm[:, hh, :], xcT[hh][:, tc_])], [slkm, xcT[hh]])
                fw.op(dve, lambda: V.tensor_tensor(out=qpT[:, :, :], in0=pq_[:, :].rearrange("p (h e) -> p h e", e=P), in1=Eb[:, :, :], op=ALU.mult), reads=[pq_, Eb], writes=[qpT])
                fw.op(act, lambda: A.activation(out=kTm[:, :, :], in_=pk_[:, :].rearrange("p (h e) -> p h e", e=P), func=ACT.Identity, scale=float(128.0 ** -0.5)), reads=[pk_], writes=[kTm])
                pS = nps()
                for hh in range(4):
                    mm(pS, pS[:, hh * P:(hh + 1) * P], [(kTm[:, hh, :], qpT[:, hh, :])], [kTm, qpT])
                for hh in range(4):
                    fw.op(dve, lambda hh=hh: V.scalar_tensor_tensor(out=smT[:, hh, :], in0=pS[:, hh * P:(hh + 1) * P], scalar=gsm["ea"][:, tt, hh:hh + 1], in1=Mbd,
                                                                    op0=ALU.mult, op1=ALU.mult), reads=[pS, gsm["ea"], cb], writes=[smT])

            def back(tt):
                i2 = tt % 2
                tc_ = slice(tt * P, (tt + 1) * P)
                vm, kpp, qpT, smT = vm2[i2], kpp2[i2], qpT2[i2], smT2[i2]
                pnum = nps(pin=True)
                pden = nps(pin=True)
                for hh in range(4):
                    hs = slice(hh * P, (hh + 1) * P)
                    mm(pnum, pnum[:, hs], [(vm[:, hs], smT[:, hh, :])], [vm, smT], start=(hh == 0), stop=False)
                    mm(pden, pden[:, hs], [(ones_b[:, :], smT[:, hh, :])], [ones_b, smT], start=(hh == 0), stop=False)
                for c in range(2):
                    cs = slice(c * 64, (c + 1) * 64)
                    for hh in range(4):
                        sb_ = Sbf[hh][sbf_i[hh] % 2]
                        mm(pnum, pnum[:, hh * P + c * 64: hh * P + (c + 1) * 64], [(sb_[:, 0:P], qpT[:, hh, cs])], [sb_, qpT], start=False, stop=True)
                        mm(pden, pden[:, hh * P + c * 64: hh * P + (c + 1) * 64], [(sb_[:, P:2 * P], qpT[:, hh, cs])], [sb_, qpT], start=False, stop=True)
                        pup = nps()
                        mm(pup, pup[:, 0:P], [(kpp[cs, hh, :], vm[cs, hh * P:(hh + 1) * P])], [kpp, vm])
                        mm(pup, pup[:, P:2 * P], [(kpp[cs, hh, :], ones_b[cs, :])], [kpp, ones_b])
                        fw.op(dve, lambda hh=hh, pup=pup, c=c: V.scalar_tensor_tensor(out=Sf[hh][:, :], in0=Sf[hh][:, :], scalar=eg[:, tt, c * 4 + hh:c * 4 + hh + 1], in1=pup[:, 0:256],
                                                                                    op0=ALU.mult, op1=ALU.add), reads=[Sf[hh], eg, pup], writes=[Sf[hh]])
                        sbf_i[hh] += 1
                        nb_ = Sbf[hh][sbf_i[hh] % 2]
                        fw.op(act, lambda hh=hh, nb_=nb_: A.copy(out=nb_[:, :], in_=Sf[hh][:, :]), reads=[Sf[hh]], writes=[nb_])
                fw.op(act, lambda: A.activation(out=dd[:, :, :], in_=pden[:, :].rearrange("p (h e) -> p h e", e=P), func=ACT.Abs), reads=[pden], writes=[dd])
                fw.op(dve, lambda: V.tensor_scalar(out=dd[:, :, :], in0=dd[:, :, :], scalar1=1.0, scalar2=None, op0=ALU.max), reads=[dd], writes=[dd])
                fw.op(dve, lambda: V.reciprocal(out=dd[:, :, :], in_=dd[:, :, :]), reads=[dd], writes=[dd])
                fw.op(dve, lambda: V.tensor_tensor(out=hg[:, :, :], in0=pnum[:, :].rearrange("p (h e) -> p h e", e=P), in1=dd[:, :, :], op=ALU.mult), reads=[pnum, dd], writes=[hg])
                unpin(pnum)
                unpin(pden)
                fw.op(dve, lambda: V.tensor_tensor(out=hg[:, :, :], in0=hg[:, :, :], in1=og[:, :, tc_], op=ALU.mult), reads=[hg, og], writes=[hg])
                fw.op(act, lambda: A.activation(out=sq4[:, :, :], in_=hg[:, :, :], func=ACT.Square), reads=[hg], writes=[sq4])
                pss4 = nps()
                mm(pss4, pss4[:, :], [(ones_b[:, :], sq4[:, :, :].rearrange("p h e -> p (h e)"))], [ones_b, sq4])
                fw.op(act, lambda: A.activation(out=rs4[:, :, :], in_=pss4[:, :].rearrange("p (h e) -> p h e", e=P), func=ACT.Sqrt, bias=epsc[128], scale=1.0), reads=[pss4, cb], writes=[rs4])
                fw.op(dve, lambda: V.reciprocal(out=rs4[:, :, :], in_=rs4[:, :, :]), reads=[rs4], writes=[rs4])
                for hh in range(4):
                    fw.op(dve, lambda hh=hh: V.scalar_tensor_tensor(out=hnT[hh][:, tc_], in0=hg[:, hh, :], scalar=mnT[:, hh:hh + 1], in1=rs4[:, hh, :],
                                                                    op0=ALU.mult, op1=ALU.mult), reads=[hg, mnT, rs4], writes=[hnT[hh]])

            front(0)
            for tt in range(NTT):
                if tt + 1 < NTT:
                    front(tt + 1)
                back(tt)
            fw.barrier()
        with contextlib.ExitStack() as es:
            yT, _ = fw.bufs("yT", [P, TG], BF16, KC, es=es)
            s0 = fw.buf("s0", [P, TG], F32, es=es)
            s1 = fw.buf("s1", [P, TG], F32, es=es)
            y1 = fw.buf("y1", [P, TG], F32, es=es)
            y2 = fw.buf("y2", [P, TG], F32, es=es)
            wmla_v = W["w_mla_out"].rearrange("(h d) n -> d h n", d=64)
            wmls_v = W["w_mlstm_out"].rearrange("(hh p) n -> p hh n", p=P)
            for dp in range(4):
                ds_ = slice(dp * 256, (dp + 1) * 256)
                sla = nslot()
                wa = sla[0:64, 0:2048].rearrange("p (a b) -> p a b", a=8)
                fw.dma(pool, wa, wmla_v[:, :, ds_], writes=[sla], key=sla)
                slb, wb = wload(wmls_v[:, :, ds_], 4, 256)
                slg0, wg0 = wload(w_in_v[:, :, 2216 + dp * 256:2216 + (dp + 1) * 256], KC, 256)
                slg1, wg1 = wload(w_in_v[:, :, 3240 + dp * 256:3240 + (dp + 1) * 256], KC, 256)
                for j in range(2):
                    dc = 2 * dp + j
                    js = slice(j * P, (j + 1) * P)
                    pa = nps()
                    pb_ = nps()
                    p0 = nps()
                    p1 = nps()
                    mm(pa, pa[:, :], [(wa[:, h, js], attnT[h][:, :]) for h in range(NH)], [sla] + attnT)
                    mm(pb_, pb_[:, :], [(wb[:, hh, js], hnT[hh][:, :]) for hh in range(4)], [slb] + hnT)
                    mm(p0, p0[:, :], [(wg0[:, kc, js], uT[kc][:, :]) for kc in range(KC)], [slg0] + uT)
                    mm(p1, p1[:, :], [(wg1[:, kc, js], uT[kc][:, :]) for kc in range(KC)], [slg1] + uT)
                    fw.op(act, lambda p0=p0: A.activation(out=s0[:, :], in_=p0[:, :], func=ACT.Sigmoid), reads=[p0], writes=[s0])
                    fw.op(act, lambda p1=p1: A.activation(out=s1[:, :], in_=p1[:, :], func=ACT.Sigmoid), reads=[p1], writes=[s1])
                    fw.op(dve, lambda pa=pa: V.tensor_tensor(out=y1[:, :], in0=s0[:, :], in1=pa[:, :], op=ALU.mult), reads=[s0, pa], writes=[y1])
                    fw.op(dve, lambda pb_=pb_: V.tensor_tensor(out=y2[:, :], in0=s1[:, :], in1=pb_[:, :], op=ALU.mult), reads=[s1, pb_], writes=[y2])
                    fw.op(dve, lambda dc=dc: V.tensor_tensor(out=yT[dc][:, :], in0=y1[:, :], in1=y2[:, :], op=ALU.add), reads=[y1, y2], writes=[yT[dc]])
            wo_v = W["w_o"].rearrange("(kc p) n -> p kc n", p=P)
            for dp in range(4):
                slo_, wov = wload(wo_v[:, :, dp * 256:(dp + 1) * 256], KC, 256)
                for j in range(2):
                    dc = 2 * dp + j
                    po = nps()
                    mm(po, po[:, :], [(wov[:, kc, j * P:(j + 1) * P], yT[kc][:, :]) for kc in range(KC)], [slo_] + yT)
                    fw.op(dve, lambda dc=dc, po=po: V.scalar_tensor_tensor(out=hb[dc][:, :], in0=po[:, :], scalar=Gcol[:, 1, b, dc:dc + 1], in1=hb[dc][:, :],
                                                                           op0=ALU.mult, op1=ALU.add), reads=[po, Gcol, hb[dc]], writes=[hb[dc]])
            fw.barrier()

    def load_tile(gi, tt):
        b, g = gi // NG, gi % NG
        r0 = b * S + g * TG
        xb_ = nio()
        fw.dma(sp, xb_[:, :], x[r0 + tt * P:r0 + (tt + 1) * P, :], writes=[xb_], key=xb_)
        for half in range(2):
            pt_ = nps()
            for i in range(4):
                kc = half * 4 + i
                fw.op(pe, lambda pt_=pt_, xb_=xb_, kc=kc, i=i: T.transpose(pt_[:, i * P:(i + 1) * P], xb_[:, kc * P:(kc + 1) * P], ident), reads=[xb_, cb], writes=[pt_])
            if half == 0:
                fw.op(act, lambda pt_=pt_, half=half, tt=tt: A.copy(out=h_full[:, half * 4:(half + 1) * 4, tt * P:(tt + 1) * P], in_=pt_[:, :].rearrange("p (a t) -> p a t", a=4)),
                      reads=[pt_], writes=hb[half * 4:(half + 1) * 4])
            else:
                fw.op(dve, lambda pt_=pt_, half=half, tt=tt: V.tensor_copy(out=h_full[:, half * 4:(half + 1) * 4, tt * P:(tt + 1) * P], in_=pt_[:, :].rearrange("p (a t) -> p a t", a=4)),
                      reads=[pt_], writes=hb[half * 4:(half + 1) * 4])

    def load_group(gi):
        for tt in range(NTT):
            load_tile(gi, tt)
        if do_mixer:
            rope_tables(gi // NG, gi % NG)

    load_group(0)
    cTb = fw.buf("cTb", [P, NSEQ, KC], BF16)
    fw.op(act, lambda: A.activation(out=cTb[:, :, :], in_=cT[:, :, :], func=ACT.Silu), reads=[cT], writes=[cTb])
    modT = fw.buf("modT", [P, NSEQ, 72], F32)
    pm = nps()
    w_ada_v = W["w_ada"].rearrange("(kc p) n -> p kc n", p=P)
    for blk in range(36):
        sl, wv = wload(w_ada_v[:, :, blk * 256:(blk + 1) * 256], KC, 256)
        for j in range(2):
            jj = 2 * blk + j
            mm(pm, pm[:, jj * 2:(jj + 1) * 2], [(wv[:, kc, j * 128:(j + 1) * 128], cTb[:, :, kc]) for kc in range(KC)], [sl, cTb])
    for b in range(NSEQ):
        fw.op(dve, lambda b=b: V.tensor_tensor(out=modT[:, b, :], in0=pm[:, 0:144].rearrange("p (j b) -> p j b", b=2)[:, :, b],
                                               in1=badaT[:, :], op=ALU.add), reads=[pm, badaT], writes=[modT])
    Acol = fw.buf("Acol", [P, 3, NSEQ, KC], F32)
    Gcol = fw.buf("Gcol", [P, 3, NSEQ, KC], F32)
    gains = [nT["norm_ff1"], nT["norm_mix"], nT["norm_ff2"]]
    for k in range(3):
        for b in range(NSEQ):
            fw.op(dve, lambda k=k, b=b: V.scalar_tensor_tensor(out=Acol[:, k, b, :], in0=modT[:, b, (3 * k + 1) * 8:(3 * k + 2) * 8], scalar=1.0,
                                                               in1=gains[k][:, :], op0=ALU.add, op1=ALU.mult), reads=[modT, gains[k]], writes=[Acol])
            fw.op(dve, lambda k=k, b=b: V.tensor_scalar(out=Gcol[:, k, b, :], in0=modT[:, b, (3 * k + 2) * 8:(3 * k + 3) * 8],
                                                        scalar1=(1.0 if k == 1 else 0.5), scalar2=None, op0=ALU.mult), reads=[modT], writes=[Gcol])

    def Bcol(k, b, kc):
        return modT[:, b, 3 * k * 8 + kc:3 * k * 8 + kc + 1]


    out_bufs = []
    for gi in range(ngroups):
        b, g = gi // NG, gi % NG
        r0 = b * S + g * TG
        with contextlib.ExitStack() as es:
            ffn(0, b, W["ff1_w_gate"], W["ff1_w_up"], W["ff1_w_down"], es)
            fw.barrier()
        if gi == 0:
            dump("h1", h_full[:, :, :], hb[0])
        if do_mixer:
            mixer(b, g)
            if gi == 0:
                dump("h2", h_full[:, :, :], hb[0])
                if dbg and "att" in dbg_out:
                    for h_ in range(NH):
                        fw.dma(pool, dbg_out["att"][h_], attnT[h_][:, :], reads=[attnT[h_]], key=attnT[h_])
                    for h_ in range(4):
                        fw.dma(pool, dbg_out["ml"][h_], hnT[h_][:, :], reads=[hnT[h_]], key=hnT[h_])
        if do_ff2:
            with contextlib.ExitStack() as es:
                ffn(2, b, W["ff2_w_gate"], W["ff2_w_up"], W["ff2_w_down"], es)
                fw.barrier()
        rstd_from([(hb[kc], hb[kc][:, :]) for kc in range(KC)], 1024)
        with contextlib.ExitStack() as es:
            onb, on_full = fw.bufs("onb", [P, TG], F32, KC, es=es)
            for kc in range(KC):
                fw.op(dve, lambda kc=kc: V.scalar_tensor_tensor(out=onb[kc][:, :], in0=hb[kc][:, :], scalar=nT["norm_final"][:, kc:kc + 1], in1=rstd[:, :],
                                                                op0=ALU.mult, op1=ALU.mult), reads=[hb[kc], nT["norm_final"], rstd], writes=[onb[kc]])
            for tt in range(NTT):
                if gi + 1 < ngroups:
                    load_tile(gi + 1, tt)
                ob_ = nio()
                for half in range(2):
                    pt_ = nps()
                    for i in range(4):
                        kc = half * 4 + i
                        fw.op(pe, lambda pt_=pt_, kc=kc, i=i, tt=tt: T.transpose(pt_[:, i * P:(i + 1) * P], onb[kc][:, tt * P:(tt + 1) * P], ident), reads=[onb[kc], cb], writes=[pt_])
                    if half == 0:
                        fw.op(act, lambda pt_=pt_, ob_=ob_, half=half: A.copy(out=ob_[:, half * 512:(half + 1) * 512], in_=pt_[:, :]), reads=[pt_], writes=[ob_])
                    else:
                        fw.op(dve, lambda pt_=pt_, ob_=ob_, half=half: V.tensor_copy(out=ob_[:, half * 512:(half + 1) * 512], in_=pt_[:, :]), reads=[pt_], writes=[ob_])
                fw.dma(sp, y[r0 + tt * P:r0 + (tt + 1) * P, :], ob_[:, :], reads=[ob_], key=ob_)
                out_bufs.append(ob_)
            if gi + 1 < ngroups and do_mixer:
                rope_tables((gi + 1) // NG, (gi + 1) % NG)
            fw.barrier()
    deps = []
    for ob_ in io:
        if ob_.dsem is not None:
            deps.append(('d', ob_.dsem, ob_.dval, id(ob_)))
    fw._wait(sp, deps)
    if dbg:
        for bb in [hb[0]]:
            if bb.dsem is not None:
                fw._wait(sp, [('d', bb.dsem, bb.dval, id(bb))])
    fw.close()
    return nc


def prep_inputs(inputs):
    w = {}
    for n, s in WNAMES:
        if n == "w_q_bp":
            continue
        a = np.asarray(inputs[n], dtype=np.float32)
        if n != "norm_final":
            a = a[0]
        w[n] = np.ascontiguousarray(a)
    wq = w["w_q_b"]
    perm = np.arange(768)
    for h in range(NH):
        base = h * 96 + 64
        perm[base:base + 16] = np.arange(base + 16, base + 32)
        perm[base + 16:base + 32] = np.arange(base, base + 16)
    w["w_q_bp"] = np.ascontiguousarray(wq[:, perm])
    return w


_CACHE = {}


def kernel(**inputs):
    x = np.asarray(inputs["x"], dtype=np.float32)
    c = np.asarray(inputs["c"], dtype=np.float32)
    pos = np.asarray(inputs["positions"], dtype=np.int32)
    w = prep_inputs(inputs)
    cst = host_consts()
    if "nc" not in _CACHE:
        _CACHE["nc"] = build_program()
    nc = _CACHE["nc"]
    in_maps = []
    for i in range(8):
        m = {"x": np.ascontiguousarray(x[2 * i:2 * i + 2].reshape(NSEQ * S, D)),
             "c": np.ascontiguousarray(c[2 * i:2 * i + 2]),
             "pos": np.ascontiguousarray(pos[2 * i:2 * i + 2]),
             "cst": cst}
        m.update(w)
        in_maps.append(m)
    res = run_bass_kernel_spmd(nc, in_maps, core_ids=list(range(8)))
    out = np.concatenate([r["y"].reshape(NSEQ, S, D) for r in res.results], axis=0)
    return out.astype(np.float32)
```

```python
import contextlib
import numpy as np
import concourse.bass as bass
import concourse.mybir as mybir
from concourse.bass_utils import run_bass_kernel_spmd

F32 = mybir.dt.float32
BF16 = mybir.dt.bfloat16
I32 = mybir.dt.int32
ACT = mybir.ActivationFunctionType
ALU = mybir.AluOpType

P = 128
D = 1024
KC = 8
DFF = 2816
FC = 22
TG = 512
NTT = 4
S = 2048
NG = S // TG
NSEQ = 2
NH = 8
DIN = 4264
EPS = 1e-6
SEM_CAP = 24000
C1 = 6.28125
C2 = float(2 * np.pi - 6.28125)


class Eng:
    def __init__(self, fw, name, h, nsem):
        self.name = name
        self.h = h
        self.sems = [fw.es.enter_context(fw.nc.semaphore(f"s_{name}{i}")) for i in range(nsem)]
        self.n = 0
        self.seen = {}
        self.seen_d = {}

    def sem_val(self, seq):
        i = (seq - 1) // SEM_CAP
        return self.sems[i], (seq - 1) % SEM_CAP + 1


class Buf:
    def __init__(self, ap, name=""):
        self.ap = ap
        self.name = name
        self.w = None
        self.r = []
        self.dsem = None
        self.dval = 0

    def __getitem__(self, k):
        return self.ap[k]


class GroupBuf:
    def __init__(self, members, ap):
        self.members = members
        self.ap = ap
        self.name = members[0].name

    def __getitem__(self, k):
        return self.ap[k]


def _flat(bs):
    out = []
    for b in bs:
        out.extend(getattr(b, "members", None) or [b])
    return out


class FW:
    def __init__(self, nc):
        self.nc = nc
        self.es = contextlib.ExitStack()
        self.es.__enter__()
        self.pe = Eng(self, "pe", nc.tensor, 3)
        self.act = Eng(self, "act", nc.scalar, 3)
        self.dve = Eng(self, "dve", nc.vector, 4)
        self.pool = Eng(self, "pool", nc.gpsimd, 1)
        self.sp = Eng(self, "sp", nc.sync, 1)
        self.engs = [self.pe, self.act, self.dve, self.pool, self.sp]

    def close(self):
        self.es.__exit__(None, None, None)

    def buf(self, name, shape, dt, es=None):
        self.uid = getattr(self, "uid", 0) + 1
        name = f"{name}_{self.uid}"
        t = (es or self.es).enter_context(self.nc.sbuf_tensor(name, list(shape), dt))
        return Buf(t[tuple(slice(None) for _ in shape)], name)

    def bufs(self, name, shape, dt, n, es=None):
        self.uid = getattr(self, "uid", 0) + 1
        name = f"{name}_{self.uid}_"
        t = (es or self.es).enter_context(self.nc.sbuf_tensor(name, [shape[0], n] + list(shape[1:]), dt))
        full = t[tuple(slice(None) for _ in range(len(shape) + 1))]
        out = []
        for i in range(n):
            idx = (slice(None), i) + tuple(slice(None) for _ in shape[1:])
            out.append(Buf(t[idx], f"{name}{i}"))
        return out, full

    def _wait(self, E, deps):
        for d in deps:
            if d[0] == 'e':
                _, F, seq = d
                if F is E and E is self.pe:
                    continue
                if E.seen.get(F.name, 0) >= seq:
                    continue
                s, v = F.sem_val(seq)
                E.h.wait_ge(s, v)
                E.seen[F.name] = seq
            else:
                _, sem, val, key = d
                if E.seen_d.get(key, 0) >= val:
                    continue
                E.h.wait_ge(sem, val)
                E.seen_d[key] = val

    @staticmethod
    def _deps(reads, writes):
        reads, writes = _flat(reads), _flat(writes)
        deps = []
        for b in reads:
            if b.w is not None:
                deps.append(b.w)
        for b in writes:
            if b.w is not None:
                deps.append(b.w)
            deps.extend(b.r)
        return deps

    @staticmethod
    def _mark(tag, reads, writes):
        reads, writes = _flat(reads), _flat(writes)
        for b in reads:
            if tag[0] == 'e':
                b.r = [t for t in b.r if not (t[0] == 'e' and t[1] is tag[1])]
            b.r.append(tag)
        for b in writes:
            b.w = tag
            b.r = []

    def op(self, E, fns, reads=(), writes=()):
        if callable(fns):
            fns = [fns]
        if E is self.pool and getattr(self, "bar", None):
            self._wait(E, self.bar)
        self._wait(E, self._deps(reads, writes))
        ins = None
        for f in fns:
            ins = f()
        seq = E.n + 1
        E.n = seq
        s, _ = E.sem_val(seq)
        ins.then_inc(s, 1)
        self._mark(('e', E, seq), reads, writes)

    def dma(self, Q, out, in_, reads=(), writes=(), key=None, **kw):
        key = (getattr(key, "members", None) or [key])[0]
        if key.dsem is None:
            key.dsem = self.es.enter_context(self.nc.semaphore(f"d_{key.name}"))
        self._wait(Q, self._deps(reads, writes))
        key.dval += 16
        Q.h.dma_start(out=out, in_=in_, **kw).then_inc(key.dsem, 16)
        self._mark(('d', key.dsem, key.dval, id(key)), reads, writes)

    def barrier(self):
        ce = [self.pe, self.act, self.dve]
        for E in ce:
            self._wait(E, [('e', F, F.n) for F in ce + [self.pool] if F is not E and F.n > 0])
        self.bar = [('e', F, F.n) for F in ce if F.n > 0]


def host_consts():
    c = np.zeros((P, 6 * P + 8), np.float32)
    idx = np.arange(P)
    c[:, 0:P] = np.eye(P, dtype=np.float32)
    s_, t_ = idx[:, None], idx[None, :]
    c[:, P:2 * P] = (t_ >= s_).astype(np.float32)
    c[:, 2 * P:3 * P] = ((t_ >= s_) & (s_ // 64 == t_ // 64)).astype(np.float32)
    c[:, 3 * P:4 * P] = (s_ // 64 == t_ // 64).astype(np.float32)
    c[:, 4 * P:5 * P] = (s_ < 64).astype(np.float32) * np.ones((1, P), np.float32)
    c[:, 5 * P:6 * P] = (s_ >= 64).astype(np.float32) * np.ones((1, P), np.float32)
    inv = (np.float32(10000.0) ** (-np.arange(0, 32, 2, dtype=np.float32) / np.float32(32))).astype(np.float32)
    o = 6 * P
    c[64:80, o] = -inv
    c[80:96, o] = inv
    c[64:80, o + 1] = inv
    c[80:96, o + 1] = inv
    c[:, o + 2] = 1024 * EPS
    c[:, o + 3] = 384 * EPS
    c[:, o + 4] = 256 * EPS
    c[:, o + 5] = 128 * EPS
    c[:, o + 6] = 1.0
    c[:, o + 7] = np.log(128.0 ** -0.5)
    return c


WNAMES = [("w_ada", [D, 9 * D]), ("b_ada", [9 * D]), ("norm_ff1", [D]), ("ff1_w_gate", [D, DFF]),
          ("ff1_w_up", [D, DFF]), ("ff1_w_down", [DFF, D]), ("norm_mix", [D]), ("w_in", [D, DIN]),
          ("q_a_norm", [384]), ("w_q_b", [384, 768]), ("w_q_bp", [384, 768]), ("kv_a_norm", [256]),
          ("w_kv_b", [256, 1024]), ("conv_w", [4, 512]), ("conv_b", [512]), ("w_q_m", [4, 128, 128]),
          ("w_k_m", [4, 128, 128]), ("b_i", [4]), ("b_f", [4]), ("mlstm_norm", [4, 128]),
          ("w_mla_out", [512, D]), ("w_mlstm_out", [512, D]), ("w_o", [D, D]), ("norm_ff2", [D]),
          ("ff2_w_gate", [D, DFF]), ("ff2_w_up", [D, DFF]), ("ff2_w_down", [DFF, D]), ("norm_final", [D])]


def build_program(do_mixer=True, do_ff2=True, dbg=None, ngroups=NSEQ * NG):
    nc = bass.Bass("TRN2", target_bir_lowering=False)
    x = nc.dram_tensor("x", [NSEQ * S, D], F32, kind="ExternalInput").ap()
    c_in = nc.dram_tensor("c", [NSEQ, D], F32, kind="ExternalInput").ap()
    pos = nc.dram_tensor("pos", [NSEQ, S], I32, kind="ExternalInput").ap()
    cst = nc.dram_tensor("cst", [P, 6 * P + 8], F32, kind="ExternalInput").ap()
    W = {n: nc.dram_tensor(n, s, F32, kind="ExternalInput").ap() for n, s in WNAMES}
    y = nc.dram_tensor("y", [NSEQ * S, D], F32, kind="ExternalOutput").ap()
    dbg_out = {}
    if dbg:
        for n, s in dbg.items():
            dbg_out[n] = nc.dram_tensor(n, s, F32, kind="ExternalOutput").ap()

    fw = FW(nc)
    pe, act, dve, pool, sp = fw.pe, fw.act, fw.dve, fw.pool, fw.sp
    V = nc.vector
    A = nc.scalar
    T = nc.tensor

    hb, h_full = fw.bufs("h", [P, TG], F32, KC)
    uT, _ = fw.bufs("uT", [P, TG], BF16, KC)
    NSLOT = 8
    SLOT = 2048
    ring_t = fw.es.enter_context(nc.sbuf_tensor("ring_t", [P, NSLOT * SLOT], BF16))
    ring = [Buf(ring_t[:, k * SLOT:(k + 1) * SLOT], f"ring{k}") for k in range(NSLOT)]
    ring_i = [0]
    io = [fw.buf(f"io{i}", [P, D], F32) for i in range(3)]
    io_i = [0]
    cb = fw.buf("cst", [P, 6 * P + 8], F32)
    ident = cb[:, 0:P]
    tri_f = cb[:, P:2 * P]
    Mbd = cb[:, 2 * P:3 * P]
    BDm = cb[:, 3 * P:4 * P]
    SEL = [cb[:, 4 * P:5 * P], cb[:, 5 * P:6 * P]]
    o_ = 6 * P
    invfS = cb[:, o_:o_ + 1]
    invfC = cb[:, o_ + 1:o_ + 2]
    epsc = {1024: cb[:, o_ + 2:o_ + 3], 384: cb[:, o_ + 3:o_ + 4], 256: cb[:, o_ + 4:o_ + 5], 128: cb[:, o_ + 5:o_ + 6]}
    onec = cb[:, o_ + 6:o_ + 7]
    lnsc = cb[:, o_ + 7:o_ + 8]
    ones_b = fw.buf("ones_b", [P, P], BF16)
    rstd = fw.buf("rstd", [P, TG], F32)
    tmpA = [fw.buf(f"tmpA{i}", [P, TG], F32) for i in range(2)]
    tmpA_i = [0]
    sqb = [fw.buf(f"sqb{i}", [P, TG], BF16) for i in range(2)]
    sq_i = [0]
    psb = [Buf(fw.es.enter_context(nc.psum_tensor(f"ps{i}", [P, TG], F32))[:, :], f"ps{i}") for i in range(8)]
    ps_i = [0]

    pinned = set()

    def nps(pin=False):
        while True:
            b = psb[ps_i[0] % 8]
            ps_i[0] += 1
            if id(b) not in pinned:
                break
        if pin:
            pinned.add(id(b))
        return b

    def unpin(b):
        pinned.discard(id(b))

    def nslot(n_elems=0):
        if n_elems > SLOT:
            if ring_i[0] % 2 == 1:
                ring_i[0] += 1
            k = ring_i[0] % NSLOT
            ring_i[0] += 2
            return GroupBuf([ring[k], ring[k + 1]], ring_t[:, k * SLOT:(k + 2) * SLOT])
        b = ring[ring_i[0] % NSLOT]
        ring_i[0] += 1
        return b

    def nio():
        b = io[io_i[0] % 3]
        io_i[0] += 1
        return b

    def ntmp():
        b = tmpA[tmpA_i[0] % 2]
        tmpA_i[0] += 1
        return b

    def nsq():
        b = sqb[sq_i[0] % 2]
        sq_i[0] += 1
        return b

    def wload(src, a, b_=None):
        n = a * (b_ or 1)
        sl = nslot(n)
        if b_ is None:
            view = sl[:, 0:n]
        else:
            view = sl[:, 0:n].rearrange("p (a b) -> p a b", a=a)
        fw.dma(pool, view, src, writes=[sl], key=sl)
        return sl, view

    def mm(outb, out_ap, terms, reads, start=True, stop=True):
        n = len(terms)
        fns = []
        for i, (l, r) in enumerate(terms):
            fns.append(lambda l=l, r=r, i=i: T.matmul(out_ap, l, r, start=(start and i == 0), stop=(stop and i == n - 1), skip_group_check=True))
        fw.op(pe, fns, reads=reads, writes=[outb])

    def dump(name, ap, b):
        if dbg and name in dbg_out:
            fw.dma(sp, dbg_out[name], ap, reads=[b], key=b)

    fw.dma(sp, cb[:, :], cst[:, :], writes=[cb], key=cb)
    fw.op(dve, lambda: V.memset(ones_b[:, :], 1.0), writes=[ones_b])

    vst = fw.buf("vst", [P, 2, P], F32)
    fw.op(dve, lambda: V.memset(vst[:, :, :], 0.0), writes=[vst])
    rowsA = [("b_ada", 72), ("norm_ff1", 8), ("norm_mix", 8), ("norm_ff2", 8), ("norm_final", 8), ("q_a_norm", 3), ("kv_a_norm", 2)]
    r_ = 0
    for n_, k_ in rowsA:
        fw.dma(sp, vst[r_:r_ + k_, 0, :], W[n_].rearrange("(j p) -> j p", p=P), writes=[vst], key=vst)
        r_ += k_
    NA = r_
    fw.dma(sp, vst[0:16, 1, :], W["conv_w"].rearrange("j (hh p) -> (j hh) p", p=P), writes=[vst], key=vst)
    fw.dma(sp, vst[16:20, 1, :], W["conv_b"].rearrange("(hh p) -> hh p", p=P), writes=[vst], key=vst)
    fw.dma(sp, vst[20:24, 1, :], W["mlstm_norm"], writes=[vst], key=vst)
    fw.dma(sp, vst[24:40, 1, :], c_in.rearrange("b (kc p) -> (b kc) p", p=P), writes=[vst], key=vst)
    NB_ = 40
    vT = fw.buf("vT", [P, NA + NB_], F32)
    pvt = nps()
    fw.op(pe, lambda: T.transpose(pvt[:, 0:NA], vst[0:NA, 0, :], ident[0:NA, 0:NA]), reads=[vst, cb], writes=[pvt])
    fw.op(pe, lambda: T.transpose(pvt[:, NA:NA + NB_], vst[0:NB_, 1, :], ident[0:NB_, 0:NB_]), reads=[vst, cb], writes=[pvt])
    fw.op(act, lambda: A.copy(out=vT[:, :], in_=pvt[:, 0:NA + NB_]), reads=[pvt], writes=[vT])

    class SubBuf:
        def __init__(self, parent, ap):
            self.__dict__["parent"] = parent
            self.__dict__["ap"] = ap

        def __getattr__(self, k):
            return getattr(self.__dict__["parent"], k)

        def __setattr__(self, k, v):
            setattr(self.__dict__["parent"], k, v)

        def __getitem__(self, k):
            return self.__dict__["ap"][k]

    badaT = SubBuf(vT, vT[:, 0:72])
    nT = {k: SubBuf(vT, vT[:, 72 + 8 * i:80 + 8 * i]) for i, k in enumerate(["norm_ff1", "norm_mix", "norm_ff2", "norm_final"])}
    qnT = SubBuf(vT, vT[:, 104:107])
    kvnT = SubBuf(vT, vT[:, 107:109])
    convwT = SubBuf(vT, vT[:, NA:NA + 16].rearrange("p (j h) -> p j h", j=4))
    convbT = SubBuf(vT, vT[:, NA + 16:NA + 20])
    mnT = SubBuf(vT, vT[:, NA + 20:NA + 24])
    cT = SubBuf(vT, vT[:, NA + 24:NA + 40].rearrange("p (b k) -> p b k", b=NSEQ))
    bi_b = fw.buf("bi_b", [P, 4], F32)
    fw.dma(sp, bi_b[:, :], W["b_i"].rearrange("(o n) -> o n", o=1).partition_broadcast(P), writes=[bi_b], key=bi_b)
    bf_b = fw.buf("bf_b", [P, 4], F32)
    fw.dma(sp, bf_b[:, :], W["b_f"].rearrange("(o n) -> o n", o=1).partition_broadcast(P), writes=[bf_b], key=bf_b)
    for b_, sc in [(nT["norm_ff1"], 32.0), (nT["norm_mix"], 32.0), (nT["norm_ff2"], 32.0), (nT["norm_final"], 32.0),
                   (qnT, float(np.sqrt(384.0))), (kvnT, 16.0), (mnT, float(np.sqrt(128.0)))]:
        fw.op(dve, lambda b_=b_, sc=sc: V.tensor_scalar(out=b_.ap, in0=b_.ap, scalar1=sc, scalar2=None, op0=ALU.mult), reads=[b_], writes=[b_])
    kpeW = fw.buf("kpeW", [P, KC, 96], BF16)
    kpeWp = fw.buf("kpeWp", [P, KC, 96], BF16)
    wif = fw.buf("wif", [P, KC, 8], BF16)
    w_in_v = W["w_in"].rearrange("(kc p) n -> p kc n", p=P)
    if do_mixer:
        fw.op(dve, lambda: V.memset(kpeW[:, :, :], 0.0), writes=[kpeW])
        fw.op(dve, lambda: V.memset(kpeWp[:, :, :], 0.0), writes=[kpeWp])
        fw.dma(pool, kpeW[:, :, 64:96], w_in_v[:, :, 640:672], writes=[kpeW], key=kpeW)
        fw.dma(pool, kpeWp[:, :, 64:80], w_in_v[:, :, 656:672], writes=[kpeWp], key=kpeWp)
        fw.dma(pool, kpeWp[:, :, 80:96], w_in_v[:, :, 640:656], writes=[kpeWp], key=kpeWp)
        fw.dma(pool, wif[:, :, :], w_in_v[:, :, 2208:2216], writes=[wif], key=wif)

    def rstd_from(chunks, n):
        pss = nps()
        nch = len(chunks)
        for i, (b_, ap) in enumerate(chunks):
            sq = nsq()
            fw.op(act, lambda ap=ap, sq=sq: A.activation(out=sq[:, :], in_=ap, func=ACT.Square), reads=[b_], writes=[sq])
            mm(pss, pss[:, :], [(ones_b[:, :], sq[:, :])], [ones_b, sq], start=(i == 0), stop=(i == nch - 1))
        fw.op(act, lambda: A.activation(out=rstd[:, :], in_=pss[:, :], func=ACT.Sqrt, bias=epsc[n], scale=1.0), reads=[pss, cb], writes=[rstd])
        fw.op(dve, lambda: V.reciprocal(out=rstd[:, :], in_=rstd[:, :]), reads=[rstd], writes=[rstd])

    def norm_mod(k, b):
        rstd_from([(hb[kc], hb[kc][:, :]) for kc in range(KC)], 1024)
        for kc in range(KC):
            t = ntmp()
            fw.op(dve, lambda kc=kc, t=t: V.scalar_tensor_tensor(out=t[:, :], in0=hb[kc][:, :], scalar=Acol[:, k, b, kc:kc + 1], in1=rstd[:, :],
                                                                 op0=ALU.mult, op1=ALU.mult), reads=[hb[kc], Acol, rstd], writes=[t])
            fw.op(act, lambda kc=kc, t=t: A.activation(out=uT[kc][:, :], in_=t[:, :], func=ACT.Identity, bias=Bcol(k, b, kc), scale=1.0),
                  reads=[t, modT], writes=[uT[kc]])

    def ffn(k, b, wg, wu, wd, es, after_norm=None):
        hT, _ = fw.bufs("hT", [P, TG], BF16, FC, es=es)
        sg = [fw.buf(f"sg{i}", [P, TG], F32, es=es) for i in range(2)]
        norm_mod(k, b)
        if after_norm is not None:
            after_norm()
        wg_v = wg.rearrange("(kc p) n -> p kc n", p=P)
        wu_v = wu.rearrange("(kc p) n -> p kc n", p=P)
        wd_v = wd.rearrange("(fc p) n -> p fc n", p=P)
        for fb in range(FC // 2):
            sg_, gv = wload(wg_v[:, :, fb * 256:(fb + 1) * 256], KC, 256)
            su_, uv = wload(wu_v[:, :, fb * 256:(fb + 1) * 256], KC, 256)
            for j in range(2):
                f = 2 * fb + j
                pg = nps()
                pu = nps()
                mm(pg, pg[:, :], [(gv[:, kc, j * 128:(j + 1) * 128], uT[kc][:, :]) for kc in range(KC)], [sg_] + uT)
                mm(pu, pu[:, :], [(uv[:, kc, j * 128:(j + 1) * 128], uT[kc][:, :]) for kc in range(KC)], [su_] + uT)
                s_ = sg[f % 2]
                fw.op(act, lambda pg=pg, s_=s_: A.activation(out=s_[:, :], in_=pg[:, :], func=ACT.Silu), reads=[pg], writes=[s_])
                fw.op(dve, lambda pu=pu, s_=s_, f=f: V.tensor_tensor(out=hT[f][:, :], in0=s_[:, :], in1=pu[:, :], op=ALU.mult), reads=[s_, pu], writes=[hT[f]])
        for half in range(2):
            accs = [nps(pin=True) for _ in range(4)]
            f0 = 0
            while f0 < FC:
                nf = min(4, FC - f0)
                sd_, dv = wload(wd_v[:, f0:f0 + nf, half * 512:(half + 1) * 512], nf, 512)
                for fi in range(nf):
                    f = f0 + fi
                    for i in range(4):
                        mm(accs[i], accs[i][:, :], [(dv[:, fi, i * 128:(i + 1) * 128], hT[f][:, :])], [sd_, hT[f]], start=(f == 0), stop=(f == FC - 1))
                f0 += nf
            for i in range(4):
                dc = half * 4 + i
                po = accs[i]
                fw.op(dve, lambda dc=dc, po=po: V.scalar_tensor_tensor(out=hb[dc][:, :], in0=po[:, :], scalar=Gcol[:, k, b, dc:dc + 1], in1=hb[dc][:, :],
                                                                       op0=ALU.mult, op1=ALU.add), reads=[po, Gcol, hb[dc]], writes=[hb[dc]])
                unpin(po)

    if do_mixer:
        kT = [[fw.buf(f"kT{h}_{g}", [P, TG], BF16) for g in range(NG)] for h in range(NH)]
        Vt = [fw.buf(f"V{t}", [P, NH * 65 + 63], BF16) for t in range(S // P)]
        for t in range(S // P):
            fw.op(dve, lambda t=t: V.memset(Vt[t][:, :], 1.0), writes=[Vt[t]])
        for h_ in range(NH):
            for g_ in range(NG):
                fw.op(dve, lambda h_=h_, g_=g_: V.memset(kT[h_][g_][:, :], 0.0), writes=[kT[h_][g_]])
        Sf = [fw.buf(f"Sf{hh}", [P, 256], F32) for hh in range(4)]
        Sbf = [[fw.buf(f"Sbf{hh}_{i}", [P, 256], BF16) for i in range(2)] for hh in range(4)]
        sbf_i = [0, 0, 0, 0]
        xtail = fw.buf("xtail", [P, 4, 3], F32)
        posb = fw.buf("posb", [96, TG], I32)
        Ct = fw.buf("Ct", [96, TG], F32)
        St = fw.buf("St", [96, TG], F32)
        tg = (fw.buf("ang", [96, TG], F32), fw.buf("ki", [96, TG], I32), fw.buf("kf", [96, TG], F32), fw.buf("mm_", [96, TG], F32))
        attnT = [fw.buf(f"attnT{h}", [64, TG], BF16) for h in range(NH)]
        hnT, hn_full = fw.bufs("hnT", [P, TG], BF16, 4)

    def rope_tables(b, g):
        c0 = g * TG
        fw.dma(sp, posb[:, :], pos[b:b + 1, c0:c0 + TG].partition_broadcast(96), writes=[posb], key=posb)
        rope_table(St, invfS, 0.0, None, tg)
        rope_table(Ct, invfC, float(np.pi / 2), None, tg)

    def rope_table(out_t, invcol, shift, es, tg):
        ang, ki, kf, m = tg
        fw.op(dve, lambda: V.tensor_copy(out=kf[:, :], in_=posb[:, :]), reads=[posb], writes=[kf])
        fw.op(dve, lambda: V.tensor_scalar(out=ang[:, :], in0=kf[:, :], scalar1=invcol[0:96, :], scalar2=shift, op0=ALU.mult, op1=ALU.add), reads=[kf, cb], writes=[ang])
        fw.op(dve, lambda: V.tensor_scalar(out=ki[:, :], in0=ang[:, :], scalar1=float(1.0 / (2 * np.pi)), scalar2=None, op0=ALU.mult), reads=[ang], writes=[ki])
        fw.op(dve, lambda: V.tensor_copy(out=kf[:, :], in_=ki[:, :]), reads=[ki], writes=[kf])
        fw.op(dve, lambda: V.scalar_tensor_tensor(out=ang[:, :], in0=kf[:, :], scalar=-C1, in1=ang[:, :], op0=ALU.mult, op1=ALU.add), reads=[kf, ang], writes=[ang])
        fw.op(dve, lambda: V.scalar_tensor_tensor(out=ang[:, :], in0=kf[:, :], scalar=-C2, in1=ang[:, :], op0=ALU.mult, op1=ALU.add), reads=[kf, ang], writes=[ang])
        fw.op(dve, lambda: V.tensor_scalar(out=m[:, :], in0=ang[:, :], scalar1=float(np.pi), scalar2=-float(2 * np.pi), op0=ALU.is_gt, op1=ALU.mult), reads=[ang], writes=[m])
        fw.op(dve, lambda: V.tensor_tensor(out=ang[:, :], in0=ang[:, :], in1=m[:, :], op=ALU.add), reads=[ang, m], writes=[ang])
        fw.op(dve, lambda: V.tensor_scalar(out=m[:, :], in0=ang[:, :], scalar1=-float(np.pi), scalar2=float(2 * np.pi), op0=ALU.is_lt, op1=ALU.mult), reads=[ang], writes=[m])
        fw.op(dve, lambda: V.tensor_tensor(out=ang[:, :], in0=ang[:, :], in1=m[:, :], op=ALU.add), reads=[ang, m], writes=[ang])
        fw.op(act, lambda: A.activation(out=out_t[:, :], in_=ang[:, :], func=ACT.Sin), reads=[ang], writes=[out_t])

    def mixer(b, g):
        c0 = g * TG
        norm_mod(1, b)
        with contextlib.ExitStack() as es:
            qln, _ = fw.bufs("qln", [P, TG], BF16, 3, es=es)
            ckn, _ = fw.bufs("ckn", [P, TG], BF16, 2, es=es)
            qT = [fw.buf(f"qT{h}", [P, TG], BF16, es=es) for h in range(NH)]
            for h in range(NH):
                fw.op(pool, lambda h=h: nc.gpsimd.memset(qT[h][:, :], 0.0), writes=[qT[h]])
            pT = [fw.buf(f"pT{i}", [P, TG], BF16, es=es) for i in range(3)]
            osb = [fw.buf(f"osb{i}", [64, TG], F32, es=es) for i in range(2)]
            rden = [fw.buf(f"rden{i}", [65, TG], F32, es=es) for i in range(2)]
            rhi = [fw.buf(f"rhi{i}", [65, TG], BF16, es=es) for i in range(2)]
            rlo = [fw.buf(f"rlo{i}", [65, TG], BF16, es=es) for i in range(2)]
            t1b = fw.buf("t1b", [96, TG], F32, es=es)
            t2b = fw.buf("t2b", [96, TG], F32, es=es)
            t1q = [t1b, fw.buf("t1c", [96, TG], F32, es=es)]
            t2q = [t2b, fw.buf("t2c", [96, TG], F32, es=es)]


            def lat(col0, nch, outs, gcol, n):
                sl, wv = wload(w_in_v[:, :, col0:col0 + nch * P], KC, nch * P)
                pl = []
                for j in range(nch):
                    pj = nps()
                    mm(pj, pj[:, :], [(wv[:, kc, j * P:(j + 1) * P], uT[kc][:, :]) for kc in range(KC)], [sl] + uT)
                    pl.append(pj)
                rstd_from([(pj, pj[:, :]) for pj in pl], n)
                for j in range(nch):
                    fw.op(dve, lambda j=j: V.scalar_tensor_tensor(out=outs[j][:, :], in0=pl[j][:, :], scalar=gcol[:, j:j + 1], in1=rstd[:, :],
                                                                  op0=ALU.mult, op1=ALU.mult), reads=[pl[j], gcol, rstd], writes=[outs[j]])
            lat(0, 3, qln, qnT, 384)
            lat(384, 2, ckn, kvnT, 256)
            pk = nps()
            pkp = nps()
            mm(pk, pk[0:96, :], [(kpeW[:, kc, :], uT[kc][:, :]) for kc in range(KC)], [kpeW] + uT)
            mm(pkp, pkp[0:96, :], [(kpeWp[:, kc, :], uT[kc][:, :]) for kc in range(KC)], [kpeWp] + uT)
            fw.op(dve, lambda: V.tensor_tensor(out=t1b[64:96, :], in0=pk[64:96, :], in1=Ct[64:96, :], op=ALU.mult), reads=[pk, Ct], writes=[t1b])
            fw.op(dve, lambda: V.tensor_tensor(out=t2b[64:96, :], in0=pkp[64:96, :], in1=St[64:96, :], op=ALU.mult), reads=[pkp, St], writes=[t2b])
            for h in range(NH):
                fw.op(pool if h % 2 else dve, lambda h=h: (nc.gpsimd if h % 2 else V).tensor_tensor(out=kT[h][g][64:96, :], in0=t1b[64:96, :], in1=t2b[64:96, :], op=ALU.add), reads=[t1b, t2b], writes=[kT[h][g]])
            slkv, wkv = wload(W["w_kv_b"].rearrange("(j p) n -> p j n", p=P), 2, 1024)
            for h in range(NH):
                pn = nps()
                mm(pn, pn[0:64, :], [(wkv[:, j, h * 128:h * 128 + 64], ckn[j][:, :]) for j in range(2)], [slkv] + ckn)
                fw.op(act, lambda h=h, pn=pn: A.copy(out=kT[h][g][0:64, :], in_=pn[0:64, :]), reads=[pn], writes=[kT[h][g]])
            for tt in range(NTT):
                pv = nps()
                mm(pv, pv[:, :],
                   [(ckn[j][:, tt * P:(tt + 1) * P], wkv[:, j, :].rearrange("p (h two d) -> p h two d", two=2, d=64)[:, :, 1, :]) for j in range(2)], [slkv] + ckn)
                vt = Vt[g * NTT + tt]
                fw.op(act, lambda pv=pv, vt=vt: A.copy(out=vt[:, 0:NH * 65].rearrange("p (h d) -> p h d", d=65)[:, :, 0:64], in_=pv[:, :].rearrange("p (h d) -> p h d", d=64)), reads=[pv], writes=[vt])
            slq, wq = wload(W["w_q_b"].rearrange("(j p) n -> p j n", p=P), 3, 768)
            slqp, wqp = wload(W["w_q_bp"].rearrange("(j p) n -> p j n", p=P), 3, 768)
            for h in range(NH):
                pq = nps()
                pqp = nps()
                mm(pq, pq[0:96, :], [(wq[:, j, h * 96:(h + 1) * 96], qln[j][:, :]) for j in range(3)], [slq] + qln)
                mm(pqp, pqp[0:96, :], [(wqp[:, j, h * 96:(h + 1) * 96], qln[j][:, :]) for j in range(3)], [slqp] + qln)
                ta, tb = t1q[h % 2], t2q[h % 2]
                fw.op(dve, lambda pq=pq, ta=ta: V.tensor_tensor(out=ta[:, :], in0=pq[0:96, :], in1=Ct[:, :], op=ALU.mult), reads=[pq, Ct], writes=[ta])
                fw.op(dve, lambda pqp=pqp, tb=tb: V.tensor_tensor(out=tb[:, :], in0=pqp[0:96, :], in1=St[:, :], op=ALU.mult), reads=[pqp, St], writes=[tb])
                fw.op(pool, lambda h=h, ta=ta, tb=tb: nc.gpsimd.tensor_tensor(out=qT[h][0:96, :], in0=ta[:, :], in1=tb[:, :], op=ALU.add), reads=[ta, tb], writes=[qT[h]])
            sc = float(96.0 ** -0.5)
            nkt = 4 * g + 4
            items = [(h, kt) for h in range(NH) for kt in range(nkt)]
            pti = [0]

            def qk(h, kt):
                jd = kt - 4 * g
                q0 = 128 * jd if jd > 0 else 0
                n = TG - q0
                pss = nps()
                kb = kT[h][kt // 4]
                mm(pss, pss[:, 0:n], [(kb[:, (kt % 4) * P:(kt % 4 + 1) * P], qT[h][:, q0:TG])], [kb, qT[h]])
                return pss, q0, n, jd

            def epi1(h, po):
                ob = osb[h % 2]
                rd, rh, rl = rden[h % 2], rhi[h % 2], rlo[h % 2]
                fw.op(act, lambda: A.copy(out=ob[:, :], in_=po[0:64, :]), reads=[po], writes=[ob])
                fw.op(dve, lambda: V.reciprocal(out=rd[64:65, :], in_=po[64:65, :]), reads=[po], writes=[rd])
                unpin(po)
                fw.op(dve, lambda: V.tensor_copy(out=rh[64:65, :], in_=rd[64:65, :]), reads=[rd], writes=[rh])
                fw.op(dve, lambda: V.tensor_tensor(out=rl[64:65, :], in0=rd[64:65, :], in1=rh[64:65, :], op=ALU.subtract), reads=[rd, rh], writes=[rl])

            def epi2(h):
                ob = osb[h % 2]
                rh, rl = rhi[h % 2], rlo[h % 2]
                pb = nps()
                mm(pb, pb[0:64, :], [(ones_b[64:65, 0:64], rh[64:65, :]), (ones_b[64:65, 0:64], rl[64:65, :])], [ones_b, rh, rl])
                fw.op(dve, lambda: V.tensor_tensor(out=attnT[h][:, :], in0=ob[:, :], in1=pb[0:64, :], op=ALU.mult), reads=[ob, pb], writes=[attnT[h]])

            AHEAD = 2
            inflight = [qk(*items[i]) for i in range(min(AHEAD, len(items)))]
            po = None
            pending = None
            for idx, (h, kt) in enumerate(items):
                if idx + AHEAD < len(items):
                    inflight.append(qk(*items[idx + AHEAD]))
                cur = inflight.pop(0)
                if kt == 0:
                    po = nps(pin=True)
                pss, q0, n, jd = cur
                pt = pT[pti[0] % 3]
                pti[0] += 1
                fw.op(act, lambda pss=pss, pt=pt, n=n: A.activation(out=pt[:, 0:n], in_=pss[:, 0:n], func=ACT.Exp, scale=sc), reads=[pss], writes=[pt])
                if jd >= 0:
                    fw.op(dve, lambda pt=pt: V.tensor_tensor(out=pt[:, 0:P], in0=pt[:, 0:P], in1=tri_f, op=ALU.mult), reads=[pt, cb], writes=[pt])
                vt = Vt[kt]
                mm(po, po[:, q0:TG], [(vt[:, h * 65:h * 65 + P], pt[:, 0:n])], [vt, pt], start=(kt == 0), stop=(kt == nkt - 1))
                if pending is not None:
                    pending[1] -= 1
                    if pending[1] <= 0:
                        epi2(pending[0])
                        pending = None
                if kt == nkt - 1:
                    if pending is not None:
                        epi2(pending[0])
                    epi1(h, po)
                    pending = [h, 3]
            if pending is not None:
                epi2(pending[0])
            fw.barrier()
        with contextlib.ExitStack() as es:
            xbuf = [fw.buf(f"xbuf{i}", [P, TG + 3], F32, es=es) for i in range(2)]
            acc = [fw.buf(f"acc{i}", [P, TG], F32, es=es) for i in range(2)]
            xcT = [fw.buf(f"xcT{hh}", [P, TG], BF16, es=es) for hh in range(4)]
            og = fw.buf("og", [P, 4, TG], BF16, es=es)
            gsm = {n: fw.buf(f"g_{n}", [P, NTT, 4], F32, es=es) for n in ["logi", "logf", "a", "ea", "ks", "t"]}
            eg = fw.buf("eg", [P, NTT, 8], F32, es=es)
            rep2 = [fw.buf("rep", [P, 4, P], F32, es=es)] * 2
            Eb2 = [fw.buf("Eb", [P, 4, P], F32, es=es)] * 2
            qpT2 = [fw.buf(f"qpT{i}", [P, 4, P], BF16, es=es) for i in range(2)]
            kTm2 = [fw.buf("kTm", [P, 4, P], BF16, es=es)] * 2
            kpp2 = [fw.buf(f"kpp{i}", [P, 4, P], BF16, es=es) for i in range(2)]
            vm2 = [fw.buf(f"vm{i}", [P, TG], BF16, es=es) for i in range(2)]
            smT2 = [fw.buf(f"smT{i}", [P, 4, P], BF16, es=es) for i in range(2)]
            dd = fw.buf("dd", [P, 4, P], F32, es=es)
            hg = fw.buf("hg", [P, 4, P], F32, es=es)
            sq4 = fw.buf("sq4", [P, 4, P], BF16, es=es)
            rs4 = fw.buf("rs4", [P, 4, P], F32, es=es)

            pgi = nps()
            for tt in range(NTT):
                mm(pgi, pgi[:, tt * 8:(tt + 1) * 8], [(uT[kc][:, tt * P:(tt + 1) * P], wif[:, kc, :]) for kc in range(KC)], [wif] + uT)
            gview = pgi[:, 0:32].rearrange("p (t c) -> p t c", c=8)
            fw.op(dve, lambda: V.tensor_tensor(out=gsm["logi"][:, :, :], in0=gview[:, :, 0:4], in1=bi_b[:, :].unsqueeze(1).broadcast_to([P, NTT, 4]), op=ALU.add),
                  reads=[pgi, bi_b], writes=[gsm["logi"]])
            fw.op(dve, lambda: V.tensor_tensor(out=gsm["t"][:, :, :], in0=gview[:, :, 4:8], in1=bf_b[:, :].unsqueeze(1).broadcast_to([P, NTT, 4]), op=ALU.add),
                  reads=[pgi, bf_b], writes=[gsm["t"]])
            fw.op(act, lambda: A.activation(out=gsm["t"][:, :, :], in_=gsm["t"][:, :, :], func=ACT.Exp, scale=-1.0), reads=[gsm["t"]], writes=[gsm["t"]])
            fw.op(act, lambda: A.activation(out=gsm["t"][:, :, :], in_=gsm["t"][:, :, :], func=ACT.Ln, bias=onec, scale=1.0), reads=[gsm["t"], cb], writes=[gsm["t"]])
            fw.op(dve, lambda: V.tensor_scalar(out=gsm["logf"][:, :, :], in0=gsm["t"][:, :, :], scalar1=-1.0, scalar2=None, op0=ALU.mult), reads=[gsm["t"]], writes=[gsm["logf"]])
            slx, wx = wload(w_in_v[:, :, 672:1184], KC, 512)
            for hh in range(4):
                px = nps()
                mm(px, px[:, :], [(wx[:, kc, hh * P:(hh + 1) * P], uT[kc][:, :]) for kc in range(KC)], [slx] + uT)
                xb_ = xbuf[hh % 2]
                ac = acc[hh % 2]
                if g == 0:
                    fw.op(dve, lambda xb_=xb_: V.memset(xb_[:, 0:3], 0.0), writes=[xb_])
                else:
                    fw.op(dve, lambda xb_=xb_, hh=hh: V.tensor_copy(out=xb_[:, 0:3], in_=xtail[:, hh, :]), reads=[xtail], writes=[xb_])
                fw.op(act, lambda xb_=xb_, px=px: A.copy(out=xb_[:, 3:TG + 3], in_=px[:, :]), reads=[px], writes=[xb_])
                fw.op(dve, lambda xb_=xb_, hh=hh: V.tensor_copy(out=xtail[:, hh, :], in_=xb_[:, TG:TG + 3]), reads=[xb_], writes=[xtail])
                fw.op(act, lambda xb_=xb_, ac=ac, hh=hh: A.activation(out=ac[:, :], in_=xb_[:, 3:TG + 3], func=ACT.Identity, bias=convbT[:, hh:hh + 1], scale=convwT[:, 3, hh:hh + 1]),
                      reads=[xb_, convbT, convwT], writes=[ac])
                for j in range(3):
                    fw.op(dve, lambda xb_=xb_, ac=ac, hh=hh, j=j: V.scalar_tensor_tensor(out=ac[:, :], in0=xb_[:, j:j + TG], scalar=convwT[:, j, hh:hh + 1], in1=ac[:, :],
                                                                                         op0=ALU.mult, op1=ALU.add), reads=[xb_, convwT, ac], writes=[ac])
                fw.op(act, lambda ac=ac, hh=hh: A.activation(out=xcT[hh][:, :], in_=ac[:, :], func=ACT.Silu), reads=[ac], writes=[xcT[hh]])
            slo, wo_ = wload(w_in_v[:, :, 1696:2208], KC, 512)
            for hh in range(4):
                pg_ = nps()
                mm(pg_, pg_[:, :], [(wo_[:, kc, hh * P:(hh + 1) * P], uT[kc][:, :]) for kc in range(KC)], [slo] + uT)
                fw.op(act, lambda pg_=pg_, hh=hh: A.activation(out=og[:, hh, :], in_=pg_[:, :], func=ACT.Sigmoid), reads=[pg_], writes=[og])
            pgs = nps()
            for tt in range(NTT):
                for i_, lhs in enumerate([Mbd, BDm, SEL[0], SEL[1]]):
                    mm(pgs, pgs[:, tt * 16 + i_ * 4: tt * 16 + i_ * 4 + 4], [(lhs, gsm["logf"][:, tt, :])], [cb, gsm["logf"]])
            gs = pgs[:, 0:64].rearrange("p (t c) -> p t c", c=16)
            fw.op(dve, lambda: V.tensor_tensor(out=gsm["a"][:, :, :], in0=gsm["logi"][:, :, :], in1=gs[:, :, 0:4], op=ALU.subtract), reads=[gsm["logi"], pgs], writes=[gsm["a"]])
            fw.op(act, lambda: A.activation(out=gsm["ea"][:, :, :], in_=gsm["a"][:, :, :], func=ACT.Exp), reads=[gsm["a"]], writes=[gsm["ea"]])
            fw.op(dve, lambda: V.tensor_tensor(out=gsm["t"][:, :, :], in0=gsm["a"][:, :, :], in1=gs[:, :, 4:8], op=ALU.add), reads=[gsm["a"], pgs], writes=[gsm["t"]])
            fw.op(act, lambda: A.activation(out=gsm["ks"][:, :, :], in_=gsm["t"][:, :, :], func=ACT.Exp, bias=lnsc, scale=1.0), reads=[gsm["t"], cb], writes=[gsm["ks"]])
            fw.op(act, lambda: A.activation(out=eg[:, :, :], in_=gs[:, :, 8:16], func=ACT.Exp), reads=[pgs], writes=[eg])
            if dbg and "logf" in dbg_out and g == 0 and b == 0:
                dump("logf", gsm["logf"][:, :, :], gsm["logf"])
                dump("logi", gsm["logi"][:, :, :], gsm["logi"])
                dump("ga", gsm["a"][:, :, :], gsm["a"])
                dump("eg", eg[:, :, :], eg)
            slv, wv_ = wload(w_in_v[:, :, 1184:1696], KC, 512)
            slqm, wqm = wload(W["w_q_m"].rearrange("h d e -> d h e"), 4, P)
            slkm, wkm = wload(W["w_k_m"].rearrange("h d e -> d h e"), 4, P)
            if g == 0:
                for hh in range(4):
                    fw.op(dve, lambda hh=hh: V.memset(Sf[hh][:, :], 0.0), writes=[Sf[hh]])
                    fw.op(dve, lambda hh=hh: V.memset(Sbf[hh][sbf_i[hh] % 2][:, :], 0.0), writes=[Sbf[hh][sbf_i[hh] % 2]])
            def front(tt):
                i2 = tt % 2
                tc_ = slice(tt * P, (tt + 1) * P)
                vm, kpp, rep, Eb, qpT, kTm, smT = vm2[i2], kpp2[i2], rep2[i2], Eb2[i2], qpT2[i2], kTm2[i2], smT2[i2]
                pvm = nps()
                mm(pvm, pvm[:, :], [(uT[kc][:, tc_], wv_[:, kc, :]) for kc in range(KC)], [slv] + uT)
                fw.op(act, lambda: A.copy(out=vm[:, :], in_=pvm[:, :]), reads=[pvm], writes=[vm])
                pk2 = nps()
                for hh in range(4):
                    mm(pk2, pk2[:, hh * P:(hh + 1) * P], [(xcT[hh][:, tc_], wkm[:, hh, :])], [xcT[hh], slkm])
                fw.op(dve, lambda: V.tensor_tensor(out=kpp[:, :, :], in0=pk2[:, :].rearrange("p (h e) -> p h e", e=P),
                                                   in1=gsm["ks"][:, tt, :].unsqueeze(2).broadcast_to([P, 4, P]), op=ALU.mult), reads=[pk2, gsm["ks"]], writes=[kpp])
                pbb = nps()
                fw.op(pool, lambda: nc.gpsimd.tensor_copy(out=rep[:, :, :], in_=gsm["logf"][:, tt, :].unsqueeze(2).broadcast_to([P, 4, P])), reads=[gsm["logf"]], writes=[rep])
                for hh in range(4):
                    mm(pbb, pbb[:, hh * P:(hh + 1) * P], [(rep[:, hh, :], Mbd)], [rep, cb])
                fw.op(act, lambda: A.activation(out=Eb[:, :, :], in_=pbb[:, :].rearrange("p (h e) -> p h e", e=P), func=ACT.Exp), reads=[pbb], writes=[Eb])
                pq_ = nps()
                pk_ = nps()
                for hh in range(4):
                    mm(pq_, pq_[:, hh * P:(hh + 1) * P], [(wqm[:, hh, :], xcT[hh][:, tc_])], [slqm, xcT[hh]])
                    mm(pk_, pk_[:, hh * P:(hh + 1) * P], [(wkm[:, hh, :], xcT[hh][:, tc_])], [slkm, xcT[hh]])
                fw.op(dve, lambda: V.tensor_tensor(out=qpT[:, :, :], in0=pq_[:, :].rearrange("p (h e) -> p h e", e=P), in1=Eb[:, :, :], op=ALU.mult), reads=[pq_, Eb], writes=[qpT])
                fw.op(act, lambda: A.activation(out=kTm[:, :, :], in_=pk_[:, :].rearrange("p (h e) -> p h e", e=P), func=ACT.Identity, scale=float(128.0 ** -0.5)), reads=[pk_], writes=[kTm])
                pS = nps()
                for hh in range(4):
                    mm(pS, pS[:, hh * P:(hh + 1) * P], [(kTm[:, hh, :], qpT[:, hh, :])], [kTm, qpT])
                for hh in range(4):
                    fw.op(dve, lambda hh=hh: V.scalar_tensor_tensor(out=smT[:, hh, :], in0=pS[:, hh * P:(hh + 1) * P], scalar=gsm["ea"][:, tt, hh:hh + 1], in1=Mbd,
                                                                    op0=ALU.mult, op1=ALU.mult), reads=[pS, gsm["ea"], cb], writes=[smT])

            def back(tt):
                i2 = tt % 2
                tc_ = slice(tt * P, (tt + 1) * P)
                vm, kpp, qpT, smT = vm2[i2], kpp2[i2], qpT2[i2], smT2[i2]
                pnum = nps(pin=True)
                pden = nps(pin=True)
                for hh in range(4):
                    hs = slice(hh * P, (hh + 1) * P)
                    mm(pnum, pnum[:, hs], [(vm[:, hs], smT[:, hh, :])], [vm, smT], start=(hh == 0), stop=False)
                    mm(pden, pden[:, hs], [(ones_b[:, :], smT[:, hh, :])], [ones_b, smT], start=(hh == 0), stop=False)
                for c in range(2):
                    cs = slice(c * 64, (c + 1) * 64)
                    for hh in range(4):
                        sb_ = Sbf[hh][sbf_i[hh] % 2]
                        mm(pnum, pnum[:, hh * P + c * 64: hh * P + (c + 1) * 64], [(sb_[:, 0:P], qpT[:, hh, cs])], [sb_, qpT], start=False, stop=True)
                        mm(pden, pden[:, hh * P + c * 64: hh * P + (c + 1) * 64], [(sb_[:, P:2 * P], qpT[:, hh, cs])], [sb_, qpT], start=False, stop=True)
                        pup = nps()
                        mm(pup, pup[:, 0:P], [(kpp[cs, hh, :], vm[cs, hh * P:(hh + 1) * P])], [kpp, vm])
                        mm(pup, pup[:, P:2 * P], [(kpp[cs, hh, :], ones_b[cs, :])], [kpp, ones_b])
                        fw.op(dve, lambda hh=hh, pup=pup, c=c: V.scalar_tensor_tensor(out=Sf[hh][:, :], in0=Sf[hh][:, :], scalar=eg[:, tt, c * 4 + hh:c * 4 + hh + 1], in1=pup[:, 0:256],
                                                                                    op0=ALU.mult, op1=ALU.add), reads=[Sf[hh], eg, pup], writes=[Sf[hh]])
                        sbf_i[hh] += 1
                        nb_ = Sbf[hh][sbf_i[hh] % 2]
                        fw.op(act, lambda hh=hh, nb_=nb_: A.copy(out=nb_[:, :], in_=Sf[hh][:, :]), reads=[Sf[hh]], writes=[nb_])
                fw.op(act, lambda: A.activation(out=dd[:, :, :], in_=pden[:, :].rearrange("p (h e) -> p h e", e=P), func=ACT.Abs), reads=[pden], writes=[dd])
                fw.op(dve, lambda: V.tensor_scalar(out=dd[:, :, :], in0=dd[:, :, :], scalar1=1.0, scalar2=None, op0=ALU.max), reads=[dd], writes=[dd])
                fw.op(dve, lambda: V.reciprocal(out=dd[:, :, :], in_=dd[:, :, :]), reads=[dd], writes=[dd])
                fw.op(dve, lambda: V.tensor_tensor(out=hg[:, :, :], in0=pnum[:, :].rearrange("p (h e) -> p h e", e=P), in1=dd[:, :, :], op=ALU.mult), reads=[pnum, dd], writes=[hg])
                unpin(pnum)
                unpin(pden)
                fw.op(pool, lambda: nc.gpsimd.tensor_tensor(out=hg[:, :, :], in0=hg[:, :, :], in1=og[:, :, tc_], op=ALU.mult), reads=[hg, og], writes=[hg])
                fw.op(act, lambda: A.activation(out=sq4[:, :, :], in_=hg[:, :, :], func=ACT.Square), reads=[hg], writes=[sq4])
                pss4 = nps()
                mm(pss4, pss4[:, :], [(ones_b[:, :], sq4[:, :, :].rearrange("p h e -> p (h e)"))], [ones_b, sq4])
                fw.op(act, lambda: A.activation(out=rs4[:, :, :], in_=pss4[:, :].rearrange("p (h e) -> p h e", e=P), func=ACT.Sqrt, bias=epsc[128], scale=1.0), reads=[pss4, cb], writes=[rs4])
                fw.op(dve, lambda: V.reciprocal(out=rs4[:, :, :], in_=rs4[:, :, :]), reads=[rs4], writes=[rs4])
                for hh in range(4):
                    fw.op(dve, lambda hh=hh: V.scalar_tensor_tensor(out=hnT[hh][:, tc_], in0=hg[:, hh, :], scalar=mnT[:, hh:hh + 1], in1=rs4[:, hh, :],
                                                                    op0=ALU.mult, op1=ALU.mult), reads=[hg, mnT, rs4], writes=[hnT[hh]])

            front(0)
            for tt in range(NTT):
                if tt + 1 < NTT:
                    front(tt + 1)
                back(tt)
            fw.barrier()
        with contextlib.ExitStack() as es:
            yT, _ = fw.bufs("yT", [P, TG], BF16, KC, es=es)
            s0 = fw.buf("s0", [P, TG], F32, es=es)
            s1 = fw.buf("s1", [P, TG], F32, es=es)
            y1 = fw.buf("y1", [P, TG], F32, es=es)
            y2 = fw.buf("y2", [P, TG], F32, es=es)
            wmla_v = W["w_mla_out"].rearrange("(h d) n -> d h n", d=64)
            wmls_v = W["w_mlstm_out"].rearrange("(hh p) n -> p hh n", p=P)
            for dp in range(4):
                ds_ = slice(dp * 256, (dp + 1) * 256)
                sla = nslot()
                wa = sla[0:64, 0:2048].rearrange("p (a b) -> p a b", a=8)
                fw.dma(pool, wa, wmla_v[:, :, ds_], writes=[sla], key=sla)
                slb, wb = wload(wmls_v[:, :, ds_], 4, 256)
                slg0, wg0 = wload(w_in_v[:, :, 2216 + dp * 256:2216 + (dp + 1) * 256], KC, 256)
                slg1, wg1 = wload(w_in_v[:, :, 3240 + dp * 256:3240 + (dp + 1) * 256], KC, 256)
                for j in range(2):
                    dc = 2 * dp + j
                    js = slice(j * P, (j + 1) * P)
                    pa = nps()
                    pb_ = nps()
                    p0 = nps()
                    p1 = nps()
                    mm(pa, pa[:, :], [(wa[:, h, js], attnT[h][:, :]) for h in range(NH)], [sla] + attnT)
                    mm(pb_, pb_[:, :], [(wb[:, hh, js], hnT[hh][:, :]) for hh in range(4)], [slb] + hnT)
                    mm(p0, p0[:, :], [(wg0[:, kc, js], uT[kc][:, :]) for kc in range(KC)], [slg0] + uT)
                    mm(p1, p1[:, :], [(wg1[:, kc, js], uT[kc][:, :]) for kc in range(KC)], [slg1] + uT)
                    fw.op(act, lambda p0=p0: A.activation(out=s0[:, :], in_=p0[:, :], func=ACT.Sigmoid), reads=[p0], writes=[s0])
                    fw.op(act, lambda p1=p1: A.activation(out=s1[:, :], in_=p1[:, :], func=ACT.Sigmoid), reads=[p1], writes=[s1])
                    fw.op(dve, lambda pa=pa: V.tensor_tensor(out=y1[:, :], in0=s0[:, :], in1=pa[:, :], op=ALU.mult), reads=[s0, pa], writes=[y1])
                    fw.op(dve, lambda pb_=pb_: V.tensor_tensor(out=y2[:, :], in0=s1[:, :], in1=pb_[:, :], op=ALU.mult), reads=[s1, pb_], writes=[y2])
                    fw.op(dve, lambda dc=dc: V.tensor_tensor(out=yT[dc][:, :], in0=y1[:, :], in1=y2[:, :], op=ALU.add), reads=[y1, y2], writes=[yT[dc]])
            wo_v = W["w_o"].rearrange("(kc p) n -> p kc n", p=P)
            for dp in range(4):
                slo_, wov = wload(wo_v[:, :, dp * 256:(dp + 1) * 256], KC, 256)
                for j in range(2):
                    dc = 2 * dp + j
                    po = nps()
                    mm(po, po[:, :], [(wov[:, kc, j * P:(j + 1) * P], yT[kc][:, :]) for kc in range(KC)], [slo_] + yT)
                    fw.op(dve, lambda dc=dc, po=po: V.scalar_tensor_tensor(out=hb[dc][:, :], in0=po[:, :], scalar=Gcol[:, 1, b, dc:dc + 1], in1=hb[dc][:, :],
                                                                           op0=ALU.mult, op1=ALU.add), reads=[po, Gcol, hb[dc]], writes=[hb[dc]])
            fw.barrier()

    xq = {}

    def load_dma(gi, tt):
        b, g = gi // NG, gi % NG
        r0 = b * S + g * TG
        xb_ = nio()
        fw.dma(sp, xb_[:, :], x[r0 + tt * P:r0 + (tt + 1) * P, :], writes=[xb_], key=xb_)
        xq[(gi, tt)] = xb_

    def load_tile(gi, tt):
        if (gi, tt) not in xq:
            load_dma(gi, tt)
        xb_ = xq.pop((gi, tt))
        for half in range(2):
            pt_ = nps()
            for i in range(4):
                kc = half * 4 + i
                fw.op(pe, lambda pt_=pt_, xb_=xb_, kc=kc, i=i: T.transpose(pt_[:, i * P:(i + 1) * P], xb_[:, kc * P:(kc + 1) * P], ident), reads=[xb_, cb], writes=[pt_])
            if half == 0:
                fw.op(act, lambda pt_=pt_, half=half, tt=tt: A.copy(out=h_full[:, half * 4:(half + 1) * 4, tt * P:(tt + 1) * P], in_=pt_[:, :].rearrange("p (a t) -> p a t", a=4)),
                      reads=[pt_], writes=hb[half * 4:(half + 1) * 4])
            else:
                fw.op(dve, lambda pt_=pt_, half=half, tt=tt: V.tensor_copy(out=h_full[:, half * 4:(half + 1) * 4, tt * P:(tt + 1) * P], in_=pt_[:, :].rearrange("p (a t) -> p a t", a=4)),
                      reads=[pt_], writes=hb[half * 4:(half + 1) * 4])

    def load_group(gi):
        for tt in range(NTT):
            load_tile(gi, tt)

    load_group(0)
    cTb = fw.buf("cTb", [P, NSEQ, KC], BF16)
    fw.op(act, lambda: A.activation(out=cTb[:, :, :], in_=cT[:, :, :], func=ACT.Silu), reads=[cT], writes=[cTb])
    modT = fw.buf("modT", [P, NSEQ, 72], F32)
    pm = nps()
    w_ada_v = W["w_ada"].rearrange("(kc p) n -> p kc n", p=P)
    for blk in range(36):
        sl, wv = wload(w_ada_v[:, :, blk * 256:(blk + 1) * 256], KC, 256)
        for j in range(2):
            jj = 2 * blk + j
            mm(pm, pm[:, jj * 2:(jj + 1) * 2], [(wv[:, kc, j * 128:(j + 1) * 128], cTb[:, :, kc]) for kc in range(KC)], [sl, cTb])
    for b in range(NSEQ):
        fw.op(dve, lambda b=b: V.tensor_tensor(out=modT[:, b, :], in0=pm[:, 0:144].rearrange("p (j b) -> p j b", b=2)[:, :, b],
                                               in1=badaT[:, :], op=ALU.add), reads=[pm, badaT], writes=[modT])
    Acol = fw.buf("Acol", [P, 3, NSEQ, KC], F32)
    Gcol = fw.buf("Gcol", [P, 3, NSEQ, KC], F32)
    gains = [nT["norm_ff1"], nT["norm_mix"], nT["norm_ff2"]]
    for k in range(3):
        for b in range(NSEQ):
            fw.op(dve, lambda k=k, b=b: V.scalar_tensor_tensor(out=Acol[:, k, b, :], in0=modT[:, b, (3 * k + 1) * 8:(3 * k + 2) * 8], scalar=1.0,
                                                               in1=gains[k][:, :], op0=ALU.add, op1=ALU.mult), reads=[modT, gains[k]], writes=[Acol])
            fw.op(dve, lambda k=k, b=b: V.tensor_scalar(out=Gcol[:, k, b, :], in0=modT[:, b, (3 * k + 2) * 8:(3 * k + 3) * 8],
                                                        scalar1=(1.0 if k == 1 else 0.5), scalar2=None, op0=ALU.mult), reads=[modT], writes=[Gcol])

    def Bcol(k, b, kc):
        return modT[:, b, 3 * k * 8 + kc:3 * k * 8 + kc + 1]


    out_bufs = []
    for gi in range(ngroups):
        b, g = gi // NG, gi % NG
        r0 = b * S + g * TG
        with contextlib.ExitStack() as es:
            ffn(0, b, W["ff1_w_gate"], W["ff1_w_up"], W["ff1_w_down"], es,
                after_norm=(lambda: rope_tables(b, g)) if do_mixer else None)
            fw.barrier()
        if gi == 0:
            dump("h1", h_full[:, :, :], hb[0])
        if do_mixer:
            mixer(b, g)
            if gi == 0:
                dump("h2", h_full[:, :, :], hb[0])
                if dbg and "att" in dbg_out:
                    for h_ in range(NH):
                        fw.dma(pool, dbg_out["att"][h_], attnT[h_][:, :], reads=[attnT[h_]], key=attnT[h_])
                    for h_ in range(4):
                        fw.dma(pool, dbg_out["ml"][h_], hnT[h_][:, :], reads=[hnT[h_]], key=hnT[h_])
        if do_ff2:
            with contextlib.ExitStack() as es:
                ffn(2, b, W["ff2_w_gate"], W["ff2_w_up"], W["ff2_w_down"], es)
                fw.barrier()
        rstd_from([(hb[kc], hb[kc][:, :]) for kc in range(KC)], 1024)
        with contextlib.ExitStack() as es:
            onb, on_full = fw.bufs("onb", [P, TG], F32, KC, es=es)
            for kc in range(KC):
                fw.op(dve, lambda kc=kc: V.scalar_tensor_tensor(out=onb[kc][:, :], in0=hb[kc][:, :], scalar=nT["norm_final"][:, kc:kc + 1], in1=rstd[:, :],
                                                                op0=ALU.mult, op1=ALU.mult), reads=[hb[kc], nT["norm_final"], rstd], writes=[onb[kc]])
            for tt in range(NTT):
                if gi + 1 < ngroups:
                    if tt == 0:
                        load_dma(gi + 1, 0)
                    if tt + 1 < NTT:
                        load_dma(gi + 1, tt + 1)
                    load_tile(gi + 1, tt)
                ob_ = nio()
                for half in range(2):
                    pt_ = nps()
                    for i in range(4):
                        kc = half * 4 + i
                        fw.op(pe, lambda pt_=pt_, kc=kc, i=i, tt=tt: T.transpose(pt_[:, i * P:(i + 1) * P], onb[kc][:, tt * P:(tt + 1) * P], ident), reads=[onb[kc], cb], writes=[pt_])
                    if half == 0:
                        fw.op(act, lambda pt_=pt_, ob_=ob_, half=half: A.copy(out=ob_[:, half * 512:(half + 1) * 512], in_=pt_[:, :]), reads=[pt_], writes=[ob_])
                    else:
                        fw.op(dve, lambda pt_=pt_, ob_=ob_, half=half: V.tensor_copy(out=ob_[:, half * 512:(half + 1) * 512], in_=pt_[:, :]), reads=[pt_], writes=[ob_])
                fw.dma(sp, y[r0 + tt * P:r0 + (tt + 1) * P, :], ob_[:, :], reads=[ob_], key=ob_)
                out_bufs.append(ob_)
            fw.barrier()
    deps = []
    for ob_ in io:
        if ob_.dsem is not None:
            deps.append(('d', ob_.dsem, ob_.dval, id(ob_)))
    fw._wait(sp, deps)
    if dbg:
        for bb in [hb[0]]:
            if bb.dsem is not None:
                fw._wait(sp, [('d', bb.dsem, bb.dval, id(bb))])
    fw.close()
    return nc


def prep_inputs(inputs):
    w = {}
    for n, s in WNAMES:
        if n == "w_q_bp":
            continue
        a = np.asarray(inputs[n], dtype=np.float32)
        if n != "norm_final":
            a = a[0]
        w[n] = np.ascontiguousarray(a)
    wq = w["w_q_b"]
    perm = np.arange(768)
    for h in range(NH):
        base = h * 96 + 64
        perm[base:base + 16] = np.arange(base + 16, base + 32)
        perm[base + 16:base + 32] = np.arange(base, base + 16)
    w["w_q_bp"] = np.ascontiguousarray(wq[:, perm])
    return w


_CACHE = {}


def kernel(**inputs):
    x = np.asarray(inputs["x"], dtype=np.float32)
    c = np.asarray(inputs["c"], dtype=np.float32)
    pos = np.asarray(inputs["positions"], dtype=np.int32)
    w = prep_inputs(inputs)
    cst = host_consts()
    if "nc" not in _CACHE:
        _CACHE["nc"] = build_program()
    nc = _CACHE["nc"]
    in_maps = []
    for i in range(8):
        m = {"x": np.ascontiguousarray(x[2 * i:2 * i + 2].reshape(NSEQ * S, D)),
             "c": np.ascontiguousarray(c[2 * i:2 * i + 2]),
             "pos": np.ascontiguousarray(pos[2 * i:2 * i + 2]),
             "cst": cst}
        m.update(w)
        in_maps.append(m)
    res = run_bass_kernel_spmd(nc, in_maps, core_ids=list(range(8)))
    out = np.concatenate([r["y"].reshape(NSEQ, S, D) for r in res.results], axis=0)
    return out.astype(np.float32)
```

```python
import contextlib
import numpy as np
import concourse.bass as bass
import concourse.mybir as mybir
from concourse.bass_utils import run_bass_kernel_spmd

F32 = mybir.dt.float32
BF16 = mybir.dt.bfloat16
I32 = mybir.dt.int32
ACT = mybir.ActivationFunctionType
ALU = mybir.AluOpType

P = 128
D = 1024
KC = 8
DFF = 2816
FC = 22
TG = 512
NTT = 4
S = 2048
NG = S // TG
NSEQ = 2
NH = 8
DIN = 4264
EPS = 1e-6
SEM_CAP = 24000
C1 = 6.28125
C2 = float(2 * np.pi - 6.28125)


class Eng:
    def __init__(self, fw, name, h, nsem):
        self.name = name
        self.h = h
        self.sems = [fw.es.enter_context(fw.nc.semaphore(f"s_{name}{i}")) for i in range(nsem)]
        self.n = 0
        self.seen = {}
        self.seen_d = {}

    def sem_val(self, seq):
        i = (seq - 1) // SEM_CAP
        return self.sems[i], (seq - 1) % SEM_CAP + 1


class Buf:
    def __init__(self, ap, name=""):
        self.ap = ap
        self.name = name
        self.w = None
        self.r = []
        self.dsem = None
        self.dval = 0

    def __getitem__(self, k):
        return self.ap[k]


class GroupBuf:
    def __init__(self, members, ap):
        self.members = members
        self.ap = ap
        self.name = members[0].name

    def __getitem__(self, k):
        return self.ap[k]


def _flat(bs):
    out = []
    for b in bs:
        out.extend(getattr(b, "members", None) or [b])
    return out


class FW:
    def __init__(self, nc):
        self.nc = nc
        self.es = contextlib.ExitStack()
        self.es.__enter__()
        self.pe = Eng(self, "pe", nc.tensor, 3)
        self.act = Eng(self, "act", nc.scalar, 3)
        self.dve = Eng(self, "dve", nc.vector, 4)
        self.pool = Eng(self, "pool", nc.gpsimd, 1)
        self.sp = Eng(self, "sp", nc.sync, 1)
        self.engs = [self.pe, self.act, self.dve, self.pool, self.sp]

    def close(self):
        self.es.__exit__(None, None, None)

    def buf(self, name, shape, dt, es=None):
        self.uid = getattr(self, "uid", 0) + 1
        name = f"{name}_{self.uid}"
        t = (es or self.es).enter_context(self.nc.sbuf_tensor(name, list(shape), dt))
        return Buf(t[tuple(slice(None) for _ in shape)], name)

    def bufs(self, name, shape, dt, n, es=None):
        self.uid = getattr(self, "uid", 0) + 1
        name = f"{name}_{self.uid}_"
        t = (es or self.es).enter_context(self.nc.sbuf_tensor(name, [shape[0], n] + list(shape[1:]), dt))
        full = t[tuple(slice(None) for _ in range(len(shape) + 1))]
        out = []
        for i in range(n):
            idx = (slice(None), i) + tuple(slice(None) for _ in shape[1:])
            out.append(Buf(t[idx], f"{name}{i}"))
        return out, full

    def _wait(self, E, deps):
        for d in deps:
            if d[0] == 'e':
                _, F, seq = d
                if F is E and E is self.pe:
                    continue
                if E.seen.get(F.name, 0) >= seq:
                    continue
                s, v = F.sem_val(seq)
                E.h.wait_ge(s, v)
                E.seen[F.name] = seq
            else:
                _, sem, val, key = d
                if E.seen_d.get(key, 0) >= val:
                    continue
                E.h.wait_ge(sem, val)
                E.seen_d[key] = val

    @staticmethod
    def _deps(reads, writes):
        reads, writes = _flat(reads), _flat(writes)
        deps = []
        for b in reads:
            if b.w is not None:
                deps.append(b.w)
        for b in writes:
            if b.w is not None:
                deps.append(b.w)
            deps.extend(b.r)
        return deps

    @staticmethod
    def _mark(tag, reads, writes):
        reads, writes = _flat(reads), _flat(writes)
        for b in reads:
            if tag[0] == 'e':
                b.r = [t for t in b.r if not (t[0] == 'e' and t[1] is tag[1])]
            b.r.append(tag)
        for b in writes:
            b.w = tag
            b.r = []

    def op(self, E, fns, reads=(), writes=()):
        if callable(fns):
            fns = [fns]
        if E is self.pool and getattr(self, "bar", None):
            self._wait(E, self.bar)
        self._wait(E, self._deps(reads, writes))
        ins = None
        for f in fns:
            ins = f()
        seq = E.n + 1
        E.n = seq
        s, _ = E.sem_val(seq)
        ins.then_inc(s, 1)
        self._mark(('e', E, seq), reads, writes)

    def dma(self, Q, out, in_, reads=(), writes=(), key=None, **kw):
        key = (getattr(key, "members", None) or [key])[0]
        if key.dsem is None:
            key.dsem = self.es.enter_context(self.nc.semaphore(f"d_{key.name}"))
        self._wait(Q, self._deps(reads, writes))
        key.dval += 16
        Q.h.dma_start(out=out, in_=in_, **kw).then_inc(key.dsem, 16)
        self._mark(('d', key.dsem, key.dval, id(key)), reads, writes)

    def barrier(self):
        ce = [self.pe, self.act, self.dve]
        for E in ce:
            self._wait(E, [('e', F, F.n) for F in ce + [self.pool] if F is not E and F.n > 0])
        self.bar = [('e', F, F.n) for F in ce if F.n > 0]


def host_consts():
    c = np.zeros((P, 6 * P + 8), np.float32)
    idx = np.arange(P)
    c[:, 0:P] = np.eye(P, dtype=np.float32)
    s_, t_ = idx[:, None], idx[None, :]
    c[:, P:2 * P] = (t_ >= s_).astype(np.float32)
    c[:, 2 * P:3 * P] = ((t_ >= s_) & (s_ // 64 == t_ // 64)).astype(np.float32)
    c[:, 3 * P:4 * P] = (s_ // 64 == t_ // 64).astype(np.float32)
    c[:, 4 * P:5 * P] = (s_ < 64).astype(np.float32) * np.ones((1, P), np.float32)
    c[:, 5 * P:6 * P] = (s_ >= 64).astype(np.float32) * np.ones((1, P), np.float32)
    inv = (np.float32(10000.0) ** (-np.arange(0, 32, 2, dtype=np.float32) / np.float32(32))).astype(np.float32)
    o = 6 * P
    c[64:80, o] = -inv
    c[80:96, o] = inv
    c[64:80, o + 1] = inv
    c[80:96, o + 1] = inv
    c[:, o + 2] = 1024 * EPS
    c[:, o + 3] = 384 * EPS
    c[:, o + 4] = 256 * EPS
    c[:, o + 5] = 128 * EPS
    c[:, o + 6] = 1.0
    c[:, o + 7] = np.log(128.0 ** -0.5)
    return c


WNAMES = [("w_ada", [D, 9 * D]), ("b_ada", [9 * D]), ("norm_ff1", [D]), ("ff1_w_gate", [D, DFF]),
          ("ff1_w_up", [D, DFF]), ("ff1_w_down", [DFF, D]), ("norm_mix", [D]), ("w_in", [D, DIN]),
          ("q_a_norm", [384]), ("w_q_b", [384, 768]), ("w_q_bp", [384, 768]), ("kv_a_norm", [256]),
          ("w_kv_b", [256, 1024]), ("conv_w", [4, 512]), ("conv_b", [512]), ("w_q_m", [4, 128, 128]),
          ("w_k_m", [4, 128, 128]), ("b_i", [4]), ("b_f", [4]), ("mlstm_norm", [4, 128]),
          ("w_mla_out", [512, D]), ("w_mlstm_out", [512, D]), ("w_o", [D, D]), ("norm_ff2", [D]),
          ("ff2_w_gate", [D, DFF]), ("ff2_w_up", [D, DFF]), ("ff2_w_down", [DFF, D]), ("norm_final", [D])]


def build_program(do_mixer=True, do_ff2=True, dbg=None, ngroups=NSEQ * NG):
    nc = bass.Bass("TRN2", target_bir_lowering=False)
    x = nc.dram_tensor("x", [NSEQ * S, D], F32, kind="ExternalInput").ap()
    c_in = nc.dram_tensor("c", [NSEQ, D], F32, kind="ExternalInput").ap()
    pos = nc.dram_tensor("pos", [NSEQ, S], I32, kind="ExternalInput").ap()
    cst = nc.dram_tensor("cst", [P, 6 * P + 8], F32, kind="ExternalInput").ap()
    W = {n: nc.dram_tensor(n, s, F32, kind="ExternalInput").ap() for n, s in WNAMES}
    y = nc.dram_tensor("y", [NSEQ * S, D], F32, kind="ExternalOutput").ap()
    dbg_out = {}
    if dbg:
        for n, s in dbg.items():
            dbg_out[n] = nc.dram_tensor(n, s, F32, kind="ExternalOutput").ap()

    fw = FW(nc)
    pe, act, dve, pool, sp = fw.pe, fw.act, fw.dve, fw.pool, fw.sp
    V = nc.vector
    A = nc.scalar
    T = nc.tensor

    hb, h_full = fw.bufs("h", [P, TG], F32, KC)
    uT, _ = fw.bufs("uT", [P, TG], BF16, KC)
    NSLOT = 8
    SLOT = 2048
    ring_t = fw.es.enter_context(nc.sbuf_tensor("ring_t", [P, NSLOT * SLOT], BF16))
    ring = [Buf(ring_t[:, k * SLOT:(k + 1) * SLOT], f"ring{k}") for k in range(NSLOT)]
    ring_i = [0]
    io = [fw.buf(f"io{i}", [P, D], F32) for i in range(3)]
    io_i = [0]
    cb = fw.buf("cst", [P, 6 * P + 8], F32)
    ident = cb[:, 0:P]
    tri_f = cb[:, P:2 * P]
    Mbd = cb[:, 2 * P:3 * P]
    BDm = cb[:, 3 * P:4 * P]
    SEL = [cb[:, 4 * P:5 * P], cb[:, 5 * P:6 * P]]
    o_ = 6 * P
    invfS = cb[:, o_:o_ + 1]
    invfC = cb[:, o_ + 1:o_ + 2]
    epsc = {1024: cb[:, o_ + 2:o_ + 3], 384: cb[:, o_ + 3:o_ + 4], 256: cb[:, o_ + 4:o_ + 5], 128: cb[:, o_ + 5:o_ + 6]}
    onec = cb[:, o_ + 6:o_ + 7]
    lnsc = cb[:, o_ + 7:o_ + 8]
    ones_b = fw.buf("ones_b", [P, P], BF16)
    rstd = fw.buf("rstd", [P, TG], F32)
    tmpA = [fw.buf(f"tmpA{i}", [P, TG], F32) for i in range(2)]
    tmpA_i = [0]
    sqb = [fw.buf(f"sqb{i}", [P, TG], BF16) for i in range(2)]
    sq_i = [0]
    psb = [Buf(fw.es.enter_context(nc.psum_tensor(f"ps{i}", [P, TG], F32))[:, :], f"ps{i}") for i in range(8)]
    ps_i = [0]

    pinned = set()

    def nps(pin=False):
        while True:
            b = psb[ps_i[0] % 8]
            ps_i[0] += 1
            if id(b) not in pinned:
                break
        if pin:
            pinned.add(id(b))
        return b

    def unpin(b):
        pinned.discard(id(b))

    def nslot(n_elems=0):
        if n_elems > SLOT:
            if ring_i[0] % 2 == 1:
                ring_i[0] += 1
            k = ring_i[0] % NSLOT
            ring_i[0] += 2
            return GroupBuf([ring[k], ring[k + 1]], ring_t[:, k * SLOT:(k + 2) * SLOT])
        b = ring[ring_i[0] % NSLOT]
        ring_i[0] += 1
        return b

    def nio():
        b = io[io_i[0] % 3]
        io_i[0] += 1
        return b

    def ntmp():
        b = tmpA[tmpA_i[0] % 2]
        tmpA_i[0] += 1
        return b

    def nsq():
        b = sqb[sq_i[0] % 2]
        sq_i[0] += 1
        return b

    def wload(src, a, b_=None):
        n = a * (b_ or 1)
        sl = nslot(n)
        if b_ is None:
            view = sl[:, 0:n]
        else:
            view = sl[:, 0:n].rearrange("p (a b) -> p a b", a=a)
        fw.dma(pool, view, src, writes=[sl], key=sl)
        return sl, view

    def mm(outb, out_ap, terms, reads, start=True, stop=True):
        n = len(terms)
        fns = []
        for i, (l, r) in enumerate(terms):
            fns.append(lambda l=l, r=r, i=i: T.matmul(out_ap, l, r, start=(start and i == 0), stop=(stop and i == n - 1), skip_group_check=True))
        fw.op(pe, fns, reads=reads, writes=[outb])

    def dump(name, ap, b):
        if dbg and name in dbg_out:
            fw.dma(sp, dbg_out[name], ap, reads=[b], key=b)

    fw.dma(sp, cb[:, :], cst[:, :], writes=[cb], key=cb)
    fw.op(dve, lambda: V.memset(ones_b[:, :], 1.0), writes=[ones_b])

    vst = fw.buf("vst", [P, 2, P], F32)
    fw.op(dve, lambda: V.memset(vst[:, :, :], 0.0), writes=[vst])
    rowsA = [("b_ada", 72), ("norm_ff1", 8), ("norm_mix", 8), ("norm_ff2", 8), ("norm_final", 8), ("q_a_norm", 3), ("kv_a_norm", 2)]
    r_ = 0
    for n_, k_ in rowsA:
        fw.dma(sp, vst[r_:r_ + k_, 0, :], W[n_].rearrange("(j p) -> j p", p=P), writes=[vst], key=vst)
        r_ += k_
    NA = r_
    fw.dma(sp, vst[0:16, 1, :], W["conv_w"].rearrange("j (hh p) -> (j hh) p", p=P), writes=[vst], key=vst)
    fw.dma(sp, vst[16:20, 1, :], W["conv_b"].rearrange("(hh p) -> hh p", p=P), writes=[vst], key=vst)
    fw.dma(sp, vst[20:24, 1, :], W["mlstm_norm"], writes=[vst], key=vst)
    fw.dma(sp, vst[24:40, 1, :], c_in.rearrange("b (kc p) -> (b kc) p", p=P), writes=[vst], key=vst)
    NB_ = 40
    vT = fw.buf("vT", [P, NA + NB_], F32)
    pvt = nps()
    fw.op(pe, lambda: T.transpose(pvt[:, 0:NA], vst[0:NA, 0, :], ident[0:NA, 0:NA]), reads=[vst, cb], writes=[pvt])
    fw.op(pe, lambda: T.transpose(pvt[:, NA:NA + NB_], vst[0:NB_, 1, :], ident[0:NB_, 0:NB_]), reads=[vst, cb], writes=[pvt])
    fw.op(act, lambda: A.copy(out=vT[:, :], in_=pvt[:, 0:NA + NB_]), reads=[pvt], writes=[vT])

    class SubBuf:
        def __init__(self, parent, ap):
            self.__dict__["parent"] = parent
            self.__dict__["ap"] = ap

        def __getattr__(self, k):
            return getattr(self.__dict__["parent"], k)

        def __setattr__(self, k, v):
            setattr(self.__dict__["parent"], k, v)

        def __getitem__(self, k):
            return self.__dict__["ap"][k]

    badaT = SubBuf(vT, vT[:, 0:72])
    nT = {k: SubBuf(vT, vT[:, 72 + 8 * i:80 + 8 * i]) for i, k in enumerate(["norm_ff1", "norm_mix", "norm_ff2", "norm_final"])}
    qnT = SubBuf(vT, vT[:, 104:107])
    kvnT = SubBuf(vT, vT[:, 107:109])
    convwT = SubBuf(vT, vT[:, NA:NA + 16].rearrange("p (j h) -> p j h", j=4))
    convbT = SubBuf(vT, vT[:, NA + 16:NA + 20])
    mnT = SubBuf(vT, vT[:, NA + 20:NA + 24])
    cT = SubBuf(vT, vT[:, NA + 24:NA + 40].rearrange("p (b k) -> p b k", b=NSEQ))
    bi_b = fw.buf("bi_b", [P, 4], F32)
    fw.dma(sp, bi_b[:, :], W["b_i"].rearrange("(o n) -> o n", o=1).partition_broadcast(P), writes=[bi_b], key=bi_b)
    bf_b = fw.buf("bf_b", [P, 4], F32)
    fw.dma(sp, bf_b[:, :], W["b_f"].rearrange("(o n) -> o n", o=1).partition_broadcast(P), writes=[bf_b], key=bf_b)
    for b_, sc in [(nT["norm_ff1"], 32.0), (nT["norm_mix"], 32.0), (nT["norm_ff2"], 32.0), (nT["norm_final"], 32.0),
                   (qnT, float(np.sqrt(384.0))), (kvnT, 16.0), (mnT, float(np.sqrt(128.0)))]:
        fw.op(dve, lambda b_=b_, sc=sc: V.tensor_scalar(out=b_.ap, in0=b_.ap, scalar1=sc, scalar2=None, op0=ALU.mult), reads=[b_], writes=[b_])
    kpeW = fw.buf("kpeW", [P, KC, 96], BF16)
    kpeWp = fw.buf("kpeWp", [P, KC, 96], BF16)
    wif = fw.buf("wif", [P, KC, 8], BF16)
    w_in_v = W["w_in"].rearrange("(kc p) n -> p kc n", p=P)
    if do_mixer:
        fw.op(dve, lambda: V.memset(kpeW[:, :, :], 0.0), writes=[kpeW])
        fw.op(dve, lambda: V.memset(kpeWp[:, :, :], 0.0), writes=[kpeWp])
        fw.dma(pool, kpeW[:, :, 64:96], w_in_v[:, :, 640:672], writes=[kpeW], key=kpeW)
        fw.dma(pool, kpeWp[:, :, 64:80], w_in_v[:, :, 656:672], writes=[kpeWp], key=kpeWp)
        fw.dma(pool, kpeWp[:, :, 80:96], w_in_v[:, :, 640:656], writes=[kpeWp], key=kpeWp)
        fw.dma(pool, wif[:, :, :], w_in_v[:, :, 2208:2216], writes=[wif], key=wif)

    def rstd_from(chunks, n, sbuf_src=False):
        pss = nps()
        nch = len(chunks)
        for i, (b_, ap) in enumerate(chunks):
            sq = nsq()
            if sbuf_src and i % 2 == 1:
                fw.op(pool, lambda ap=ap, sq=sq: nc.gpsimd.tensor_tensor(out=sq[:, :], in0=ap, in1=ap, op=ALU.mult), reads=[b_], writes=[sq])
            else:
                fw.op(act, lambda ap=ap, sq=sq: A.activation(out=sq[:, :], in_=ap, func=ACT.Square), reads=[b_], writes=[sq])
            mm(pss, pss[:, :], [(ones_b[:, :], sq[:, :])], [ones_b, sq], start=(i == 0), stop=(i == nch - 1))
        fw.op(act, lambda: A.activation(out=rstd[:, :], in_=pss[:, :], func=ACT.Ln, bias=epsc[n], scale=1.0), reads=[pss, cb], writes=[rstd])
        fw.op(act, lambda: A.activation(out=rstd[:, :], in_=rstd[:, :], func=ACT.Exp, scale=-0.5), reads=[rstd], writes=[rstd])

    def norm_mod(k, b):
        rstd_from([(hb[kc], hb[kc][:, :]) for kc in range(KC)], 1024, sbuf_src=True)
        for kc in range(KC):
            t = ntmp()
            fw.op(dve, lambda kc=kc, t=t: V.scalar_tensor_tensor(out=t[:, :], in0=hb[kc][:, :], scalar=Acol[:, k, b, kc:kc + 1], in1=rstd[:, :],
                                                                 op0=ALU.mult, op1=ALU.mult), reads=[hb[kc], Acol, rstd], writes=[t])
            fw.op(act, lambda kc=kc, t=t: A.activation(out=uT[kc][:, :], in_=t[:, :], func=ACT.Identity, bias=Bcol(k, b, kc), scale=1.0),
                  reads=[t, modT], writes=[uT[kc]])

    def ffn(k, b, wg, wu, wd, es, after_norm=None):
        hT, _ = fw.bufs("hT", [P, TG], BF16, FC, es=es)
        sg = [fw.buf(f"sg{i}", [P, TG], F32, es=es) for i in range(2)]
        norm_mod(k, b)
        if after_norm is not None:
            after_norm()
        wg_v = wg.rearrange("(kc p) n -> p kc n", p=P)
        wu_v = wu.rearrange("(kc p) n -> p kc n", p=P)
        wd_v = wd.rearrange("(fc p) n -> p fc n", p=P)
        for fb in range(FC // 2):
            sg_, gv = wload(wg_v[:, :, fb * 256:(fb + 1) * 256], KC, 256)
            su_, uv = wload(wu_v[:, :, fb * 256:(fb + 1) * 256], KC, 256)
            for j in range(2):
                f = 2 * fb + j
                pg = nps()
                pu = nps()
                mm(pg, pg[:, :], [(gv[:, kc, j * 128:(j + 1) * 128], uT[kc][:, :]) for kc in range(KC)], [sg_] + uT)
                mm(pu, pu[:, :], [(uv[:, kc, j * 128:(j + 1) * 128], uT[kc][:, :]) for kc in range(KC)], [su_] + uT)
                s_ = sg[f % 2]
                fw.op(act, lambda pg=pg, s_=s_: A.activation(out=s_[:, :], in_=pg[:, :], func=ACT.Silu), reads=[pg], writes=[s_])
                fw.op(dve, lambda pu=pu, s_=s_, f=f: V.tensor_tensor(out=hT[f][:, :], in0=s_[:, :], in1=pu[:, :], op=ALU.mult), reads=[s_, pu], writes=[hT[f]])
        for half in range(2):
            accs = [nps(pin=True) for _ in range(4)]
            f0 = 0
            while f0 < FC:
                nf = min(4, FC - f0)
                sd_, dv = wload(wd_v[:, f0:f0 + nf, half * 512:(half + 1) * 512], nf, 512)
                for fi in range(nf):
                    f = f0 + fi
                    for i in range(4):
                        mm(accs[i], accs[i][:, :], [(dv[:, fi, i * 128:(i + 1) * 128], hT[f][:, :])], [sd_, hT[f]], start=(f == 0), stop=(f == FC - 1))
                f0 += nf
            for i in range(4):
                dc = half * 4 + i
                po = accs[i]
                fw.op(dve, lambda dc=dc, po=po: V.scalar_tensor_tensor(out=hb[dc][:, :], in0=po[:, :], scalar=Gcol[:, k, b, dc:dc + 1], in1=hb[dc][:, :],
                                                                       op0=ALU.mult, op1=ALU.add), reads=[po, Gcol, hb[dc]], writes=[hb[dc]])
                unpin(po)

    if do_mixer:
        kT = [[fw.buf(f"kT{h}_{g}", [P, TG], BF16) for g in range(NG)] for h in range(NH)]
        Vt = [fw.buf(f"V{t}", [P, NH * 65 + 63], BF16) for t in range(S // P)]
        for t in range(S // P):
            fw.op(dve, lambda t=t: V.memset(Vt[t][:, :], 1.0), writes=[Vt[t]])
        for h_ in range(NH):
            for g_ in range(NG):
                fw.op(dve, lambda h_=h_, g_=g_: V.memset(kT[h_][g_][:, :], 0.0), writes=[kT[h_][g_]])
        Sf = [fw.buf(f"Sf{hh}", [P, 256], F32) for hh in range(4)]
        Sbf = [[fw.buf(f"Sbf{hh}_{i}", [P, 256], BF16) for i in range(2)] for hh in range(4)]
        sbf_i = [0, 0, 0, 0]
        xtail = fw.buf("xtail", [P, 4, 3], F32)
        posb = fw.buf("posb", [96, TG], I32)
        Ct = fw.buf("Ct", [96, TG], F32)
        St = fw.buf("St", [96, TG], F32)
        tg = (fw.buf("ang", [96, TG], F32), fw.buf("ki", [96, TG], I32), fw.buf("kf", [96, TG], F32), fw.buf("mm_", [96, TG], F32))
        attnT = [fw.buf(f"attnT{h}", [64, TG], BF16) for h in range(NH)]
        hnT, hn_full = fw.bufs("hnT", [P, TG], BF16, 4)

    def rope_tables(b, g):
        c0 = g * TG
        fw.dma(sp, posb[:, :], pos[b:b + 1, c0:c0 + TG].partition_broadcast(96), writes=[posb], key=posb)
        rope_table(St, invfS, 0.0, None, tg)
        rope_table(Ct, invfC, float(np.pi / 2), None, tg)

    def rope_table(out_t, invcol, shift, es, tg):
        ang, ki, kf, m = tg
        fw.op(dve, lambda: V.tensor_copy(out=kf[:, :], in_=posb[:, :]), reads=[posb], writes=[kf])
        fw.op(dve, lambda: V.tensor_scalar(out=ang[:, :], in0=kf[:, :], scalar1=invcol[0:96, :], scalar2=shift, op0=ALU.mult, op1=ALU.add), reads=[kf, cb], writes=[ang])
        fw.op(dve, lambda: V.tensor_scalar(out=ki[:, :], in0=ang[:, :], scalar1=float(1.0 / (2 * np.pi)), scalar2=None, op0=ALU.mult), reads=[ang], writes=[ki])
        fw.op(dve, lambda: V.tensor_copy(out=kf[:, :], in_=ki[:, :]), reads=[ki], writes=[kf])
        fw.op(dve, lambda: V.scalar_tensor_tensor(out=ang[:, :], in0=kf[:, :], scalar=-C1, in1=ang[:, :], op0=ALU.mult, op1=ALU.add), reads=[kf, ang], writes=[ang])
        fw.op(dve, lambda: V.scalar_tensor_tensor(out=ang[:, :], in0=kf[:, :], scalar=-C2, in1=ang[:, :], op0=ALU.mult, op1=ALU.add), reads=[kf, ang], writes=[ang])
        fw.op(dve, lambda: V.tensor_scalar(out=m[:, :], in0=ang[:, :], scalar1=float(np.pi), scalar2=-float(2 * np.pi), op0=ALU.is_gt, op1=ALU.mult), reads=[ang], writes=[m])
        fw.op(dve, lambda: V.tensor_tensor(out=ang[:, :], in0=ang[:, :], in1=m[:, :], op=ALU.add), reads=[ang, m], writes=[ang])
        fw.op(dve, lambda: V.tensor_scalar(out=m[:, :], in0=ang[:, :], scalar1=-float(np.pi), scalar2=float(2 * np.pi), op0=ALU.is_lt, op1=ALU.mult), reads=[ang], writes=[m])
        fw.op(dve, lambda: V.tensor_tensor(out=ang[:, :], in0=ang[:, :], in1=m[:, :], op=ALU.add), reads=[ang, m], writes=[ang])
        fw.op(act, lambda: A.activation(out=out_t[:, :], in_=ang[:, :], func=ACT.Sin), reads=[ang], writes=[out_t])

    def mixer(b, g):
        c0 = g * TG
        norm_mod(1, b)
        with contextlib.ExitStack() as es:
            qln, _ = fw.bufs("qln", [P, TG], BF16, 3, es=es)
            ckn, _ = fw.bufs("ckn", [P, TG], BF16, 2, es=es)
            qT = [fw.buf(f"qT{h}", [P, TG], BF16, es=es) for h in range(NH)]
            for h in range(NH):
                fw.op(pool, lambda h=h: nc.gpsimd.memset(qT[h][:, :], 0.0), writes=[qT[h]])
            pT = [fw.buf(f"pT{i}", [P, TG], BF16, es=es) for i in range(3)]
            osb = [fw.buf(f"osb{i}", [64, TG], F32, es=es) for i in range(2)]
            rden = [fw.buf(f"rden{i}", [65, TG], F32, es=es) for i in range(2)]
            rhi = [fw.buf(f"rhi{i}", [65, TG], BF16, es=es) for i in range(2)]
            rlo = [fw.buf(f"rlo{i}", [65, TG], BF16, es=es) for i in range(2)]
            t1b = fw.buf("t1b", [96, TG], F32, es=es)
            t2b = fw.buf("t2b", [96, TG], F32, es=es)
            t1q = [t1b, fw.buf("t1c", [96, TG], F32, es=es)]
            t2q = [t2b, fw.buf("t2c", [96, TG], F32, es=es)]


            def lat(col0, nch, outs, gcol, n):
                sl, wv = wload(w_in_v[:, :, col0:col0 + nch * P], KC, nch * P)
                pl = []
                for j in range(nch):
                    pj = nps()
                    mm(pj, pj[:, :], [(wv[:, kc, j * P:(j + 1) * P], uT[kc][:, :]) for kc in range(KC)], [sl] + uT)
                    pl.append(pj)
                rstd_from([(pj, pj[:, :]) for pj in pl], n)
                for j in range(nch):
                    fw.op(dve, lambda j=j: V.scalar_tensor_tensor(out=outs[j][:, :], in0=pl[j][:, :], scalar=gcol[:, j:j + 1], in1=rstd[:, :],
                                                                  op0=ALU.mult, op1=ALU.mult), reads=[pl[j], gcol, rstd], writes=[outs[j]])
            lat(0, 3, qln, qnT, 384)
            lat(384, 2, ckn, kvnT, 256)
            pk = nps()
            pkp = nps()
            mm(pk, pk[0:96, :], [(kpeW[:, kc, :], uT[kc][:, :]) for kc in range(KC)], [kpeW] + uT)
            mm(pkp, pkp[0:96, :], [(kpeWp[:, kc, :], uT[kc][:, :]) for kc in range(KC)], [kpeWp] + uT)
            fw.op(dve, lambda: V.tensor_tensor(out=t1b[64:96, :], in0=pk[64:96, :], in1=Ct[64:96, :], op=ALU.mult), reads=[pk, Ct], writes=[t1b])
            fw.op(dve, lambda: V.tensor_tensor(out=t2b[64:96, :], in0=pkp[64:96, :], in1=St[64:96, :], op=ALU.mult), reads=[pkp, St], writes=[t2b])
            for h in range(NH):
                fw.op(pool if h % 2 else dve, lambda h=h: (nc.gpsimd if h % 2 else V).tensor_tensor(out=kT[h][g][64:96, :], in0=t1b[64:96, :], in1=t2b[64:96, :], op=ALU.add), reads=[t1b, t2b], writes=[kT[h][g]])
            slkv, wkv = wload(W["w_kv_b"].rearrange("(j p) n -> p j n", p=P), 2, 1024)
            for h in range(NH):
                pn = nps()
                mm(pn, pn[0:64, :], [(wkv[:, j, h * 128:h * 128 + 64], ckn[j][:, :]) for j in range(2)], [slkv] + ckn)
                fw.op(act, lambda h=h, pn=pn: A.copy(out=kT[h][g][0:64, :], in_=pn[0:64, :]), reads=[pn], writes=[kT[h][g]])
            for tt in range(NTT):
                pv = nps()
                mm(pv, pv[:, :],
                   [(ckn[j][:, tt * P:(tt + 1) * P], wkv[:, j, :].rearrange("p (h two d) -> p h two d", two=2, d=64)[:, :, 1, :]) for j in range(2)], [slkv] + ckn)
                vt = Vt[g * NTT + tt]
                fw.op(act, lambda pv=pv, vt=vt: A.copy(out=vt[:, 0:NH * 65].rearrange("p (h d) -> p h d", d=65)[:, :, 0:64], in_=pv[:, :].rearrange("p (h d) -> p h d", d=64)), reads=[pv], writes=[vt])
            slq, wq = wload(W["w_q_b"].rearrange("(j p) n -> p j n", p=P), 3, 768)
            slqp, wqp = wload(W["w_q_bp"].rearrange("(j p) n -> p j n", p=P), 3, 768)
            for h in range(NH):
                pq = nps()
                pqp = nps()
                mm(pq, pq[0:96, :], [(wq[:, j, h * 96:(h + 1) * 96], qln[j][:, :]) for j in range(3)], [slq] + qln)
                mm(pqp, pqp[0:96, :], [(wqp[:, j, h * 96:(h + 1) * 96], qln[j][:, :]) for j in range(3)], [slqp] + qln)
                ta, tb = t1q[h % 2], t2q[h % 2]
                fw.op(dve, lambda pq=pq, ta=ta: V.tensor_tensor(out=ta[:, :], in0=pq[0:96, :], in1=Ct[:, :], op=ALU.mult), reads=[pq, Ct], writes=[ta])
                fw.op(dve, lambda pqp=pqp, tb=tb: V.tensor_tensor(out=tb[:, :], in0=pqp[0:96, :], in1=St[:, :], op=ALU.mult), reads=[pqp, St], writes=[tb])
                fw.op(pool, lambda h=h, ta=ta, tb=tb: nc.gpsimd.tensor_tensor(out=qT[h][0:96, :], in0=ta[:, :], in1=tb[:, :], op=ALU.add), reads=[ta, tb], writes=[qT[h]])
            sc = float(96.0 ** -0.5)
            nkt = 4 * g + 4
            items = [(h, kt) for h in range(NH) for kt in range(nkt)]
            pti = [0]

            def qk(h, kt):
                jd = kt - 4 * g
                q0 = 128 * jd if jd > 0 else 0
                n = TG - q0
                pss = nps()
                kb = kT[h][kt // 4]
                mm(pss, pss[:, 0:n], [(kb[:, (kt % 4) * P:(kt % 4 + 1) * P], qT[h][:, q0:TG])], [kb, qT[h]])
                return pss, q0, n, jd

            def epi1(h, po):
                ob = osb[h % 2]
                rd, rh, rl = rden[h % 2], rhi[h % 2], rlo[h % 2]
                fw.op(act, lambda: A.copy(out=ob[:, :], in_=po[0:64, :]), reads=[po], writes=[ob])
                fw.op(act, lambda: A.activation(out=rd[64:65, :], in_=po[64:65, :], func=ACT.Ln), reads=[po], writes=[rd])
                fw.op(act, lambda: A.activation(out=rd[64:65, :], in_=rd[64:65, :], func=ACT.Exp, scale=-1.0), reads=[rd], writes=[rd])
                unpin(po)
                fw.op(dve, lambda: V.tensor_copy(out=rh[64:65, :], in_=rd[64:65, :]), reads=[rd], writes=[rh])
                fw.op(dve, lambda: V.tensor_tensor(out=rl[64:65, :], in0=rd[64:65, :], in1=rh[64:65, :], op=ALU.subtract), reads=[rd, rh], writes=[rl])

            def epi2(h):
                ob = osb[h % 2]
                rh, rl = rhi[h % 2], rlo[h % 2]
                pb = nps()
                mm(pb, pb[0:64, :], [(ones_b[64:65, 0:64], rh[64:65, :]), (ones_b[64:65, 0:64], rl[64:65, :])], [ones_b, rh, rl])
                fw.op(dve, lambda: V.tensor_tensor(out=attnT[h][:, :], in0=ob[:, :], in1=pb[0:64, :], op=ALU.mult), reads=[ob, pb], writes=[attnT[h]])

            AHEAD = 2
            inflight = [qk(*items[i]) for i in range(min(AHEAD, len(items)))]
            po = None
            pending = None
            for idx, (h, kt) in enumerate(items):
                if idx + AHEAD < len(items):
                    inflight.append(qk(*items[idx + AHEAD]))
                cur = inflight.pop(0)
                if kt == 0:
                    po = nps(pin=True)
                pss, q0, n, jd = cur
                pt = pT[pti[0] % 3]
                pti[0] += 1
                fw.op(act, lambda pss=pss, pt=pt, n=n: A.activation(out=pt[:, 0:n], in_=pss[:, 0:n], func=ACT.Exp, scale=sc), reads=[pss], writes=[pt])
                if jd >= 0:
                    fw.op(dve, lambda pt=pt: V.tensor_tensor(out=pt[:, 0:P], in0=pt[:, 0:P], in1=tri_f, op=ALU.mult), reads=[pt, cb], writes=[pt])
                vt = Vt[kt]
                mm(po, po[:, q0:TG], [(vt[:, h * 65:h * 65 + P], pt[:, 0:n])], [vt, pt], start=(kt == 0), stop=(kt == nkt - 1))
                if pending is not None:
                    pending[1] -= 1
                    if pending[1] <= 0:
                        epi2(pending[0])
                        pending = None
                if kt == nkt - 1:
                    if pending is not None:
                        epi2(pending[0])
                    epi1(h, po)
                    pending = [h, 3]
            if pending is not None:
                epi2(pending[0])
            fw.barrier()
        with contextlib.ExitStack() as es:
            xbuf = [fw.buf(f"xbuf{i}", [P, TG + 3], F32, es=es) for i in range(2)]
            acc = [fw.buf(f"acc{i}", [P, TG], F32, es=es) for i in range(2)]
            xcT = [fw.buf(f"xcT{hh}", [P, TG], BF16, es=es) for hh in range(4)]
            og = fw.buf("og", [P, 4, TG], BF16, es=es)
            gsm = {n: fw.buf(f"g_{n}", [P, NTT, 4], F32, es=es) for n in ["logi", "logf", "a", "ea", "ks", "t"]}
            eg = fw.buf("eg", [P, NTT, 8], F32, es=es)
            rep2 = [fw.buf("rep", [P, 4, P], F32, es=es)] * 2
            Eb2 = [fw.buf("Eb", [P, 4, P], F32, es=es)] * 2
            qpT2 = [fw.buf(f"qpT{i}", [P, 4, P], BF16, es=es) for i in range(2)]
            kTm2 = [fw.buf("kTm", [P, 4, P], BF16, es=es)] * 2
            kpp2 = [fw.buf(f"kpp{i}", [P, 4, P], BF16, es=es) for i in range(2)]
            vm2 = [fw.buf(f"vm{i}", [P, TG], BF16, es=es) for i in range(2)]
            smT2 = [fw.buf(f"smT{i}", [P, 4, P], BF16, es=es) for i in range(2)]
            dd = fw.buf("dd", [P, 4, P], F32, es=es)
            hg = fw.buf("hg", [P, 4, P], F32, es=es)
            sq4 = fw.buf("sq4", [P, 4, P], BF16, es=es)
            rs4 = fw.buf("rs4", [P, 4, P], F32, es=es)

            pgi = nps()
            for tt in range(NTT):
                mm(pgi, pgi[:, tt * 8:(tt + 1) * 8], [(uT[kc][:, tt * P:(tt + 1) * P], wif[:, kc, :]) for kc in range(KC)], [wif] + uT)
            gview = pgi[:, 0:32].rearrange("p (t c) -> p t c", c=8)
            fw.op(dve, lambda: V.tensor_tensor(out=gsm["logi"][:, :, :], in0=gview[:, :, 0:4], in1=bi_b[:, :].unsqueeze(1).broadcast_to([P, NTT, 4]), op=ALU.add),
                  reads=[pgi, bi_b], writes=[gsm["logi"]])
            fw.op(dve, lambda: V.tensor_tensor(out=gsm["t"][:, :, :], in0=gview[:, :, 4:8], in1=bf_b[:, :].unsqueeze(1).broadcast_to([P, NTT, 4]), op=ALU.add),
                  reads=[pgi, bf_b], writes=[gsm["t"]])
            fw.op(act, lambda: A.activation(out=gsm["t"][:, :, :], in_=gsm["t"][:, :, :], func=ACT.Exp, scale=-1.0), reads=[gsm["t"]], writes=[gsm["t"]])
            fw.op(act, lambda: A.activation(out=gsm["t"][:, :, :], in_=gsm["t"][:, :, :], func=ACT.Ln, bias=onec, scale=1.0), reads=[gsm["t"], cb], writes=[gsm["t"]])
            fw.op(dve, lambda: V.tensor_scalar(out=gsm["logf"][:, :, :], in0=gsm["t"][:, :, :], scalar1=-1.0, scalar2=None, op0=ALU.mult), reads=[gsm["t"]], writes=[gsm["logf"]])
            slx, wx = wload(w_in_v[:, :, 672:1184], KC, 512)
            for hh in range(4):
                px = nps()
                mm(px, px[:, :], [(wx[:, kc, hh * P:(hh + 1) * P], uT[kc][:, :]) for kc in range(KC)], [slx] + uT)
                xb_ = xbuf[hh % 2]
                ac = acc[hh % 2]
                if g == 0:
                    fw.op(dve, lambda xb_=xb_: V.memset(xb_[:, 0:3], 0.0), writes=[xb_])
                else:
                    fw.op(dve, lambda xb_=xb_, hh=hh: V.tensor_copy(out=xb_[:, 0:3], in_=xtail[:, hh, :]), reads=[xtail], writes=[xb_])
                fw.op(act, lambda xb_=xb_, px=px: A.copy(out=xb_[:, 3:TG + 3], in_=px[:, :]), reads=[px], writes=[xb_])
                fw.op(dve, lambda xb_=xb_, hh=hh: V.tensor_copy(out=xtail[:, hh, :], in_=xb_[:, TG:TG + 3]), reads=[xb_], writes=[xtail])
                fw.op(act, lambda xb_=xb_, ac=ac, hh=hh: A.activation(out=ac[:, :], in_=xb_[:, 3:TG + 3], func=ACT.Identity, bias=convbT[:, hh:hh + 1], scale=convwT[:, 3, hh:hh + 1]),
                      reads=[xb_, convbT, convwT], writes=[ac])
                for j in range(3):
                    fw.op(dve, lambda xb_=xb_, ac=ac, hh=hh, j=j: V.scalar_tensor_tensor(out=ac[:, :], in0=xb_[:, j:j + TG], scalar=convwT[:, j, hh:hh + 1], in1=ac[:, :],
                                                                                         op0=ALU.mult, op1=ALU.add), reads=[xb_, convwT, ac], writes=[ac])
                fw.op(act, lambda ac=ac, hh=hh: A.activation(out=xcT[hh][:, :], in_=ac[:, :], func=ACT.Silu), reads=[ac], writes=[xcT[hh]])
            slo, wo_ = wload(w_in_v[:, :, 1696:2208], KC, 512)
            for hh in range(4):
                pg_ = nps()
                mm(pg_, pg_[:, :], [(wo_[:, kc, hh * P:(hh + 1) * P], uT[kc][:, :]) for kc in range(KC)], [slo] + uT)
                fw.op(act, lambda pg_=pg_, hh=hh: A.activation(out=og[:, hh, :], in_=pg_[:, :], func=ACT.Sigmoid), reads=[pg_], writes=[og])
            pgs = nps()
            for tt in range(NTT):
                for i_, lhs in enumerate([Mbd, BDm, SEL[0], SEL[1]]):
                    mm(pgs, pgs[:, tt * 16 + i_ * 4: tt * 16 + i_ * 4 + 4], [(lhs, gsm["logf"][:, tt, :])], [cb, gsm["logf"]])
            gs = pgs[:, 0:64].rearrange("p (t c) -> p t c", c=16)
            fw.op(dve, lambda: V.tensor_tensor(out=gsm["a"][:, :, :], in0=gsm["logi"][:, :, :], in1=gs[:, :, 0:4], op=ALU.subtract), reads=[gsm["logi"], pgs], writes=[gsm["a"]])
            fw.op(act, lambda: A.activation(out=gsm["ea"][:, :, :], in_=gsm["a"][:, :, :], func=ACT.Exp), reads=[gsm["a"]], writes=[gsm["ea"]])
            fw.op(dve, lambda: V.tensor_tensor(out=gsm["t"][:, :, :], in0=gsm["a"][:, :, :], in1=gs[:, :, 4:8], op=ALU.add), reads=[gsm["a"], pgs], writes=[gsm["t"]])
            fw.op(act, lambda: A.activation(out=gsm["ks"][:, :, :], in_=gsm["t"][:, :, :], func=ACT.Exp, bias=lnsc, scale=1.0), reads=[gsm["t"], cb], writes=[gsm["ks"]])
            fw.op(act, lambda: A.activation(out=eg[:, :, :], in_=gs[:, :, 8:16], func=ACT.Exp), reads=[pgs], writes=[eg])
            if dbg and "logf" in dbg_out and g == 0 and b == 0:
                dump("logf", gsm["logf"][:, :, :], gsm["logf"])
                dump("logi", gsm["logi"][:, :, :], gsm["logi"])
                dump("ga", gsm["a"][:, :, :], gsm["a"])
                dump("eg", eg[:, :, :], eg)
            slv, wv_ = wload(w_in_v[:, :, 1184:1696], KC, 512)
            slqm, wqm = wload(W["w_q_m"].rearrange("h d e -> d h e"), 4, P)
            slkm, wkm = wload(W["w_k_m"].rearrange("h d e -> d h e"), 4, P)
            if g == 0:
                for hh in range(4):
                    fw.op(dve, lambda hh=hh: V.memset(Sf[hh][:, :], 0.0), writes=[Sf[hh]])
                    fw.op(dve, lambda hh=hh: V.memset(Sbf[hh][sbf_i[hh] % 2][:, :], 0.0), writes=[Sbf[hh][sbf_i[hh] % 2]])
            def front(tt):
                i2 = tt % 2
                tc_ = slice(tt * P, (tt + 1) * P)
                vm, kpp, rep, Eb, qpT, kTm, smT = vm2[i2], kpp2[i2], rep2[i2], Eb2[i2], qpT2[i2], kTm2[i2], smT2[i2]
                pvm = nps()
                mm(pvm, pvm[:, :], [(uT[kc][:, tc_], wv_[:, kc, :]) for kc in range(KC)], [slv] + uT)
                fw.op(act, lambda: A.copy(out=vm[:, :], in_=pvm[:, :]), reads=[pvm], writes=[vm])
                pk2 = nps()
                for hh in range(4):
                    mm(pk2, pk2[:, hh * P:(hh + 1) * P], [(xcT[hh][:, tc_], wkm[:, hh, :])], [xcT[hh], slkm])
                fw.op(dve, lambda: V.tensor_tensor(out=kpp[:, :, :], in0=pk2[:, :].rearrange("p (h e) -> p h e", e=P),
                                                   in1=gsm["ks"][:, tt, :].unsqueeze(2).broadcast_to([P, 4, P]), op=ALU.mult), reads=[pk2, gsm["ks"]], writes=[kpp])
                pbb = nps()
                fw.op(pool, lambda: nc.gpsimd.tensor_copy(out=rep[:, :, :], in_=gsm["logf"][:, tt, :].unsqueeze(2).broadcast_to([P, 4, P])), reads=[gsm["logf"]], writes=[rep])
                for hh in range(4):
                    mm(pbb, pbb[:, hh * P:(hh + 1) * P], [(rep[:, hh, :], Mbd)], [rep, cb])
                fw.op(act, lambda: A.activation(out=Eb[:, :, :], in_=pbb[:, :].rearrange("p (h e) -> p h e", e=P), func=ACT.Exp), reads=[pbb], writes=[Eb])
                pq_ = nps()
                pk_ = nps()
                for hh in range(4):
                    mm(pq_, pq_[:, hh * P:(hh + 1) * P], [(wqm[:, hh, :], xcT[hh][:, tc_])], [slqm, xcT[hh]])
                    mm(pk_, pk_[:, hh * P:(hh + 1) * P], [(wkm[:, hh, :], xcT[hh][:, tc_])], [slkm, xcT[hh]])
                fw.op(dve, lambda: V.tensor_tensor(out=qpT[:, :, :], in0=pq_[:, :].rearrange("p (h e) -> p h e", e=P), in1=Eb[:, :, :], op=ALU.mult), reads=[pq_, Eb], writes=[qpT])
                fw.op(act, lambda: A.activation(out=kTm[:, :, :], in_=pk_[:, :].rearrange("p (h e) -> p h e", e=P), func=ACT.Identity, scale=float(128.0 ** -0.5)), reads=[pk_], writes=[kTm])
                pS = nps()
                for hh in range(4):
                    mm(pS, pS[:, hh * P:(hh + 1) * P], [(kTm[:, hh, :], qpT[:, hh, :])], [kTm, qpT])
                for hh in range(4):
                    fw.op(dve, lambda hh=hh: V.scalar_tensor_tensor(out=smT[:, hh, :], in0=pS[:, hh * P:(hh + 1) * P], scalar=gsm["ea"][:, tt, hh:hh + 1], in1=Mbd,
                                                                    op0=ALU.mult, op1=ALU.mult), reads=[pS, gsm["ea"], cb], writes=[smT])

            def back(tt):
                i2 = tt % 2
                tc_ = slice(tt * P, (tt + 1) * P)
                vm, kpp, qpT, smT = vm2[i2], kpp2[i2], qpT2[i2], smT2[i2]
                pnum = nps(pin=True)
                pden = nps(pin=True)
                for hh in range(4):
                    hs = slice(hh * P, (hh + 1) * P)
                    mm(pnum, pnum[:, hs], [(vm[:, hs], smT[:, hh, :])], [vm, smT], start=(hh == 0), stop=False)
                    mm(pden, pden[:, hs], [(ones_b[:, :], smT[:, hh, :])], [ones_b, smT], start=(hh == 0), stop=False)
                for c in range(2):
                    cs = slice(c * 64, (c + 1) * 64)
                    for hh in range(4):
                        sb_ = Sbf[hh][sbf_i[hh] % 2]
                        mm(pnum, pnum[:, hh * P + c * 64: hh * P + (c + 1) * 64], [(sb_[:, 0:P], qpT[:, hh, cs])], [sb_, qpT], start=False, stop=True)
                        mm(pden, pden[:, hh * P + c * 64: hh * P + (c + 1) * 64], [(sb_[:, P:2 * P], qpT[:, hh, cs])], [sb_, qpT], start=False, stop=True)
                        pup = nps()
                        mm(pup, pup[:, 0:P], [(kpp[cs, hh, :], vm[cs, hh * P:(hh + 1) * P])], [kpp, vm])
                        mm(pup, pup[:, P:2 * P], [(kpp[cs, hh, :], ones_b[cs, :])], [kpp, ones_b])
                        fw.op(dve, lambda hh=hh, pup=pup, c=c: V.scalar_tensor_tensor(out=Sf[hh][:, :], in0=Sf[hh][:, :], scalar=eg[:, tt, c * 4 + hh:c * 4 + hh + 1], in1=pup[:, 0:256],
                                                                                    op0=ALU.mult, op1=ALU.add), reads=[Sf[hh], eg, pup], writes=[Sf[hh]])
                        sbf_i[hh] += 1
                        nb_ = Sbf[hh][sbf_i[hh] % 2]
                        fw.op(act, lambda hh=hh, nb_=nb_: A.copy(out=nb_[:, :], in_=Sf[hh][:, :]), reads=[Sf[hh]], writes=[nb_])
                fw.op(act, lambda: A.activation(out=dd[:, :, :], in_=pden[:, :].rearrange("p (h e) -> p h e", e=P), func=ACT.Abs), reads=[pden], writes=[dd])
                fw.op(dve, lambda: V.tensor_scalar(out=dd[:, :, :], in0=dd[:, :, :], scalar1=1.0, scalar2=None, op0=ALU.max), reads=[dd], writes=[dd])
                fw.op(act, lambda: A.activation(out=dd[:, :, :], in_=dd[:, :, :], func=ACT.Ln), reads=[dd], writes=[dd])
                fw.op(act, lambda: A.activation(out=dd[:, :, :], in_=dd[:, :, :], func=ACT.Exp, scale=-1.0), reads=[dd], writes=[dd])
                fw.op(dve, lambda: V.tensor_tensor(out=hg[:, :, :], in0=pnum[:, :].rearrange("p (h e) -> p h e", e=P), in1=dd[:, :, :], op=ALU.mult), reads=[pnum, dd], writes=[hg])
                unpin(pnum)
                unpin(pden)
                fw.op(pool, lambda: nc.gpsimd.tensor_tensor(out=hg[:, :, :], in0=hg[:, :, :], in1=og[:, :, tc_], op=ALU.mult), reads=[hg, og], writes=[hg])
                fw.op(act, lambda: A.activation(out=sq4[:, :, :], in_=hg[:, :, :], func=ACT.Square), reads=[hg], writes=[sq4])
                pss4 = nps()
                mm(pss4, pss4[:, :], [(ones_b[:, :], sq4[:, :, :].rearrange("p h e -> p (h e)"))], [ones_b, sq4])
                fw.op(act, lambda: A.activation(out=rs4[:, :, :], in_=pss4[:, :].rearrange("p (h e) -> p h e", e=P), func=ACT.Ln, bias=epsc[128], scale=1.0), reads=[pss4, cb], writes=[rs4])
                fw.op(act, lambda: A.activation(out=rs4[:, :, :], in_=rs4[:, :, :], func=ACT.Exp, scale=-0.5), reads=[rs4], writes=[rs4])
                for hh in range(4):
                    fw.op(dve, lambda hh=hh: V.scalar_tensor_tensor(out=hnT[hh][:, tc_], in0=hg[:, hh, :], scalar=mnT[:, hh:hh + 1], in1=rs4[:, hh, :],
                                                                    op0=ALU.mult, op1=ALU.mult), reads=[hg, mnT, rs4], writes=[hnT[hh]])

            front(0)
            for tt in range(NTT):
                if tt + 1 < NTT:
                    front(tt + 1)
                back(tt)
            fw.barrier()
        with contextlib.ExitStack() as es:
            yT, _ = fw.bufs("yT", [P, TG], BF16, KC, es=es)
            s0 = fw.buf("s0", [P, TG], F32, es=es)
            s1 = fw.buf("s1", [P, TG], F32, es=es)
            y1 = fw.buf("y1", [P, TG], F32, es=es)
            y2 = fw.buf("y2", [P, TG], F32, es=es)
            wmla_v = W["w_mla_out"].rearrange("(h d) n -> d h n", d=64)
            wmls_v = W["w_mlstm_out"].rearrange("(hh p) n -> p hh n", p=P)
            for dp in range(4):
                ds_ = slice(dp * 256, (dp + 1) * 256)
                sla = nslot()
                wa = sla[0:64, 0:2048].rearrange("p (a b) -> p a b", a=8)
                fw.dma(pool, wa, wmla_v[:, :, ds_], writes=[sla], key=sla)
                slb, wb = wload(wmls_v[:, :, ds_], 4, 256)
                slg0, wg0 = wload(w_in_v[:, :, 2216 + dp * 256:2216 + (dp + 1) * 256], KC, 256)
                slg1, wg1 = wload(w_in_v[:, :, 3240 + dp * 256:3240 + (dp + 1) * 256], KC, 256)
                for j in range(2):
                    dc = 2 * dp + j
                    js = slice(j * P, (j + 1) * P)
                    pa = nps()
                    pb_ = nps()
                    p0 = nps()
                    p1 = nps()
                    mm(pa, pa[:, :], [(wa[:, h, js], attnT[h][:, :]) for h in range(NH)], [sla] + attnT)
                    mm(pb_, pb_[:, :], [(wb[:, hh, js], hnT[hh][:, :]) for hh in range(4)], [slb] + hnT)
                    mm(p0, p0[:, :], [(wg0[:, kc, js], uT[kc][:, :]) for kc in range(KC)], [slg0] + uT)
                    mm(p1, p1[:, :], [(wg1[:, kc, js], uT[kc][:, :]) for kc in range(KC)], [slg1] + uT)
                    fw.op(act, lambda p0=p0: A.activation(out=s0[:, :], in_=p0[:, :], func=ACT.Sigmoid), reads=[p0], writes=[s0])
                    fw.op(act, lambda p1=p1: A.activation(out=s1[:, :], in_=p1[:, :], func=ACT.Sigmoid), reads=[p1], writes=[s1])
                    fw.op(dve, lambda pa=pa: V.tensor_tensor(out=y1[:, :], in0=s0[:, :], in1=pa[:, :], op=ALU.mult), reads=[s0, pa], writes=[y1])
                    fw.op(dve, lambda pb_=pb_: V.tensor_tensor(out=y2[:, :], in0=s1[:, :], in1=pb_[:, :], op=ALU.mult), reads=[s1, pb_], writes=[y2])
                    fw.op(dve, lambda dc=dc: V.tensor_tensor(out=yT[dc][:, :], in0=y1[:, :], in1=y2[:, :], op=ALU.add), reads=[y1, y2], writes=[yT[dc]])
            wo_v = W["w_o"].rearrange("(kc p) n -> p kc n", p=P)
            for dp in range(4):
                slo_, wov = wload(wo_v[:, :, dp * 256:(dp + 1) * 256], KC, 256)
                for j in range(2):
                    dc = 2 * dp + j
                    po = nps()
                    mm(po, po[:, :], [(wov[:, kc, j * P:(j + 1) * P], yT[kc][:, :]) for kc in range(KC)], [slo_] + yT)
                    fw.op(dve, lambda dc=dc, po=po: V.scalar_tensor_tensor(out=hb[dc][:, :], in0=po[:, :], scalar=Gcol[:, 1, b, dc:dc + 1], in1=hb[dc][:, :],
                                                                           op0=ALU.mult, op1=ALU.add), reads=[po, Gcol, hb[dc]], writes=[hb[dc]])
            fw.barrier()

    xq = {}

    def load_dma(gi, tt):
        b, g = gi // NG, gi % NG
        r0 = b * S + g * TG
        xb_ = nio()
        fw.dma(sp, xb_[:, :], x[r0 + tt * P:r0 + (tt + 1) * P, :], writes=[xb_], key=xb_)
        xq[(gi, tt)] = xb_

    def load_tile(gi, tt):
        if (gi, tt) not in xq:
            load_dma(gi, tt)
        xb_ = xq.pop((gi, tt))
        for half in range(2):
            pt_ = nps()
            for i in range(4):
                kc = half * 4 + i
                fw.op(pe, lambda pt_=pt_, xb_=xb_, kc=kc, i=i: T.transpose(pt_[:, i * P:(i + 1) * P], xb_[:, kc * P:(kc + 1) * P], ident), reads=[xb_, cb], writes=[pt_])
            if half == 0:
                fw.op(act, lambda pt_=pt_, half=half, tt=tt: A.copy(out=h_full[:, half * 4:(half + 1) * 4, tt * P:(tt + 1) * P], in_=pt_[:, :].rearrange("p (a t) -> p a t", a=4)),
                      reads=[pt_], writes=hb[half * 4:(half + 1) * 4])
            else:
                fw.op(dve, lambda pt_=pt_, half=half, tt=tt: V.tensor_copy(out=h_full[:, half * 4:(half + 1) * 4, tt * P:(tt + 1) * P], in_=pt_[:, :].rearrange("p (a t) -> p a t", a=4)),
                      reads=[pt_], writes=hb[half * 4:(half + 1) * 4])

    def load_group(gi):
        for tt in range(NTT):
            load_tile(gi, tt)

    load_group(0)
    cTb = fw.buf("cTb", [P, NSEQ, KC], BF16)
    fw.op(act, lambda: A.activation(out=cTb[:, :, :], in_=cT[:, :, :], func=ACT.Silu), reads=[cT], writes=[cTb])
    modT = fw.buf("modT", [P, NSEQ, 72], F32)
    pm = nps()
    w_ada_v = W["w_ada"].rearrange("(kc p) n -> p kc n", p=P)
    for blk in range(36):
        sl, wv = wload(w_ada_v[:, :, blk * 256:(blk + 1) * 256], KC, 256)
        for j in range(2):
            jj = 2 * blk + j
            mm(pm, pm[:, jj * 2:(jj + 1) * 2], [(wv[:, kc, j * 128:(j + 1) * 128], cTb[:, :, kc]) for kc in range(KC)], [sl, cTb])
    for b in range(NSEQ):
        fw.op(dve, lambda b=b: V.tensor_tensor(out=modT[:, b, :], in0=pm[:, 0:144].rearrange("p (j b) -> p j b", b=2)[:, :, b],
                                               in1=badaT[:, :], op=ALU.add), reads=[pm, badaT], writes=[modT])
    Acol = fw.buf("Acol", [P, 3, NSEQ, KC], F32)
    Gcol = fw.buf("Gcol", [P, 3, NSEQ, KC], F32)
    gains = [nT["norm_ff1"], nT["norm_mix"], nT["norm_ff2"]]
    for k in range(3):
        for b in range(NSEQ):
            fw.op(dve, lambda k=k, b=b: V.scalar_tensor_tensor(out=Acol[:, k, b, :], in0=modT[:, b, (3 * k + 1) * 8:(3 * k + 2) * 8], scalar=1.0,
                                                               in1=gains[k][:, :], op0=ALU.add, op1=ALU.mult), reads=[modT, gains[k]], writes=[Acol])
            fw.op(dve, lambda k=k, b=b: V.tensor_scalar(out=Gcol[:, k, b, :], in0=modT[:, b, (3 * k + 2) * 8:(3 * k + 3) * 8],
                                                        scalar1=(1.0 if k == 1 else 0.5), scalar2=None, op0=ALU.mult), reads=[modT], writes=[Gcol])

    def Bcol(k, b, kc):
        return modT[:, b, 3 * k * 8 + kc:3 * k * 8 + kc + 1]


    out_bufs = []
    for gi in range(ngroups):
        b, g = gi // NG, gi % NG
        r0 = b * S + g * TG
        with contextlib.ExitStack() as es:
            ffn(0, b, W["ff1_w_gate"], W["ff1_w_up"], W["ff1_w_down"], es,
                after_norm=(lambda: rope_tables(b, g)) if do_mixer else None)
            fw.barrier()
        if gi == 0:
            dump("h1", h_full[:, :, :], hb[0])
        if do_mixer:
            mixer(b, g)
            if gi == 0:
                dump("h2", h_full[:, :, :], hb[0])
                if dbg and "att" in dbg_out:
                    for h_ in range(NH):
                        fw.dma(pool, dbg_out["att"][h_], attnT[h_][:, :], reads=[attnT[h_]], key=attnT[h_])
                    for h_ in range(4):
                        fw.dma(pool, dbg_out["ml"][h_], hnT[h_][:, :], reads=[hnT[h_]], key=hnT[h_])
        if do_ff2:
            with contextlib.ExitStack() as es:
                ffn(2, b, W["ff2_w_gate"], W["ff2_w_up"], W["ff2_w_down"], es)
                fw.barrier()
        rstd_from([(hb[kc], hb[kc][:, :]) for kc in range(KC)], 1024, sbuf_src=True)
        with contextlib.ExitStack() as es:
            onb, on_full = fw.bufs("onb", [P, TG], F32, KC, es=es)
            for kc in range(KC):
                fw.op(dve, lambda kc=kc: V.scalar_tensor_tensor(out=onb[kc][:, :], in0=hb[kc][:, :], scalar=nT["norm_final"][:, kc:kc + 1], in1=rstd[:, :],
                                                                op0=ALU.mult, op1=ALU.mult), reads=[hb[kc], nT["norm_final"], rstd], writes=[onb[kc]])
            for tt in range(NTT):
                if gi + 1 < ngroups:
                    if tt == 0:
                        load_dma(gi + 1, 0)
                    if tt + 1 < NTT:
                        load_dma(gi + 1, tt + 1)
                    load_tile(gi + 1, tt)
                ob_ = nio()
                for half in range(2):
                    pt_ = nps()
                    for i in range(4):
                        kc = half * 4 + i
                        fw.op(pe, lambda pt_=pt_, kc=kc, i=i, tt=tt: T.transpose(pt_[:, i * P:(i + 1) * P], onb[kc][:, tt * P:(tt + 1) * P], ident), reads=[onb[kc], cb], writes=[pt_])
                    if half == 0:
                        fw.op(act, lambda pt_=pt_, ob_=ob_, half=half: A.copy(out=ob_[:, half * 512:(half + 1) * 512], in_=pt_[:, :]), reads=[pt_], writes=[ob_])
                    else:
                        fw.op(dve, lambda pt_=pt_, ob_=ob_, half=half: V.tensor_copy(out=ob_[:, half * 512:(half + 1) * 512], in_=pt_[:, :]), reads=[pt_], writes=[ob_])
                fw.dma(sp, y[r0 + tt * P:r0 + (tt + 1) * P, :], ob_[:, :], reads=[ob_], key=ob_)
                out_bufs.append(ob_)
            fw.barrier()
    deps = []
    for ob_ in io:
        if ob_.dsem is not None:
            deps.append(('d', ob_.dsem, ob_.dval, id(ob_)))
    fw._wait(sp, deps)
    if dbg:
        for bb in [hb[0]]:
            if bb.dsem is not None:
                fw._wait(sp, [('d', bb.dsem, bb.dval, id(bb))])
    fw.close()
    return nc


def prep_inputs(inputs):
    w = {}
    for n, s in WNAMES:
        if n == "w_q_bp":
            continue
        a = np.asarray(inputs[n], dtype=np.float32)
        if n != "norm_final":
            a = a[0]
        w[n] = np.ascontiguousarray(a)
    wq = w["w_q_b"]
    perm = np.arange(768)
    for h in range(NH):
        base = h * 96 + 64
        perm[base:base + 16] = np.arange(base + 16, base + 32)
        perm[base + 16:base + 32] = np.arange(base, base + 16)
    w["w_q_bp"] = np.ascontiguousarray(wq[:, perm])
    return w


_CACHE = {}


def kernel(**inputs):
    x = np.asarray(inputs["x"], dtype=np.float32)
    c = np.asarray(inputs["c"], dtype=np.float32)
    pos = np.asarray(inputs["positions"], dtype=np.int32)
    w = prep_inputs(inputs)
    cst = host_consts()
    if "nc" not in _CACHE:
        _CACHE["nc"] = build_program()
    nc = _CACHE["nc"]
    in_maps = []
    for i in range(8):
        m = {"x": np.ascontiguousarray(x[2 * i:2 * i + 2].reshape(NSEQ * S, D)),
             "c": np.ascontiguousarray(c[2 * i:2 * i + 2]),
             "pos": np.ascontiguousarray(pos[2 * i:2 * i + 2]),
             "cst": cst}
        m.update(w)
        in_maps.append(m)
    res = run_bass_kernel_spmd(nc, in_maps, core_ids=list(range(8)))
    out = np.concatenate([r["y"].reshape(NSEQ, S, D) for r in res.results], axis=0)
    return out.astype(np.float32)
```

```python
import contextlib
import numpy as np
import concourse.bass as bass
import concourse.mybir as mybir
from concourse.bass_utils import run_bass_kernel_spmd

F32 = mybir.dt.float32
BF16 = mybir.dt.bfloat16
I32 = mybir.dt.int32
ACT = mybir.ActivationFunctionType
ALU = mybir.AluOpType

P = 128
D = 1024
KC = 8
DFF = 2816
FC = 22
TG = 512
NTT = 4
S = 2048
NG = S // TG
NSEQ = 2
NH = 8
DIN = 4264
EPS = 1e-6
SEM_CAP = 24000
C1 = 6.28125
C2 = float(2 * np.pi - 6.28125)


class Eng:
    def __init__(self, fw, name, h, nsem):
        self.name = name
        self.h = h
        self.sems = [fw.es.enter_context(fw.nc.semaphore(f"s_{name}{i}")) for i in range(nsem)]
        self.n = 0
        self.seen = {}
        self.seen_d = {}

    def sem_val(self, seq):
        i = (seq - 1) // SEM_CAP
        return self.sems[i], (seq - 1) % SEM_CAP + 1


class Buf:
    def __init__(self, ap, name=""):
        self.ap = ap
        self.name = name
        self.w = None
        self.r = []
        self.dsem = None
        self.dval = 0

    def __getitem__(self, k):
        return self.ap[k]


class GroupBuf:
    def __init__(self, members, ap):
        self.members = members
        self.ap = ap
        self.name = members[0].name

    def __getitem__(self, k):
        return self.ap[k]


def _flat(bs):
    out = []
    for b in bs:
        out.extend(getattr(b, "members", None) or [b])
    return out


class FW:
    def __init__(self, nc):
        self.nc = nc
        self.es = contextlib.ExitStack()
        self.es.__enter__()
        self.pe = Eng(self, "pe", nc.tensor, 3)
        self.act = Eng(self, "act", nc.scalar, 3)
        self.dve = Eng(self, "dve", nc.vector, 4)
        self.pool = Eng(self, "pool", nc.gpsimd, 1)
        self.sp = Eng(self, "sp", nc.sync, 1)
        self.engs = [self.pe, self.act, self.dve, self.pool, self.sp]

    def close(self):
        self.es.__exit__(None, None, None)

    def buf(self, name, shape, dt, es=None):
        self.uid = getattr(self, "uid", 0) + 1
        name = f"{name}_{self.uid}"
        t = (es or self.es).enter_context(self.nc.sbuf_tensor(name, list(shape), dt))
        return Buf(t[tuple(slice(None) for _ in shape)], name)

    def bufs(self, name, shape, dt, n, es=None):
        self.uid = getattr(self, "uid", 0) + 1
        name = f"{name}_{self.uid}_"
        t = (es or self.es).enter_context(self.nc.sbuf_tensor(name, [shape[0], n] + list(shape[1:]), dt))
        full = t[tuple(slice(None) for _ in range(len(shape) + 1))]
        out = []
        for i in range(n):
            idx = (slice(None), i) + tuple(slice(None) for _ in shape[1:])
            out.append(Buf(t[idx], f"{name}{i}"))
        return out, full

    def _wait(self, E, deps):
        for d in deps:
            if d[0] == 'e':
                _, F, seq = d
                if F is E and E is self.pe:
                    continue
                if E.seen.get(F.name, 0) >= seq:
                    continue
                s, v = F.sem_val(seq)
                E.h.wait_ge(s, v)
                E.seen[F.name] = seq
            else:
                _, sem, val, key = d
                if E.seen_d.get(key, 0) >= val:
                    continue
                E.h.wait_ge(sem, val)
                E.seen_d[key] = val

    @staticmethod
    def _deps(reads, writes):
        reads, writes = _flat(reads), _flat(writes)
        deps = []
        for b in reads:
            if b.w is not None:
                deps.append(b.w)
        for b in writes:
            if b.w is not None:
                deps.append(b.w)
            deps.extend(b.r)
        return deps

    @staticmethod
    def _mark(tag, reads, writes):
        reads, writes = _flat(reads), _flat(writes)
        for b in reads:
            if tag[0] == 'e':
                b.r = [t for t in b.r if not (t[0] == 'e' and t[1] is tag[1])]
            b.r.append(tag)
        for b in writes:
            b.w = tag
            b.r = []

    def op(self, E, fns, reads=(), writes=()):
        if callable(fns):
            fns = [fns]
        if E is self.pool and getattr(self, "bar", None):
            self._wait(E, self.bar)
        self._wait(E, self._deps(reads, writes))
        ins = None
        for f in fns:
            ins = f()
        seq = E.n + 1
        E.n = seq
        s, _ = E.sem_val(seq)
        ins.then_inc(s, 1)
        self._mark(('e', E, seq), reads, writes)

    def dma(self, Q, out, in_, reads=(), writes=(), key=None, **kw):
        key = (getattr(key, "members", None) or [key])[0]
        if key.dsem is None:
            key.dsem = self.es.enter_context(self.nc.semaphore(f"d_{key.name}"))
        self._wait(Q, self._deps(reads, writes))
        key.dval += 16
        Q.h.dma_start(out=out, in_=in_, **kw).then_inc(key.dsem, 16)
        self._mark(('d', key.dsem, key.dval, id(key)), reads, writes)

    def barrier(self):
        ce = [self.pe, self.act, self.dve]
        for E in ce:
            self._wait(E, [('e', F, F.n) for F in ce + [self.pool] if F is not E and F.n > 0])
        self.bar = [('e', F, F.n) for F in ce if F.n > 0]


def host_consts():
    c = np.zeros((P, 6 * P + 8), np.float32)
    idx = np.arange(P)
    c[:, 0:P] = np.eye(P, dtype=np.float32)
    s_, t_ = idx[:, None], idx[None, :]
    c[:, P:2 * P] = (t_ >= s_).astype(np.float32)
    c[:, 2 * P:3 * P] = ((t_ >= s_) & (s_ // 64 == t_ // 64)).astype(np.float32)
    c[:, 3 * P:4 * P] = (s_ // 64 == t_ // 64).astype(np.float32)
    c[:, 4 * P:5 * P] = (s_ < 64).astype(np.float32) * np.ones((1, P), np.float32)
    c[:, 5 * P:6 * P] = (s_ >= 64).astype(np.float32) * np.ones((1, P), np.float32)
    inv = (np.float32(10000.0) ** (-np.arange(0, 32, 2, dtype=np.float32) / np.float32(32))).astype(np.float32)
    o = 6 * P
    c[64:80, o] = -inv
    c[80:96, o] = inv
    c[64:80, o + 1] = inv
    c[80:96, o + 1] = inv
    c[:, o + 2] = 1024 * EPS
    c[:, o + 3] = 384 * EPS
    c[:, o + 4] = 256 * EPS
    c[:, o + 5] = 128 * EPS
    c[:, o + 6] = 1.0
    c[:, o + 7] = np.log(128.0 ** -0.5)
    return c


WNAMES = [("w_ada", [D, 9 * D]), ("b_ada", [9 * D]), ("norm_ff1", [D]), ("ff1_w_gate", [D, DFF]),
          ("ff1_w_up", [D, DFF]), ("ff1_w_down", [DFF, D]), ("norm_mix", [D]), ("w_in", [D, DIN]),
          ("q_a_norm", [384]), ("w_q_b", [384, 768]), ("w_q_bp", [384, 768]), ("kv_a_norm", [256]),
          ("w_kv_b", [256, 1024]), ("conv_w", [4, 512]), ("conv_b", [512]), ("w_q_m", [4, 128, 128]),
          ("w_k_m", [4, 128, 128]), ("b_i", [4]), ("b_f", [4]), ("mlstm_norm", [4, 128]),
          ("w_mla_out", [512, D]), ("w_mlstm_out", [512, D]), ("w_o", [D, D]), ("norm_ff2", [D]),
          ("ff2_w_gate", [D, DFF]), ("ff2_w_up", [D, DFF]), ("ff2_w_down", [DFF, D]), ("norm_final", [D])]


def build_program(do_mixer=True, do_ff2=True, dbg=None, ngroups=NSEQ * NG):
    nc = bass.Bass("TRN2", target_bir_lowering=False)
    x = nc.dram_tensor("x", [NSEQ * S, D], F32, kind="ExternalInput").ap()
    c_in = nc.dram_tensor("c", [NSEQ, D], F32, kind="ExternalInput").ap()
    pos = nc.dram_tensor("pos", [NSEQ, S], I32, kind="ExternalInput").ap()
    cst = nc.dram_tensor("cst", [P, 6 * P + 8], F32, kind="ExternalInput").ap()
    W = {n: nc.dram_tensor(n, s, F32, kind="ExternalInput").ap() for n, s in WNAMES}
    y = nc.dram_tensor("y", [NSEQ * S, D], F32, kind="ExternalOutput").ap()
    dbg_out = {}
    if dbg:
        for n, s in dbg.items():
            dbg_out[n] = nc.dram_tensor(n, s, F32, kind="ExternalOutput").ap()

    fw = FW(nc)
    pe, act, dve, pool, sp = fw.pe, fw.act, fw.dve, fw.pool, fw.sp
    V = nc.vector
    A = nc.scalar
    T = nc.tensor

    hb, h_full = fw.bufs("h", [P, TG], F32, KC)
    uT, _ = fw.bufs("uT", [P, TG], BF16, KC)
    NSLOT = 8
    SLOT = 2048
    ring_t = fw.es.enter_context(nc.sbuf_tensor("ring_t", [P, NSLOT * SLOT], BF16))
    ring = [Buf(ring_t[:, k * SLOT:(k + 1) * SLOT], f"ring{k}") for k in range(NSLOT)]
    ring_i = [0]
    io = [fw.buf(f"io{i}", [P, D], F32) for i in range(3)]
    io_i = [0]
    cb = fw.buf("cst", [P, 6 * P + 8], F32)
    ident = cb[:, 0:P]
    tri_f = cb[:, P:2 * P]
    Mbd = cb[:, 2 * P:3 * P]
    BDm = cb[:, 3 * P:4 * P]
    SEL = [cb[:, 4 * P:5 * P], cb[:, 5 * P:6 * P]]
    o_ = 6 * P
    invfS = cb[:, o_:o_ + 1]
    invfC = cb[:, o_ + 1:o_ + 2]
    epsc = {1024: cb[:, o_ + 2:o_ + 3], 384: cb[:, o_ + 3:o_ + 4], 256: cb[:, o_ + 4:o_ + 5], 128: cb[:, o_ + 5:o_ + 6]}
    onec = cb[:, o_ + 6:o_ + 7]
    lnsc = cb[:, o_ + 7:o_ + 8]
    ones_b = fw.buf("ones_b", [P, P], BF16)
    rstd = fw.buf("rstd", [P, TG], F32)
    tmpA = [fw.buf(f"tmpA{i}", [P, TG], F32) for i in range(2)]
    tmpA_i = [0]
    sqb = [fw.buf(f"sqb{i}", [P, TG], BF16) for i in range(2)]
    sq_i = [0]
    psb = [Buf(fw.es.enter_context(nc.psum_tensor(f"ps{i}", [P, TG], F32))[:, :], f"ps{i}") for i in range(8)]
    ps_i = [0]

    pinned = set()

    def nps(pin=False):
        while True:
            b = psb[ps_i[0] % 8]
            ps_i[0] += 1
            if id(b) not in pinned:
                break
        if pin:
            pinned.add(id(b))
        return b

    def unpin(b):
        pinned.discard(id(b))

    def nslot(n_elems=0):
        if n_elems > SLOT:
            if ring_i[0] % 2 == 1:
                ring_i[0] += 1
            k = ring_i[0] % NSLOT
            ring_i[0] += 2
            return GroupBuf([ring[k], ring[k + 1]], ring_t[:, k * SLOT:(k + 2) * SLOT])
        b = ring[ring_i[0] % NSLOT]
        ring_i[0] += 1
        return b

    def nio():
        b = io[io_i[0] % 3]
        io_i[0] += 1
        return b

    def ntmp():
        b = tmpA[tmpA_i[0] % 2]
        tmpA_i[0] += 1
        return b

    def nsq():
        b = sqb[sq_i[0] % 2]
        sq_i[0] += 1
        return b

    def wload(src, a, b_=None):
        n = a * (b_ or 1)
        sl = nslot(n)
        if b_ is None:
            view = sl[:, 0:n]
        else:
            view = sl[:, 0:n].rearrange("p (a b) -> p a b", a=a)
        fw.dma(pool, view, src, writes=[sl], key=sl)
        return sl, view

    def mm(outb, out_ap, terms, reads, start=True, stop=True):
        n = len(terms)
        fns = []
        for i, (l, r) in enumerate(terms):
            fns.append(lambda l=l, r=r, i=i: T.matmul(out_ap, l, r, start=(start and i == 0), stop=(stop and i == n - 1), skip_group_check=True))
        fw.op(pe, fns, reads=reads, writes=[outb])

    def dump(name, ap, b):
        if dbg and name in dbg_out:
            fw.dma(sp, dbg_out[name], ap, reads=[b], key=b)

    fw.dma(sp, cb[:, :], cst[:, :], writes=[cb], key=cb)
    fw.op(dve, lambda: V.memset(ones_b[:, :], 1.0), writes=[ones_b])

    vst = fw.buf("vst", [P, 2, P], F32)
    fw.op(dve, lambda: V.memset(vst[:, :, :], 0.0), writes=[vst])
    rowsA = [("b_ada", 72), ("norm_ff1", 8), ("norm_mix", 8), ("norm_ff2", 8), ("norm_final", 8), ("q_a_norm", 3), ("kv_a_norm", 2)]
    r_ = 0
    for n_, k_ in rowsA:
        fw.dma(sp, vst[r_:r_ + k_, 0, :], W[n_].rearrange("(j p) -> j p", p=P), writes=[vst], key=vst)
        r_ += k_
    NA = r_
    fw.dma(sp, vst[0:16, 1, :], W["conv_w"].rearrange("j (hh p) -> (j hh) p", p=P), writes=[vst], key=vst)
    fw.dma(sp, vst[16:20, 1, :], W["conv_b"].rearrange("(hh p) -> hh p", p=P), writes=[vst], key=vst)
    fw.dma(sp, vst[20:24, 1, :], W["mlstm_norm"], writes=[vst], key=vst)
    fw.dma(sp, vst[24:40, 1, :], c_in.rearrange("b (kc p) -> (b kc) p", p=P), writes=[vst], key=vst)
    NB_ = 40
    vT = fw.buf("vT", [P, NA + NB_], F32)
    pvt = nps()
    fw.op(pe, lambda: T.transpose(pvt[:, 0:NA], vst[0:NA, 0, :], ident[0:NA, 0:NA]), reads=[vst, cb], writes=[pvt])
    fw.op(pe, lambda: T.transpose(pvt[:, NA:NA + NB_], vst[0:NB_, 1, :], ident[0:NB_, 0:NB_]), reads=[vst, cb], writes=[pvt])
    fw.op(act, lambda: A.copy(out=vT[:, :], in_=pvt[:, 0:NA + NB_]), reads=[pvt], writes=[vT])

    class SubBuf:
        def __init__(self, parent, ap):
            self.__dict__["parent"] = parent
            self.__dict__["ap"] = ap

        def __getattr__(self, k):
            return getattr(self.__dict__["parent"], k)

        def __setattr__(self, k, v):
            setattr(self.__dict__["parent"], k, v)

        def __getitem__(self, k):
            return self.__dict__["ap"][k]

    badaT = SubBuf(vT, vT[:, 0:72])
    nT = {k: SubBuf(vT, vT[:, 72 + 8 * i:80 + 8 * i]) for i, k in enumerate(["norm_ff1", "norm_mix", "norm_ff2", "norm_final"])}
    qnT = SubBuf(vT, vT[:, 104:107])
    kvnT = SubBuf(vT, vT[:, 107:109])
    convwT = SubBuf(vT, vT[:, NA:NA + 16].rearrange("p (j h) -> p j h", j=4))
    convbT = SubBuf(vT, vT[:, NA + 16:NA + 20])
    mnT = SubBuf(vT, vT[:, NA + 20:NA + 24])
    cT = SubBuf(vT, vT[:, NA + 24:NA + 40].rearrange("p (b k) -> p b k", b=NSEQ))
    bi_b = fw.buf("bi_b", [P, 4], F32)
    fw.dma(sp, bi_b[:, :], W["b_i"].rearrange("(o n) -> o n", o=1).partition_broadcast(P), writes=[bi_b], key=bi_b)
    bf_b = fw.buf("bf_b", [P, 4], F32)
    fw.dma(sp, bf_b[:, :], W["b_f"].rearrange("(o n) -> o n", o=1).partition_broadcast(P), writes=[bf_b], key=bf_b)
    for b_, sc in [(nT["norm_ff1"], 32.0), (nT["norm_mix"], 32.0), (nT["norm_ff2"], 32.0), (nT["norm_final"], 32.0),
                   (qnT, float(np.sqrt(384.0))), (kvnT, 16.0), (mnT, float(np.sqrt(128.0)))]:
        fw.op(dve, lambda b_=b_, sc=sc: V.tensor_scalar(out=b_.ap, in0=b_.ap, scalar1=sc, scalar2=None, op0=ALU.mult), reads=[b_], writes=[b_])
    kpeW = fw.buf("kpeW", [P, KC, 96], BF16)
    kpeWp = fw.buf("kpeWp", [P, KC, 96], BF16)
    wif = fw.buf("wif", [P, KC, 8], BF16)
    w_in_v = W["w_in"].rearrange("(kc p) n -> p kc n", p=P)
    if do_mixer:
        fw.op(dve, lambda: V.memset(kpeW[:, :, :], 0.0), writes=[kpeW])
        fw.op(dve, lambda: V.memset(kpeWp[:, :, :], 0.0), writes=[kpeWp])
        fw.dma(pool, kpeW[:, :, 64:96], w_in_v[:, :, 640:672], writes=[kpeW], key=kpeW)
        fw.dma(pool, kpeWp[:, :, 64:80], w_in_v[:, :, 656:672], writes=[kpeWp], key=kpeWp)
        fw.dma(pool, kpeWp[:, :, 80:96], w_in_v[:, :, 640:656], writes=[kpeWp], key=kpeWp)
        fw.dma(pool, wif[:, :, :], w_in_v[:, :, 2208:2216], writes=[wif], key=wif)

    def rstd_from(chunks, n, sbuf_src=False):
        pss = nps()
        nch = len(chunks)
        for i, (b_, ap) in enumerate(chunks):
            sq = nsq()
            if sbuf_src and i % 2 == 1:
                fw.op(pool, lambda ap=ap, sq=sq: nc.gpsimd.tensor_tensor(out=sq[:, :], in0=ap, in1=ap, op=ALU.mult), reads=[b_], writes=[sq])
            else:
                fw.op(act, lambda ap=ap, sq=sq: A.activation(out=sq[:, :], in_=ap, func=ACT.Square), reads=[b_], writes=[sq])
            mm(pss, pss[:, :], [(ones_b[:, :], sq[:, :])], [ones_b, sq], start=(i == 0), stop=(i == nch - 1))
        fw.op(act, lambda: A.activation(out=rstd[:, :], in_=pss[:, :], func=ACT.Ln, bias=epsc[n], scale=1.0), reads=[pss, cb], writes=[rstd])
        fw.op(act, lambda: A.activation(out=rstd[:, :], in_=rstd[:, :], func=ACT.Exp, scale=-0.5), reads=[rstd], writes=[rstd])

    def norm_mod(k, b):
        rstd_from([(hb[kc], hb[kc][:, :]) for kc in range(KC)], 1024, sbuf_src=True)
        for kc in range(KC):
            t = ntmp()
            fw.op(dve, lambda kc=kc, t=t: V.scalar_tensor_tensor(out=t[:, :], in0=hb[kc][:, :], scalar=Acol[:, k, b, kc:kc + 1], in1=rstd[:, :],
                                                                 op0=ALU.mult, op1=ALU.mult), reads=[hb[kc], Acol, rstd], writes=[t])
            fw.op(act, lambda kc=kc, t=t: A.activation(out=uT[kc][:, :], in_=t[:, :], func=ACT.Identity, bias=Bcol(k, b, kc), scale=1.0),
                  reads=[t, modT], writes=[uT[kc]])

    def ffn(k, b, wg, wu, wd, es, after_norm=None):
        hT, _ = fw.bufs("hT", [P, TG], BF16, FC, es=es)
        sg = [fw.buf(f"sg{i}", [P, TG], F32, es=es) for i in range(2)]
        norm_mod(k, b)
        if after_norm is not None:
            after_norm()
        wg_v = wg.rearrange("(kc p) n -> p kc n", p=P)
        wu_v = wu.rearrange("(kc p) n -> p kc n", p=P)
        wd_v = wd.rearrange("(fc p) n -> p fc n", p=P)
        for fb in range(FC // 2):
            sg_, gv = wload(wg_v[:, :, fb * 256:(fb + 1) * 256], KC, 256)
            su_, uv = wload(wu_v[:, :, fb * 256:(fb + 1) * 256], KC, 256)
            for j in range(2):
                f = 2 * fb + j
                pg = nps()
                pu = nps()
                mm(pg, pg[:, :], [(gv[:, kc, j * 128:(j + 1) * 128], uT[kc][:, :]) for kc in range(KC)], [sg_] + uT)
                mm(pu, pu[:, :], [(uv[:, kc, j * 128:(j + 1) * 128], uT[kc][:, :]) for kc in range(KC)], [su_] + uT)
                s_ = sg[f % 2]
                fw.op(act, lambda pg=pg, s_=s_: A.activation(out=s_[:, :], in_=pg[:, :], func=ACT.Silu), reads=[pg], writes=[s_])
                fw.op(dve, lambda pu=pu, s_=s_, f=f: V.tensor_tensor(out=hT[f][:, :], in0=s_[:, :], in1=pu[:, :], op=ALU.mult), reads=[s_, pu], writes=[hT[f]])
        for half in range(2):
            accs = [nps(pin=True) for _ in range(4)]
            f0 = 0
            while f0 < FC:
                nf = min(4, FC - f0)
                sd_, dv = wload(wd_v[:, f0:f0 + nf, half * 512:(half + 1) * 512], nf, 512)
                for fi in range(nf):
                    f = f0 + fi
                    for i in range(4):
                        mm(accs[i], accs[i][:, :], [(dv[:, fi, i * 128:(i + 1) * 128], hT[f][:, :])], [sd_, hT[f]], start=(f == 0), stop=(f == FC - 1))
                f0 += nf
            for i in range(4):
                dc = half * 4 + i
                po = accs[i]
                fw.op(dve, lambda dc=dc, po=po: V.scalar_tensor_tensor(out=hb[dc][:, :], in0=po[:, :], scalar=Gcol[:, k, b, dc:dc + 1], in1=hb[dc][:, :],
                                                                       op0=ALU.mult, op1=ALU.add), reads=[po, Gcol, hb[dc]], writes=[hb[dc]])
                unpin(po)

    if do_mixer:
        kT = [[fw.buf(f"kT{h}_{g}", [P, TG], BF16) for g in range(NG)] for h in range(NH)]
        Vt = [fw.buf(f"V{t}", [P, NH * 65 + 63], BF16) for t in range(S // P)]
        for t in range(S // P):
            fw.op(dve, lambda t=t: V.memset(Vt[t][:, :], 1.0), writes=[Vt[t]])
        for h_ in range(NH):
            for g_ in range(NG):
                fw.op(dve, lambda h_=h_, g_=g_: V.memset(kT[h_][g_][:, :], 0.0), writes=[kT[h_][g_]])
        Sf = [fw.buf(f"Sf{hh}", [P, 256], F32) for hh in range(4)]
        Sbf = [[fw.buf(f"Sbf{hh}_{i}", [P, 256], BF16) for i in range(2)] for hh in range(4)]
        sbf_i = [0, 0, 0, 0]
        xtail = fw.buf("xtail", [P, 4, 3], F32)
        posb = fw.buf("posb", [96, TG], I32)
        Ct = fw.buf("Ct", [96, TG], F32)
        St = fw.buf("St", [96, TG], F32)
        tg = (fw.buf("ang", [96, TG], F32), fw.buf("ki", [96, TG], I32), fw.buf("kf", [96, TG], F32), fw.buf("mm_", [96, TG], F32))
        attnT = [fw.buf(f"attnT{h}", [64, TG], BF16) for h in range(NH)]
        hnT, hn_full = fw.bufs("hnT", [P, TG], BF16, 4)

    def rope_tables(b, g):
        c0 = g * TG
        fw.dma(sp, posb[:, :], pos[b:b + 1, c0:c0 + TG].partition_broadcast(96), writes=[posb], key=posb)
        rope_table(St, invfS, 0.0, None, tg)
        rope_table(Ct, invfC, float(np.pi / 2), None, tg)

    def rope_table(out_t, invcol, shift, es, tg):
        ang, ki, kf, m = tg
        fw.op(dve, lambda: V.tensor_copy(out=kf[:, :], in_=posb[:, :]), reads=[posb], writes=[kf])
        fw.op(dve, lambda: V.tensor_scalar(out=ang[:, :], in0=kf[:, :], scalar1=invcol[0:96, :], scalar2=shift, op0=ALU.mult, op1=ALU.add), reads=[kf, cb], writes=[ang])
        fw.op(dve, lambda: V.tensor_scalar(out=ki[:, :], in0=ang[:, :], scalar1=float(1.0 / (2 * np.pi)), scalar2=None, op0=ALU.mult), reads=[ang], writes=[ki])
        fw.op(dve, lambda: V.tensor_copy(out=kf[:, :], in_=ki[:, :]), reads=[ki], writes=[kf])
        fw.op(dve, lambda: V.scalar_tensor_tensor(out=ang[:, :], in0=kf[:, :], scalar=-C1, in1=ang[:, :], op0=ALU.mult, op1=ALU.add), reads=[kf, ang], writes=[ang])
        fw.op(dve, lambda: V.scalar_tensor_tensor(out=ang[:, :], in0=kf[:, :], scalar=-C2, in1=ang[:, :], op0=ALU.mult, op1=ALU.add), reads=[kf, ang], writes=[ang])
        fw.op(dve, lambda: V.tensor_scalar(out=m[:, :], in0=ang[:, :], scalar1=float(np.pi), scalar2=-float(2 * np.pi), op0=ALU.is_gt, op1=ALU.mult), reads=[ang], writes=[m])
        fw.op(dve, lambda: V.tensor_tensor(out=ang[:, :], in0=ang[:, :], in1=m[:, :], op=ALU.add), reads=[ang, m], writes=[ang])
        fw.op(dve, lambda: V.tensor_scalar(out=m[:, :], in0=ang[:, :], scalar1=-float(np.pi), scalar2=float(2 * np.pi), op0=ALU.is_lt, op1=ALU.mult), reads=[ang], writes=[m])
        fw.op(dve, lambda: V.tensor_tensor(out=ang[:, :], in0=ang[:, :], in1=m[:, :], op=ALU.add), reads=[ang, m], writes=[ang])
        fw.op(act, lambda: A.activation(out=out_t[:, :], in_=ang[:, :], func=ACT.Sin), reads=[ang], writes=[out_t])

    def mixer(b, g):
        c0 = g * TG
        norm_mod(1, b)
        with contextlib.ExitStack() as es:
            qln, _ = fw.bufs("qln", [P, TG], BF16, 3, es=es)
            ckn, _ = fw.bufs("ckn", [P, TG], BF16, 2, es=es)
            qT = [fw.buf(f"qT{h}", [P, TG], BF16, es=es) for h in range(NH)]
            for h in range(NH):
                fw.op(pool, lambda h=h: nc.gpsimd.memset(qT[h][:, :], 0.0), writes=[qT[h]])
            pT = [fw.buf(f"pT{i}", [P, TG], BF16, es=es) for i in range(3)]
            osb = [fw.buf(f"osb{i}", [64, TG], F32, es=es) for i in range(2)]
            rden = [fw.buf(f"rden{i}", [65, TG], F32, es=es) for i in range(2)]
            rhi = [fw.buf(f"rhi{i}", [65, TG], BF16, es=es) for i in range(2)]
            rlo = [fw.buf(f"rlo{i}", [65, TG], BF16, es=es) for i in range(2)]
            t1b = fw.buf("t1b", [96, TG], F32, es=es)
            t2b = fw.buf("t2b", [96, TG], F32, es=es)
            t1q = [t1b, fw.buf("t1c", [96, TG], F32, es=es)]
            t2q = [t2b, fw.buf("t2c", [96, TG], F32, es=es)]


            def lat(col0, nch, outs, gcol, n):
                sl, wv = wload(w_in_v[:, :, col0:col0 + nch * P], KC, nch * P)
                pl = []
                for j in range(nch):
                    pj = nps()
                    mm(pj, pj[:, :], [(wv[:, kc, j * P:(j + 1) * P], uT[kc][:, :]) for kc in range(KC)], [sl] + uT)
                    pl.append(pj)
                rstd_from([(pj, pj[:, :]) for pj in pl], n)
                for j in range(nch):
                    fw.op(dve, lambda j=j: V.scalar_tensor_tensor(out=outs[j][:, :], in0=pl[j][:, :], scalar=gcol[:, j:j + 1], in1=rstd[:, :],
                                                                  op0=ALU.mult, op1=ALU.mult), reads=[pl[j], gcol, rstd], writes=[outs[j]])
            lat(0, 3, qln, qnT, 384)
            lat(384, 2, ckn, kvnT, 256)
            pk = nps()
            pkp = nps()
            mm(pk, pk[0:96, :], [(kpeW[:, kc, :], uT[kc][:, :]) for kc in range(KC)], [kpeW] + uT)
            mm(pkp, pkp[0:96, :], [(kpeWp[:, kc, :], uT[kc][:, :]) for kc in range(KC)], [kpeWp] + uT)
            fw.op(dve, lambda: V.tensor_tensor(out=t1b[64:96, :], in0=pk[64:96, :], in1=Ct[64:96, :], op=ALU.mult), reads=[pk, Ct], writes=[t1b])
            fw.op(dve, lambda: V.tensor_tensor(out=t2b[64:96, :], in0=pkp[64:96, :], in1=St[64:96, :], op=ALU.mult), reads=[pkp, St], writes=[t2b])
            for h in range(NH):
                fw.op(pool if h % 2 else dve, lambda h=h: (nc.gpsimd if h % 2 else V).tensor_tensor(out=kT[h][g][64:96, :], in0=t1b[64:96, :], in1=t2b[64:96, :], op=ALU.add), reads=[t1b, t2b], writes=[kT[h][g]])
            slkv, wkv = wload(W["w_kv_b"].rearrange("(j p) n -> p j n", p=P), 2, 1024)
            for h in range(NH):
                pn = nps()
                mm(pn, pn[0:64, :], [(wkv[:, j, h * 128:h * 128 + 64], ckn[j][:, :]) for j in range(2)], [slkv] + ckn)
                fw.op(act, lambda h=h, pn=pn: A.copy(out=kT[h][g][0:64, :], in_=pn[0:64, :]), reads=[pn], writes=[kT[h][g]])
            for tt in range(NTT):
                pv = nps()
                mm(pv, pv[:, :],
                   [(ckn[j][:, tt * P:(tt + 1) * P], wkv[:, j, :].rearrange("p (h two d) -> p h two d", two=2, d=64)[:, :, 1, :]) for j in range(2)], [slkv] + ckn)
                vt = Vt[g * NTT + tt]
                fw.op(act, lambda pv=pv, vt=vt: A.copy(out=vt[:, 0:NH * 65].rearrange("p (h d) -> p h d", d=65)[:, :, 0:64], in_=pv[:, :].rearrange("p (h d) -> p h d", d=64)), reads=[pv], writes=[vt])
            slq, wq = wload(W["w_q_b"].rearrange("(j p) n -> p j n", p=P), 3, 768)
            slqp, wqp = wload(W["w_q_bp"].rearrange("(j p) n -> p j n", p=P), 3, 768)
            for h in range(NH):
                pq = nps()
                pqp = nps()
                mm(pq, pq[0:96, :], [(wq[:, j, h * 96:(h + 1) * 96], qln[j][:, :]) for j in range(3)], [slq] + qln)
                mm(pqp, pqp[0:96, :], [(wqp[:, j, h * 96:(h + 1) * 96], qln[j][:, :]) for j in range(3)], [slqp] + qln)
                ta, tb = t1q[h % 2], t2q[h % 2]
                fw.op(dve, lambda pq=pq, ta=ta: V.tensor_tensor(out=ta[:, :], in0=pq[0:96, :], in1=Ct[:, :], op=ALU.mult), reads=[pq, Ct], writes=[ta])
                fw.op(dve, lambda pqp=pqp, tb=tb: V.tensor_tensor(out=tb[:, :], in0=pqp[0:96, :], in1=St[:, :], op=ALU.mult), reads=[pqp, St], writes=[tb])
                fw.op(pool, lambda h=h, ta=ta, tb=tb: nc.gpsimd.tensor_tensor(out=qT[h][0:96, :], in0=ta[:, :], in1=tb[:, :], op=ALU.add), reads=[ta, tb], writes=[qT[h]])
            sc = float(96.0 ** -0.5)
            nkt = 4 * g + 4
            items = [(h, kt) for h in range(NH) for kt in range(nkt)]
            pti = [0]

            def qk(h, kt):
                jd = kt - 4 * g
                q0 = 128 * jd if jd > 0 else 0
                n = TG - q0
                pss = nps()
                kb = kT[h][kt // 4]
                mm(pss, pss[:, 0:n], [(kb[:, (kt % 4) * P:(kt % 4 + 1) * P], qT[h][:, q0:TG])], [kb, qT[h]])
                return pss, q0, n, jd

            def epi1(h, po):
                ob = osb[h % 2]
                rd, rh, rl = rden[h % 2], rhi[h % 2], rlo[h % 2]
                fw.op(dve, lambda: V.tensor_copy(out=ob[:, :], in_=po[0:64, :]), reads=[po], writes=[ob])
                fw.op(act, lambda: A.activation(out=rd[64:65, :], in_=po[64:65, :], func=ACT.Ln), reads=[po], writes=[rd])
                fw.op(act, lambda: A.activation(out=rd[64:65, :], in_=rd[64:65, :], func=ACT.Exp, scale=-1.0), reads=[rd], writes=[rd])
                unpin(po)
                fw.op(dve, lambda: V.tensor_copy(out=rh[64:65, :], in_=rd[64:65, :]), reads=[rd], writes=[rh])
                fw.op(dve, lambda: V.tensor_tensor(out=rl[64:65, :], in0=rd[64:65, :], in1=rh[64:65, :], op=ALU.subtract), reads=[rd, rh], writes=[rl])

            def epi2(h):
                ob = osb[h % 2]
                rh, rl = rhi[h % 2], rlo[h % 2]
                pb = nps()
                mm(pb, pb[0:64, :], [(ones_b[64:65, 0:64], rh[64:65, :]), (ones_b[64:65, 0:64], rl[64:65, :])], [ones_b, rh, rl])
                fw.op(dve, lambda: V.tensor_tensor(out=attnT[h][:, :], in0=ob[:, :], in1=pb[0:64, :], op=ALU.mult), reads=[ob, pb], writes=[attnT[h]])

            AHEAD = 3
            inflight = [qk(*items[i]) for i in range(min(AHEAD, len(items)))]
            po = None
            pending = None
            for idx, (h, kt) in enumerate(items):
                if idx + AHEAD < len(items):
                    inflight.append(qk(*items[idx + AHEAD]))
                cur = inflight.pop(0)
                if kt == 0:
                    po = nps(pin=True)
                pss, q0, n, jd = cur
                pt = pT[pti[0] % 3]
                pti[0] += 1
                fw.op(act, lambda pss=pss, pt=pt, n=n: A.activation(out=pt[:, 0:n], in_=pss[:, 0:n], func=ACT.Exp, scale=sc), reads=[pss], writes=[pt])
                if jd >= 0:
                    fw.op(dve, lambda pt=pt: V.tensor_tensor(out=pt[:, 0:P], in0=pt[:, 0:P], in1=tri_f, op=ALU.mult), reads=[pt, cb], writes=[pt])
                vt = Vt[kt]
                mm(po, po[:, q0:TG], [(vt[:, h * 65:h * 65 + P], pt[:, 0:n])], [vt, pt], start=(kt == 0), stop=(kt == nkt - 1))
                if pending is not None:
                    pending[1] -= 1
                    if pending[1] <= 0:
                        epi2(pending[0])
                        pending = None
                if kt == nkt - 1:
                    if pending is not None:
                        epi2(pending[0])
                    epi1(h, po)
                    pending = [h, 3]
            if pending is not None:
                epi2(pending[0])
            fw.barrier()
        with contextlib.ExitStack() as es:
            xbuf = [fw.buf(f"xbuf{i}", [P, TG + 3], F32, es=es) for i in range(2)]
            acc = [fw.buf(f"acc{i}", [P, TG], F32, es=es) for i in range(2)]
            xcT = [fw.buf(f"xcT{hh}", [P, TG], BF16, es=es) for hh in range(4)]
            og = fw.buf("og", [P, 4, TG], BF16, es=es)
            gsm = {n: fw.buf(f"g_{n}", [P, NTT, 4], F32, es=es) for n in ["logi", "logf", "a", "ea", "ks", "t"]}
            eg = fw.buf("eg", [P, NTT, 8], F32, es=es)
            rep2 = [fw.buf("rep", [P, 4, P], F32, es=es)] * 2
            Eb2 = [fw.buf("Eb", [P, 4, P], F32, es=es)] * 2
            qpT2 = [fw.buf(f"qpT{i}", [P, 4, P], BF16, es=es) for i in range(2)]
            kTm2 = [fw.buf("kTm", [P, 4, P], BF16, es=es)] * 2
            kpp2 = [fw.buf(f"kpp{i}", [P, 4, P], BF16, es=es) for i in range(2)]
            vm2 = [fw.buf(f"vm{i}", [P, TG], BF16, es=es) for i in range(2)]
            smT2 = [fw.buf(f"smT{i}", [P, 4, P], BF16, es=es) for i in range(2)]
            dd = fw.buf("dd", [P, 4, P], F32, es=es)
            hg = fw.buf("hg", [P, 4, P], F32, es=es)
            sq4 = fw.buf("sq4", [P, 4, P], BF16, es=es)
            rs4 = fw.buf("rs4", [P, 4, P], F32, es=es)

            pgi = nps()
            for tt in range(NTT):
                mm(pgi, pgi[:, tt * 8:(tt + 1) * 8], [(uT[kc][:, tt * P:(tt + 1) * P], wif[:, kc, :]) for kc in range(KC)], [wif] + uT)
            gview = pgi[:, 0:32].rearrange("p (t c) -> p t c", c=8)
            fw.op(dve, lambda: V.tensor_tensor(out=gsm["logi"][:, :, :], in0=gview[:, :, 0:4], in1=bi_b[:, :].unsqueeze(1).broadcast_to([P, NTT, 4]), op=ALU.add),
                  reads=[pgi, bi_b], writes=[gsm["logi"]])
            fw.op(dve, lambda: V.tensor_tensor(out=gsm["t"][:, :, :], in0=gview[:, :, 4:8], in1=bf_b[:, :].unsqueeze(1).broadcast_to([P, NTT, 4]), op=ALU.add),
                  reads=[pgi, bf_b], writes=[gsm["t"]])
            fw.op(act, lambda: A.activation(out=gsm["t"][:, :, :], in_=gsm["t"][:, :, :], func=ACT.Exp, scale=-1.0), reads=[gsm["t"]], writes=[gsm["t"]])
            fw.op(act, lambda: A.activation(out=gsm["t"][:, :, :], in_=gsm["t"][:, :, :], func=ACT.Ln, bias=onec, scale=1.0), reads=[gsm["t"], cb], writes=[gsm["t"]])
            fw.op(dve, lambda: V.tensor_scalar(out=gsm["logf"][:, :, :], in0=gsm["t"][:, :, :], scalar1=-1.0, scalar2=None, op0=ALU.mult), reads=[gsm["t"]], writes=[gsm["logf"]])
            slx, wx = wload(w_in_v[:, :, 672:1184], KC, 512)
            for hh in range(4):
                px = nps()
                mm(px, px[:, :], [(wx[:, kc, hh * P:(hh + 1) * P], uT[kc][:, :]) for kc in range(KC)], [slx] + uT)
                xb_ = xbuf[hh % 2]
                ac = acc[hh % 2]
                if g == 0:
                    fw.op(dve, lambda xb_=xb_: V.memset(xb_[:, 0:3], 0.0), writes=[xb_])
                else:
                    fw.op(dve, lambda xb_=xb_, hh=hh: V.tensor_copy(out=xb_[:, 0:3], in_=xtail[:, hh, :]), reads=[xtail], writes=[xb_])
                fw.op(act, lambda xb_=xb_, px=px: A.copy(out=xb_[:, 3:TG + 3], in_=px[:, :]), reads=[px], writes=[xb_])
                fw.op(dve, lambda xb_=xb_, hh=hh: V.tensor_copy(out=xtail[:, hh, :], in_=xb_[:, TG:TG + 3]), reads=[xb_], writes=[xtail])
                fw.op(act, lambda xb_=xb_, ac=ac, hh=hh: A.activation(out=ac[:, :], in_=xb_[:, 3:TG + 3], func=ACT.Identity, bias=convbT[:, hh:hh + 1], scale=convwT[:, 3, hh:hh + 1]),
                      reads=[xb_, convbT, convwT], writes=[ac])
                for j in range(3):
                    fw.op(dve, lambda xb_=xb_, ac=ac, hh=hh, j=j: V.scalar_tensor_tensor(out=ac[:, :], in0=xb_[:, j:j + TG], scalar=convwT[:, j, hh:hh + 1], in1=ac[:, :],
                                                                                         op0=ALU.mult, op1=ALU.add), reads=[xb_, convwT, ac], writes=[ac])
                fw.op(act, lambda ac=ac, hh=hh: A.activation(out=xcT[hh][:, :], in_=ac[:, :], func=ACT.Silu), reads=[ac], writes=[xcT[hh]])
            slo, wo_ = wload(w_in_v[:, :, 1696:2208], KC, 512)
            for hh in range(4):
                pg_ = nps()
                mm(pg_, pg_[:, :], [(wo_[:, kc, hh * P:(hh + 1) * P], uT[kc][:, :]) for kc in range(KC)], [slo] + uT)
                fw.op(act, lambda pg_=pg_, hh=hh: A.activation(out=og[:, hh, :], in_=pg_[:, :], func=ACT.Sigmoid), reads=[pg_], writes=[og])
            pgs = nps()
            for tt in range(NTT):
                for i_, lhs in enumerate([Mbd, BDm, SEL[0], SEL[1]]):
                    mm(pgs, pgs[:, tt * 16 + i_ * 4: tt * 16 + i_ * 4 + 4], [(lhs, gsm["logf"][:, tt, :])], [cb, gsm["logf"]])
            gs = pgs[:, 0:64].rearrange("p (t c) -> p t c", c=16)
            fw.op(dve, lambda: V.tensor_tensor(out=gsm["a"][:, :, :], in0=gsm["logi"][:, :, :], in1=gs[:, :, 0:4], op=ALU.subtract), reads=[gsm["logi"], pgs], writes=[gsm["a"]])
            fw.op(act, lambda: A.activation(out=gsm["ea"][:, :, :], in_=gsm["a"][:, :, :], func=ACT.Exp), reads=[gsm["a"]], writes=[gsm["ea"]])
            fw.op(dve, lambda: V.tensor_tensor(out=gsm["t"][:, :, :], in0=gsm["a"][:, :, :], in1=gs[:, :, 4:8], op=ALU.add), reads=[gsm["a"], pgs], writes=[gsm["t"]])
            fw.op(act, lambda: A.activation(out=gsm["ks"][:, :, :], in_=gsm["t"][:, :, :], func=ACT.Exp, bias=lnsc, scale=1.0), reads=[gsm["t"], cb], writes=[gsm["ks"]])
            fw.op(act, lambda: A.activation(out=eg[:, :, :], in_=gs[:, :, 8:16], func=ACT.Exp), reads=[pgs], writes=[eg])
            if dbg and "logf" in dbg_out and g == 0 and b == 0:
                dump("logf", gsm["logf"][:, :, :], gsm["logf"])
                dump("logi", gsm["logi"][:, :, :], gsm["logi"])
                dump("ga", gsm["a"][:, :, :], gsm["a"])
                dump("eg", eg[:, :, :], eg)
            slv, wv_ = wload(w_in_v[:, :, 1184:1696], KC, 512)
            slqm, wqm = wload(W["w_q_m"].rearrange("h d e -> d h e"), 4, P)
            slkm, wkm = wload(W["w_k_m"].rearrange("h d e -> d h e"), 4, P)
            if g == 0:
                for hh in range(4):
                    fw.op(dve, lambda hh=hh: V.memset(Sf[hh][:, :], 0.0), writes=[Sf[hh]])
                    fw.op(dve, lambda hh=hh: V.memset(Sbf[hh][sbf_i[hh] % 2][:, :], 0.0), writes=[Sbf[hh][sbf_i[hh] % 2]])
            def front(tt):
                i2 = tt % 2
                tc_ = slice(tt * P, (tt + 1) * P)
                vm, kpp, rep, Eb, qpT, kTm, smT = vm2[i2], kpp2[i2], rep2[i2], Eb2[i2], qpT2[i2], kTm2[i2], smT2[i2]
                pvm = nps()
                mm(pvm, pvm[:, :], [(uT[kc][:, tc_], wv_[:, kc, :]) for kc in range(KC)], [slv] + uT)
                fw.op(act, lambda: A.copy(out=vm[:, :], in_=pvm[:, :]), reads=[pvm], writes=[vm])
                pk2 = nps()
                for hh in range(4):
                    mm(pk2, pk2[:, hh * P:(hh + 1) * P], [(xcT[hh][:, tc_], wkm[:, hh, :])], [xcT[hh], slkm])
                fw.op(dve, lambda: V.tensor_tensor(out=kpp[:, :, :], in0=pk2[:, :].rearrange("p (h e) -> p h e", e=P),
                                                   in1=gsm["ks"][:, tt, :].unsqueeze(2).broadcast_to([P, 4, P]), op=ALU.mult), reads=[pk2, gsm["ks"]], writes=[kpp])
                pbb = nps()
                fw.op(pool, lambda: nc.gpsimd.tensor_copy(out=rep[:, :, :], in_=gsm["logf"][:, tt, :].unsqueeze(2).broadcast_to([P, 4, P])), reads=[gsm["logf"]], writes=[rep])
                for hh in range(4):
                    mm(pbb, pbb[:, hh * P:(hh + 1) * P], [(rep[:, hh, :], Mbd)], [rep, cb])
                fw.op(act, lambda: A.activation(out=Eb[:, :, :], in_=pbb[:, :].rearrange("p (h e) -> p h e", e=P), func=ACT.Exp), reads=[pbb], writes=[Eb])
                pq_ = nps()
                pk_ = nps()
                for hh in range(4):
                    mm(pq_, pq_[:, hh * P:(hh + 1) * P], [(wqm[:, hh, :], xcT[hh][:, tc_])], [slqm, xcT[hh]])
                    mm(pk_, pk_[:, hh * P:(hh + 1) * P], [(wkm[:, hh, :], xcT[hh][:, tc_])], [slkm, xcT[hh]])
                fw.op(dve, lambda: V.tensor_tensor(out=qpT[:, :, :], in0=pq_[:, :].rearrange("p (h e) -> p h e", e=P), in1=Eb[:, :, :], op=ALU.mult), reads=[pq_, Eb], writes=[qpT])
                fw.op(act, lambda: A.activation(out=kTm[:, :, :], in_=pk_[:, :].rearrange("p (h e) -> p h e", e=P), func=ACT.Identity, scale=float(128.0 ** -0.5)), reads=[pk_], writes=[kTm])
                pS = nps()
                for hh in range(4):
                    mm(pS, pS[:, hh * P:(hh + 1) * P], [(kTm[:, hh, :], qpT[:, hh, :])], [kTm, qpT])
                for hh in range(4):
                    fw.op(dve, lambda hh=hh: V.scalar_tensor_tensor(out=smT[:, hh, :], in0=pS[:, hh * P:(hh + 1) * P], scalar=gsm["ea"][:, tt, hh:hh + 1], in1=Mbd,
                                                                    op0=ALU.mult, op1=ALU.mult), reads=[pS, gsm["ea"], cb], writes=[smT])

            def back(tt):
                i2 = tt % 2
                tc_ = slice(tt * P, (tt + 1) * P)
                vm, kpp, qpT, smT = vm2[i2], kpp2[i2], qpT2[i2], smT2[i2]
                pnum = nps(pin=True)
                pden = nps(pin=True)
                for hh in range(4):
                    hs = slice(hh * P, (hh + 1) * P)
                    mm(pnum, pnum[:, hs], [(vm[:, hs], smT[:, hh, :])], [vm, smT], start=(hh == 0), stop=False)
                    mm(pden, pden[:, hs], [(ones_b[:, :], smT[:, hh, :])], [ones_b, smT], start=(hh == 0), stop=False)
                for c in range(2):
                    cs = slice(c * 64, (c + 1) * 64)
                    for hh in range(4):
                        sb_ = Sbf[hh][sbf_i[hh] % 2]
                        mm(pnum, pnum[:, hh * P + c * 64: hh * P + (c + 1) * 64], [(sb_[:, 0:P], qpT[:, hh, cs])], [sb_, qpT], start=False, stop=True)
                        mm(pden, pden[:, hh * P + c * 64: hh * P + (c + 1) * 64], [(sb_[:, P:2 * P], qpT[:, hh, cs])], [sb_, qpT], start=False, stop=True)
                        pup = nps()
                        mm(pup, pup[:, 0:P], [(kpp[cs, hh, :], vm[cs, hh * P:(hh + 1) * P])], [kpp, vm])
                        mm(pup, pup[:, P:2 * P], [(kpp[cs, hh, :], ones_b[cs, :])], [kpp, ones_b])
                        fw.op(dve, lambda hh=hh, pup=pup, c=c: V.scalar_tensor_tensor(out=Sf[hh][:, :], in0=Sf[hh][:, :], scalar=eg[:, tt, c * 4 + hh:c * 4 + hh + 1], in1=pup[:, 0:256],
                                                                                    op0=ALU.mult, op1=ALU.add), reads=[Sf[hh], eg, pup], writes=[Sf[hh]])
                        sbf_i[hh] += 1
                        nb_ = Sbf[hh][sbf_i[hh] % 2]
                        fw.op(act, lambda hh=hh, nb_=nb_: A.copy(out=nb_[:, :], in_=Sf[hh][:, :]), reads=[Sf[hh]], writes=[nb_])
                fw.op(act, lambda: A.activation(out=dd[:, :, :], in_=pden[:, :].rearrange("p (h e) -> p h e", e=P), func=ACT.Abs), reads=[pden], writes=[dd])
                fw.op(dve, lambda: V.tensor_scalar(out=dd[:, :, :], in0=dd[:, :, :], scalar1=1.0, scalar2=None, op0=ALU.max), reads=[dd], writes=[dd])
                fw.op(act, lambda: A.activation(out=dd[:, :, :], in_=dd[:, :, :], func=ACT.Ln), reads=[dd], writes=[dd])
                fw.op(act, lambda: A.activation(out=dd[:, :, :], in_=dd[:, :, :], func=ACT.Exp, scale=-1.0), reads=[dd], writes=[dd])
                fw.op(dve, lambda: V.tensor_tensor(out=hg[:, :, :], in0=pnum[:, :].rearrange("p (h e) -> p h e", e=P), in1=dd[:, :, :], op=ALU.mult), reads=[pnum, dd], writes=[hg])
                unpin(pnum)
                unpin(pden)
                fw.op(pool, lambda: nc.gpsimd.tensor_tensor(out=hg[:, :, :], in0=hg[:, :, :], in1=og[:, :, tc_], op=ALU.mult), reads=[hg, og], writes=[hg])
                fw.op(act, lambda: A.activation(out=sq4[:, :, :], in_=hg[:, :, :], func=ACT.Square), reads=[hg], writes=[sq4])
                pss4 = nps()
                mm(pss4, pss4[:, :], [(ones_b[:, :], sq4[:, :, :].rearrange("p h e -> p (h e)"))], [ones_b, sq4])
                fw.op(act, lambda: A.activation(out=rs4[:, :, :], in_=pss4[:, :].rearrange("p (h e) -> p h e", e=P), func=ACT.Ln, bias=epsc[128], scale=1.0), reads=[pss4, cb], writes=[rs4])
                fw.op(act, lambda: A.activation(out=rs4[:, :, :], in_=rs4[:, :, :], func=ACT.Exp, scale=-0.5), reads=[rs4], writes=[rs4])
                for hh in range(4):
                    fw.op(dve, lambda hh=hh: V.scalar_tensor_tensor(out=hnT[hh][:, tc_], in0=hg[:, hh, :], scalar=mnT[:, hh:hh + 1], in1=rs4[:, hh, :],
                                                                    op0=ALU.mult, op1=ALU.mult), reads=[hg, mnT, rs4], writes=[hnT[hh]])

            front(0)
            for tt in range(NTT):
                if tt + 1 < NTT:
                    front(tt + 1)
                back(tt)
            fw.barrier()
        with contextlib.ExitStack() as es:
            yT, _ = fw.bufs("yT", [P, TG], BF16, KC, es=es)
            s0 = fw.buf("s0", [P, TG], F32, es=es)
            s1 = fw.buf("s1", [P, TG], F32, es=es)
            y1 = fw.buf("y1", [P, TG], F32, es=es)
            y2 = fw.buf("y2", [P, TG], F32, es=es)
            wmla_v = W["w_mla_out"].rearrange("(h d) n -> d h n", d=64)
            wmls_v = W["w_mlstm_out"].rearrange("(hh p) n -> p hh n", p=P)
            for dp in range(4):
                ds_ = slice(dp * 256, (dp + 1) * 256)
                sla = nslot()
                wa = sla[0:64, 0:2048].rearrange("p (a b) -> p a b", a=8)
                fw.dma(pool, wa, wmla_v[:, :, ds_], writes=[sla], key=sla)
                slb, wb = wload(wmls_v[:, :, ds_], 4, 256)
                slg0, wg0 = wload(w_in_v[:, :, 2216 + dp * 256:2216 + (dp + 1) * 256], KC, 256)
                slg1, wg1 = wload(w_in_v[:, :, 3240 + dp * 256:3240 + (dp + 1) * 256], KC, 256)
                for j in range(2):
                    dc = 2 * dp + j
                    js = slice(j * P, (j + 1) * P)
                    pa = nps()
                    pb_ = nps()
                    p0 = nps()
                    p1 = nps()
                    mm(pa, pa[:, :], [(wa[:, h, js], attnT[h][:, :]) for h in range(NH)], [sla] + attnT)
                    mm(pb_, pb_[:, :], [(wb[:, hh, js], hnT[hh][:, :]) for hh in range(4)], [slb] + hnT)
                    mm(p0, p0[:, :], [(wg0[:, kc, js], uT[kc][:, :]) for kc in range(KC)], [slg0] + uT)
                    mm(p1, p1[:, :], [(wg1[:, kc, js], uT[kc][:, :]) for kc in range(KC)], [slg1] + uT)
                    fw.op(act, lambda p0=p0: A.activation(out=s0[:, :], in_=p0[:, :], func=ACT.Sigmoid), reads=[p0], writes=[s0])
                    fw.op(act, lambda p1=p1: A.activation(out=s1[:, :], in_=p1[:, :], func=ACT.Sigmoid), reads=[p1], writes=[s1])
                    fw.op(dve, lambda pa=pa: V.tensor_tensor(out=y1[:, :], in0=s0[:, :], in1=pa[:, :], op=ALU.mult), reads=[s0, pa], writes=[y1])
                    fw.op(dve, lambda pb_=pb_: V.tensor_tensor(out=y2[:, :], in0=s1[:, :], in1=pb_[:, :], op=ALU.mult), reads=[s1, pb_], writes=[y2])
                    fw.op(dve, lambda dc=dc: V.tensor_tensor(out=yT[dc][:, :], in0=y1[:, :], in1=y2[:, :], op=ALU.add), reads=[y1, y2], writes=[yT[dc]])
            wo_v = W["w_o"].rearrange("(kc p) n -> p kc n", p=P)
            for dp in range(4):
                slo_, wov = wload(wo_v[:, :, dp * 256:(dp + 1) * 256], KC, 256)
                for j in range(2):
                    dc = 2 * dp + j
                    po = nps()
                    mm(po, po[:, :], [(wov[:, kc, j * P:(j + 1) * P], yT[kc][:, :]) for kc in range(KC)], [slo_] + yT)
                    fw.op(dve, lambda dc=dc, po=po: V.scalar_tensor_tensor(out=hb[dc][:, :], in0=po[:, :], scalar=Gcol[:, 1, b, dc:dc + 1], in1=hb[dc][:, :],
                                                                           op0=ALU.mult, op1=ALU.add), reads=[po, Gcol, hb[dc]], writes=[hb[dc]])
            fw.barrier()

    xq = {}

    def load_dma(gi, tt):
        b, g = gi // NG, gi % NG
        r0 = b * S + g * TG
        xb_ = nio()
        fw.dma(sp, xb_[:, :], x[r0 + tt * P:r0 + (tt + 1) * P, :], writes=[xb_], key=xb_)
        xq[(gi, tt)] = xb_

    def load_tile(gi, tt):
        if (gi, tt) not in xq:
            load_dma(gi, tt)
        xb_ = xq.pop((gi, tt))
        for half in range(2):
            pt_ = nps()
            for i in range(4):
                kc = half * 4 + i
                fw.op(pe, lambda pt_=pt_, xb_=xb_, kc=kc, i=i: T.transpose(pt_[:, i * P:(i + 1) * P], xb_[:, kc * P:(kc + 1) * P], ident), reads=[xb_, cb], writes=[pt_])
            if half == 0:
                fw.op(act, lambda pt_=pt_, half=half, tt=tt: A.copy(out=h_full[:, half * 4:(half + 1) * 4, tt * P:(tt + 1) * P], in_=pt_[:, :].rearrange("p (a t) -> p a t", a=4)),
                      reads=[pt_], writes=hb[half * 4:(half + 1) * 4])
            else:
                fw.op(dve, lambda pt_=pt_, half=half, tt=tt: V.tensor_copy(out=h_full[:, half * 4:(half + 1) * 4, tt * P:(tt + 1) * P], in_=pt_[:, :].rearrange("p (a t) -> p a t", a=4)),
                      reads=[pt_], writes=hb[half * 4:(half + 1) * 4])

    def load_group(gi):
        for tt in range(NTT):
            load_tile(gi, tt)

    load_group(0)
    cTb = fw.buf("cTb", [P, NSEQ, KC], BF16)
    fw.op(act, lambda: A.activation(out=cTb[:, :, :], in_=cT[:, :, :], func=ACT.Silu), reads=[cT], writes=[cTb])
    modT = fw.buf("modT", [P, NSEQ, 72], F32)
    pm = nps()
    w_ada_v = W["w_ada"].rearrange("(kc p) n -> p kc n", p=P)
    for blk in range(36):
        sl, wv = wload(w_ada_v[:, :, blk * 256:(blk + 1) * 256], KC, 256)
        for j in range(2):
            jj = 2 * blk + j
            mm(pm, pm[:, jj * 2:(jj + 1) * 2], [(wv[:, kc, j * 128:(j + 1) * 128], cTb[:, :, kc]) for kc in range(KC)], [sl, cTb])
    for b in range(NSEQ):
        fw.op(dve, lambda b=b: V.tensor_tensor(out=modT[:, b, :], in0=pm[:, 0:144].rearrange("p (j b) -> p j b", b=2)[:, :, b],
                                               in1=badaT[:, :], op=ALU.add), reads=[pm, badaT], writes=[modT])
    Acol = fw.buf("Acol", [P, 3, NSEQ, KC], F32)
    Gcol = fw.buf("Gcol", [P, 3, NSEQ, KC], F32)
    gains = [nT["norm_ff1"], nT["norm_mix"], nT["norm_ff2"]]
    for k in range(3):
        for b in range(NSEQ):
            fw.op(dve, lambda k=k, b=b: V.scalar_tensor_tensor(out=Acol[:, k, b, :], in0=modT[:, b, (3 * k + 1) * 8:(3 * k + 2) * 8], scalar=1.0,
                                                               in1=gains[k][:, :], op0=ALU.add, op1=ALU.mult), reads=[modT, gains[k]], writes=[Acol])
            fw.op(dve, lambda k=k, b=b: V.tensor_scalar(out=Gcol[:, k, b, :], in0=modT[:, b, (3 * k + 2) * 8:(3 * k + 3) * 8],
                                                        scalar1=(1.0 if k == 1 else 0.5), scalar2=None, op0=ALU.mult), reads=[modT], writes=[Gcol])

    def Bcol(k, b, kc):
        return modT[:, b, 3 * k * 8 + kc:3 * k * 8 + kc + 1]


    out_bufs = []
    for gi in range(ngroups):
        b, g = gi // NG, gi % NG
        r0 = b * S + g * TG
        with contextlib.ExitStack() as es:
            ffn(0, b, W["ff1_w_gate"], W["ff1_w_up"], W["ff1_w_down"], es,
                after_norm=(lambda: rope_tables(b, g)) if do_mixer else None)
            fw.barrier()
        if gi == 0:
            dump("h1", h_full[:, :, :], hb[0])
        if do_mixer:
            mixer(b, g)
            if gi == 0:
                dump("h2", h_full[:, :, :], hb[0])
                if dbg and "att" in dbg_out:
                    for h_ in range(NH):
                        fw.dma(pool, dbg_out["att"][h_], attnT[h_][:, :], reads=[attnT[h_]], key=attnT[h_])
                    for h_ in range(4):
                        fw.dma(pool, dbg_out["ml"][h_], hnT[h_][:, :], reads=[hnT[h_]], key=hnT[h_])
        if do_ff2:
            with contextlib.ExitStack() as es:
                ffn(2, b, W["ff2_w_gate"], W["ff2_w_up"], W["ff2_w_down"], es)
                fw.barrier()
        rstd_from([(hb[kc], hb[kc][:, :]) for kc in range(KC)], 1024, sbuf_src=True)
        with contextlib.ExitStack() as es:
            onb, on_full = fw.bufs("onb", [P, TG], F32, KC, es=es)
            for kc in range(KC):
                fw.op(dve, lambda kc=kc: V.scalar_tensor_tensor(out=onb[kc][:, :], in0=hb[kc][:, :], scalar=nT["norm_final"][:, kc:kc + 1], in1=rstd[:, :],
                                                                op0=ALU.mult, op1=ALU.mult), reads=[hb[kc], nT["norm_final"], rstd], writes=[onb[kc]])
            for tt in range(NTT):
                if gi + 1 < ngroups:
                    if tt == 0:
                        load_dma(gi + 1, 0)
                    if tt + 1 < NTT:
                        load_dma(gi + 1, tt + 1)
                    load_tile(gi + 1, tt)
                ob_ = nio()
                for half in range(2):
                    pt_ = nps()
                    for i in range(4):
                        kc = half * 4 + i
                        fw.op(pe, lambda pt_=pt_, kc=kc, i=i, tt=tt: T.transpose(pt_[:, i * P:(i + 1) * P], onb[kc][:, tt * P:(tt + 1) * P], ident), reads=[onb[kc], cb], writes=[pt_])
                    if half == 0:
                        fw.op(act, lambda pt_=pt_, ob_=ob_, half=half: A.copy(out=ob_[:, half * 512:(half + 1) * 512], in_=pt_[:, :]), reads=[pt_], writes=[ob_])
                    else:
                        fw.op(dve, lambda pt_=pt_, ob_=ob_, half=half: V.tensor_copy(out=ob_[:, half * 512:(half + 1) * 512], in_=pt_[:, :]), reads=[pt_], writes=[ob_])
                fw.dma(sp, y[r0 + tt * P:r0 + (tt + 1) * P, :], ob_[:, :], reads=[ob_], key=ob_)
                out_bufs.append(ob_)
            fw.barrier()
    deps = []
    for ob_ in io:
        if ob_.dsem is not None:
            deps.append(('d', ob_.dsem, ob_.dval, id(ob_)))
    fw._wait(sp, deps)
    if dbg:
        for bb in [hb[0]]:
            if bb.dsem is not None:
                fw._wait(sp, [('d', bb.dsem, bb.dval, id(bb))])
    fw.close()
    return nc


def prep_inputs(inputs):
    w = {}
    for n, s in WNAMES:
        if n == "w_q_bp":
            continue
        a = np.asarray(inputs[n], dtype=np.float32)
        if n != "norm_final":
            a = a[0]
        w[n] = np.ascontiguousarray(a)
    wq = w["w_q_b"]
    perm = np.arange(768)
    for h in range(NH):
        base = h * 96 + 64
        perm[base:base + 16] = np.arange(base + 16, base + 32)
        perm[base + 16:base + 32] = np.arange(base, base + 16)
    w["w_q_bp"] = np.ascontiguousarray(wq[:, perm])
    return w


_CACHE = {}


def kernel(**inputs):
    x = np.asarray(inputs["x"], dtype=np.float32)
    c = np.asarray(inputs["c"], dtype=np.float32)
    pos = np.asarray(inputs["positions"], dtype=np.int32)
    w = prep_inputs(inputs)
    cst = host_consts()
    if "nc" not in _CACHE:
        _CACHE["nc"] = build_program()
    nc = _CACHE["nc"]
    in_maps = []
    for i in range(8):
        m = {"x": np.ascontiguousarray(x[2 * i:2 * i + 2].reshape(NSEQ * S, D)),
             "c": np.ascontiguousarray(c[2 * i:2 * i + 2]),
             "pos": np.ascontiguousarray(pos[2 * i:2 * i + 2]),
             "cst": cst}
        m.update(w)
        in_maps.append(m)
    res = run_bass_kernel_spmd(nc, in_maps, core_ids=list(range(8)))
    out = np.concatenate([r["y"].reshape(NSEQ, S, D) for r in res.results], axis=0)
    return out.astype(np.float32)
```

```python
import contextlib
import numpy as np
import concourse.bass as bass
import concourse.mybir as mybir
from concourse.bass_utils import run_bass_kernel_spmd

F32 = mybir.dt.float32
BF16 = mybir.dt.bfloat16
I32 = mybir.dt.int32
ACT = mybir.ActivationFunctionType
ALU = mybir.AluOpType

P = 128
D = 1024
KC = 8
DFF = 2816
FC = 22
TG = 512
NTT = 4
S = 2048
NG = S // TG
NSEQ = 2
NH = 8
DIN = 4264
EPS = 1e-6
SEM_CAP = 24000
C1 = 6.28125
C2 = float(2 * np.pi - 6.28125)


class Eng:
    def __init__(self, fw, name, h, nsem):
        self.name = name
        self.h = h
        self.sems = [fw.es.enter_context(fw.nc.semaphore(f"s_{name}{i}")) for i in range(nsem)]
        self.n = 0
        self.seen = {}
        self.seen_d = {}

    def sem_val(self, seq):
        i = (seq - 1) // SEM_CAP
        return self.sems[i], (seq - 1) % SEM_CAP + 1


class Buf:
    def __init__(self, ap, name=""):
        self.ap = ap
        self.name = name
        self.w = None
        self.r = []
        self.dsem = None
        self.dval = 0

    def __getitem__(self, k):
        return self.ap[k]


class GroupBuf:
    def __init__(self, members, ap):
        self.members = members
        self.ap = ap
        self.name = members[0].name

    def __getitem__(self, k):
        return self.ap[k]


def _flat(bs):
    out = []
    for b in bs:
        out.extend(getattr(b, "members", None) or [b])
    return out


class FW:
    def __init__(self, nc):
        self.nc = nc
        self.es = contextlib.ExitStack()
        self.es.__enter__()
        self.pe = Eng(self, "pe", nc.tensor, 3)
        self.act = Eng(self, "act", nc.scalar, 3)
        self.dve = Eng(self, "dve", nc.vector, 4)
        self.pool = Eng(self, "pool", nc.gpsimd, 1)
        self.sp = Eng(self, "sp", nc.sync, 1)
        self.engs = [self.pe, self.act, self.dve, self.pool, self.sp]

    def close(self):
        self.es.__exit__(None, None, None)

    def buf(self, name, shape, dt, es=None):
        self.uid = getattr(self, "uid", 0) + 1
        name = f"{name}_{self.uid}"
        t = (es or self.es).enter_context(self.nc.sbuf_tensor(name, list(shape), dt))
        return Buf(t[tuple(slice(None) for _ in shape)], name)

    def bufs(self, name, shape, dt, n, es=None):
        self.uid = getattr(self, "uid", 0) + 1
        name = f"{name}_{self.uid}_"
        t = (es or self.es).enter_context(self.nc.sbuf_tensor(name, [shape[0], n] + list(shape[1:]), dt))
        full = t[tuple(slice(None) for _ in range(len(shape) + 1))]
        out = []
        for i in range(n):
            idx = (slice(None), i) + tuple(slice(None) for _ in shape[1:])
            out.append(Buf(t[idx], f"{name}{i}"))
        return out, full

    def _wait(self, E, deps):
        for d in deps:
            if d[0] == 'e':
                _, F, seq = d
                if F is E and E is self.pe:
                    continue
                if E.seen.get(F.name, 0) >= seq:
                    continue
                s, v = F.sem_val(seq)
                E.h.wait_ge(s, v)
                E.seen[F.name] = seq
            else:
                _, sem, val, key = d
                if E.seen_d.get(key, 0) >= val:
                    continue
                E.h.wait_ge(sem, val)
                E.seen_d[key] = val

    @staticmethod
    def _deps(reads, writes):
        reads, writes = _flat(reads), _flat(writes)
        deps = []
        for b in reads:
            if b.w is not None:
                deps.append(b.w)
        for b in writes:
            if b.w is not None:
                deps.append(b.w)
            deps.extend(b.r)
        return deps

    @staticmethod
    def _mark(tag, reads, writes):
        reads, writes = _flat(reads), _flat(writes)
        for b in reads:
            if tag[0] == 'e':
                b.r = [t for t in b.r if not (t[0] == 'e' and t[1] is tag[1])]
            b.r.append(tag)
        for b in writes:
            b.w = tag
            b.r = []

    def op(self, E, fns, reads=(), writes=()):
        if callable(fns):
            fns = [fns]
        if E is self.pool and getattr(self, "bar", None):
            self._wait(E, self.bar)
        self._wait(E, self._deps(reads, writes))
        ins = None
        for f in fns:
            ins = f()
        seq = E.n + 1
        E.n = seq
        s, _ = E.sem_val(seq)
        ins.then_inc(s, 1)
        self._mark(('e', E, seq), reads, writes)

    def dma(self, Q, out, in_, reads=(), writes=(), key=None, **kw):
        key = (getattr(key, "members", None) or [key])[0]
        if key.dsem is None:
            key.dsem = self.es.enter_context(self.nc.semaphore(f"d_{key.name}"))
        self._wait(Q, self._deps(reads, writes))
        key.dval += 16
        Q.h.dma_start(out=out, in_=in_, **kw).then_inc(key.dsem, 16)
        self._mark(('d', key.dsem, key.dval, id(key)), reads, writes)

    def barrier(self):
        ce = [self.pe, self.act, self.dve]
        for E in ce:
            self._wait(E, [('e', F, F.n) for F in ce + [self.pool] if F is not E and F.n > 0])
        self.bar = [('e', F, F.n) for F in ce if F.n > 0]


def host_consts():
    c = np.zeros((P, 6 * P + 8), np.float32)
    idx = np.arange(P)
    c[:, 0:P] = np.eye(P, dtype=np.float32)
    s_, t_ = idx[:, None], idx[None, :]
    c[:, P:2 * P] = (t_ >= s_).astype(np.float32)
    c[:, 2 * P:3 * P] = ((t_ >= s_) & (s_ // 64 == t_ // 64)).astype(np.float32)
    c[:, 3 * P:4 * P] = (s_ // 64 == t_ // 64).astype(np.float32)
    c[:, 4 * P:5 * P] = (s_ < 64).astype(np.float32) * np.ones((1, P), np.float32)
    c[:, 5 * P:6 * P] = (s_ >= 64).astype(np.float32) * np.ones((1, P), np.float32)
    inv = (np.float32(10000.0) ** (-np.arange(0, 32, 2, dtype=np.float32) / np.float32(32))).astype(np.float32)
    o = 6 * P
    c[64:80, o] = -inv
    c[80:96, o] = inv
    c[64:80, o + 1] = inv
    c[80:96, o + 1] = inv
    c[:, o + 2] = 1024 * EPS
    c[:, o + 3] = 384 * EPS
    c[:, o + 4] = 256 * EPS
    c[:, o + 5] = 128 * EPS
    c[:, o + 6] = 1.0
    c[:, o + 7] = np.log(128.0 ** -0.5)
    return c


WNAMES = [("w_ada", [D, 9 * D]), ("b_ada", [9 * D]), ("norm_ff1", [D]), ("ff1_w_gate", [D, DFF]),
          ("ff1_w_up", [D, DFF]), ("ff1_w_down", [DFF, D]), ("norm_mix", [D]), ("w_in", [D, DIN]),
          ("q_a_norm", [384]), ("w_q_b", [384, 768]), ("w_q_bp", [384, 768]), ("kv_a_norm", [256]),
          ("w_kv_b", [256, 1024]), ("conv_w", [4, 512]), ("conv_b", [512]), ("w_q_m", [4, 128, 128]),
          ("w_k_m", [4, 128, 128]), ("b_i", [4]), ("b_f", [4]), ("mlstm_norm", [4, 128]),
          ("w_mla_out", [512, D]), ("w_mlstm_out", [512, D]), ("w_o", [D, D]), ("norm_ff2", [D]),
          ("ff2_w_gate", [D, DFF]), ("ff2_w_up", [D, DFF]), ("ff2_w_down", [DFF, D]), ("norm_final", [D])]


def build_program(do_mixer=True, do_ff2=True, dbg=None, ngroups=NSEQ * NG):
    nc = bass.Bass("TRN2", target_bir_lowering=False)
    x = nc.dram_tensor("x", [NSEQ * S, D], F32, kind="ExternalInput").ap()
    c_in = nc.dram_tensor("c", [NSEQ, D], F32, kind="ExternalInput").ap()
    pos = nc.dram_tensor("pos", [NSEQ, S], I32, kind="ExternalInput").ap()
    cst = nc.dram_tensor("cst", [P, 6 * P + 8], F32, kind="ExternalInput").ap()
    W = {n: nc.dram_tensor(n, s, F32, kind="ExternalInput").ap() for n, s in WNAMES}
    y = nc.dram_tensor("y", [NSEQ * S, D], F32, kind="ExternalOutput").ap()
    dbg_out = {}
    if dbg:
        for n, s in dbg.items():
            dbg_out[n] = nc.dram_tensor(n, s, F32, kind="ExternalOutput").ap()

    fw = FW(nc)
    pe, act, dve, pool, sp = fw.pe, fw.act, fw.dve, fw.pool, fw.sp
    V = nc.vector
    A = nc.scalar
    T = nc.tensor

    hb, h_full = fw.bufs("h", [P, TG], F32, KC)
    uT, _ = fw.bufs("uT", [P, TG], BF16, KC)
    NSLOT = 8
    SLOT = 2048
    ring_t = fw.es.enter_context(nc.sbuf_tensor("ring_t", [P, NSLOT * SLOT], BF16))
    ring = [Buf(ring_t[:, k * SLOT:(k + 1) * SLOT], f"ring{k}") for k in range(NSLOT)]
    ring_i = [0]
    io = [fw.buf(f"io{i}", [P, D], F32) for i in range(3)]
    io_i = [0]
    cb = fw.buf("cst", [P, 6 * P + 8], F32)
    ident = cb[:, 0:P]
    tri_f = cb[:, P:2 * P]
    Mbd = cb[:, 2 * P:3 * P]
    BDm = cb[:, 3 * P:4 * P]
    SEL = [cb[:, 4 * P:5 * P], cb[:, 5 * P:6 * P]]
    o_ = 6 * P
    invfS = cb[:, o_:o_ + 1]
    invfC = cb[:, o_ + 1:o_ + 2]
    epsc = {1024: cb[:, o_ + 2:o_ + 3], 384: cb[:, o_ + 3:o_ + 4], 256: cb[:, o_ + 4:o_ + 5], 128: cb[:, o_ + 5:o_ + 6]}
    onec = cb[:, o_ + 6:o_ + 7]
    lnsc = cb[:, o_ + 7:o_ + 8]
    ones_b = fw.buf("ones_b", [P, P], BF16)
    rstd = fw.buf("rstd", [P, TG], F32)
    tmpA = [fw.buf(f"tmpA{i}", [P, TG], F32) for i in range(2)]
    tmpA_i = [0]
    sqb = [fw.buf(f"sqb{i}", [P, TG], BF16) for i in range(3)]
    sq_i = [0]
    psb = [Buf(fw.es.enter_context(nc.psum_tensor(f"ps{i}", [P, TG], F32))[:, :], f"ps{i}") for i in range(8)]
    ps_i = [0]

    pinned = set()

    def nps(pin=False):
        while True:
            b = psb[ps_i[0] % 8]
            ps_i[0] += 1
            if id(b) not in pinned:
                break
        if pin:
            pinned.add(id(b))
        return b

    def unpin(b):
        pinned.discard(id(b))

    def nslot(n_elems=0):
        if n_elems > SLOT:
            if ring_i[0] % 2 == 1:
                ring_i[0] += 1
            k = ring_i[0] % NSLOT
            ring_i[0] += 2
            return GroupBuf([ring[k], ring[k + 1]], ring_t[:, k * SLOT:(k + 2) * SLOT])
        b = ring[ring_i[0] % NSLOT]
        ring_i[0] += 1
        return b

    def nio():
        b = io[io_i[0] % 3]
        io_i[0] += 1
        return b

    def ntmp():
        b = tmpA[tmpA_i[0] % 2]
        tmpA_i[0] += 1
        return b

    def nsq():
        b = sqb[sq_i[0] % 3]
        sq_i[0] += 1
        return b

    def wload(src, a, b_=None):
        n = a * (b_ or 1)
        sl = nslot(n)
        if b_ is None:
            view = sl[:, 0:n]
        else:
            view = sl[:, 0:n].rearrange("p (a b) -> p a b", a=a)
        fw.dma(pool, view, src, writes=[sl], key=sl)
        return sl, view

    def mm(outb, out_ap, terms, reads, start=True, stop=True):
        n = len(terms)
        fns = []
        for i, (l, r) in enumerate(terms):
            fns.append(lambda l=l, r=r, i=i: T.matmul(out_ap, l, r, start=(start and i == 0), stop=(stop and i == n - 1), skip_group_check=True))
        fw.op(pe, fns, reads=reads, writes=[outb])

    def dump(name, ap, b):
        if dbg and name in dbg_out:
            fw.dma(sp, dbg_out[name], ap, reads=[b], key=b)

    fw.dma(sp, cb[:, :], cst[:, :], writes=[cb], key=cb)
    fw.op(dve, lambda: V.memset(ones_b[:, :], 1.0), writes=[ones_b])

    vst = fw.buf("vst", [P, 2, P], F32)
    fw.op(dve, lambda: V.memset(vst[:, :, :], 0.0), writes=[vst])
    rowsA = [("b_ada", 72), ("norm_ff1", 8), ("norm_mix", 8), ("norm_ff2", 8), ("norm_final", 8), ("q_a_norm", 3), ("kv_a_norm", 2)]
    r_ = 0
    for n_, k_ in rowsA:
        fw.dma(sp, vst[r_:r_ + k_, 0, :], W[n_].rearrange("(j p) -> j p", p=P), writes=[vst], key=vst)
        r_ += k_
    NA = r_
    fw.dma(sp, vst[0:16, 1, :], W["conv_w"].rearrange("j (hh p) -> (j hh) p", p=P), writes=[vst], key=vst)
    fw.dma(sp, vst[16:20, 1, :], W["conv_b"].rearrange("(hh p) -> hh p", p=P), writes=[vst], key=vst)
    fw.dma(sp, vst[20:24, 1, :], W["mlstm_norm"], writes=[vst], key=vst)
    fw.dma(sp, vst[24:40, 1, :], c_in.rearrange("b (kc p) -> (b kc) p", p=P), writes=[vst], key=vst)
    NB_ = 40
    vT = fw.buf("vT", [P, NA + NB_], F32)
    pvt = nps()
    fw.op(pe, lambda: T.transpose(pvt[:, 0:NA], vst[0:NA, 0, :], ident[0:NA, 0:NA]), reads=[vst, cb], writes=[pvt])
    fw.op(pe, lambda: T.transpose(pvt[:, NA:NA + NB_], vst[0:NB_, 1, :], ident[0:NB_, 0:NB_]), reads=[vst, cb], writes=[pvt])
    fw.op(act, lambda: A.copy(out=vT[:, :], in_=pvt[:, 0:NA + NB_]), reads=[pvt], writes=[vT])

    class SubBuf:
        def __init__(self, parent, ap):
            self.__dict__["parent"] = parent
            self.__dict__["ap"] = ap

        def __getattr__(self, k):
            return getattr(self.__dict__["parent"], k)

        def __setattr__(self, k, v):
            setattr(self.__dict__["parent"], k, v)

        def __getitem__(self, k):
            return self.__dict__["ap"][k]

    badaT = SubBuf(vT, vT[:, 0:72])
    nT = {k: SubBuf(vT, vT[:, 72 + 8 * i:80 + 8 * i]) for i, k in enumerate(["norm_ff1", "norm_mix", "norm_ff2", "norm_final"])}
    qnT = SubBuf(vT, vT[:, 104:107])
    kvnT = SubBuf(vT, vT[:, 107:109])
    convwT = SubBuf(vT, vT[:, NA:NA + 16].rearrange("p (j h) -> p j h", j=4))
    convbT = SubBuf(vT, vT[:, NA + 16:NA + 20])
    mnT = SubBuf(vT, vT[:, NA + 20:NA + 24])
    cT = SubBuf(vT, vT[:, NA + 24:NA + 40].rearrange("p (b k) -> p b k", b=NSEQ))
    bi_b = fw.buf("bi_b", [P, 4], F32)
    fw.dma(sp, bi_b[:, :], W["b_i"].rearrange("(o n) -> o n", o=1).partition_broadcast(P), writes=[bi_b], key=bi_b)
    bf_b = fw.buf("bf_b", [P, 4], F32)
    fw.dma(sp, bf_b[:, :], W["b_f"].rearrange("(o n) -> o n", o=1).partition_broadcast(P), writes=[bf_b], key=bf_b)
    for b_, sc in [(nT["norm_ff1"], 32.0), (nT["norm_mix"], 32.0), (nT["norm_ff2"], 32.0), (nT["norm_final"], 32.0),
                   (qnT, float(np.sqrt(384.0))), (kvnT, 16.0), (mnT, float(np.sqrt(128.0)))]:
        fw.op(dve, lambda b_=b_, sc=sc: V.tensor_scalar(out=b_.ap, in0=b_.ap, scalar1=sc, scalar2=None, op0=ALU.mult), reads=[b_], writes=[b_])
    kpeW = fw.buf("kpeW", [P, KC, 96], BF16)
    kpeWp = fw.buf("kpeWp", [P, KC, 96], BF16)
    wif = fw.buf("wif", [P, KC, 8], BF16)
    w_in_v = W["w_in"].rearrange("(kc p) n -> p kc n", p=P)
    if do_mixer:
        fw.op(dve, lambda: V.memset(kpeW[:, :, :], 0.0), writes=[kpeW])
        fw.op(dve, lambda: V.memset(kpeWp[:, :, :], 0.0), writes=[kpeWp])
        fw.dma(pool, kpeW[:, :, 64:96], w_in_v[:, :, 640:672], writes=[kpeW], key=kpeW)
        fw.dma(pool, kpeWp[:, :, 64:80], w_in_v[:, :, 656:672], writes=[kpeWp], key=kpeWp)
        fw.dma(pool, kpeWp[:, :, 80:96], w_in_v[:, :, 640:656], writes=[kpeWp], key=kpeWp)
        fw.dma(pool, wif[:, :, :], w_in_v[:, :, 2208:2216], writes=[wif], key=wif)

    def rstd_from(chunks, n, sbuf_src=False):
        pss = nps()
        nch = len(chunks)
        for i, (b_, ap) in enumerate(chunks):
            sq = nsq()
            if sbuf_src and i % 2 == 1:
                fw.op(pool, lambda ap=ap, sq=sq: nc.gpsimd.tensor_tensor(out=sq[:, :], in0=ap, in1=ap, op=ALU.mult), reads=[b_], writes=[sq])
            else:
                fw.op(act, lambda ap=ap, sq=sq: A.activation(out=sq[:, :], in_=ap, func=ACT.Square), reads=[b_], writes=[sq])
            mm(pss, pss[:, :], [(ones_b[:, :], sq[:, :])], [ones_b, sq], start=(i == 0), stop=(i == nch - 1))
        fw.op(act, lambda: A.activation(out=rstd[:, :], in_=pss[:, :], func=ACT.Ln, bias=epsc[n], scale=1.0), reads=[pss, cb], writes=[rstd])
        fw.op(act, lambda: A.activation(out=rstd[:, :], in_=rstd[:, :], func=ACT.Exp, scale=-0.5), reads=[rstd], writes=[rstd])

    def norm_mod(k, b):
        rstd_from([(hb[kc], hb[kc][:, :]) for kc in range(KC)], 1024, sbuf_src=True)
        for kc in range(KC):
            t = ntmp()
            fw.op(dve, lambda kc=kc, t=t: V.scalar_tensor_tensor(out=t[:, :], in0=hb[kc][:, :], scalar=Acol[:, k, b, kc:kc + 1], in1=rstd[:, :],
                                                                 op0=ALU.mult, op1=ALU.mult), reads=[hb[kc], Acol, rstd], writes=[t])
            fw.op(act, lambda kc=kc, t=t: A.activation(out=uT[kc][:, :], in_=t[:, :], func=ACT.Identity, bias=Bcol(k, b, kc), scale=1.0),
                  reads=[t, modT], writes=[uT[kc]])

    def ffn(k, b, wg, wu, wd, es, after_norm=None):
        hT, _ = fw.bufs("hT", [P, TG], BF16, FC, es=es)
        sg = [fw.buf(f"sg{i}", [P, TG], F32, es=es) for i in range(2)]
        norm_mod(k, b)
        if after_norm is not None:
            after_norm()
        wg_v = wg.rearrange("(kc p) n -> p kc n", p=P)
        wu_v = wu.rearrange("(kc p) n -> p kc n", p=P)
        wd_v = wd.rearrange("(fc p) n -> p fc n", p=P)
        for fb in range(FC // 2):
            sg_, gv = wload(wg_v[:, :, fb * 256:(fb + 1) * 256], KC, 256)
            su_, uv = wload(wu_v[:, :, fb * 256:(fb + 1) * 256], KC, 256)
            for j in range(2):
                f = 2 * fb + j
                pg = nps()
                pu = nps()
                mm(pg, pg[:, :], [(gv[:, kc, j * 128:(j + 1) * 128], uT[kc][:, :]) for kc in range(KC)], [sg_] + uT)
                mm(pu, pu[:, :], [(uv[:, kc, j * 128:(j + 1) * 128], uT[kc][:, :]) for kc in range(KC)], [su_] + uT)
                s_ = sg[f % 2]
                fw.op(act, lambda pg=pg, s_=s_: A.activation(out=s_[:, :], in_=pg[:, :], func=ACT.Silu), reads=[pg], writes=[s_])
                fw.op(dve, lambda pu=pu, s_=s_, f=f: V.tensor_tensor(out=hT[f][:, :], in0=s_[:, :], in1=pu[:, :], op=ALU.mult), reads=[s_, pu], writes=[hT[f]])
        for half in range(2):
            accs = [nps(pin=True) for _ in range(4)]
            f0 = 0
            while f0 < FC:
                nf = min(4, FC - f0)
                sd_, dv = wload(wd_v[:, f0:f0 + nf, half * 512:(half + 1) * 512], nf, 512)
                for fi in range(nf):
                    f = f0 + fi
                    for i in range(4):
                        mm(accs[i], accs[i][:, :], [(dv[:, fi, i * 128:(i + 1) * 128], hT[f][:, :])], [sd_, hT[f]], start=(f == 0), stop=(f == FC - 1))
                f0 += nf
            for i in range(4):
                dc = half * 4 + i
                po = accs[i]
                fw.op(dve, lambda dc=dc, po=po: V.scalar_tensor_tensor(out=hb[dc][:, :], in0=po[:, :], scalar=Gcol[:, k, b, dc:dc + 1], in1=hb[dc][:, :],
                                                                       op0=ALU.mult, op1=ALU.add), reads=[po, Gcol, hb[dc]], writes=[hb[dc]])
                unpin(po)

    if do_mixer:
        kT = [[fw.buf(f"kT{h}_{g}", [P, TG], BF16) for g in range(NG)] for h in range(NH)]
        Vt = [fw.buf(f"V{t}", [P, NH * 65 + 63], BF16) for t in range(S // P)]
        for t in range(S // P):
            fw.op(dve, lambda t=t: V.memset(Vt[t][:, :], 1.0), writes=[Vt[t]])
        for h_ in range(NH):
            for g_ in range(NG):
                fw.op(dve, lambda h_=h_, g_=g_: V.memset(kT[h_][g_][:, :], 0.0), writes=[kT[h_][g_]])
        Sf = [fw.buf(f"Sf{hh}", [P, 256], F32) for hh in range(4)]
        Sbf = [[fw.buf(f"Sbf{hh}_{i}", [P, 256], BF16) for i in range(2)] for hh in range(4)]
        sbf_i = [0, 0, 0, 0]
        xtail = fw.buf("xtail", [P, 4, 3], F32)
        posb = fw.buf("posb", [96, TG], I32)
        Ct = fw.buf("Ct", [96, TG], F32)
        St = fw.buf("St", [96, TG], F32)
        tg = (fw.buf("ang", [96, TG], F32), fw.buf("ki", [96, TG], I32), fw.buf("kf", [96, TG], F32), fw.buf("mm_", [96, TG], F32))
        attnT = [fw.buf(f"attnT{h}", [64, TG], BF16) for h in range(NH)]
        hnT, hn_full = fw.bufs("hnT", [P, TG], BF16, 4)

    def rope_tables(b, g):
        c0 = g * TG
        fw.dma(sp, posb[:, :], pos[b:b + 1, c0:c0 + TG].partition_broadcast(96), writes=[posb], key=posb)
        rope_table(St, invfS, 0.0, None, tg)
        rope_table(Ct, invfC, float(np.pi / 2), None, tg)

    def rope_table(out_t, invcol, shift, es, tg):
        ang, ki, kf, m = tg
        fw.op(dve, lambda: V.tensor_copy(out=kf[:, :], in_=posb[:, :]), reads=[posb], writes=[kf])
        fw.op(dve, lambda: V.tensor_scalar(out=ang[:, :], in0=kf[:, :], scalar1=invcol[0:96, :], scalar2=shift, op0=ALU.mult, op1=ALU.add), reads=[kf, cb], writes=[ang])
        fw.op(dve, lambda: V.tensor_scalar(out=ki[:, :], in0=ang[:, :], scalar1=float(1.0 / (2 * np.pi)), scalar2=None, op0=ALU.mult), reads=[ang], writes=[ki])
        fw.op(dve, lambda: V.tensor_copy(out=kf[:, :], in_=ki[:, :]), reads=[ki], writes=[kf])
        fw.op(dve, lambda: V.scalar_tensor_tensor(out=ang[:, :], in0=kf[:, :], scalar=-C1, in1=ang[:, :], op0=ALU.mult, op1=ALU.add), reads=[kf, ang], writes=[ang])
        fw.op(dve, lambda: V.scalar_tensor_tensor(out=ang[:, :], in0=kf[:, :], scalar=-C2, in1=ang[:, :], op0=ALU.mult, op1=ALU.add), reads=[kf, ang], writes=[ang])
        fw.op(dve, lambda: V.tensor_scalar(out=m[:, :], in0=ang[:, :], scalar1=float(np.pi), scalar2=-float(2 * np.pi), op0=ALU.is_gt, op1=ALU.mult), reads=[ang], writes=[m])
        fw.op(dve, lambda: V.tensor_tensor(out=ang[:, :], in0=ang[:, :], in1=m[:, :], op=ALU.add), reads=[ang, m], writes=[ang])
        fw.op(dve, lambda: V.tensor_scalar(out=m[:, :], in0=ang[:, :], scalar1=-float(np.pi), scalar2=float(2 * np.pi), op0=ALU.is_lt, op1=ALU.mult), reads=[ang], writes=[m])
        fw.op(dve, lambda: V.tensor_tensor(out=ang[:, :], in0=ang[:, :], in1=m[:, :], op=ALU.add), reads=[ang, m], writes=[ang])
        fw.op(act, lambda: A.activation(out=out_t[:, :], in_=ang[:, :], func=ACT.Sin), reads=[ang], writes=[out_t])

    def mixer(b, g):
        c0 = g * TG
        norm_mod(1, b)
        with contextlib.ExitStack() as es:
            qln, _ = fw.bufs("qln", [P, TG], BF16, 3, es=es)
            ckn, _ = fw.bufs("ckn", [P, TG], BF16, 2, es=es)
            qT = [fw.buf(f"qT{h}", [P, TG], BF16, es=es) for h in range(NH)]
            for h in range(NH):
                fw.op(pool, lambda h=h: nc.gpsimd.memset(qT[h][:, :], 0.0), writes=[qT[h]])
            pT = [fw.buf(f"pT{i}", [P, TG], BF16, es=es) for i in range(3)]
            osb = [fw.buf(f"osb{i}", [64, TG], F32, es=es) for i in range(2)]
            rden = [fw.buf(f"rden{i}", [65, TG], F32, es=es) for i in range(2)]
            rhi = [fw.buf(f"rhi{i}", [65, TG], BF16, es=es) for i in range(2)]
            rlo = [fw.buf(f"rlo{i}", [65, TG], BF16, es=es) for i in range(2)]
            t1b = fw.buf("t1b", [96, TG], F32, es=es)
            t2b = fw.buf("t2b", [96, TG], F32, es=es)
            t1q = [t1b, fw.buf("t1c", [96, TG], F32, es=es)]
            t2q = [t2b, fw.buf("t2c", [96, TG], F32, es=es)]


            pre_w = {}
            pre_w[0] = wload(w_in_v[:, :, 0:384], KC, 384)
            pre_w[384] = wload(w_in_v[:, :, 384:640], KC, 256)
            slkv, wkv = wload(W["w_kv_b"].rearrange("(j p) n -> p j n", p=P), 2, 1024)
            slq, wq = wload(W["w_q_b"].rearrange("(j p) n -> p j n", p=P), 3, 768)
            slqp, wqp = wload(W["w_q_bp"].rearrange("(j p) n -> p j n", p=P), 3, 768)

            def lat(col0, nch, outs, gcol, n):
                sl, wv = pre_w[col0]
                pl = []
                for j in range(nch):
                    pj = nps()
                    mm(pj, pj[:, :], [(wv[:, kc, j * P:(j + 1) * P], uT[kc][:, :]) for kc in range(KC)], [sl] + uT)
                    pl.append(pj)
                rstd_from([(pj, pj[:, :]) for pj in pl], n)
                for j in range(nch):
                    fw.op(dve, lambda j=j: V.scalar_tensor_tensor(out=outs[j][:, :], in0=pl[j][:, :], scalar=gcol[:, j:j + 1], in1=rstd[:, :],
                                                                  op0=ALU.mult, op1=ALU.mult), reads=[pl[j], gcol, rstd], writes=[outs[j]])
            lat(0, 3, qln, qnT, 384)
            lat(384, 2, ckn, kvnT, 256)
            pk = nps()
            pkp = nps()
            mm(pk, pk[0:96, :], [(kpeW[:, kc, :], uT[kc][:, :]) for kc in range(KC)], [kpeW] + uT)
            mm(pkp, pkp[0:96, :], [(kpeWp[:, kc, :], uT[kc][:, :]) for kc in range(KC)], [kpeWp] + uT)
            fw.op(dve, lambda: V.tensor_tensor(out=t1b[64:96, :], in0=pk[64:96, :], in1=Ct[64:96, :], op=ALU.mult), reads=[pk, Ct], writes=[t1b])
            fw.op(dve, lambda: V.tensor_tensor(out=t2b[64:96, :], in0=pkp[64:96, :], in1=St[64:96, :], op=ALU.mult), reads=[pkp, St], writes=[t2b])
            for h in range(NH):
                fw.op(dve, lambda h=h: V.tensor_tensor(out=kT[h][g][64:96, :], in0=t1b[64:96, :], in1=t2b[64:96, :], op=ALU.add), reads=[t1b, t2b], writes=[kT[h][g]])
            for h in range(NH):
                pn = nps()
                mm(pn, pn[0:64, :], [(wkv[:, j, h * 128:h * 128 + 64], ckn[j][:, :]) for j in range(2)], [slkv] + ckn)
                fw.op(act, lambda h=h, pn=pn: A.copy(out=kT[h][g][0:64, :], in_=pn[0:64, :]), reads=[pn], writes=[kT[h][g]])
            for tt in range(NTT):
                pv = nps()
                mm(pv, pv[:, :],
                   [(ckn[j][:, tt * P:(tt + 1) * P], wkv[:, j, :].rearrange("p (h two d) -> p h two d", two=2, d=64)[:, :, 1, :]) for j in range(2)], [slkv] + ckn)
                vt = Vt[g * NTT + tt]
                fw.op(act, lambda pv=pv, vt=vt: A.copy(out=vt[:, 0:NH * 65].rearrange("p (h d) -> p h d", d=65)[:, :, 0:64], in_=pv[:, :].rearrange("p (h d) -> p h d", d=64)), reads=[pv], writes=[vt])
            for h in range(NH):
                pq = nps()
                pqp = nps()
                mm(pq, pq[0:96, :], [(wq[:, j, h * 96:(h + 1) * 96], qln[j][:, :]) for j in range(3)], [slq] + qln)
                mm(pqp, pqp[0:96, :], [(wqp[:, j, h * 96:(h + 1) * 96], qln[j][:, :]) for j in range(3)], [slqp] + qln)
                ta, tb = t1q[h % 2], t2q[h % 2]
                fw.op(dve, lambda pq=pq, ta=ta: V.tensor_tensor(out=ta[:, :], in0=pq[0:96, :], in1=Ct[:, :], op=ALU.mult), reads=[pq, Ct], writes=[ta])
                fw.op(dve, lambda pqp=pqp, tb=tb: V.tensor_tensor(out=tb[:, :], in0=pqp[0:96, :], in1=St[:, :], op=ALU.mult), reads=[pqp, St], writes=[tb])
                fw.op(pool, lambda h=h, ta=ta, tb=tb: nc.gpsimd.tensor_tensor(out=qT[h][0:96, :], in0=ta[:, :], in1=tb[:, :], op=ALU.add), reads=[ta, tb], writes=[qT[h]])
            sc = float(96.0 ** -0.5)
            nkt = 4 * g + 4
            items = [(h, kt) for h in range(NH) for kt in range(nkt)]
            pti = [0]

            def qk(h, kt):
                jd = kt - 4 * g
                q0 = 128 * jd if jd > 0 else 0
                n = TG - q0
                pss = nps()
                kb = kT[h][kt // 4]
                mm(pss, pss[:, 0:n], [(kb[:, (kt % 4) * P:(kt % 4 + 1) * P], qT[h][:, q0:TG])], [kb, qT[h]])
                return pss, q0, n, jd

            def epi1(h, po):
                ob = osb[h % 2]
                rd, rh, rl = rden[h % 2], rhi[h % 2], rlo[h % 2]
                fw.op(dve, lambda: V.tensor_copy(out=ob[:, :], in_=po[0:64, :]), reads=[po], writes=[ob])
                fw.op(act, lambda: A.activation(out=rd[64:65, :], in_=po[64:65, :], func=ACT.Ln), reads=[po], writes=[rd])
                fw.op(act, lambda: A.activation(out=rd[64:65, :], in_=rd[64:65, :], func=ACT.Exp, scale=-1.0), reads=[rd], writes=[rd])
                unpin(po)
                fw.op(dve, lambda: V.tensor_copy(out=rh[64:65, :], in_=rd[64:65, :]), reads=[rd], writes=[rh])
                fw.op(dve, lambda: V.tensor_tensor(out=rl[64:65, :], in0=rd[64:65, :], in1=rh[64:65, :], op=ALU.subtract), reads=[rd, rh], writes=[rl])

            def epi2(h):
                ob = osb[h % 2]
                rh, rl = rhi[h % 2], rlo[h % 2]
                pb = nps()
                mm(pb, pb[0:64, :], [(ones_b[64:65, 0:64], rh[64:65, :]), (ones_b[64:65, 0:64], rl[64:65, :])], [ones_b, rh, rl])
                fw.op(dve, lambda: V.tensor_tensor(out=attnT[h][:, :], in0=ob[:, :], in1=pb[0:64, :], op=ALU.mult), reads=[ob, pb], writes=[attnT[h]])

            AHEAD = 3
            inflight = [qk(*items[i]) for i in range(min(AHEAD, len(items)))]
            po = None
            pending = None
            for idx, (h, kt) in enumerate(items):
                if idx + AHEAD < len(items):
                    inflight.append(qk(*items[idx + AHEAD]))
                cur = inflight.pop(0)
                if kt == 0:
                    po = nps(pin=True)
                pss, q0, n, jd = cur
                pt = pT[pti[0] % 3]
                pti[0] += 1
                fw.op(act, lambda pss=pss, pt=pt, n=n: A.activation(out=pt[:, 0:n], in_=pss[:, 0:n], func=ACT.Exp, scale=sc), reads=[pss], writes=[pt])
                if jd >= 0:
                    fw.op(dve, lambda pt=pt: V.tensor_tensor(out=pt[:, 0:P], in0=pt[:, 0:P], in1=tri_f, op=ALU.mult), reads=[pt, cb], writes=[pt])
                vt = Vt[kt]
                mm(po, po[:, q0:TG], [(vt[:, h * 65:h * 65 + P], pt[:, 0:n])], [vt, pt], start=(kt == 0), stop=(kt == nkt - 1))
                if pending is not None:
                    pending[1] -= 1
                    if pending[1] <= 0:
                        epi2(pending[0])
                        pending = None
                if kt == nkt - 1:
                    if pending is not None:
                        epi2(pending[0])
                    epi1(h, po)
                    pending = [h, 3]
            if pending is not None:
                epi2(pending[0])
            fw.barrier()
        with contextlib.ExitStack() as es:
            xbuf = [fw.buf(f"xbuf{i}", [P, TG + 3], F32, es=es) for i in range(2)]
            acc = [fw.buf(f"acc{i}", [P, TG], F32, es=es) for i in range(2)]
            xcT = [fw.buf(f"xcT{hh}", [P, TG], BF16, es=es) for hh in range(4)]
            og = fw.buf("og", [P, 4, TG], BF16, es=es)
            gsm = {n: fw.buf(f"g_{n}", [P, NTT, 4], F32, es=es) for n in ["logi", "logf", "a", "ea", "ks", "t"]}
            eg = fw.buf("eg", [P, NTT, 8], F32, es=es)
            rep2 = [fw.buf("rep", [P, 4, P], F32, es=es)] * 2
            Eb2 = [fw.buf("Eb", [P, 4, P], F32, es=es)] * 2
            qpT2 = [fw.buf(f"qpT{i}", [P, 4, P], BF16, es=es) for i in range(2)]
            kTm2 = [fw.buf("kTm", [P, 4, P], BF16, es=es)] * 2
            kpp2 = [fw.buf(f"kpp{i}", [P, 4, P], BF16, es=es) for i in range(2)]
            vm2 = [fw.buf(f"vm{i}", [P, TG], BF16, es=es) for i in range(2)]
            smT2 = [fw.buf(f"smT{i}", [P, 4, P], BF16, es=es) for i in range(2)]
            dd = fw.buf("dd", [P, 4, P], F32, es=es)
            hg = fw.buf("hg", [P, 4, P], F32, es=es)
            sq4 = fw.buf("sq4", [P, 4, P], BF16, es=es)
            rs4 = fw.buf("rs4", [P, 4, P], F32, es=es)

            pgi = nps()
            for tt in range(NTT):
                mm(pgi, pgi[:, tt * 8:(tt + 1) * 8], [(uT[kc][:, tt * P:(tt + 1) * P], wif[:, kc, :]) for kc in range(KC)], [wif] + uT)
            gview = pgi[:, 0:32].rearrange("p (t c) -> p t c", c=8)
            fw.op(dve, lambda: V.tensor_tensor(out=gsm["logi"][:, :, :], in0=gview[:, :, 0:4], in1=bi_b[:, :].unsqueeze(1).broadcast_to([P, NTT, 4]), op=ALU.add),
                  reads=[pgi, bi_b], writes=[gsm["logi"]])
            fw.op(dve, lambda: V.tensor_tensor(out=gsm["t"][:, :, :], in0=gview[:, :, 4:8], in1=bf_b[:, :].unsqueeze(1).broadcast_to([P, NTT, 4]), op=ALU.add),
                  reads=[pgi, bf_b], writes=[gsm["t"]])
            fw.op(act, lambda: A.activation(out=gsm["t"][:, :, :], in_=gsm["t"][:, :, :], func=ACT.Exp, scale=-1.0), reads=[gsm["t"]], writes=[gsm["t"]])
            fw.op(act, lambda: A.activation(out=gsm["t"][:, :, :], in_=gsm["t"][:, :, :], func=ACT.Ln, bias=onec, scale=1.0), reads=[gsm["t"], cb], writes=[gsm["t"]])
            fw.op(dve, lambda: V.tensor_scalar(out=gsm["logf"][:, :, :], in0=gsm["t"][:, :, :], scalar1=-1.0, scalar2=None, op0=ALU.mult), reads=[gsm["t"]], writes=[gsm["logf"]])
            slx, wx = wload(w_in_v[:, :, 672:1184], KC, 512)
            for hh in range(4):
                px = nps()
                mm(px, px[:, :], [(wx[:, kc, hh * P:(hh + 1) * P], uT[kc][:, :]) for kc in range(KC)], [slx] + uT)
                xb_ = xbuf[hh % 2]
                ac = acc[hh % 2]
                if g == 0:
                    fw.op(dve, lambda xb_=xb_: V.memset(xb_[:, 0:3], 0.0), writes=[xb_])
                else:
                    fw.op(dve, lambda xb_=xb_, hh=hh: V.tensor_copy(out=xb_[:, 0:3], in_=xtail[:, hh, :]), reads=[xtail], writes=[xb_])
                fw.op(act, lambda xb_=xb_, px=px: A.copy(out=xb_[:, 3:TG + 3], in_=px[:, :]), reads=[px], writes=[xb_])
                fw.op(dve, lambda xb_=xb_, hh=hh: V.tensor_copy(out=xtail[:, hh, :], in_=xb_[:, TG:TG + 3]), reads=[xb_], writes=[xtail])
                fw.op(act, lambda xb_=xb_, ac=ac, hh=hh: A.activation(out=ac[:, :], in_=xb_[:, 3:TG + 3], func=ACT.Identity, bias=convbT[:, hh:hh + 1], scale=convwT[:, 3, hh:hh + 1]),
                      reads=[xb_, convbT, convwT], writes=[ac])
                for j in range(3):
                    fw.op(dve, lambda xb_=xb_, ac=ac, hh=hh, j=j: V.scalar_tensor_tensor(out=ac[:, :], in0=xb_[:, j:j + TG], scalar=convwT[:, j, hh:hh + 1], in1=ac[:, :],
                                                                                         op0=ALU.mult, op1=ALU.add), reads=[xb_, convwT, ac], writes=[ac])
                fw.op(act, lambda ac=ac, hh=hh: A.activation(out=xcT[hh][:, :], in_=ac[:, :], func=ACT.Silu), reads=[ac], writes=[xcT[hh]])
            slo, wo_ = wload(w_in_v[:, :, 1696:2208], KC, 512)
            for hh in range(4):
                pg_ = nps()
                mm(pg_, pg_[:, :], [(wo_[:, kc, hh * P:(hh + 1) * P], uT[kc][:, :]) for kc in range(KC)], [slo] + uT)
                fw.op(act, lambda pg_=pg_, hh=hh: A.activation(out=og[:, hh, :], in_=pg_[:, :], func=ACT.Sigmoid), reads=[pg_], writes=[og])
            pgs = nps()
            for tt in range(NTT):
                for i_, lhs in enumerate([Mbd, BDm, SEL[0], SEL[1]]):
                    mm(pgs, pgs[:, tt * 16 + i_ * 4: tt * 16 + i_ * 4 + 4], [(lhs, gsm["logf"][:, tt, :])], [cb, gsm["logf"]])
            gs = pgs[:, 0:64].rearrange("p (t c) -> p t c", c=16)
            fw.op(dve, lambda: V.tensor_tensor(out=gsm["a"][:, :, :], in0=gsm["logi"][:, :, :], in1=gs[:, :, 0:4], op=ALU.subtract), reads=[gsm["logi"], pgs], writes=[gsm["a"]])
            fw.op(act, lambda: A.activation(out=gsm["ea"][:, :, :], in_=gsm["a"][:, :, :], func=ACT.Exp), reads=[gsm["a"]], writes=[gsm["ea"]])
            fw.op(dve, lambda: V.tensor_tensor(out=gsm["t"][:, :, :], in0=gsm["a"][:, :, :], in1=gs[:, :, 4:8], op=ALU.add), reads=[gsm["a"], pgs], writes=[gsm["t"]])
            fw.op(act, lambda: A.activation(out=gsm["ks"][:, :, :], in_=gsm["t"][:, :, :], func=ACT.Exp, bias=lnsc, scale=1.0), reads=[gsm["t"], cb], writes=[gsm["ks"]])
            fw.op(act, lambda: A.activation(out=eg[:, :, :], in_=gs[:, :, 8:16], func=ACT.Exp), reads=[pgs], writes=[eg])
            if dbg and "logf" in dbg_out and g == 0 and b == 0:
                dump("logf", gsm["logf"][:, :, :], gsm["logf"])
                dump("logi", gsm["logi"][:, :, :], gsm["logi"])
                dump("ga", gsm["a"][:, :, :], gsm["a"])
                dump("eg", eg[:, :, :], eg)
            slv, wv_ = wload(w_in_v[:, :, 1184:1696], KC, 512)
            slqm, wqm = wload(W["w_q_m"].rearrange("h d e -> d h e"), 4, P)
            slkm, wkm = wload(W["w_k_m"].rearrange("h d e -> d h e"), 4, P)
            if g == 0:
                for hh in range(4):
                    fw.op(dve, lambda hh=hh: V.memset(Sf[hh][:, :], 0.0), writes=[Sf[hh]])
                    fw.op(dve, lambda hh=hh: V.memset(Sbf[hh][sbf_i[hh] % 2][:, :], 0.0), writes=[Sbf[hh][sbf_i[hh] % 2]])
            def front(tt):
                i2 = tt % 2
                tc_ = slice(tt * P, (tt + 1) * P)
                vm, kpp, rep, Eb, qpT, kTm, smT = vm2[i2], kpp2[i2], rep2[i2], Eb2[i2], qpT2[i2], kTm2[i2], smT2[i2]
                pvm = nps()
                mm(pvm, pvm[:, :], [(uT[kc][:, tc_], wv_[:, kc, :]) for kc in range(KC)], [slv] + uT)
                fw.op(act, lambda: A.copy(out=vm[:, :], in_=pvm[:, :]), reads=[pvm], writes=[vm])
                pk2 = nps()
                for hh in range(4):
                    mm(pk2, pk2[:, hh * P:(hh + 1) * P], [(xcT[hh][:, tc_], wkm[:, hh, :])], [xcT[hh], slkm])
                fw.op(dve, lambda: V.tensor_tensor(out=kpp[:, :, :], in0=pk2[:, :].rearrange("p (h e) -> p h e", e=P),
                                                   in1=gsm["ks"][:, tt, :].unsqueeze(2).broadcast_to([P, 4, P]), op=ALU.mult), reads=[pk2, gsm["ks"]], writes=[kpp])
                pbb = nps()
                fw.op(pool, lambda: nc.gpsimd.tensor_copy(out=rep[:, :, :], in_=gsm["logf"][:, tt, :].unsqueeze(2).broadcast_to([P, 4, P])), reads=[gsm["logf"]], writes=[rep])
                for hh in range(4):
                    mm(pbb, pbb[:, hh * P:(hh + 1) * P], [(rep[:, hh, :], Mbd)], [rep, cb])
                fw.op(act, lambda: A.activation(out=Eb[:, :, :], in_=pbb[:, :].rearrange("p (h e) -> p h e", e=P), func=ACT.Exp), reads=[pbb], writes=[Eb])
                pq_ = nps()
                pk_ = nps()
                for hh in range(4):
                    mm(pq_, pq_[:, hh * P:(hh + 1) * P], [(wqm[:, hh, :], xcT[hh][:, tc_])], [slqm, xcT[hh]])
                    mm(pk_, pk_[:, hh * P:(hh + 1) * P], [(wkm[:, hh, :], xcT[hh][:, tc_])], [slkm, xcT[hh]])
                fw.op(dve, lambda: V.tensor_tensor(out=qpT[:, :, :], in0=pq_[:, :].rearrange("p (h e) -> p h e", e=P), in1=Eb[:, :, :], op=ALU.mult), reads=[pq_, Eb], writes=[qpT])
                fw.op(act, lambda: A.activation(out=kTm[:, :, :], in_=pk_[:, :].rearrange("p (h e) -> p h e", e=P), func=ACT.Identity, scale=float(128.0 ** -0.5)), reads=[pk_], writes=[kTm])
                pS = nps()
                for hh in range(4):
                    mm(pS, pS[:, hh * P:(hh + 1) * P], [(kTm[:, hh, :], qpT[:, hh, :])], [kTm, qpT])
                for hh in range(4):
                    fw.op(dve, lambda hh=hh: V.scalar_tensor_tensor(out=smT[:, hh, :], in0=pS[:, hh * P:(hh + 1) * P], scalar=gsm["ea"][:, tt, hh:hh + 1], in1=Mbd,
                                                                    op0=ALU.mult, op1=ALU.mult), reads=[pS, gsm["ea"], cb], writes=[smT])

            def back(tt):
                i2 = tt % 2
                tc_ = slice(tt * P, (tt + 1) * P)
                vm, kpp, qpT, smT = vm2[i2], kpp2[i2], qpT2[i2], smT2[i2]
                pnum = nps(pin=True)
                pden = nps(pin=True)
                for hh in range(4):
                    hs = slice(hh * P, (hh + 1) * P)
                    mm(pnum, pnum[:, hs], [(vm[:, hs], smT[:, hh, :])], [vm, smT], start=(hh == 0), stop=False)
                    mm(pden, pden[:, hs], [(ones_b[:, :], smT[:, hh, :])], [ones_b, smT], start=(hh == 0), stop=False)
                for c in range(2):
                    cs = slice(c * 64, (c + 1) * 64)
                    for hh in range(4):
                        sb_ = Sbf[hh][sbf_i[hh] % 2]
                        mm(pnum, pnum[:, hh * P + c * 64: hh * P + (c + 1) * 64], [(sb_[:, 0:P], qpT[:, hh, cs])], [sb_, qpT], start=False, stop=True)
                        mm(pden, pden[:, hh * P + c * 64: hh * P + (c + 1) * 64], [(sb_[:, P:2 * P], qpT[:, hh, cs])], [sb_, qpT], start=False, stop=True)
                        pup = nps()
                        mm(pup, pup[:, 0:P], [(kpp[cs, hh, :], vm[cs, hh * P:(hh + 1) * P])], [kpp, vm])
                        mm(pup, pup[:, P:2 * P], [(kpp[cs, hh, :], ones_b[cs, :])], [kpp, ones_b])
                        fw.op(dve, lambda hh=hh, pup=pup, c=c: V.scalar_tensor_tensor(out=Sf[hh][:, :], in0=Sf[hh][:, :], scalar=eg[:, tt, c * 4 + hh:c * 4 + hh + 1], in1=pup[:, 0:256],
                                                                                    op0=ALU.mult, op1=ALU.add), reads=[Sf[hh], eg, pup], writes=[Sf[hh]])
                        sbf_i[hh] += 1
                        nb_ = Sbf[hh][sbf_i[hh] % 2]
                        fw.op(act, lambda hh=hh, nb_=nb_: A.copy(out=nb_[:, :], in_=Sf[hh][:, :]), reads=[Sf[hh]], writes=[nb_])
                fw.op(act, lambda: A.activation(out=dd[:, :, :], in_=pden[:, :].rearrange("p (h e) -> p h e", e=P), func=ACT.Abs), reads=[pden], writes=[dd])
                fw.op(dve, lambda: V.tensor_scalar(out=dd[:, :, :], in0=dd[:, :, :], scalar1=1.0, scalar2=None, op0=ALU.max), reads=[dd], writes=[dd])
                fw.op(act, lambda: A.activation(out=dd[:, :, :], in_=dd[:, :, :], func=ACT.Ln), reads=[dd], writes=[dd])
                fw.op(act, lambda: A.activation(out=dd[:, :, :], in_=dd[:, :, :], func=ACT.Exp, scale=-1.0), reads=[dd], writes=[dd])
                fw.op(dve, lambda: V.tensor_tensor(out=hg[:, :, :], in0=pnum[:, :].rearrange("p (h e) -> p h e", e=P), in1=dd[:, :, :], op=ALU.mult), reads=[pnum, dd], writes=[hg])
                unpin(pnum)
                unpin(pden)
                fw.op(pool, lambda: nc.gpsimd.tensor_tensor(out=hg[:, :, :], in0=hg[:, :, :], in1=og[:, :, tc_], op=ALU.mult), reads=[hg, og], writes=[hg])
                fw.op(act, lambda: A.activation(out=sq4[:, :, :], in_=hg[:, :, :], func=ACT.Square), reads=[hg], writes=[sq4])
                pss4 = nps()
                mm(pss4, pss4[:, :], [(ones_b[:, :], sq4[:, :, :].rearrange("p h e -> p (h e)"))], [ones_b, sq4])
                fw.op(act, lambda: A.activation(out=rs4[:, :, :], in_=pss4[:, :].rearrange("p (h e) -> p h e", e=P), func=ACT.Ln, bias=epsc[128], scale=1.0), reads=[pss4, cb], writes=[rs4])
                fw.op(act, lambda: A.activation(out=rs4[:, :, :], in_=rs4[:, :, :], func=ACT.Exp, scale=-0.5), reads=[rs4], writes=[rs4])
                for hh in range(4):
                    fw.op(dve, lambda hh=hh: V.scalar_tensor_tensor(out=hnT[hh][:, tc_], in0=hg[:, hh, :], scalar=mnT[:, hh:hh + 1], in1=rs4[:, hh, :],
                                                                    op0=ALU.mult, op1=ALU.mult), reads=[hg, mnT, rs4], writes=[hnT[hh]])

            front(0)
            for tt in range(NTT):
                if tt + 1 < NTT:
                    front(tt + 1)
                back(tt)
            fw.barrier()
        with contextlib.ExitStack() as es:
            yT, _ = fw.bufs("yT", [P, TG], BF16, KC, es=es)
            s0 = fw.buf("s0", [P, TG], F32, es=es)
            s1 = fw.buf("s1", [P, TG], F32, es=es)
            y1 = fw.buf("y1", [P, TG], F32, es=es)
            y2 = fw.buf("y2", [P, TG], F32, es=es)
            wmla_v = W["w_mla_out"].rearrange("(h d) n -> d h n", d=64)
            wmls_v = W["w_mlstm_out"].rearrange("(hh p) n -> p hh n", p=P)
            for dp in range(4):
                ds_ = slice(dp * 256, (dp + 1) * 256)
                sla = nslot()
                wa = sla[0:64, 0:2048].rearrange("p (a b) -> p a b", a=8)
                fw.dma(pool, wa, wmla_v[:, :, ds_], writes=[sla], key=sla)
                slb, wb = wload(wmls_v[:, :, ds_], 4, 256)
                slg0, wg0 = wload(w_in_v[:, :, 2216 + dp * 256:2216 + (dp + 1) * 256], KC, 256)
                slg1, wg1 = wload(w_in_v[:, :, 3240 + dp * 256:3240 + (dp + 1) * 256], KC, 256)
                for j in range(2):
                    dc = 2 * dp + j
                    js = slice(j * P, (j + 1) * P)
                    pa = nps()
                    pb_ = nps()
                    p0 = nps()
                    p1 = nps()
                    mm(pa, pa[:, :], [(wa[:, h, js], attnT[h][:, :]) for h in range(NH)], [sla] + attnT)
                    mm(pb_, pb_[:, :], [(wb[:, hh, js], hnT[hh][:, :]) for hh in range(4)], [slb] + hnT)
                    mm(p0, p0[:, :], [(wg0[:, kc, js], uT[kc][:, :]) for kc in range(KC)], [slg0] + uT)
                    mm(p1, p1[:, :], [(wg1[:, kc, js], uT[kc][:, :]) for kc in range(KC)], [slg1] + uT)
                    fw.op(act, lambda p0=p0: A.activation(out=s0[:, :], in_=p0[:, :], func=ACT.Sigmoid), reads=[p0], writes=[s0])
                    fw.op(act, lambda p1=p1: A.activation(out=s1[:, :], in_=p1[:, :], func=ACT.Sigmoid), reads=[p1], writes=[s1])
                    fw.op(dve, lambda pa=pa: V.tensor_tensor(out=y1[:, :], in0=s0[:, :], in1=pa[:, :], op=ALU.mult), reads=[s0, pa], writes=[y1])
                    fw.op(dve, lambda pb_=pb_: V.tensor_tensor(out=y2[:, :], in0=s1[:, :], in1=pb_[:, :], op=ALU.mult), reads=[s1, pb_], writes=[y2])
                    fw.op(dve, lambda dc=dc: V.tensor_tensor(out=yT[dc][:, :], in0=y1[:, :], in1=y2[:, :], op=ALU.add), reads=[y1, y2], writes=[yT[dc]])
            wo_v = W["w_o"].rearrange("(kc p) n -> p kc n", p=P)
            for dp in range(4):
                slo_, wov = wload(wo_v[:, :, dp * 256:(dp + 1) * 256], KC, 256)
                for j in range(2):
                    dc = 2 * dp + j
                    po = nps()
                    mm(po, po[:, :], [(wov[:, kc, j * P:(j + 1) * P], yT[kc][:, :]) for kc in range(KC)], [slo_] + yT)
                    fw.op(dve, lambda dc=dc, po=po: V.scalar_tensor_tensor(out=hb[dc][:, :], in0=po[:, :], scalar=Gcol[:, 1, b, dc:dc + 1], in1=hb[dc][:, :],
                                                                           op0=ALU.mult, op1=ALU.add), reads=[po, Gcol, hb[dc]], writes=[hb[dc]])
            fw.barrier()

    xq = {}

    def load_dma(gi, tt):
        b, g = gi // NG, gi % NG
        r0 = b * S + g * TG
        xb_ = nio()
        fw.dma(sp, xb_[:, :], x[r0 + tt * P:r0 + (tt + 1) * P, :], writes=[xb_], key=xb_)
        xq[(gi, tt)] = xb_

    def load_tile(gi, tt):
        if (gi, tt) not in xq:
            load_dma(gi, tt)
        xb_ = xq.pop((gi, tt))
        for half in range(2):
            pt_ = nps()
            for i in range(4):
                kc = half * 4 + i
                fw.op(pe, lambda pt_=pt_, xb_=xb_, kc=kc, i=i: T.transpose(pt_[:, i * P:(i + 1) * P], xb_[:, kc * P:(kc + 1) * P], ident), reads=[xb_, cb], writes=[pt_])
            if half == 0:
                fw.op(act, lambda pt_=pt_, half=half, tt=tt: A.copy(out=h_full[:, half * 4:(half + 1) * 4, tt * P:(tt + 1) * P], in_=pt_[:, :].rearrange("p (a t) -> p a t", a=4)),
                      reads=[pt_], writes=hb[half * 4:(half + 1) * 4])
            else:
                fw.op(dve, lambda pt_=pt_, half=half, tt=tt: V.tensor_copy(out=h_full[:, half * 4:(half + 1) * 4, tt * P:(tt + 1) * P], in_=pt_[:, :].rearrange("p (a t) -> p a t", a=4)),
                      reads=[pt_], writes=hb[half * 4:(half + 1) * 4])

    def load_group(gi):
        for tt in range(NTT):
            load_tile(gi, tt)

    load_group(0)
    cTb = fw.buf("cTb", [P, NSEQ, KC], BF16)
    fw.op(act, lambda: A.activation(out=cTb[:, :, :], in_=cT[:, :, :], func=ACT.Silu), reads=[cT], writes=[cTb])
    modT = fw.buf("modT", [P, NSEQ, 72], F32)
    pm = nps()
    w_ada_v = W["w_ada"].rearrange("(kc p) n -> p kc n", p=P)
    for blk in range(36):
        sl, wv = wload(w_ada_v[:, :, blk * 256:(blk + 1) * 256], KC, 256)
        for j in range(2):
            jj = 2 * blk + j
            mm(pm, pm[:, jj * 2:(jj + 1) * 2], [(wv[:, kc, j * 128:(j + 1) * 128], cTb[:, :, kc]) for kc in range(KC)], [sl, cTb])
    for b in range(NSEQ):
        fw.op(dve, lambda b=b: V.tensor_tensor(out=modT[:, b, :], in0=pm[:, 0:144].rearrange("p (j b) -> p j b", b=2)[:, :, b],
                                               in1=badaT[:, :], op=ALU.add), reads=[pm, badaT], writes=[modT])
    Acol = fw.buf("Acol", [P, 3, NSEQ, KC], F32)
    Gcol = fw.buf("Gcol", [P, 3, NSEQ, KC], F32)
    gains = [nT["norm_ff1"], nT["norm_mix"], nT["norm_ff2"]]
    for k in range(3):
        for b in range(NSEQ):
            fw.op(dve, lambda k=k, b=b: V.scalar_tensor_tensor(out=Acol[:, k, b, :], in0=modT[:, b, (3 * k + 1) * 8:(3 * k + 2) * 8], scalar=1.0,
                                                               in1=gains[k][:, :], op0=ALU.add, op1=ALU.mult), reads=[modT, gains[k]], writes=[Acol])
            fw.op(dve, lambda k=k, b=b: V.tensor_scalar(out=Gcol[:, k, b, :], in0=modT[:, b, (3 * k + 2) * 8:(3 * k + 3) * 8],
                                                        scalar1=(1.0 if k == 1 else 0.5), scalar2=None, op0=ALU.mult), reads=[modT], writes=[Gcol])

    def Bcol(k, b, kc):
        return modT[:, b, 3 * k * 8 + kc:3 * k * 8 + kc + 1]


    out_bufs = []
    for gi in range(ngroups):
        b, g = gi // NG, gi % NG
        r0 = b * S + g * TG
        with contextlib.ExitStack() as es:
            ffn(0, b, W["ff1_w_gate"], W["ff1_w_up"], W["ff1_w_down"], es,
                after_norm=(lambda: rope_tables(b, g)) if do_mixer else None)
            fw.barrier()
        if gi == 0:
            dump("h1", h_full[:, :, :], hb[0])
        if do_mixer:
            mixer(b, g)
            if gi == 0:
                dump("h2", h_full[:, :, :], hb[0])
                if dbg and "att" in dbg_out:
                    for h_ in range(NH):
                        fw.dma(pool, dbg_out["att"][h_], attnT[h_][:, :], reads=[attnT[h_]], key=attnT[h_])
                    for h_ in range(4):
                        fw.dma(pool, dbg_out["ml"][h_], hnT[h_][:, :], reads=[hnT[h_]], key=hnT[h_])
        if do_ff2:
            with contextlib.ExitStack() as es:
                ffn(2, b, W["ff2_w_gate"], W["ff2_w_up"], W["ff2_w_down"], es)
                fw.barrier()
        rstd_from([(hb[kc], hb[kc][:, :]) for kc in range(KC)], 1024, sbuf_src=True)
        with contextlib.ExitStack() as es:
            onb, on_full = fw.bufs("onb", [P, TG], F32, KC, es=es)
            for kc in range(KC):
                fw.op(dve, lambda kc=kc: V.scalar_tensor_tensor(out=onb[kc][:, :], in0=hb[kc][:, :], scalar=nT["norm_final"][:, kc:kc + 1], in1=rstd[:, :],
                                                                op0=ALU.mult, op1=ALU.mult), reads=[hb[kc], nT["norm_final"], rstd], writes=[onb[kc]])
            for tt in range(NTT):
                if gi + 1 < ngroups:
                    if tt == 0:
                        load_dma(gi + 1, 0)
                    if tt + 1 < NTT:
                        load_dma(gi + 1, tt + 1)
                    load_tile(gi + 1, tt)
                ob_ = nio()
                for half in range(2):
                    pt_ = nps()
                    for i in range(4):
                        kc = half * 4 + i
                        fw.op(pe, lambda pt_=pt_, kc=kc, i=i, tt=tt: T.transpose(pt_[:, i * P:(i + 1) * P], onb[kc][:, tt * P:(tt + 1) * P], ident), reads=[onb[kc], cb], writes=[pt_])
                    if half == 0:
                        fw.op(act, lambda pt_=pt_, ob_=ob_, half=half: A.copy(out=ob_[:, half * 512:(half + 1) * 512], in_=pt_[:, :]), reads=[pt_], writes=[ob_])
                    else:
                        fw.op(dve, lambda pt_=pt_, ob_=ob_, half=half: V.tensor_copy(out=ob_[:, half * 512:(half + 1) * 512], in_=pt_[:, :]), reads=[pt_], writes=[ob_])
                fw.dma(sp, y[r0 + tt * P:r0 + (tt + 1) * P, :], ob_[:, :], reads=[ob_], key=ob_)
                out_bufs.append(ob_)
            fw.barrier()
    deps = []
    for ob_ in io:
        if ob_.dsem is not None:
            deps.append(('d', ob_.dsem, ob_.dval, id(ob_)))
    fw._wait(sp, deps)
    if dbg:
        for bb in [hb[0]]:
            if bb.dsem is not None:
                fw._wait(sp, [('d', bb.dsem, bb.dval, id(bb))])
    fw.close()
    return nc


def prep_inputs(inputs):
    w = {}
    for n, s in WNAMES:
        if n == "w_q_bp":
            continue
        a = np.asarray(inputs[n], dtype=np.float32)
        if n != "norm_final":
            a = a[0]
        w[n] = np.ascontiguousarray(a)
    wq = w["w_q_b"]
    perm = np.arange(768)
    for h in range(NH):
        base = h * 96 + 64
        perm[base:base + 16] = np.arange(base + 16, base + 32)
        perm[base + 16:base + 32] = np.arange(base, base + 16)
    w["w_q_bp"] = np.ascontiguousarray(wq[:, perm])
    return w


_CACHE = {}


def kernel(**inputs):
    x = np.asarray(inputs["x"], dtype=np.float32)
    c = np.asarray(inputs["c"], dtype=np.float32)
    pos = np.asarray(inputs["positions"], dtype=np.int32)
    w = prep_inputs(inputs)
    cst = host_consts()
    if "nc" not in _CACHE:
        _CACHE["nc"] = build_program()
    nc = _CACHE["nc"]
    in_maps = []
    for i in range(8):
        m = {"x": np.ascontiguousarray(x[2 * i:2 * i + 2].reshape(NSEQ * S, D)),
             "c": np.ascontiguousarray(c[2 * i:2 * i + 2]),
             "pos": np.ascontiguousarray(pos[2 * i:2 * i + 2]),
             "cst": cst}
        m.update(w)
        in_maps.append(m)
    res = run_bass_kernel_spmd(nc, in_maps, core_ids=list(range(8)))
    out = np.concatenate([r["y"].reshape(NSEQ, S, D) for r in res.results], axis=0)
    return out.astype(np.float32)
```

```python
import contextlib
import numpy as np
import concourse.bass as bass
import concourse.mybir as mybir
from concourse.bass_utils import run_bass_kernel_spmd

F32 = mybir.dt.float32
BF16 = mybir.dt.bfloat16
I32 = mybir.dt.int32
ACT = mybir.ActivationFunctionType
ALU = mybir.AluOpType

P = 128
D = 1024
KC = 8
DFF = 2816
FC = 22
TG = 512
NTT = 4
S = 2048
NG = S // TG
NSEQ = 2
NH = 8
DIN = 4264
EPS = 1e-6
SEM_CAP = 24000
C1 = 6.28125
C2 = float(2 * np.pi - 6.28125)


class Eng:
    def __init__(self, fw, name, h, nsem):
        self.name = name
        self.h = h
        self.sems = [fw.es.enter_context(fw.nc.semaphore(f"s_{name}{i}")) for i in range(nsem)]
        self.n = 0
        self.seen = {}
        self.seen_d = {}

    def sem_val(self, seq):
        i = (seq - 1) // SEM_CAP
        return self.sems[i], (seq - 1) % SEM_CAP + 1


class Buf:
    def __init__(self, ap, name=""):
        self.ap = ap
        self.name = name
        self.w = None
        self.r = []
        self.dsem = None
        self.dval = 0

    def __getitem__(self, k):
        return self.ap[k]


class GroupBuf:
    def __init__(self, members, ap):
        self.members = members
        self.ap = ap
        self.name = members[0].name

    def __getitem__(self, k):
        return self.ap[k]


def _flat(bs):
    out = []
    for b in bs:
        out.extend(getattr(b, "members", None) or [b])
    return out


class FW:
    def __init__(self, nc):
        self.nc = nc
        self.es = contextlib.ExitStack()
        self.es.__enter__()
        self.pe = Eng(self, "pe", nc.tensor, 3)
        self.act = Eng(self, "act", nc.scalar, 3)
        self.dve = Eng(self, "dve", nc.vector, 4)
        self.pool = Eng(self, "pool", nc.gpsimd, 1)
        self.sp = Eng(self, "sp", nc.sync, 1)
        self.engs = [self.pe, self.act, self.dve, self.pool, self.sp]

    def close(self):
        self.es.__exit__(None, None, None)

    def buf(self, name, shape, dt, es=None):
        self.uid = getattr(self, "uid", 0) + 1
        name = f"{name}_{self.uid}"
        t = (es or self.es).enter_context(self.nc.sbuf_tensor(name, list(shape), dt))
        return Buf(t[tuple(slice(None) for _ in shape)], name)

    def bufs(self, name, shape, dt, n, es=None):
        self.uid = getattr(self, "uid", 0) + 1
        name = f"{name}_{self.uid}_"
        t = (es or self.es).enter_context(self.nc.sbuf_tensor(name, [shape[0], n] + list(shape[1:]), dt))
        full = t[tuple(slice(None) for _ in range(len(shape) + 1))]
        out = []
        for i in range(n):
            idx = (slice(None), i) + tuple(slice(None) for _ in shape[1:])
            out.append(Buf(t[idx], f"{name}{i}"))
        return out, full

    def _wait(self, E, deps):
        for d in deps:
            if d[0] == 'e':
                _, F, seq = d
                if F is E and E is self.pe:
                    continue
                if E.seen.get(F.name, 0) >= seq:
                    continue
                s, v = F.sem_val(seq)
                E.h.wait_ge(s, v)
                E.seen[F.name] = seq
            else:
                _, sem, val, key = d
                if E.seen_d.get(key, 0) >= val:
                    continue
                E.h.wait_ge(sem, val)
                E.seen_d[key] = val

    @staticmethod
    def _deps(reads, writes):
        reads, writes = _flat(reads), _flat(writes)
        deps = []
        for b in reads:
            if b.w is not None:
                deps.append(b.w)
        for b in writes:
            if b.w is not None:
                deps.append(b.w)
            deps.extend(b.r)
        return deps

    @staticmethod
    def _mark(tag, reads, writes):
        reads, writes = _flat(reads), _flat(writes)
        for b in reads:
            if tag[0] == 'e':
                b.r = [t for t in b.r if not (t[0] == 'e' and t[1] is tag[1])]
            b.r.append(tag)
        for b in writes:
            b.w = tag
            b.r = []

    def op(self, E, fns, reads=(), writes=()):
        if callable(fns):
            fns = [fns]
        if E is self.pool and getattr(self, "bar", None):
            self._wait(E, self.bar)
        self._wait(E, self._deps(reads, writes))
        ins = None
        for f in fns:
            ins = f()
        seq = E.n + 1
        E.n = seq
        s, _ = E.sem_val(seq)
        ins.then_inc(s, 1)
        self._mark(('e', E, seq), reads, writes)

    def dma(self, Q, out, in_, reads=(), writes=(), key=None, **kw):
        key = (getattr(key, "members", None) or [key])[0]
        if key.dsem is None:
            key.dsem = self.es.enter_context(self.nc.semaphore(f"d_{key.name}"))
        self._wait(Q, self._deps(reads, writes))
        key.dval += 16
        Q.h.dma_start(out=out, in_=in_, **kw).then_inc(key.dsem, 16)
        self._mark(('d', key.dsem, key.dval, id(key)), reads, writes)

    def barrier(self):
        ce = [self.pe, self.act, self.dve]
        for E in ce:
            self._wait(E, [('e', F, F.n) for F in ce + [self.pool] if F is not E and F.n > 0])
        self.bar = [('e', F, F.n) for F in ce if F.n > 0]


def host_consts():
    c = np.zeros((P, 6 * P + 8), np.float32)
    idx = np.arange(P)
    c[:, 0:P] = np.eye(P, dtype=np.float32)
    s_, t_ = idx[:, None], idx[None, :]
    c[:, P:2 * P] = (t_ >= s_).astype(np.float32)
    c[:, 2 * P:3 * P] = ((t_ >= s_) & (s_ // 64 == t_ // 64)).astype(np.float32)
    c[:, 3 * P:4 * P] = (s_ // 64 == t_ // 64).astype(np.float32)
    c[:, 4 * P:5 * P] = (s_ < 64).astype(np.float32) * np.ones((1, P), np.float32)
    c[:, 5 * P:6 * P] = (s_ >= 64).astype(np.float32) * np.ones((1, P), np.float32)
    inv = (np.float32(10000.0) ** (-np.arange(0, 32, 2, dtype=np.float32) / np.float32(32))).astype(np.float32)
    o = 6 * P
    c[64:80, o] = -inv
    c[80:96, o] = inv
    c[64:80, o + 1] = inv
    c[80:96, o + 1] = inv
    c[:, o + 2] = 1024 * EPS
    c[:, o + 3] = 384 * EPS
    c[:, o + 4] = 256 * EPS
    c[:, o + 5] = 128 * EPS
    c[:, o + 6] = 1.0
    c[:, o + 7] = np.log(128.0 ** -0.5)
    return c


WNAMES = [("w_ada", [D, 9 * D]), ("b_ada", [9 * D]), ("norm_ff1", [D]), ("ff1_w_gate", [D, DFF]),
          ("ff1_w_up", [D, DFF]), ("ff1_w_down", [DFF, D]), ("norm_mix", [D]), ("w_in", [D, DIN]),
          ("q_a_norm", [384]), ("w_q_b", [384, 768]), ("w_q_bp", [384, 768]), ("kv_a_norm", [256]),
          ("w_kv_b", [256, 1024]), ("conv_w", [4, 512]), ("conv_b", [512]), ("w_q_m", [4, 128, 128]),
          ("w_k_m", [4, 128, 128]), ("b_i", [4]), ("b_f", [4]), ("mlstm_norm", [4, 128]),
          ("w_mla_out", [512, D]), ("w_mlstm_out", [512, D]), ("w_o", [D, D]), ("norm_ff2", [D]),
          ("ff2_w_gate", [D, DFF]), ("ff2_w_up", [D, DFF]), ("ff2_w_down", [DFF, D]), ("norm_final", [D])]


def build_program(do_mixer=True, do_ff2=True, dbg=None, ngroups=NSEQ * NG):
    nc = bass.Bass("TRN2", target_bir_lowering=False)
    x = nc.dram_tensor("x", [NSEQ * S, D], F32, kind="ExternalInput").ap()
    c_in = nc.dram_tensor("c", [NSEQ, D], F32, kind="ExternalInput").ap()
    pos = nc.dram_tensor("pos", [NSEQ, S], I32, kind="ExternalInput").ap()
    cst = nc.dram_tensor("cst", [P, 6 * P + 8], F32, kind="ExternalInput").ap()
    W = {n: nc.dram_tensor(n, s, F32, kind="ExternalInput").ap() for n, s in WNAMES}
    y = nc.dram_tensor("y", [NSEQ * S, D], F32, kind="ExternalOutput").ap()
    dbg_out = {}
    if dbg:
        for n, s in dbg.items():
            dbg_out[n] = nc.dram_tensor(n, s, F32, kind="ExternalOutput").ap()

    fw = FW(nc)
    pe, act, dve, pool, sp = fw.pe, fw.act, fw.dve, fw.pool, fw.sp
    V = nc.vector
    A = nc.scalar
    T = nc.tensor

    hb, h_full = fw.bufs("h", [P, TG], F32, KC)
    uT, _ = fw.bufs("uT", [P, TG], BF16, KC)
    NSLOT = 8
    SLOT = 2048
    ring_t = fw.es.enter_context(nc.sbuf_tensor("ring_t", [P, NSLOT * SLOT], BF16))
    ring = [Buf(ring_t[:, k * SLOT:(k + 1) * SLOT], f"ring{k}") for k in range(NSLOT)]
    ring_i = [0]
    io = [fw.buf(f"io{i}", [P, D], F32) for i in range(3)]
    io_i = [0]
    cb = fw.buf("cst", [P, 6 * P + 8], F32)
    ident = cb[:, 0:P]
    tri_f = cb[:, P:2 * P]
    Mbd = cb[:, 2 * P:3 * P]
    BDm = cb[:, 3 * P:4 * P]
    SEL = [cb[:, 4 * P:5 * P], cb[:, 5 * P:6 * P]]
    o_ = 6 * P
    invfS = cb[:, o_:o_ + 1]
    invfC = cb[:, o_ + 1:o_ + 2]
    epsc = {1024: cb[:, o_ + 2:o_ + 3], 384: cb[:, o_ + 3:o_ + 4], 256: cb[:, o_ + 4:o_ + 5], 128: cb[:, o_ + 5:o_ + 6]}
    onec = cb[:, o_ + 6:o_ + 7]
    lnsc = cb[:, o_ + 7:o_ + 8]
    ones_b = fw.buf("ones_b", [P, P], BF16)
    rstd = fw.buf("rstd", [P, TG], F32)
    tmpA = [fw.buf(f"tmpA{i}", [P, TG], F32) for i in range(2)]
    tmpA_i = [0]
    sqb = [fw.buf(f"sqb{i}", [P, TG], BF16) for i in range(3)]
    sq_i = [0]
    psb = [Buf(fw.es.enter_context(nc.psum_tensor(f"ps{i}", [P, TG], F32))[:, :], f"ps{i}") for i in range(8)]
    ps_i = [0]

    pinned = set()

    def nps(pin=False):
        while True:
            b = psb[ps_i[0] % 8]
            ps_i[0] += 1
            if id(b) not in pinned:
                break
        if pin:
            pinned.add(id(b))
        return b

    def unpin(b):
        pinned.discard(id(b))

    def nslot(n_elems=0):
        if n_elems > SLOT:
            if ring_i[0] % 2 == 1:
                ring_i[0] += 1
            k = ring_i[0] % NSLOT
            ring_i[0] += 2
            return GroupBuf([ring[k], ring[k + 1]], ring_t[:, k * SLOT:(k + 2) * SLOT])
        b = ring[ring_i[0] % NSLOT]
        ring_i[0] += 1
        return b

    def nio():
        b = io[io_i[0] % 3]
        io_i[0] += 1
        return b

    def ntmp():
        b = tmpA[tmpA_i[0] % 2]
        tmpA_i[0] += 1
        return b

    def nsq():
        b = sqb[sq_i[0] % 3]
        sq_i[0] += 1
        return b

    def wload(src, a, b_=None):
        n = a * (b_ or 1)
        sl = nslot(n)
        if b_ is None:
            view = sl[:, 0:n]
        else:
            view = sl[:, 0:n].rearrange("p (a b) -> p a b", a=a)
        fw.dma(pool, view, src, writes=[sl], key=sl)
        return sl, view

    def mm(outb, out_ap, terms, reads, start=True, stop=True):
        n = len(terms)
        fns = []
        for i, (l, r) in enumerate(terms):
            fns.append(lambda l=l, r=r, i=i: T.matmul(out_ap, l, r, start=(start and i == 0), stop=(stop and i == n - 1), skip_group_check=True))
        fw.op(pe, fns, reads=reads, writes=[outb])

    def dump(name, ap, b):
        if dbg and name in dbg_out:
            fw.dma(sp, dbg_out[name], ap, reads=[b], key=b)

    fw.dma(sp, cb[:, :], cst[:, :], writes=[cb], key=cb)
    fw.op(dve, lambda: V.memset(ones_b[:, :], 1.0), writes=[ones_b])

    vst = fw.buf("vst", [P, 2, P], F32)
    fw.op(dve, lambda: V.memset(vst[:, :, :], 0.0), writes=[vst])
    rowsA = [("b_ada", 72), ("norm_ff1", 8), ("norm_mix", 8), ("norm_ff2", 8), ("norm_final", 8), ("q_a_norm", 3), ("kv_a_norm", 2)]
    r_ = 0
    for n_, k_ in rowsA:
        fw.dma(sp, vst[r_:r_ + k_, 0, :], W[n_].rearrange("(j p) -> j p", p=P), writes=[vst], key=vst)
        r_ += k_
    NA = r_
    fw.dma(sp, vst[0:16, 1, :], W["conv_w"].rearrange("j (hh p) -> (j hh) p", p=P), writes=[vst], key=vst)
    fw.dma(sp, vst[16:20, 1, :], W["conv_b"].rearrange("(hh p) -> hh p", p=P), writes=[vst], key=vst)
    fw.dma(sp, vst[20:24, 1, :], W["mlstm_norm"], writes=[vst], key=vst)
    fw.dma(sp, vst[24:40, 1, :], c_in.rearrange("b (kc p) -> (b kc) p", p=P), writes=[vst], key=vst)
    NB_ = 40
    vT = fw.buf("vT", [P, NA + NB_], F32)
    pvt = nps()
    fw.op(pe, lambda: T.transpose(pvt[:, 0:NA], vst[0:NA, 0, :], ident[0:NA, 0:NA]), reads=[vst, cb], writes=[pvt])
    fw.op(pe, lambda: T.transpose(pvt[:, NA:NA + NB_], vst[0:NB_, 1, :], ident[0:NB_, 0:NB_]), reads=[vst, cb], writes=[pvt])
    fw.op(act, lambda: A.copy(out=vT[:, :], in_=pvt[:, 0:NA + NB_]), reads=[pvt], writes=[vT])

    class SubBuf:
        def __init__(self, parent, ap):
            self.__dict__["parent"] = parent
            self.__dict__["ap"] = ap

        def __getattr__(self, k):
            return getattr(self.__dict__["parent"], k)

        def __setattr__(self, k, v):
            setattr(self.__dict__["parent"], k, v)

        def __getitem__(self, k):
            return self.__dict__["ap"][k]

    badaT = SubBuf(vT, vT[:, 0:72])
    nT = {k: SubBuf(vT, vT[:, 72 + 8 * i:80 + 8 * i]) for i, k in enumerate(["norm_ff1", "norm_mix", "norm_ff2", "norm_final"])}
    qnT = SubBuf(vT, vT[:, 104:107])
    kvnT = SubBuf(vT, vT[:, 107:109])
    convwT = SubBuf(vT, vT[:, NA:NA + 16].rearrange("p (j h) -> p j h", j=4))
    convbT = SubBuf(vT, vT[:, NA + 16:NA + 20])
    mnT = SubBuf(vT, vT[:, NA + 20:NA + 24])
    cT = SubBuf(vT, vT[:, NA + 24:NA + 40].rearrange("p (b k) -> p b k", b=NSEQ))
    bi_b = fw.buf("bi_b", [P, 4], F32)
    fw.dma(sp, bi_b[:, :], W["b_i"].rearrange("(o n) -> o n", o=1).partition_broadcast(P), writes=[bi_b], key=bi_b)
    bf_b = fw.buf("bf_b", [P, 4], F32)
    fw.dma(sp, bf_b[:, :], W["b_f"].rearrange("(o n) -> o n", o=1).partition_broadcast(P), writes=[bf_b], key=bf_b)
    for b_, sc in [(nT["norm_ff1"], 32.0), (nT["norm_mix"], 32.0), (nT["norm_ff2"], 32.0), (nT["norm_final"], 32.0),
                   (qnT, float(np.sqrt(384.0))), (kvnT, 16.0), (mnT, float(np.sqrt(128.0)))]:
        fw.op(dve, lambda b_=b_, sc=sc: V.tensor_scalar(out=b_.ap, in0=b_.ap, scalar1=sc, scalar2=None, op0=ALU.mult), reads=[b_], writes=[b_])
    kpeW = fw.buf("kpeW", [P, KC, 96], BF16)
    kpeWp = fw.buf("kpeWp", [P, KC, 96], BF16)
    wif = fw.buf("wif", [P, KC, 8], BF16)
    w_in_v = W["w_in"].rearrange("(kc p) n -> p kc n", p=P)
    if do_mixer:
        fw.op(dve, lambda: V.memset(kpeW[:, :, :], 0.0), writes=[kpeW])
        fw.op(dve, lambda: V.memset(kpeWp[:, :, :], 0.0), writes=[kpeWp])
        fw.dma(pool, kpeW[:, :, 64:96], w_in_v[:, :, 640:672], writes=[kpeW], key=kpeW)
        fw.dma(pool, kpeWp[:, :, 64:80], w_in_v[:, :, 656:672], writes=[kpeWp], key=kpeWp)
        fw.dma(pool, kpeWp[:, :, 80:96], w_in_v[:, :, 640:656], writes=[kpeWp], key=kpeWp)
        fw.dma(pool, wif[:, :, :], w_in_v[:, :, 2208:2216], writes=[wif], key=wif)

    def rstd_from(chunks, n, sbuf_src=False):
        pss = nps()
        nch = len(chunks)
        for i, (b_, ap) in enumerate(chunks):
            sq = nsq()
            if sbuf_src and i % 2 == 1:
                fw.op(pool, lambda ap=ap, sq=sq: nc.gpsimd.tensor_tensor(out=sq[:, :], in0=ap, in1=ap, op=ALU.mult), reads=[b_], writes=[sq])
            else:
                fw.op(act, lambda ap=ap, sq=sq: A.activation(out=sq[:, :], in_=ap, func=ACT.Square), reads=[b_], writes=[sq])
            mm(pss, pss[:, :], [(ones_b[:, :], sq[:, :])], [ones_b, sq], start=(i == 0), stop=(i == nch - 1))
        fw.op(act, lambda: A.activation(out=rstd[:, :], in_=pss[:, :], func=ACT.Ln, bias=epsc[n], scale=1.0), reads=[pss, cb], writes=[rstd])
        fw.op(act, lambda: A.activation(out=rstd[:, :], in_=rstd[:, :], func=ACT.Exp, scale=-0.5), reads=[rstd], writes=[rstd])

    def norm_mod(k, b):
        rstd_from([(hb[kc], hb[kc][:, :]) for kc in range(KC)], 1024, sbuf_src=True)
        for kc in range(KC):
            t = ntmp()
            fw.op(dve, lambda kc=kc, t=t: V.scalar_tensor_tensor(out=t[:, :], in0=hb[kc][:, :], scalar=Acol[:, k, b, kc:kc + 1], in1=rstd[:, :],
                                                                 op0=ALU.mult, op1=ALU.mult), reads=[hb[kc], Acol, rstd], writes=[t])
            fw.op(act, lambda kc=kc, t=t: A.activation(out=uT[kc][:, :], in_=t[:, :], func=ACT.Identity, bias=Bcol(k, b, kc), scale=1.0),
                  reads=[t, modT], writes=[uT[kc]])

    def ffn(k, b, wg, wu, wd, es, after_norm=None):
        hT, _ = fw.bufs("hT", [P, TG], BF16, FC, es=es)
        sg = [fw.buf(f"sg{i}", [P, TG], F32, es=es) for i in range(2)]
        norm_mod(k, b)
        if after_norm is not None:
            after_norm()
        wg_v = wg.rearrange("(kc p) n -> p kc n", p=P)
        wu_v = wu.rearrange("(kc p) n -> p kc n", p=P)
        wd_v = wd.rearrange("(fc p) n -> p fc n", p=P)
        for fb in range(FC // 2):
            sg_, gv = wload(wg_v[:, :, fb * 256:(fb + 1) * 256], KC, 256)
            su_, uv = wload(wu_v[:, :, fb * 256:(fb + 1) * 256], KC, 256)
            for j in range(2):
                f = 2 * fb + j
                pg = nps()
                pu = nps()
                mm(pg, pg[:, :], [(gv[:, kc, j * 128:(j + 1) * 128], uT[kc][:, :]) for kc in range(KC)], [sg_] + uT)
                mm(pu, pu[:, :], [(uv[:, kc, j * 128:(j + 1) * 128], uT[kc][:, :]) for kc in range(KC)], [su_] + uT)
                s_ = sg[f % 2]
                fw.op(act, lambda pg=pg, s_=s_: A.activation(out=s_[:, :], in_=pg[:, :], func=ACT.Silu), reads=[pg], writes=[s_])
                fw.op(dve, lambda pu=pu, s_=s_, f=f: V.tensor_tensor(out=hT[f][:, :], in0=s_[:, :], in1=pu[:, :], op=ALU.mult), reads=[s_, pu], writes=[hT[f]])
        for half in range(2):
            accs = [nps(pin=True) for _ in range(4)]
            f0 = 0
            while f0 < FC:
                nf = min(4, FC - f0)
                sd_, dv = wload(wd_v[:, f0:f0 + nf, half * 512:(half + 1) * 512], nf, 512)
                for fi in range(nf):
                    f = f0 + fi
                    for i in range(4):
                        mm(accs[i], accs[i][:, :], [(dv[:, fi, i * 128:(i + 1) * 128], hT[f][:, :])], [sd_, hT[f]], start=(f == 0), stop=(f == FC - 1))
                f0 += nf
            for i in range(4):
                dc = half * 4 + i
                po = accs[i]
                fw.op(dve, lambda dc=dc, po=po: V.scalar_tensor_tensor(out=hb[dc][:, :], in0=po[:, :], scalar=Gcol[:, k, b, dc:dc + 1], in1=hb[dc][:, :],
                                                                       op0=ALU.mult, op1=ALU.add), reads=[po, Gcol, hb[dc]], writes=[hb[dc]])
                unpin(po)

    if do_mixer:
        kT = [[fw.buf(f"kT{h}_{g}", [P, TG], BF16) for g in range(NG)] for h in range(NH)]
        Vt = [fw.buf(f"V{t}", [P, NH * 65 + 63], BF16) for t in range(S // P)]
        for t in range(S // P):
            fw.op(dve, lambda t=t: V.memset(Vt[t][:, :], 1.0), writes=[Vt[t]])
        for h_ in range(NH):
            for g_ in range(NG):
                fw.op(dve, lambda h_=h_, g_=g_: V.memset(kT[h_][g_][:, :], 0.0), writes=[kT[h_][g_]])
        Sf = [fw.buf(f"Sf{hh}", [P, 256], F32) for hh in range(4)]
        Sbf = [[fw.buf(f"Sbf{hh}_{i}", [P, 256], BF16) for i in range(2)] for hh in range(4)]
        sbf_i = [0, 0, 0, 0]
        xtail = fw.buf("xtail", [P, 4, 3], F32)
        posb = fw.buf("posb", [96, TG], I32)
        Ct = fw.buf("Ct", [96, TG], F32)
        St = fw.buf("St", [96, TG], F32)
        tg = (fw.buf("ang", [96, TG], F32), fw.buf("ki", [96, TG], I32), fw.buf("kf", [96, TG], F32), fw.buf("mm_", [96, TG], F32))
        attnT = [fw.buf(f"attnT{h}", [64, TG], BF16) for h in range(NH)]
        hnT, hn_full = fw.bufs("hnT", [P, TG], BF16, 4)

    def rope_tables(b, g):
        c0 = g * TG
        fw.dma(sp, posb[:, :], pos[b:b + 1, c0:c0 + TG].partition_broadcast(96), writes=[posb], key=posb)
        rope_table(St, invfS, 0.0, None, tg)
        rope_table(Ct, invfC, float(np.pi / 2), None, tg)

    def rope_table(out_t, invcol, shift, es, tg):
        ang, ki, kf, m = tg
        fw.op(dve, lambda: V.tensor_copy(out=kf[:, :], in_=posb[:, :]), reads=[posb], writes=[kf])
        fw.op(dve, lambda: V.tensor_scalar(out=ang[:, :], in0=kf[:, :], scalar1=invcol[0:96, :], scalar2=shift, op0=ALU.mult, op1=ALU.add), reads=[kf, cb], writes=[ang])
        fw.op(dve, lambda: V.tensor_scalar(out=ki[:, :], in0=ang[:, :], scalar1=float(1.0 / (2 * np.pi)), scalar2=None, op0=ALU.mult), reads=[ang], writes=[ki])
        fw.op(dve, lambda: V.tensor_copy(out=kf[:, :], in_=ki[:, :]), reads=[ki], writes=[kf])
        fw.op(dve, lambda: V.scalar_tensor_tensor(out=ang[:, :], in0=kf[:, :], scalar=-C1, in1=ang[:, :], op0=ALU.mult, op1=ALU.add), reads=[kf, ang], writes=[ang])
        fw.op(dve, lambda: V.scalar_tensor_tensor(out=ang[:, :], in0=kf[:, :], scalar=-C2, in1=ang[:, :], op0=ALU.mult, op1=ALU.add), reads=[kf, ang], writes=[ang])
        fw.op(dve, lambda: V.tensor_scalar(out=m[:, :], in0=ang[:, :], scalar1=float(np.pi), scalar2=-float(2 * np.pi), op0=ALU.is_gt, op1=ALU.mult), reads=[ang], writes=[m])
        fw.op(dve, lambda: V.tensor_tensor(out=ang[:, :], in0=ang[:, :], in1=m[:, :], op=ALU.add), reads=[ang, m], writes=[ang])
        fw.op(dve, lambda: V.tensor_scalar(out=m[:, :], in0=ang[:, :], scalar1=-float(np.pi), scalar2=float(2 * np.pi), op0=ALU.is_lt, op1=ALU.mult), reads=[ang], writes=[m])
        fw.op(dve, lambda: V.tensor_tensor(out=ang[:, :], in0=ang[:, :], in1=m[:, :], op=ALU.add), reads=[ang, m], writes=[ang])
        fw.op(act, lambda: A.activation(out=out_t[:, :], in_=ang[:, :], func=ACT.Sin), reads=[ang], writes=[out_t])

    def mixer(b, g):
        c0 = g * TG
        norm_mod(1, b)
        with contextlib.ExitStack() as es:
            qln, _ = fw.bufs("qln", [P, TG], BF16, 3, es=es)
            ckn, _ = fw.bufs("ckn", [P, TG], BF16, 2, es=es)
            qT = [fw.buf(f"qT{h}", [P, TG], BF16, es=es) for h in range(NH)]
            for h in range(NH):
                fw.op(pool, lambda h=h: nc.gpsimd.memset(qT[h][:, :], 0.0), writes=[qT[h]])
            pT = [fw.buf(f"pT{i}", [P, TG], BF16, es=es) for i in range(3)]
            osb = [fw.buf(f"osb{i}", [64, TG], F32, es=es) for i in range(2)]
            rden = [fw.buf(f"rden{i}", [65, TG], F32, es=es) for i in range(2)]
            rhi = [fw.buf(f"rhi{i}", [65, TG], BF16, es=es) for i in range(2)]
            rlo = [fw.buf(f"rlo{i}", [65, TG], BF16, es=es) for i in range(2)]
            t1b = fw.buf("t1b", [96, TG], F32, es=es)
            t2b = fw.buf("t2b", [96, TG], F32, es=es)
            t1q = [t1b, fw.buf("t1c", [96, TG], F32, es=es)]
            t2q = [t2b, fw.buf("t2c", [96, TG], F32, es=es)]


            pre_w = {}
            pre_w[0] = wload(w_in_v[:, :, 0:384], KC, 384)
            pre_w[384] = wload(w_in_v[:, :, 384:640], KC, 256)
            slkv, wkv = wload(W["w_kv_b"].rearrange("(j p) n -> p j n", p=P), 2, 1024)
            slq, wq = wload(W["w_q_b"].rearrange("(j p) n -> p j n", p=P), 3, 768)
            slqp, wqp = wload(W["w_q_bp"].rearrange("(j p) n -> p j n", p=P), 3, 768)

            def lat(col0, nch, outs, gcol, n):
                sl, wv = pre_w[col0]
                pl = []
                for j in range(nch):
                    pj = nps()
                    mm(pj, pj[:, :], [(wv[:, kc, j * P:(j + 1) * P], uT[kc][:, :]) for kc in range(KC)], [sl] + uT)
                    pl.append(pj)
                rstd_from([(pj, pj[:, :]) for pj in pl], n)
                for j in range(nch):
                    fw.op(dve, lambda j=j: V.scalar_tensor_tensor(out=outs[j][:, :], in0=pl[j][:, :], scalar=gcol[:, j:j + 1], in1=rstd[:, :],
                                                                  op0=ALU.mult, op1=ALU.mult), reads=[pl[j], gcol, rstd], writes=[outs[j]])
            lat(0, 3, qln, qnT, 384)
            lat(384, 2, ckn, kvnT, 256)
            pk = nps()
            pkp = nps()
            mm(pk, pk[0:96, :], [(kpeW[:, kc, :], uT[kc][:, :]) for kc in range(KC)], [kpeW] + uT)
            mm(pkp, pkp[0:96, :], [(kpeWp[:, kc, :], uT[kc][:, :]) for kc in range(KC)], [kpeWp] + uT)
            fw.op(dve, lambda: V.tensor_tensor(out=t1b[64:96, :], in0=pk[64:96, :], in1=Ct[64:96, :], op=ALU.mult), reads=[pk, Ct], writes=[t1b])
            fw.op(dve, lambda: V.tensor_tensor(out=t2b[64:96, :], in0=pkp[64:96, :], in1=St[64:96, :], op=ALU.mult), reads=[pkp, St], writes=[t2b])
            for h in range(NH):
                fw.op(dve, lambda h=h: V.tensor_tensor(out=kT[h][g][64:96, :], in0=t1b[64:96, :], in1=t2b[64:96, :], op=ALU.add), reads=[t1b, t2b], writes=[kT[h][g]])
            for h in range(NH):
                pn = nps()
                mm(pn, pn[0:64, :], [(wkv[:, j, h * 128:h * 128 + 64], ckn[j][:, :]) for j in range(2)], [slkv] + ckn)
                if h % 2 == 0:
                    fw.op(act, lambda h=h, pn=pn: A.copy(out=kT[h][g][0:64, :], in_=pn[0:64, :]), reads=[pn], writes=[kT[h][g]])
                else:
                    fw.op(dve, lambda h=h, pn=pn: V.tensor_copy(out=kT[h][g][0:64, :], in_=pn[0:64, :]), reads=[pn], writes=[kT[h][g]])
            for tt in range(NTT):
                pv = nps()
                mm(pv, pv[:, :],
                   [(ckn[j][:, tt * P:(tt + 1) * P], wkv[:, j, :].rearrange("p (h two d) -> p h two d", two=2, d=64)[:, :, 1, :]) for j in range(2)], [slkv] + ckn)
                vt = Vt[g * NTT + tt]
                fw.op(act, lambda pv=pv, vt=vt: A.copy(out=vt[:, 0:NH * 65].rearrange("p (h d) -> p h d", d=65)[:, :, 0:64], in_=pv[:, :].rearrange("p (h d) -> p h d", d=64)), reads=[pv], writes=[vt])
            for h in range(NH):
                pq = nps()
                pqp = nps()
                mm(pq, pq[0:96, :], [(wq[:, j, h * 96:(h + 1) * 96], qln[j][:, :]) for j in range(3)], [slq] + qln)
                mm(pqp, pqp[0:96, :], [(wqp[:, j, h * 96:(h + 1) * 96], qln[j][:, :]) for j in range(3)], [slqp] + qln)
                ta, tb = t1q[h % 2], t2q[h % 2]
                fw.op(dve, lambda pq=pq, ta=ta: V.tensor_tensor(out=ta[:, :], in0=pq[0:96, :], in1=Ct[:, :], op=ALU.mult), reads=[pq, Ct], writes=[ta])
                fw.op(dve, lambda pqp=pqp, tb=tb: V.tensor_tensor(out=tb[:, :], in0=pqp[0:96, :], in1=St[:, :], op=ALU.mult), reads=[pqp, St], writes=[tb])
                fw.op(pool, lambda h=h, ta=ta, tb=tb: nc.gpsimd.tensor_tensor(out=qT[h][0:96, :], in0=ta[:, :], in1=tb[:, :], op=ALU.add), reads=[ta, tb], writes=[qT[h]])
            sc = float(96.0 ** -0.5)
            nkt = 4 * g + 4
            items = [(h, kt) for h in range(NH) for kt in range(nkt)]
            pti = [0]

            def qk(h, kt):
                jd = kt - 4 * g
                q0 = 128 * jd if jd > 0 else 0
                n = TG - q0
                pss = nps()
                kb = kT[h][kt // 4]
                mm(pss, pss[:, 0:n], [(kb[:, (kt % 4) * P:(kt % 4 + 1) * P], qT[h][:, q0:TG])], [kb, qT[h]])
                return pss, q0, n, jd

            def epi1(h, po):
                ob = osb[h % 2]
                rd, rh, rl = rden[h % 2], rhi[h % 2], rlo[h % 2]
                fw.op(dve, lambda: V.tensor_copy(out=ob[:, :], in_=po[0:64, :]), reads=[po], writes=[ob])
                fw.op(act, lambda: A.activation(out=rd[64:65, :], in_=po[64:65, :], func=ACT.Ln), reads=[po], writes=[rd])
                fw.op(act, lambda: A.activation(out=rd[64:65, :], in_=rd[64:65, :], func=ACT.Exp, scale=-1.0), reads=[rd], writes=[rd])
                unpin(po)
                fw.op(dve, lambda: V.tensor_copy(out=rh[64:65, :], in_=rd[64:65, :]), reads=[rd], writes=[rh])
                fw.op(dve, lambda: V.tensor_tensor(out=rl[64:65, :], in0=rd[64:65, :], in1=rh[64:65, :], op=ALU.subtract), reads=[rd, rh], writes=[rl])

            def epi2(h):
                ob = osb[h % 2]
                rh, rl = rhi[h % 2], rlo[h % 2]
                pb = nps()
                mm(pb, pb[0:64, :], [(ones_b[64:65, 0:64], rh[64:65, :]), (ones_b[64:65, 0:64], rl[64:65, :])], [ones_b, rh, rl])
                fw.op(dve, lambda: V.tensor_tensor(out=attnT[h][:, :], in0=ob[:, :], in1=pb[0:64, :], op=ALU.mult), reads=[ob, pb], writes=[attnT[h]])

            AHEAD = 3
            inflight = [qk(*items[i]) for i in range(min(AHEAD, len(items)))]
            po = None
            pending = None
            for idx, (h, kt) in enumerate(items):
                if idx + AHEAD < len(items):
                    inflight.append(qk(*items[idx + AHEAD]))
                cur = inflight.pop(0)
                if kt == 0:
                    po = nps(pin=True)
                pss, q0, n, jd = cur
                pt = pT[pti[0] % 3]
                pti[0] += 1
                fw.op(act, lambda pss=pss, pt=pt, n=n: A.activation(out=pt[:, 0:n], in_=pss[:, 0:n], func=ACT.Exp, scale=sc), reads=[pss], writes=[pt])
                if jd >= 0:
                    fw.op(dve, lambda pt=pt: V.tensor_tensor(out=pt[:, 0:P], in0=pt[:, 0:P], in1=tri_f, op=ALU.mult), reads=[pt, cb], writes=[pt])
                vt = Vt[kt]
                mm(po, po[:, q0:TG], [(vt[:, h * 65:h * 65 + P], pt[:, 0:n])], [vt, pt], start=(kt == 0), stop=(kt == nkt - 1))
                if pending is not None:
                    pending[1] -= 1
                    if pending[1] <= 0:
                        epi2(pending[0])
                        pending = None
                if kt == nkt - 1:
                    if pending is not None:
                        epi2(pending[0])
                    epi1(h, po)
                    pending = [h, 3]
            if pending is not None:
                epi2(pending[0])
            fw.barrier()
        with contextlib.ExitStack() as es:
            xbuf = [fw.buf(f"xbuf{i}", [P, TG + 3], F32, es=es) for i in range(2)]
            acc = [fw.buf(f"acc{i}", [P, TG], F32, es=es) for i in range(2)]
            xcT = [fw.buf(f"xcT{hh}", [P, TG], BF16, es=es) for hh in range(4)]
            og = fw.buf("og", [P, 4, TG], BF16, es=es)
            gsm = {n: fw.buf(f"g_{n}", [P, NTT, 4], F32, es=es) for n in ["logi", "logf", "a", "ea", "ks", "t"]}
            eg = fw.buf("eg", [P, NTT, 8], F32, es=es)
            rep2 = [fw.buf("rep", [P, 4, P], F32, es=es)] * 2
            Eb2 = [fw.buf("Eb", [P, 4, P], F32, es=es)] * 2
            qpT2 = [fw.buf(f"qpT{i}", [P, 4, P], BF16, es=es) for i in range(2)]
            kTm2 = [fw.buf("kTm", [P, 4, P], BF16, es=es)] * 2
            kpp2 = [fw.buf(f"kpp{i}", [P, 4, P], BF16, es=es) for i in range(2)]
            vm2 = [fw.buf(f"vm{i}", [P, TG], BF16, es=es) for i in range(2)]
            smT2 = [fw.buf(f"smT{i}", [P, 4, P], BF16, es=es) for i in range(2)]
            dd = fw.buf("dd", [P, 4, P], F32, es=es)
            hg = fw.buf("hg", [P, 4, P], F32, es=es)
            sq4 = fw.buf("sq4", [P, 4, P], BF16, es=es)
            rs4 = fw.buf("rs4", [P, 4, P], F32, es=es)

            pgi = nps()
            for tt in range(NTT):
                mm(pgi, pgi[:, tt * 8:(tt + 1) * 8], [(uT[kc][:, tt * P:(tt + 1) * P], wif[:, kc, :]) for kc in range(KC)], [wif] + uT)
            gview = pgi[:, 0:32].rearrange("p (t c) -> p t c", c=8)
            fw.op(dve, lambda: V.tensor_tensor(out=gsm["logi"][:, :, :], in0=gview[:, :, 0:4], in1=bi_b[:, :].unsqueeze(1).broadcast_to([P, NTT, 4]), op=ALU.add),
                  reads=[pgi, bi_b], writes=[gsm["logi"]])
            fw.op(dve, lambda: V.tensor_tensor(out=gsm["t"][:, :, :], in0=gview[:, :, 4:8], in1=bf_b[:, :].unsqueeze(1).broadcast_to([P, NTT, 4]), op=ALU.add),
                  reads=[pgi, bf_b], writes=[gsm["t"]])
            fw.op(act, lambda: A.activation(out=gsm["t"][:, :, :], in_=gsm["t"][:, :, :], func=ACT.Exp, scale=-1.0), reads=[gsm["t"]], writes=[gsm["t"]])
            fw.op(act, lambda: A.activation(out=gsm["t"][:, :, :], in_=gsm["t"][:, :, :], func=ACT.Ln, bias=onec, scale=1.0), reads=[gsm["t"], cb], writes=[gsm["t"]])
            fw.op(dve, lambda: V.tensor_scalar(out=gsm["logf"][:, :, :], in0=gsm["t"][:, :, :], scalar1=-1.0, scalar2=None, op0=ALU.mult), reads=[gsm["t"]], writes=[gsm["logf"]])
            slx, wx = wload(w_in_v[:, :, 672:1184], KC, 512)
            for hh in range(4):
                px = nps()
                mm(px, px[:, :], [(wx[:, kc, hh * P:(hh + 1) * P], uT[kc][:, :]) for kc in range(KC)], [slx] + uT)
                xb_ = xbuf[hh % 2]
                ac = acc[hh % 2]
                if g == 0:
                    fw.op(dve, lambda xb_=xb_: V.memset(xb_[:, 0:3], 0.0), writes=[xb_])
                else:
                    fw.op(dve, lambda xb_=xb_, hh=hh: V.tensor_copy(out=xb_[:, 0:3], in_=xtail[:, hh, :]), reads=[xtail], writes=[xb_])
                fw.op(act, lambda xb_=xb_, px=px: A.copy(out=xb_[:, 3:TG + 3], in_=px[:, :]), reads=[px], writes=[xb_])
                fw.op(dve, lambda xb_=xb_, hh=hh: V.tensor_copy(out=xtail[:, hh, :], in_=xb_[:, TG:TG + 3]), reads=[xb_], writes=[xtail])
                fw.op(act, lambda xb_=xb_, ac=ac, hh=hh: A.activation(out=ac[:, :], in_=xb_[:, 3:TG + 3], func=ACT.Identity, bias=convbT[:, hh:hh + 1], scale=convwT[:, 3, hh:hh + 1]),
                      reads=[xb_, convbT, convwT], writes=[ac])
                for j in range(3):
                    fw.op(dve, lambda xb_=xb_, ac=ac, hh=hh, j=j: V.scalar_tensor_tensor(out=ac[:, :], in0=xb_[:, j:j + TG], scalar=convwT[:, j, hh:hh + 1], in1=ac[:, :],
                                                                                         op0=ALU.mult, op1=ALU.add), reads=[xb_, convwT, ac], writes=[ac])
                fw.op(act, lambda ac=ac, hh=hh: A.activation(out=xcT[hh][:, :], in_=ac[:, :], func=ACT.Silu), reads=[ac], writes=[xcT[hh]])
            slo, wo_ = wload(w_in_v[:, :, 1696:2208], KC, 512)
            for hh in range(4):
                pg_ = nps()
                mm(pg_, pg_[:, :], [(wo_[:, kc, hh * P:(hh + 1) * P], uT[kc][:, :]) for kc in range(KC)], [slo] + uT)
                fw.op(act, lambda pg_=pg_, hh=hh: A.activation(out=og[:, hh, :], in_=pg_[:, :], func=ACT.Sigmoid), reads=[pg_], writes=[og])
            pgs = nps()
            for tt in range(NTT):
                for i_, lhs in enumerate([Mbd, BDm, SEL[0], SEL[1]]):
                    mm(pgs, pgs[:, tt * 16 + i_ * 4: tt * 16 + i_ * 4 + 4], [(lhs, gsm["logf"][:, tt, :])], [cb, gsm["logf"]])
            gs = pgs[:, 0:64].rearrange("p (t c) -> p t c", c=16)
            fw.op(dve, lambda: V.tensor_tensor(out=gsm["a"][:, :, :], in0=gsm["logi"][:, :, :], in1=gs[:, :, 0:4], op=ALU.subtract), reads=[gsm["logi"], pgs], writes=[gsm["a"]])
            fw.op(act, lambda: A.activation(out=gsm["ea"][:, :, :], in_=gsm["a"][:, :, :], func=ACT.Exp), reads=[gsm["a"]], writes=[gsm["ea"]])
            fw.op(dve, lambda: V.tensor_tensor(out=gsm["t"][:, :, :], in0=gsm["a"][:, :, :], in1=gs[:, :, 4:8], op=ALU.add), reads=[gsm["a"], pgs], writes=[gsm["t"]])
            fw.op(act, lambda: A.activation(out=gsm["ks"][:, :, :], in_=gsm["t"][:, :, :], func=ACT.Exp, bias=lnsc, scale=1.0), reads=[gsm["t"], cb], writes=[gsm["ks"]])
            fw.op(act, lambda: A.activation(out=eg[:, :, :], in_=gs[:, :, 8:16], func=ACT.Exp), reads=[pgs], writes=[eg])
            if dbg and "logf" in dbg_out and g == 0 and b == 0:
                dump("logf", gsm["logf"][:, :, :], gsm["logf"])
                dump("logi", gsm["logi"][:, :, :], gsm["logi"])
                dump("ga", gsm["a"][:, :, :], gsm["a"])
                dump("eg", eg[:, :, :], eg)
            slv, wv_ = wload(w_in_v[:, :, 1184:1696], KC, 512)
            slqm, wqm = wload(W["w_q_m"].rearrange("h d e -> d h e"), 4, P)
            slkm, wkm = wload(W["w_k_m"].rearrange("h d e -> d h e"), 4, P)
            if g == 0:
                for hh in range(4):
                    fw.op(dve, lambda hh=hh: V.memset(Sf[hh][:, :], 0.0), writes=[Sf[hh]])
                    fw.op(dve, lambda hh=hh: V.memset(Sbf[hh][sbf_i[hh] % 2][:, :], 0.0), writes=[Sbf[hh][sbf_i[hh] % 2]])
            def front(tt):
                i2 = tt % 2
                tc_ = slice(tt * P, (tt + 1) * P)
                vm, kpp, rep, Eb, qpT, kTm, smT = vm2[i2], kpp2[i2], rep2[i2], Eb2[i2], qpT2[i2], kTm2[i2], smT2[i2]
                pvm = nps()
                mm(pvm, pvm[:, :], [(uT[kc][:, tc_], wv_[:, kc, :]) for kc in range(KC)], [slv] + uT)
                fw.op(act, lambda: A.copy(out=vm[:, :], in_=pvm[:, :]), reads=[pvm], writes=[vm])
                yield
                pk2 = nps()
                for hh in range(4):
                    mm(pk2, pk2[:, hh * P:(hh + 1) * P], [(xcT[hh][:, tc_], wkm[:, hh, :])], [xcT[hh], slkm])
                fw.op(dve, lambda: V.tensor_tensor(out=kpp[:, :, :], in0=pk2[:, :].rearrange("p (h e) -> p h e", e=P),
                                                   in1=gsm["ks"][:, tt, :].unsqueeze(2).broadcast_to([P, 4, P]), op=ALU.mult), reads=[pk2, gsm["ks"]], writes=[kpp])
                yield
                pbb = nps()
                fw.op(pool, lambda: nc.gpsimd.tensor_copy(out=rep[:, :, :], in_=gsm["logf"][:, tt, :].unsqueeze(2).broadcast_to([P, 4, P])), reads=[gsm["logf"]], writes=[rep])
                for hh in range(4):
                    mm(pbb, pbb[:, hh * P:(hh + 1) * P], [(rep[:, hh, :], Mbd)], [rep, cb])
                fw.op(act, lambda: A.activation(out=Eb[:, :, :], in_=pbb[:, :].rearrange("p (h e) -> p h e", e=P), func=ACT.Exp), reads=[pbb], writes=[Eb])
                yield
                pq_ = nps()
                pk_ = nps()
                for hh in range(4):
                    mm(pq_, pq_[:, hh * P:(hh + 1) * P], [(wqm[:, hh, :], xcT[hh][:, tc_])], [slqm, xcT[hh]])
                    mm(pk_, pk_[:, hh * P:(hh + 1) * P], [(wkm[:, hh, :], xcT[hh][:, tc_])], [slkm, xcT[hh]])
                fw.op(dve, lambda: V.tensor_tensor(out=qpT[:, :, :], in0=pq_[:, :].rearrange("p (h e) -> p h e", e=P), in1=Eb[:, :, :], op=ALU.mult), reads=[pq_, Eb], writes=[qpT])
                yield
                fw.op(act, lambda: A.activation(out=kTm[:, :, :], in_=pk_[:, :].rearrange("p (h e) -> p h e", e=P), func=ACT.Identity, scale=float(128.0 ** -0.5)), reads=[pk_], writes=[kTm])
                yield
                pS = nps()
                for hh in range(4):
                    mm(pS, pS[:, hh * P:(hh + 1) * P], [(kTm[:, hh, :], qpT[:, hh, :])], [kTm, qpT])
                for hh in range(4):
                    fw.op(dve, lambda hh=hh: V.scalar_tensor_tensor(out=smT[:, hh, :], in0=pS[:, hh * P:(hh + 1) * P], scalar=gsm["ea"][:, tt, hh:hh + 1], in1=Mbd,
                                                                    op0=ALU.mult, op1=ALU.mult), reads=[pS, gsm["ea"], cb], writes=[smT])
                    yield

            def back_state(tt):
                i2 = tt % 2
                tc_ = slice(tt * P, (tt + 1) * P)
                vm, kpp, qpT, smT = vm2[i2], kpp2[i2], qpT2[i2], smT2[i2]
                pnum = nps(pin=True)
                pden = nps(pin=True)
                for hh in range(4):
                    hs = slice(hh * P, (hh + 1) * P)
                    mm(pnum, pnum[:, hs], [(vm[:, hs], smT[:, hh, :])], [vm, smT], start=(hh == 0), stop=False)
                    mm(pden, pden[:, hs], [(ones_b[:, :], smT[:, hh, :])], [ones_b, smT], start=(hh == 0), stop=False)
                for c in range(2):
                    cs = slice(c * 64, (c + 1) * 64)
                    for hh in range(4):
                        sb_ = Sbf[hh][sbf_i[hh] % 2]
                        mm(pnum, pnum[:, hh * P + c * 64: hh * P + (c + 1) * 64], [(sb_[:, 0:P], qpT[:, hh, cs])], [sb_, qpT], start=False, stop=True)
                        mm(pden, pden[:, hh * P + c * 64: hh * P + (c + 1) * 64], [(sb_[:, P:2 * P], qpT[:, hh, cs])], [sb_, qpT], start=False, stop=True)
                        pup = nps()
                        mm(pup, pup[:, 0:P], [(kpp[cs, hh, :], vm[cs, hh * P:(hh + 1) * P])], [kpp, vm])
                        mm(pup, pup[:, P:2 * P], [(kpp[cs, hh, :], ones_b[cs, :])], [kpp, ones_b])
                        fw.op(dve, lambda hh=hh, pup=pup, c=c: V.scalar_tensor_tensor(out=Sf[hh][:, :], in0=Sf[hh][:, :], scalar=eg[:, tt, c * 4 + hh:c * 4 + hh + 1], in1=pup[:, 0:256],
                                                                                    op0=ALU.mult, op1=ALU.add), reads=[Sf[hh], eg, pup], writes=[Sf[hh]])
                        sbf_i[hh] += 1
                        nb_ = Sbf[hh][sbf_i[hh] % 2]
                        fw.op(act, lambda hh=hh, nb_=nb_: A.copy(out=nb_[:, :], in_=Sf[hh][:, :]), reads=[Sf[hh]], writes=[nb_])
                return pnum, pden

            def norm(tt, pnum, pden):
                tc_ = slice(tt * P, (tt + 1) * P)
                fw.op(act, lambda: A.activation(out=dd[:, :, :], in_=pden[:, :].rearrange("p (h e) -> p h e", e=P), func=ACT.Abs), reads=[pden], writes=[dd])
                yield
                fw.op(dve, lambda: V.tensor_scalar(out=dd[:, :, :], in0=dd[:, :, :], scalar1=1.0, scalar2=None, op0=ALU.max), reads=[dd], writes=[dd])
                yield
                fw.op(act, lambda: A.activation(out=dd[:, :, :], in_=dd[:, :, :], func=ACT.Ln), reads=[dd], writes=[dd])
                yield
                fw.op(act, lambda: A.activation(out=dd[:, :, :], in_=dd[:, :, :], func=ACT.Exp, scale=-1.0), reads=[dd], writes=[dd])
                yield
                fw.op(dve, lambda: V.tensor_tensor(out=hg[:, :, :], in0=pnum[:, :].rearrange("p (h e) -> p h e", e=P), in1=dd[:, :, :], op=ALU.mult), reads=[pnum, dd], writes=[hg])
                unpin(pnum)
                unpin(pden)
                yield
                fw.op(pool, lambda: nc.gpsimd.tensor_tensor(out=hg[:, :, :], in0=hg[:, :, :], in1=og[:, :, tc_], op=ALU.mult), reads=[hg, og], writes=[hg])
                yield
                fw.op(act, lambda: A.activation(out=sq4[:, :, :], in_=hg[:, :, :], func=ACT.Square), reads=[hg], writes=[sq4])
                yield
                pss4 = nps()
                mm(pss4, pss4[:, :], [(ones_b[:, :], sq4[:, :, :].rearrange("p h e -> p (h e)"))], [ones_b, sq4])
                yield
                fw.op(act, lambda: A.activation(out=rs4[:, :, :], in_=pss4[:, :].rearrange("p (h e) -> p h e", e=P), func=ACT.Ln, bias=epsc[128], scale=1.0), reads=[pss4, cb], writes=[rs4])
                yield
                fw.op(act, lambda: A.activation(out=rs4[:, :, :], in_=rs4[:, :, :], func=ACT.Exp, scale=-0.5), reads=[rs4], writes=[rs4])
                yield
                for hh in range(4):
                    fw.op(dve, lambda hh=hh: V.scalar_tensor_tensor(out=hnT[hh][:, tc_], in0=hg[:, hh, :], scalar=mnT[:, hh:hh + 1], in1=rs4[:, hh, :],
                                                                    op0=ALU.mult, op1=ALU.mult), reads=[hg, mnT, rs4], writes=[hnT[hh]])
                    yield

            def run_interleaved(*gens):
                gens = list(gens)
                while gens:
                    for g_ in list(gens):
                        try:
                            next(g_)
                        except StopIteration:
                            gens.remove(g_)

            run_interleaved(front(0))
            for tt in range(NTT):
                pn_, pd_ = back_state(tt)
                if tt + 1 < NTT:
                    run_interleaved(norm(tt, pn_, pd_), front(tt + 1))
                else:
                    run_interleaved(norm(tt, pn_, pd_))
            fw.barrier()
        with contextlib.ExitStack() as es:
            yT, _ = fw.bufs("yT", [P, TG], BF16, KC, es=es)
            s0 = fw.buf("s0", [P, TG], F32, es=es)
            s1 = fw.buf("s1", [P, TG], F32, es=es)
            y1 = fw.buf("y1", [P, TG], F32, es=es)
            y2 = fw.buf("y2", [P, TG], F32, es=es)
            wmla_v = W["w_mla_out"].rearrange("(h d) n -> d h n", d=64)
            wmls_v = W["w_mlstm_out"].rearrange("(hh p) n -> p hh n", p=P)
            for dp in range(4):
                ds_ = slice(dp * 256, (dp + 1) * 256)
                sla = nslot()
                wa = sla[0:64, 0:2048].rearrange("p (a b) -> p a b", a=8)
                fw.dma(pool, wa, wmla_v[:, :, ds_], writes=[sla], key=sla)
                slb, wb = wload(wmls_v[:, :, ds_], 4, 256)
                slg0, wg0 = wload(w_in_v[:, :, 2216 + dp * 256:2216 + (dp + 1) * 256], KC, 256)
                slg1, wg1 = wload(w_in_v[:, :, 3240 + dp * 256:3240 + (dp + 1) * 256], KC, 256)
                for j in range(2):
                    dc = 2 * dp + j
                    js = slice(j * P, (j + 1) * P)
                    pa = nps()
                    pb_ = nps()
                    p0 = nps()
                    p1 = nps()
                    mm(pa, pa[:, :], [(wa[:, h, js], attnT[h][:, :]) for h in range(NH)], [sla] + attnT)
                    mm(pb_, pb_[:, :], [(wb[:, hh, js], hnT[hh][:, :]) for hh in range(4)], [slb] + hnT)
                    mm(p0, p0[:, :], [(wg0[:, kc, js], uT[kc][:, :]) for kc in range(KC)], [slg0] + uT)
                    mm(p1, p1[:, :], [(wg1[:, kc, js], uT[kc][:, :]) for kc in range(KC)], [slg1] + uT)
                    fw.op(act, lambda p0=p0: A.activation(out=s0[:, :], in_=p0[:, :], func=ACT.Sigmoid), reads=[p0], writes=[s0])
                    fw.op(act, lambda p1=p1: A.activation(out=s1[:, :], in_=p1[:, :], func=ACT.Sigmoid), reads=[p1], writes=[s1])
                    fw.op(dve, lambda pa=pa: V.tensor_tensor(out=y1[:, :], in0=s0[:, :], in1=pa[:, :], op=ALU.mult), reads=[s0, pa], writes=[y1])
                    fw.op(dve, lambda pb_=pb_: V.tensor_tensor(out=y2[:, :], in0=s1[:, :], in1=pb_[:, :], op=ALU.mult), reads=[s1, pb_], writes=[y2])
                    fw.op(dve, lambda dc=dc: V.tensor_tensor(out=yT[dc][:, :], in0=y1[:, :], in1=y2[:, :], op=ALU.add), reads=[y1, y2], writes=[yT[dc]])
            wo_v = W["w_o"].rearrange("(kc p) n -> p kc n", p=P)
            for dp in range(4):
                slo_, wov = wload(wo_v[:, :, dp * 256:(dp + 1) * 256], KC, 256)
                for j in range(2):
                    dc = 2 * dp + j
                    po = nps()
                    mm(po, po[:, :], [(wov[:, kc, j * P:(j + 1) * P], yT[kc][:, :]) for kc in range(KC)], [slo_] + yT)
                    fw.op(dve, lambda dc=dc, po=po: V.scalar_tensor_tensor(out=hb[dc][:, :], in0=po[:, :], scalar=Gcol[:, 1, b, dc:dc + 1], in1=hb[dc][:, :],
                                                                           op0=ALU.mult, op1=ALU.add), reads=[po, Gcol, hb[dc]], writes=[hb[dc]])
            fw.barrier()

    xq = {}

    def load_dma(gi, tt):
        b, g = gi // NG, gi % NG
        r0 = b * S + g * TG
        xb_ = nio()
        fw.dma(sp, xb_[:, :], x[r0 + tt * P:r0 + (tt + 1) * P, :], writes=[xb_], key=xb_)
        xq[(gi, tt)] = xb_

    def load_tile(gi, tt):
        if (gi, tt) not in xq:
            load_dma(gi, tt)
        xb_ = xq.pop((gi, tt))
        for half in range(2):
            pt_ = nps()
            for i in range(4):
                kc = half * 4 + i
                fw.op(pe, lambda pt_=pt_, xb_=xb_, kc=kc, i=i: T.transpose(pt_[:, i * P:(i + 1) * P], xb_[:, kc * P:(kc + 1) * P], ident), reads=[xb_, cb], writes=[pt_])
            if half == 0:
                fw.op(act, lambda pt_=pt_, half=half, tt=tt: A.copy(out=h_full[:, half * 4:(half + 1) * 4, tt * P:(tt + 1) * P], in_=pt_[:, :].rearrange("p (a t) -> p a t", a=4)),
                      reads=[pt_], writes=hb[half * 4:(half + 1) * 4])
            else:
                fw.op(dve, lambda pt_=pt_, half=half, tt=tt: V.tensor_copy(out=h_full[:, half * 4:(half + 1) * 4, tt * P:(tt + 1) * P], in_=pt_[:, :].rearrange("p (a t) -> p a t", a=4)),
                      reads=[pt_], writes=hb[half * 4:(half + 1) * 4])

    def load_group(gi):
        for tt in range(NTT):
            load_tile(gi, tt)

    load_group(0)
    cTb = fw.buf("cTb", [P, NSEQ, KC], BF16)
    fw.op(act, lambda: A.activation(out=cTb[:, :, :], in_=cT[:, :, :], func=ACT.Silu), reads=[cT], writes=[cTb])
    modT = fw.buf("modT", [P, NSEQ, 72], F32)
    pm = nps()
    w_ada_v = W["w_ada"].rearrange("(kc p) n -> p kc n", p=P)
    for blk in range(36):
        sl, wv = wload(w_ada_v[:, :, blk * 256:(blk + 1) * 256], KC, 256)
        for j in range(2):
            jj = 2 * blk + j
            mm(pm, pm[:, jj * 2:(jj + 1) * 2], [(wv[:, kc, j * 128:(j + 1) * 128], cTb[:, :, kc]) for kc in range(KC)], [sl, cTb])
    for b in range(NSEQ):
        fw.op(dve, lambda b=b: V.tensor_tensor(out=modT[:, b, :], in0=pm[:, 0:144].rearrange("p (j b) -> p j b", b=2)[:, :, b],
                                               in1=badaT[:, :], op=ALU.add), reads=[pm, badaT], writes=[modT])
    Acol = fw.buf("Acol", [P, 3, NSEQ, KC], F32)
    Gcol = fw.buf("Gcol", [P, 3, NSEQ, KC], F32)
    gains = [nT["norm_ff1"], nT["norm_mix"], nT["norm_ff2"]]
    for k in range(3):
        for b in range(NSEQ):
            fw.op(dve, lambda k=k, b=b: V.scalar_tensor_tensor(out=Acol[:, k, b, :], in0=modT[:, b, (3 * k + 1) * 8:(3 * k + 2) * 8], scalar=1.0,
                                                               in1=gains[k][:, :], op0=ALU.add, op1=ALU.mult), reads=[modT, gains[k]], writes=[Acol])
            fw.op(dve, lambda k=k, b=b: V.tensor_scalar(out=Gcol[:, k, b, :], in0=modT[:, b, (3 * k + 2) * 8:(3 * k + 3) * 8],
                                                        scalar1=(1.0 if k == 1 else 0.5), scalar2=None, op0=ALU.mult), reads=[modT], writes=[Gcol])

    def Bcol(k, b, kc):
        return modT[:, b, 3 * k * 8 + kc:3 * k * 8 + kc + 1]


    out_bufs = []
    for gi in range(ngroups):
        b, g = gi // NG, gi % NG
        r0 = b * S + g * TG
        with contextlib.ExitStack() as es:
            ffn(0, b, W["ff1_w_gate"], W["ff1_w_up"], W["ff1_w_down"], es,
                after_norm=(lambda: rope_tables(b, g)) if do_mixer else None)
            fw.barrier()
        if gi == 0:
            dump("h1", h_full[:, :, :], hb[0])
        if do_mixer:
            mixer(b, g)
            if gi == 0:
                dump("h2", h_full[:, :, :], hb[0])
                if dbg and "att" in dbg_out:
                    for h_ in range(NH):
                        fw.dma(pool, dbg_out["att"][h_], attnT[h_][:, :], reads=[attnT[h_]], key=attnT[h_])
                    for h_ in range(4):
                        fw.dma(pool, dbg_out["ml"][h_], hnT[h_][:, :], reads=[hnT[h_]], key=hnT[h_])
        if do_ff2:
            with contextlib.ExitStack() as es:
                ffn(2, b, W["ff2_w_gate"], W["ff2_w_up"], W["ff2_w_down"], es)
                fw.barrier()
        rstd_from([(hb[kc], hb[kc][:, :]) for kc in range(KC)], 1024, sbuf_src=True)
        with contextlib.ExitStack() as es:
            onb, on_full = fw.bufs("onb", [P, TG], F32, KC, es=es)
            for kc in range(KC):
                fw.op(dve, lambda kc=kc: V.scalar_tensor_tensor(out=onb[kc][:, :], in0=hb[kc][:, :], scalar=nT["norm_final"][:, kc:kc + 1], in1=rstd[:, :],
                                                                op0=ALU.mult, op1=ALU.mult), reads=[hb[kc], nT["norm_final"], rstd], writes=[onb[kc]])
            for tt in range(NTT):
                if gi + 1 < ngroups:
                    if tt == 0:
                        load_dma(gi + 1, 0)
                    if tt + 1 < NTT:
                        load_dma(gi + 1, tt + 1)
                    load_tile(gi + 1, tt)
                ob_ = nio()
                for half in range(2):
                    pt_ = nps()
                    for i in range(4):
                        kc = half * 4 + i
                        fw.op(pe, lambda pt_=pt_, kc=kc, i=i, tt=tt: T.transpose(pt_[:, i * P:(i + 1) * P], onb[kc][:, tt * P:(tt + 1) * P], ident), reads=[onb[kc], cb], writes=[pt_])
                    if half == 0:
                        fw.op(act, lambda pt_=pt_, ob_=ob_, half=half: A.copy(out=ob_[:, half * 512:(half + 1) * 512], in_=pt_[:, :]), reads=[pt_], writes=[ob_])
                    else:
                        fw.op(dve, lambda pt_=pt_, ob_=ob_, half=half: V.tensor_copy(out=ob_[:, half * 512:(half + 1) * 512], in_=pt_[:, :]), reads=[pt_], writes=[ob_])
                fw.dma(sp, y[r0 + tt * P:r0 + (tt + 1) * P, :], ob_[:, :], reads=[ob_], key=ob_)
                out_bufs.append(ob_)
            fw.barrier()
    deps = []
    for ob_ in io:
        if ob_.dsem is not None:
            deps.append(('d', ob_.dsem, ob_.dval, id(ob_)))
    fw._wait(sp, deps)
    if dbg:
        for bb in [hb[0]]:
            if bb.dsem is not None:
                fw._wait(sp, [('d', bb.dsem, bb.dval, id(bb))])
    fw.close()
    return nc


def prep_inputs(inputs):
    w = {}
    for n, s in WNAMES:
        if n == "w_q_bp":
            continue
        a = np.asarray(inputs[n], dtype=np.float32)
        if n != "norm_final":
            a = a[0]
        w[n] = np.ascontiguousarray(a)
    wq = w["w_q_b"]
    perm = np.arange(768)
    for h in range(NH):
        base = h * 96 + 64
        perm[base:base + 16] = np.arange(base + 16, base + 32)
        perm[base + 16:base + 32] = np.arange(base, base + 16)
    w["w_q_bp"] = np.ascontiguousarray(wq[:, perm])
    return w


_CACHE = {}


def kernel(**inputs):
    x = np.asarray(inputs["x"], dtype=np.float32)
    c = np.asarray(inputs["c"], dtype=np.float32)
    pos = np.asarray(inputs["positions"], dtype=np.int32)
    w = prep_inputs(inputs)
    cst = host_consts()
    if "nc" not in _CACHE:
        _CACHE["nc"] = build_program()
    nc = _CACHE["nc"]
    in_maps = []
    for i in range(8):
        m = {"x": np.ascontiguousarray(x[2 * i:2 * i + 2].reshape(NSEQ * S, D)),
             "c": np.ascontiguousarray(c[2 * i:2 * i + 2]),
             "pos": np.ascontiguousarray(pos[2 * i:2 * i + 2]),
             "cst": cst}
        m.update(w)
        in_maps.append(m)
    res = run_bass_kernel_spmd(nc, in_maps, core_ids=list(range(8)))
    out = np.concatenate([r["y"].reshape(NSEQ, S, D) for r in res.results], axis=0)
    return out.astype(np.float32)
```
